# Optimizing a Trainium2 kernel written in Bass

```python
import math
import jax
import jax.numpy as jnp
from jax import lax
import numpy as np

D_MODEL = 1024
BATCH = 8
SEQ = 4096
DEPTH = 4

N_BRANCHES = 3
NORM_EPS = 1e-6
D_HYENA = 512
HYENA_EMB_DIM = 33
HYENA_FILTER_ORDER = 64
HYENA_FAST_DECAY_PCT = 0.3
HYENA_SLOW_DECAY_PCT = 1.5
HYENA_DECAY_TARGET = 1e-2
GLA_HEADS = 4
GLA_KEY_DIM = 512
GLA_VALUE_DIM = 512
GLA_HEAD_K = GLA_KEY_DIM // GLA_HEADS
GLA_HEAD_V = GLA_VALUE_DIM // GLA_HEADS
GLA_GATE_RANK = 16
GLA_GATE_NORMALIZER = 16.0
GLA_LOG_GATE_MIN = -1.0
GLA_CHUNK = 64
D_POOL = 512
POOL_WINDOWS = (2, 4, 8, 16)
POOL_GROUP = D_POOL // len(POOL_WINDOWS)
D_FF = 2816
IN_SPLITS = (3 * D_HYENA, GLA_KEY_DIM, GLA_KEY_DIM, GLA_VALUE_DIM, GLA_VALUE_DIM,
             2 * GLA_GATE_RANK, D_POOL, N_BRANCHES * D_MODEL)
D_IN = 3 * D_HYENA + 2 * GLA_KEY_DIM + 2 * GLA_VALUE_DIM + 2 * GLA_GATE_RANK + D_POOL + N_BRANCHES * D_MODEL

kernel_name = "hybrid_hyena_gla_pool_encoder"


def _split(t, sizes):
    out, off = [], 0
    for s in sizes:
        out.append(t[..., off:off + s])
        off += s
    return out


def _rms_norm(x, g):
    xf = x.astype(jnp.float32)
    y = xf * lax.rsqrt(jnp.mean(xf * xf, axis=-1, keepdims=True) + NORM_EPS)
    return (y * g.astype(jnp.float32)).astype(x.dtype)


def _dwconv3_centred(x, w, b):
    L = x.shape[1]
    xp = jnp.pad(x, ((0, 0), (1, 1), (0, 0)))
    return xp[:, :L] * w[0] + xp[:, 1:L + 1] * w[1] + xp[:, 2:] * w[2] + b


def _hyena_positional_features(L):
    t = jnp.linspace(0.0, 1.0, L, dtype=jnp.float32)[:, None]
    bands = (HYENA_EMB_DIM - 1) // 2
    w = 2.0 * math.pi * jnp.arange(L, dtype=jnp.float32)[:, None] / L
    f = jnp.linspace(1e-4, bands - 1, bands, dtype=jnp.float32)[None, :]
    ang = f * w
    z = jnp.concatenate([t, jnp.cos(ang), -jnp.sin(ang)], axis=-1)
    return t, z


def _hyena_filter(t, z, w1, b1, fr1, w2, b2, fr2, w3):
    L = t.shape[0]
    h = jnp.sin(fr1 * (z @ w1 + b1))
    h = jnp.sin(fr2 * (h @ w2 + b2))
    h = (h @ w3).astype(jnp.float32)
    max_decay = math.log(HYENA_DECAY_TARGET) / HYENA_FAST_DECAY_PCT
    min_decay = math.log(HYENA_DECAY_TARGET) / HYENA_SLOW_DECAY_PCT
    deltas = jnp.linspace(min_decay, max_decay, D_HYENA, dtype=jnp.float32)
    decay = jnp.exp(-t * jnp.abs(deltas))
    h_fwd = h[:, :D_HYENA] * decay
    h_bwd = h[:, D_HYENA:] * decay
    zero = jnp.zeros((1, D_HYENA), jnp.float32)
    return jnp.concatenate([h_fwd, zero, h_bwd[:0:-1]], axis=0)


def _hyena_mixer(u, conv_w, conv_b, t, z, w1, b1, fr1, w2, b2, fr2, w3, bias):
    B, L, _ = u.shape
    u = _dwconv3_centred(u, conv_w, conv_b)
    x0, x1, v = _split(u, (D_HYENA, D_HYENA, D_HYENA))
    zv = (x1 * v).astype(jnp.float32)
    k = _hyena_filter(t, z, w1, b1, fr1, w2, b2, fr2, w3)
    Zf = jnp.fft.rfft(zv, n=2 * L, axis=1)
    Kf = jnp.fft.rfft(k, n=2 * L, axis=0)
    y = jnp.fft.irfft(Zf * Kf[None], n=2 * L, axis=1)[:, :L]
    y = y + zv * bias.astype(jnp.float32)
    return (x0.astype(jnp.float32) * y).astype(u.dtype)


def _gla_direction(q, k, v, log_a, include_diag):
    B, L, H, DK = q.shape
    DV = v.shape[-1]
    C = GLA_CHUNK
    N = L // C
    q = q.reshape(B, N, C, H, DK)
    k = k.reshape(B, N, C, H, DK)
    v = v.reshape(B, N, C, H, DV)
    b = jnp.cumsum(log_a.reshape(B, N, C, H, DK), axis=2)
    b_last = b[:, :, -1]
    q_e = q * jnp.exp(b)
    k_e = k * jnp.exp(-b)
    scores = jnp.einsum('bnthd,bnshd->bnhts', q_e, k_e)
    mask = jnp.tril(jnp.ones((C, C), dtype=bool), k=0 if include_diag else -1)
    scores = jnp.where(mask, scores, 0.0)
    o_intra = jnp.einsum('bnhts,bnshe->bnthe', scores, v)
    k_dec = k * jnp.exp(b_last[:, :, None] - b)
    U = jnp.einsum('bnshd,bnshe->bnhde', k_dec, v)
    decay = jnp.exp(b_last)

    def step(S, inp):
        dec, inc = inp
        return dec[..., None] * S + inc, S

    S0 = jnp.zeros((B, H, DK, DV), U.dtype)
    _, S_prev = lax.scan(step, S0, (jnp.moveaxis(decay, 1, 0), jnp.moveaxis(U, 1, 0)))
    S_prev = jnp.moveaxis(S_prev, 0, 1)
    o_inter = jnp.einsum('bnthd,bnhde->bnthe', q_e, S_prev)
    return (o_intra + o_inter).reshape(B, L, H, DV)


def _gla_mixer(q, k, v, g, gate_lr, gate_w2, gate_b, norm_g):
    B, L, _ = q.shape
    q = q.reshape(B, L, GLA_HEADS, GLA_HEAD_K) * (GLA_HEAD_K ** -0.5)
    k = k.reshape(B, L, GLA_HEADS, GLA_HEAD_K)
    v = v.reshape(B, L, GLA_HEADS, GLA_HEAD_V)
    lr = gate_lr.reshape(B, L, 2, GLA_GATE_RANK)
    gk = jnp.einsum('blzr,zrd->blzd', lr, gate_w2) + gate_b
    log_a = jnp.maximum(jax.nn.log_sigmoid(gk.astype(jnp.float32)) / GLA_GATE_NORMALIZER,
                        GLA_LOG_GATE_MIN).reshape(B, L, 2, GLA_HEADS, GLA_HEAD_K)
    o_fwd = _gla_direction(q, k, v, log_a[:, :, 0], True)
    fl = lambda a: jnp.flip(a, axis=1)
    o_bwd = fl(_gla_direction(fl(q), fl(k), fl(v), fl(log_a[:, :, 1]), False))
    o = _rms_norm(o_fwd + o_bwd, norm_g)
    return o.reshape(B, L, GLA_VALUE_DIM).astype(g.dtype) * jax.nn.silu(g)


def _pool_mixer(u, pool_w, pool_scale):
    B, L, _ = u.shape
    pos = jnp.arange(L)
    groups = []
    for gi, w in enumerate(POOL_WINDOWS):
        xg = u[..., gi * POOL_GROUP:(gi + 1) * POOL_GROUP].astype(jnp.float32)
        half = w // 2
        xp = jnp.pad(xg, ((0, 0), (half, half), (0, 0)))
        cs = jnp.pad(jnp.cumsum(xp, axis=1), ((0, 0), (1, 0), (0, 0)))
        wsum = cs[:, w:w + L] - cs[:, :L]
        count = (jnp.minimum(pos + half, L) - jnp.maximum(pos - half, 0)).astype(jnp.float32)
        groups.append(wsum / count[None, :, None] - xg)
    d = jnp.stack(groups, axis=2)
    y = jnp.einsum('blgc,gcd->blgd', d, pool_w.astype(jnp.float32)).reshape(B, L, D_POOL)
    return (y * pool_scale.astype(jnp.float32)).astype(u.dtype)


def setup_inputs(seed: int = 0) -> dict:
    key = jax.random.key(seed)
    ks = iter(jax.random.split(key, 40))
    f32 = jnp.float32
    n = DEPTH
    FO = HYENA_FILTER_ORDER

    def nrm(shape, scale):
        return jax.random.normal(next(ks), shape, f32) * scale

    def gain(shape, noise=0.05):
        return 1.0 + nrm(shape, noise)

    return {
        "x": nrm((BATCH, SEQ, D_MODEL), 1.0),
        "norm_mix_pre": gain((n, D_MODEL)),
        "norm_mix_post": gain((n, D_MODEL)),
        "norm_ffn_pre": gain((n, D_MODEL)),
        "norm_ffn_post": gain((n, D_MODEL)),
        "w_in": nrm((n, D_MODEL, D_IN), D_MODEL ** -0.5),
        "hy_conv_w": nrm((n, 3, 3 * D_HYENA), 3 ** -0.5),
        "hy_conv_b": nrm((n, 3 * D_HYENA), 0.02),
        "hy_filt_w1": nrm((n, HYENA_EMB_DIM, FO), HYENA_EMB_DIM ** -0.5),
        "hy_filt_b1": nrm((n, FO), 0.1),
        "hy_filt_freq1": gain((n, FO), 0.01),
        "hy_filt_w2": nrm((n, FO, FO), FO ** -0.5),
        "hy_filt_b2": nrm((n, FO), 0.1),
        "hy_filt_freq2": gain((n, FO), 0.01),
        "hy_filt_w3": nrm((n, FO, 2 * D_HYENA), 0.005),
        "hy_bias": nrm((n, D_HYENA), 1.0),
        "gla_gate_w2": nrm((n, 2, GLA_GATE_RANK, GLA_KEY_DIM), GLA_GATE_RANK ** -0.5),
        "gla_gate_b": nrm((n, 2, GLA_KEY_DIM), 0.1),
        "gla_norm": gain((n, GLA_HEAD_V), 0.01),
        "pool_w": nrm((n, len(POOL_WINDOWS), POOL_GROUP, POOL_GROUP), POOL_GROUP ** -0.5),
        "pool_scale": gain((n, D_POOL), 0.1),
        "w_br_hyena": nrm((n, D_HYENA, D_MODEL), D_HYENA ** -0.5),
        "w_br_gla": nrm((n, GLA_VALUE_DIM, D_MODEL), GLA_VALUE_DIM ** -0.5),
        "w_br_pool": nrm((n, D_POOL, D_MODEL), D_POOL ** -0.5),
        "w_out": nrm((n, D_MODEL, D_MODEL), D_MODEL ** -0.5),
        "ffn_w_up": nrm((n, D_MODEL, 2 * D_FF), D_MODEL ** -0.5),
        "ffn_conv_w": nrm((n, 3, D_FF), 3 ** -0.5),
        "ffn_conv_b": nrm((n, D_FF), 0.02),
        "ffn_w_down": nrm((n, D_FF, D_MODEL), D_FF ** -0.5),
    }


def reference(x, norm_mix_pre, norm_mix_post, norm_ffn_pre, norm_ffn_post, w_in,
              hy_conv_w, hy_conv_b, hy_filt_w1, hy_filt_b1, hy_filt_freq1, hy_filt_w2,
              hy_filt_b2, hy_filt_freq2, hy_filt_w3, hy_bias, gla_gate_w2, gla_gate_b,
              gla_norm, pool_w, pool_scale, w_br_hyena, w_br_gla, w_br_pool, w_out,
              ffn_w_up, ffn_conv_w, ffn_conv_b, ffn_w_down):
    B, L, _ = x.shape
    t_pos, z_pos = _hyena_positional_features(L)
    for i in range(DEPTH):
        h = _rms_norm(x, norm_mix_pre[i])
        proj = h @ w_in[i]
        u_hy, q, k, v, g_out, gate_lr, u_pool, gate_logits = _split(proj, IN_SPLITS)
        y_hy = _hyena_mixer(u_hy, hy_conv_w[i], hy_conv_b[i], t_pos, z_pos,
                            hy_filt_w1[i], hy_filt_b1[i], hy_filt_freq1[i], hy_filt_w2[i],
                            hy_filt_b2[i], hy_filt_freq2[i], hy_filt_w3[i], hy_bias[i])
        y_gla = _gla_mixer(q, k, v, g_out, gate_lr, gla_gate_w2[i], gla_gate_b[i], gla_norm[i])
        y_pool = _pool_mixer(u_pool, pool_w[i], pool_scale[i])
        gates = jax.nn.sigmoid(gate_logits).reshape(B, L, N_BRANCHES, D_MODEL)
        merged = (gates[:, :, 0] * (y_hy @ w_br_hyena[i])
                  + gates[:, :, 1] * (y_gla @ w_br_gla[i])
                  + gates[:, :, 2] * (y_pool @ w_br_pool[i]))
        x = x + _rms_norm(merged @ w_out[i], norm_mix_post[i])
        h = _rms_norm(x, norm_ffn_pre[i])
        a, b = _split(h @ ffn_w_up[i], (D_FF, D_FF))
        a = _dwconv3_centred(a, ffn_conv_w[i], ffn_conv_b[i])
        y = (jax.nn.gelu(a, approximate=False) * b) @ ffn_w_down[i]
        x = x + _rms_norm(y, norm_ffn_post[i])
    return x
```

```python
import contextlib
import math
import numpy as np
import ml_dtypes
import concourse.bass as bass
import concourse.mybir as mybir
from concourse.bass_utils import run_bass_kernel_spmd

F32 = mybir.dt.float32
BF16 = mybir.dt.bfloat16
AF = mybir.ActivationFunctionType
ALU = mybir.AluOpType
NDMA_SEM = 8

D = 1024
DH = 512
DIN = 7200
DFF = 2816
EPS = 1e-6


class T:
    def __init__(self, name, ap, parent=None):
        self.name = name
        self.t = ap
        if parent is None:
            self.w = {}
            self.r = {}
            self.root = self
        else:
            self.root = parent.root

    def __getitem__(self, idx):
        return self.t[idx]

    def view(self, ap):
        return T(self.name, ap, parent=self)


class Op:
    __slots__ = ("eng", "fn", "deps", "marked", "val", "sem", "isdma")

    def __init__(self, eng, fn):
        self.eng = eng
        self.fn = fn
        self.deps = []
        self.marked = False
        self.val = 0
        self.sem = None
        self.isdma = False


class _Rec:
    def __getattr__(self, name):
        def f(*a, **k):
            self.call = (name, a, k)
        return f


class KB:
    def __init__(self, sb_bytes):
        self.nc = bass.Bass("TRN2", target_bir_lowering=False)
        self.es = contextlib.ExitStack()
        self.ops = []
        nc = self.nc
        self.handles = {"pe": nc.tensor, "act": nc.scalar, "dve": nc.vector,
                        "pool": nc.gpsimd, "sp": nc.sync}
        self.sems = {e: self.es.enter_context(nc.semaphore("s_" + e)) for e in self.handles}
        self.dq = {}
        for q in ("sp", "act", "pool"):
            self.dq[q] = {"sems": [self.es.enter_context(nc.semaphore("d_%s%d" % (q, i)))
                                   for i in range(NDMA_SEM)],
                          "n": 0, "last": [None] * NDMA_SEM, "cnt": [0] * NDMA_SEM}
        self.last = {}
        self.arena = self.es.enter_context(nc.sbuf_tensor("arena", [128, sb_bytes // 2], BF16))
        self.sb_bytes = sb_bytes
        self.top = 0
        self.nbuf = 0
        self.pbanks = [self.es.enter_context(nc.psum_tensor("pb%d" % i, [128, 512], F32))
                       for i in range(8)]

    def sb(self, shape, dt, name=None):
        esz = 4 if dt == F32 else 2
        n = int(np.prod(shape[1:])) * esz
        n = (n + 63) // 64 * 64
        off = self.top
        self.top += n
        assert self.top <= self.sb_bytes, "SBUF arena overflow %d" % self.top
        ap = self.arena[:, off // 2:(off + n) // 2]
        if dt == F32:
            ap = ap.bitcast(F32)
        ap = ap[:, 0:int(np.prod(shape[1:]))]
        if len(shape) == 3:
            ap = ap.rearrange("p (a b) -> p a b", a=shape[1])
        elif len(shape) == 4:
            ap = ap.rearrange("p (a b c) -> p a b c", a=shape[1], b=shape[2])
        if shape[0] < 128:
            ap = ap[0:shape[0]]
        self.nbuf += 1
        return T(name or "sb%d" % self.nbuf, ap)

    def ps(self, bank, shape=None, dt=F32):
        ap = self.pbanks[bank][:]
        if dt == BF16:
            ap = ap.bitcast(BF16)
        if shape is not None and len(shape) == 3:
            ap = ap[:, 0:shape[1] * shape[2]].rearrange("p (a b) -> p a b", a=shape[1])
        elif shape is not None:
            ap = ap[:, 0:shape[1]]
        if shape is not None and shape[0] < 128:
            ap = ap[0:shape[0]]
        return ap

    def psT(self, bank):
        if not hasattr(self, "_pst"):
            self._pst = [T("pbank%d" % i, self.pbanks[i][:]) for i in range(8)]
        return self._pst[bank]

    def dram(self, name, shape, dt, kind="Internal"):
        return T(name, self.nc.dram_tensor(name, list(shape), dt, kind=kind).ap())

    @staticmethod
    def _norm(lst):
        out = []
        for x in lst:
            if isinstance(x, tuple):
                out.append((x[0].root, x[1]))
            else:
                out.append((x.root, None))
        return out

    def _hazards(self, op, reads, writes):
        deps = op.deps
        for t, key in reads:
            if key is None:
                deps.extend(t.w.values())
            else:
                for k in (key, None):
                    p = t.w.get(k)
                    if p is not None:
                        deps.append(p)
        for t, key in writes:
            if key is None:
                deps.extend(t.w.values())
                for l in t.r.values():
                    deps.extend(x for x in l if x.isdma or op.isdma or x.eng != op.eng or op.eng != 'pe')
            else:
                for k in (key, None):
                    p = t.w.get(k)
                    if p is not None:
                        deps.append(p)
                    deps.extend(x for x in t.r.get(k, ()) if x.isdma or op.isdma or x.eng != op.eng or op.eng != 'pe')
        for t, key in reads:
            l = t.r.setdefault(key, [])
            if not op.isdma:
                l[:] = [o for o in l if o.eng != op.eng or o.isdma]
            l.append(op)
        for t, key in writes:
            if key is None:
                t.w = {None: op}
                t.r = {}
            else:
                t.w[key] = op
                t.r[key] = []

    def op(self, eng, fn, reads=(), writes=()):
        rec = _Rec()
        fn(rec)
        name, a, k = rec.call
        o = Op(eng, lambda h: getattr(h, name)(*a, **k))
        self._hazards(o, self._norm(reads), self._norm(writes))
        if eng == "pe":
            o.deps = [d for d in o.deps if not (d.eng == "pe" and not d.isdma)]
        self.ops.append(o)
        self.last[eng] = o
        return o

    def dma(self, q, out, in_, reads=(), writes=()):
        o = Op(q, lambda e: e.dma_start(out=out, in_=in_))
        o.isdma = True
        dq = self.dq[q]
        i = dq["n"] % NDMA_SEM
        dq["n"] += 1
        if dq["last"][i] is not None:
            o.deps.append(dq["last"][i])
        dq["last"][i] = o
        dq["cnt"][i] += 16
        o.sem = dq["sems"][i]
        o.val = dq["cnt"][i]
        self._hazards(o, self._norm(reads), self._norm(writes))
        self.ops.append(o)
        return o

    def mark(self, label):
        o = Op("sp", None)
        o.sem = label
        o.marked = "label"
        self.ops.append(o)

    def barrier(self):
        deps = list(self.last.values())
        for q in self.dq.values():
            deps.extend(x for x in q["last"] if x is not None)
        for e in self.handles:
            o = Op(e, None)
            o.deps = list(deps)
            self.ops.append(o)

    def finish(self):
        self.barrier()
        for o in self.ops:
            for d in o.deps:
                if not d.isdma:
                    d.marked = True
        cnt = {e: 0 for e in self.handles}
        for o in self.ops:
            if not o.isdma and o.marked is True:
                cnt[o.eng] += 1
                o.val = cnt[o.eng]
                o.sem = self.sems[o.eng]
        seen = {e: {} for e in self.handles}
        nwait = 0
        self.marks = []
        for o in self.ops:
            if o.marked == "label":
                self.marks.append((o.sem, self.nc.get_next_instruction_name()))
                continue
            h = self.handles[o.eng]
            sn = seen[o.eng]
            for d in o.deps:
                k = id(d.sem)
                if sn.get(k, 0) >= d.val:
                    continue
                h.wait_ge(d.sem, d.val)
                nwait += 1
                sn[k] = d.val
            if o.fn is None:
                continue
            ins = o.fn(h)
            if o.isdma:
                ins.then_inc(o.sem, 16)
            elif o.marked is True:
                ins.then_inc(o.sem, 1)
        self.stats = {"ops": len(self.ops), "waits": nwait, "marked": dict(cnt)}
        return self.nc


def make_consts(L):
    bf = ml_dtypes.bfloat16
    c = {}
    c["ident"] = np.eye(128, dtype=np.float32).astype(bf)
    c["ones"] = np.ones((128, 128), np.float32).astype(bf)
    a = np.arange(128)
    uti = (a[:, None] <= a[None, :]).astype(np.float32)
    uts = (a[:, None] < a[None, :]).astype(np.float32)
    lti = (a[:, None] >= a[None, :]).astype(np.float32)
    lts = (a[:, None] > a[None, :]).astype(np.float32)
    c["tri32"] = np.stack([uti, uts, lti, lts], 1).astype(bf)
    t = np.linspace(0.0, 1.0, L, dtype=np.float32)
    bands = 16
    w = (2.0 * np.float32(math.pi) * np.arange(L, dtype=np.float32) / np.float32(L)).astype(np.float32)
    f = np.linspace(1e-4, bands - 1, bands, dtype=np.float32)
    ang = (f[None, :] * w[:, None]).astype(np.float32)
    z = np.concatenate([t[:, None], np.cos(ang), -np.sin(ang)], -1).astype(np.float32)
    c["zT"] = np.ascontiguousarray(z.T)
    c["tcol"] = np.ascontiguousarray(-t.reshape(L // 128, 128).T)
    max_decay = math.log(1e-2) / 0.3
    min_decay = math.log(1e-2) / 1.5
    deltas = np.abs(np.linspace(min_decay, max_decay, DH, dtype=np.float32))
    c["absd"] = np.ascontiguousarray(np.broadcast_to(deltas[None, :], (128, DH))).astype(np.float32)
    N = 2 * L
    N1 = N // 64
    H = N1 // 2
    n1 = np.arange(H, dtype=np.float64)[:, None, None]
    n2 = np.arange(64, dtype=np.float64)[None, :, None]
    f1 = np.arange(H, dtype=np.float64)[None, None, :]
    al = 2 * np.pi * ((f1 + 0.5) * n1 / N1 + (f1 + 0.5) * n2 / N)
    c["tw1"] = np.ascontiguousarray(np.stack([np.cos(al), -np.sin(al)], 1)).astype(bf)
    a64 = np.arange(64, dtype=np.float64)
    be = 2 * np.pi * np.outer(a64, a64) / 64
    c["dftm"] = np.ascontiguousarray(np.stack([np.cos(be), np.sin(be), -np.sin(be)], 1)).astype(bf)
    f2 = a64[:, None, None]
    f1b = np.arange(H, dtype=np.float64)[None, :, None]
    t2 = a64[None, None, :]
    ga = 2 * np.pi * (f2 * t2 / 64 + (f1b + 0.5) * t2 / N)
    c["gtw"] = np.ascontiguousarray(np.stack([np.cos(ga), np.sin(ga), -np.sin(ga)], 1)).astype(bf)
    f1c = np.arange(H, dtype=np.float64)[:, None]
    t1 = np.arange(H, dtype=np.float64)[None, :]
    ph = 2 * np.pi * (f1c + 0.5) * t1 / N1
    c["m4"] = np.ascontiguousarray(np.stack([(2.0 / N) * np.cos(ph), -(2.0 / N) * np.sin(ph)], 1)).astype(bf)
    pos = np.arange(L)
    inv = []
    for wv in (2, 4, 8, 16):
        half = wv // 2
        cntv = (np.minimum(pos + half, L) - np.maximum(pos - half, 0)).astype(np.float32)
        inv.append(1.0 / cntv)
    c["invcnt"] = np.ascontiguousarray(
        np.broadcast_to(np.stack(inv, 0)[:, None, :], (4, 128, L))).astype(np.float32)
    return c


def relayout_params(p, depth):
    f = np.float32
    o = {}

    def pk(v, nb):
        return np.ascontiguousarray(np.asarray(v, f).reshape(nb, 128).T)

    o["g_mix_pre"] = np.stack([pk(p["norm_mix_pre"][i], 8) for i in range(depth)])
    o["g_ffn_pre"] = np.stack([pk(p["norm_ffn_pre"][i], 8) for i in range(depth)])
    o["g_mix_post"] = np.ascontiguousarray(np.broadcast_to(
        np.asarray(p["norm_mix_post"], f)[:depth, None, :], (depth, 128, D)))
    o["g_ffn_post"] = np.ascontiguousarray(np.broadcast_to(
        np.asarray(p["norm_ffn_post"], f)[:depth, None, :], (depth, 128, D)))
    o["w_in"] = np.asarray(p["w_in"], f)[:depth]
    cw = np.asarray(p["hy_conv_w"], f)[:depth]
    o["hy_cw"] = np.ascontiguousarray(cw.reshape(depth, 3, 12, 128).transpose(0, 3, 2, 1))
    o["hy_cb"] = np.stack([pk(p["hy_conv_b"][i], 12) for i in range(depth)])
    o["hy_w1"] = np.asarray(p["hy_filt_w1"], f)[:depth]
    o["hy_w2"] = np.asarray(p["hy_filt_w2"], f)[:depth]
    o["hy_w3"] = np.asarray(p["hy_filt_w3"], f)[:depth]
    vec = np.stack([np.asarray(p["hy_filt_b1"], f)[:depth], np.asarray(p["hy_filt_freq1"], f)[:depth],
                    np.asarray(p["hy_filt_b2"], f)[:depth], np.asarray(p["hy_filt_freq2"], f)[:depth]], -1)
    o["hy_vec"] = np.ascontiguousarray(vec)
    o["hy_bias"] = np.stack([pk(p["hy_bias"][i], 4) for i in range(depth)])
    w2 = np.asarray(p["gla_gate_w2"], f)[:depth]
    gb = np.asarray(p["gla_gate_b"], f)[:depth]
    w2x = np.zeros((depth, 2, 33, 512), f)
    w2x[:, 0, 0:16] = w2[:, 0]
    w2x[:, 1, 16:32] = w2[:, 1]
    w2x[:, :, 32] = gb
    o["gla_w2x"] = np.ascontiguousarray(w2x.transpose(0, 2, 1, 3))
    o["gla_norm"] = np.asarray(p["gla_norm"], f)[:depth].reshape(depth, 128, 1)
    o["pool_w"] = np.ascontiguousarray(np.asarray(p["pool_w"], f)[:depth].transpose(0, 2, 1, 3))
    o["pool_scale"] = np.stack([pk(p["pool_scale"][i], 4) for i in range(depth)])
    o["w_br"] = np.ascontiguousarray(np.stack(
        [np.asarray(p["w_br_hyena"], f)[:depth], np.asarray(p["w_br_gla"], f)[:depth],
         np.asarray(p["w_br_pool"], f)[:depth]], 1))
    o["w_out"] = np.asarray(p["w_out"], f)[:depth]
    o["ffn_up"] = np.asarray(p["ffn_w_up"], f)[:depth]
    fw = np.asarray(p["ffn_conv_w"], f)[:depth]
    o["ffn_cw"] = np.ascontiguousarray(fw.reshape(depth, 3, 22, 128).transpose(0, 3, 2, 1))
    o["ffn_cb"] = np.stack([pk(p["ffn_conv_b"][i], 22) for i in range(depth)])
    o["ffn_down"] = np.asarray(p["ffn_w_down"], f)[:depth]
    return o


def build(L, depth, dbg=()):
    NT = L // 128
    NQ = L // 512
    NFB = 2 * NT
    HH = (2 * L // 64) // 2
    kb = KB(sb_bytes=200 * 1024)

    def din(name, shape, dt=F32):
        return kb.dram(name, shape, dt, kind="ExternalInput")

    def scr(name, shape, dt=BF16):
        return kb.dram(name, shape, dt, kind=("ExternalOutput" if name in dbg else "Internal"))

    x_in = din("x", [L, D])
    C = {k: din(k, list(v.shape), BF16 if v.dtype != np.float32 else F32)
         for k, v in make_consts(L if L <= 512 else 128 * 4).items()} if False else None
    cshapes = {"ident": ([128, 128], BF16), "ones": ([128, 128], BF16), "tri32": ([128, 4, 128], BF16), "zT": ([33, L], F32), "tcol": ([128, NT], F32),
               "absd": ([128, DH], F32), "tw1": ([HH, 2, 64, HH], BF16), "dftm": ([64, 3, 64], BF16),
               "gtw": ([64, 3, HH, 64], BF16), "m4": ([HH, 2, HH], BF16),
               "invcnt": ([4, 128, L], F32)}
    C = {k: din(k, s, dt) for k, (s, dt) in cshapes.items()}
    n = depth
    pshapes = {"g_mix_pre": [n, 128, 8], "g_ffn_pre": [n, 128, 8], "g_mix_post": [n, 128, D],
               "g_ffn_post": [n, 128, D], "w_in": [n, D, DIN], "hy_cw": [n, 128, 12, 3],
               "hy_cb": [n, 128, 12], "hy_w1": [n, 33, 64], "hy_w2": [n, 64, 64], "hy_w3": [n, 64, 1024],
               "hy_vec": [n, 64, 4], "hy_bias": [n, 128, 4], "gla_w2x": [n, 33, 2, 512],
               "gla_norm": [n, 128, 1], "pool_w": [n, 128, 4, 128], "pool_scale": [n, 128, 4],
               "w_br": [n, 3, DH, D], "w_out": [n, D, D], "ffn_up": [n, D, 2 * DFF],
               "ffn_cw": [n, 128, 22, 3], "ffn_cb": [n, 128, 22], "ffn_down": [n, DFF, D]}
    P = {k: din(k, s) for k, s in pshapes.items()}
    out = kb.dram("out", [L, D], F32, kind="ExternalOutput")

    XA = scr("XA", [L, D], F32)
    XB = scr("XB", [L, D], F32)
    X0fm = scr("X0fm", [DH, L])
    ZVfm = scr("ZVfm", [DH, L])
    ZVT = scr("ZVT", [L, DH])
    HSD = scr("HSD", [2, L, DH])
    A1Z = scr("A1Z", [2, HH, 64, DH])
    A1K = scr("A1K", [2, 2, HH, 64, DH])
    KS = scr("KS", [2, HH, 64, DH])
    B1 = scr("B1", [2, 64, HH, DH])
    Qfm = scr("Qfm", [DH, L])
    Kfm = scr("Kfm", [DH, L])
    Gfm = scr("Gfm", [DH, L])
    Ktok = scr("Ktok", [L, DH])
    Vtok = scr("Vtok", [L, DH])
    YH = scr("YH", [DH, L])
    YG = scr("YG", [DH, L])
    YP = scr("YP", [DH, L])
    GATES = scr("GATES", [3 * D, L])
    ACTfm = scr("ACTfm", [DFF, L])

    HT = kb.sb([128, 8, L], BF16, "HT")
    ident = kb.sb([128, 128], BF16, "ident")
    ones = kb.sb([128, 128], BF16, "ones")
    tri32 = kb.sb([128, 4, 128], BF16, "tri32")
    LRH = kb.sb([33, L], BF16, "LRH")
    LRL = kb.sb([33, L], BF16, "LRL")
    kb.dma("sp", ident[:], C["ident"][:, :], writes=[ident])
    kb.dma("sp", ones[:], C["ones"][:, :], writes=[ones])
    kb.dma("sp", tri32[:], C["tri32"][:, :, :], writes=[tri32])
    kb.op("dve", lambda e: e.memset(LRH[32:33, :], 1.0), writes=[LRH])
    kb.op("dve", lambda e: e.memset(LRL[32:33, :], 0.0), writes=[LRL])
    base_top = kb.top
    PB = [kb.psT(i) for i in range(8)]
    st = {"wb": 0, "pb": 0}

    def dump(name, t_, shape, dt=F32):
        if name in dbg:
            d_ = kb.dram(name, shape, dt, kind="ExternalOutput")
            kb.dma("sp", d_.t, t_.t, reads=[t_])

    def phase_begin(nwb=0, wf=False, label=None):
        kb.barrier()
        import inspect
        kb.mark(label or inspect.stack()[1].function + ":%d" % inspect.stack()[1].lineno)
        kb.top = base_top
        if wf or nwb:
            st["WF"] = kb.sb([128, 4096], F32, "WF")
        st["WB"] = [kb.sb([128, 6144], BF16, "WB%d" % i) for i in range(nwb)]

    def load_w(wap, kblocks, c0, ncols, gt=None):
        i = st["wb"] % len(st["WB"])
        st["wb"] += 1
        wb = st["WB"][i]
        WF = st["WF"]
        wbv = wb.view(wb.t[:, 0:kblocks * ncols].rearrange("p (k c) -> p k c", k=kblocks))
        kper = max(1, 4096 // ncols)
        src = wap.rearrange("(k p) c -> p k c", p=128)
        k0 = 0
        while k0 < kblocks:
            kn = min(kper, kblocks - k0)
            wfv = WF.t[:, 0:kn * ncols].rearrange("p (k c) -> p k c", k=kn)
            kb.dma("sp", wfv, src[:, k0:k0 + kn, c0:c0 + ncols], writes=[WF])
            if gt is None:
                kb.op("pool", lambda e, wfv=wfv, k0=k0, kn=kn: e.tensor_copy(out=wbv.t[:, k0:k0 + kn, :], in_=wfv),
                      reads=[WF], writes=[(wb, k0)])
            else:
                kb.op("pool", lambda e, wfv=wfv, k0=k0, kn=kn: e.tensor_tensor(
                    out=wbv.t[:, k0:k0 + kn, :], in0=wfv,
                    in1=gt.t[:, k0:k0 + kn, None].to_broadcast([128, kn, ncols]), op=ALU.mult),
                    reads=[WF, gt], writes=[(wb, k0)])
            k0 += kn
        return wbv

    def next_pb(nb=6):
        b = st["pb"] % nb
        st["pb"] += 1
        return b

    def gemm_fm(wbv, kblocks, mcol, mw, src, j, bank):
        for k in range(kblocks):
            kb.op("pe", lambda e, k=k: e.matmul(kb.ps(bank)[0:mw, :], lhsT=wbv.t[:, k, mcol:mcol + mw],
                                                rhs=src.t[:, k, j * 512:(j + 1) * 512],
                                                start=(k == 0), stop=(k == kblocks - 1)),
                  reads=[wbv, src], writes=[PB[bank]])

    def gemm_tok(wbv, kblocks, c0, ncols, src, i, bank, srckey=None):
        for k in range(kblocks):
            kb.op("pe", lambda e, k=k: e.matmul(kb.ps(bank)[:, 0:ncols], lhsT=src.t[:, k, i * 128:(i + 1) * 128],
                                                rhs=wbv.t[:, k, c0:c0 + ncols],
                                                start=(k == 0), stop=(k == kblocks - 1)),
                  reads=[wbv, (src, srckey) if srckey is not None else src], writes=[PB[bank]])

    def norm_transpose(xt, i, tmp):
        junk, sq, rs, hn = tmp["junk"], tmp["sq"], tmp["rs"], tmp["hn"]
        kb.op("act", lambda e: e.activation(out=junk[:], in_=xt[:], func=AF.Square, scale=1.0 / 32.0,
                                            accum_out=sq[:]), reads=[xt], writes=[junk, sq])
        kb.op("act", lambda e: e.activation(out=rs[:], in_=sq[:], func=AF.Ln, bias=EPS), reads=[sq], writes=[rs])
        kb.op("act", lambda e: e.activation(out=rs[:], in_=rs[:], func=AF.Exp, scale=-0.5), reads=[rs], writes=[rs])
        kb.op("dve", lambda e: e.tensor_scalar(out=hn[:], in0=xt[:], scalar1=rs[:], scalar2=None, op0=ALU.mult),
              reads=[xt, rs], writes=[hn])
        for k in range(8):
            kb.op("pe", lambda e, k=k: e.transpose(out=kb.ps(7, [128, 8, 128], BF16)[:, k, :],
                                                   in_=hn[:, k * 128:(k + 1) * 128], identity=ident[:]),
                  reads=[hn, ident], writes=[PB[7]])
        kb.op("act", lambda e: e.copy(out=HT[:, :, i * 128:(i + 1) * 128], in_=kb.ps(7, [128, 8, 128], BF16)),
              reads=[PB[7]], writes=[(HT, i)])

    def norm_tmp():
        return [{"junk": kb.sb([128, D], BF16), "sq": kb.sb([128, 1], F32), "rs": kb.sb([128, 1], F32),
                 "hn": kb.sb([128, D], BF16)} for _ in range(2)]

    TWO_PI = 2.0 * math.pi

    def phase_filter(li):
        phase_begin()
        HS = HT.view(HT.t[:, :, :].rearrange("p a b -> p (a b)")[:, 0:NT * 1024].rearrange(
            "p (n c) -> p n c", n=NT))
        w1 = kb.sb([33, 64], F32)
        w2 = kb.sb([64, 64], F32)
        w3 = kb.sb([64, 1024], F32)
        vec = kb.sb([64, 4], F32)
        pv = kb.sb([64, 2], F32)
        absd = kb.sb([128, DH], F32)
        tcol = kb.sb([128, NT], F32)
        H1 = kb.sb([64, L], F32)
        H2 = kb.sb([64, L], F32)
        kb.dma("sp", w1[:], P["hy_w1"][li], writes=[w1])
        kb.dma("sp", w2[:], P["hy_w2"][li], writes=[w2])
        kb.dma("sp", w3[:], P["hy_w3"][li], writes=[w3])
        kb.dma("sp", vec[:], P["hy_vec"][li], writes=[vec])
        kb.dma("sp", absd[:], C["absd"][:, :], writes=[absd])
        kb.dma("sp", tcol[:], C["tcol"][:, :], writes=[tcol])
        kb.op("dve", lambda e: e.tensor_tensor(out=pv[:, 0:1], in0=vec[:, 0:1], in1=vec[:, 1:2], op=ALU.mult),
              reads=[vec], writes=[pv])
        kb.op("dve", lambda e: e.tensor_tensor(out=pv[:, 1:2], in0=vec[:, 2:3], in1=vec[:, 3:4], op=ALU.mult),
              reads=[vec], writes=[pv])
        zt = [kb.sb([33, 512], F32) for _ in range(2)]
        arg = [kb.sb([64, 512], F32) for _ in range(2)]

        def sin_layer(wt, kdim, srcfn, dst, frcol, pvcol, j, bank):
            a = arg[j % 2]
            src, srcT = srcfn(j)
            kb.op("pe", lambda e: e.matmul(kb.ps(bank)[0:64, :], lhsT=wt[0:kdim, :], rhs=src,
                                           start=True, stop=True), reads=[wt, srcT], writes=[PB[bank]])
            kb.op("dve", lambda e: e.tensor_scalar(out=a[:], in0=kb.ps(bank)[0:64, :], scalar1=vec[:, frcol:frcol + 1],
                                                   scalar2=pv[:, pvcol:pvcol + 1], op0=ALU.mult, op1=ALU.add),
                  reads=[PB[bank], vec, pv], writes=[a])
            kb.op("dve", lambda e: e.tensor_scalar(out=ni[:], in0=a[:], scalar1=1.0 / TWO_PI, scalar2=None, op0=ALU.mult),
                  reads=[a], writes=[ni])
            kb.op("dve", lambda e: e.scalar_tensor_tensor(out=a[:], in0=ni[:], scalar=-TWO_PI, in1=a[:], op0=ALU.mult, op1=ALU.add),
                  reads=[ni, a], writes=[a])
            kb.op("dve", lambda e: e.tensor_scalar(out=m1[:], in0=a[:], scalar1=math.pi, scalar2=-TWO_PI, op0=ALU.is_gt, op1=ALU.mult),
                  reads=[a], writes=[m1])
            kb.op("dve", lambda e: e.tensor_scalar(out=m2[:], in0=a[:], scalar1=-math.pi, scalar2=TWO_PI, op0=ALU.is_lt, op1=ALU.mult),
                  reads=[a], writes=[m2])
            kb.op("dve", lambda e: e.tensor_tensor(out=a[:], in0=a[:], in1=m1[:], op=ALU.add), reads=[a, m1], writes=[a])
            kb.op("dve", lambda e: e.tensor_tensor(out=a[:], in0=a[:], in1=m2[:], op=ALU.add), reads=[a, m2], writes=[a])
            kb.op("dve", lambda e: e.tensor_scalar(out=a[:], in0=a[:], scalar1=math.pi, scalar2=-math.pi, op0=ALU.min, op1=ALU.max),
                  reads=[a], writes=[a])
            kb.op("act", lambda e: e.activation(out=dst[:, j * 512:(j + 1) * 512], in_=a[:], func=AF.Sin), reads=[a], writes=[(dst, j)])

        negpi = kb.sb([128, 1], F32)
        ni = kb.sb([64, 512], F32)
        ni = ni.view(ni.t.bitcast(mybir.dt.int32))
        m1 = kb.sb([64, 512], F32)
        m2 = kb.sb([64, 512], F32)
        kb.op("dve", lambda e: e.memset(negpi[:], -math.pi), writes=[negpi])
        for j in range(NQ):
            z = zt[j % 2]
            kb.dma("sp", z[:], C["zT"][:, j * 512:(j + 1) * 512], writes=[z])
            sin_layer(w1, 33, lambda j, z=z: (z[:], z), H1, 1, 0, j, next_pb())
        for j in range(NQ):
            sin_layer(w2, 64, lambda j: (H1[:, j * 512:(j + 1) * 512], H1), H2, 3, 1, j, next_pb())
        hsds = [kb.sb([128, 2, DH], BF16) for _ in range(2)]
        dec = [kb.sb([128, DH], F32) for _ in range(2)]
        t1 = [kb.sb([128, DH], F32) for _ in range(2)]
        t2 = [kb.sb([128, DH], F32) for _ in range(2)]
        for i in range(NT):
            dc, a1, a2 = dec[i % 2], t1[i % 2], t2[i % 2]
            b0, b1 = next_pb(), next_pb()
            for half, bank in ((0, b0), (1, b1)):
                kb.op("pe", lambda e, half=half, bank=bank: e.matmul(
                    kb.ps(bank), lhsT=H2[:, i * 128:(i + 1) * 128], rhs=w3[:, half * 512:(half + 1) * 512],
                    start=True, stop=True), reads=[H2, w3], writes=[PB[bank]])
            kb.op("act", lambda e, dc=dc: e.activation(out=dc[:], in_=absd[:], func=AF.Exp, scale=tcol[:, i:i + 1]),
                  reads=[absd, tcol], writes=[dc])
            kb.op("dve", lambda e, dc=dc, a1=a1, b0=b0: e.tensor_tensor(out=a1[:], in0=kb.ps(b0), in1=dc[:], op=ALU.mult),
                  reads=[PB[b0], dc], writes=[a1])
            kb.op("dve", lambda e, dc=dc, a2=a2, b1=b1: e.tensor_tensor(out=a2[:], in0=kb.ps(b1), in1=dc[:], op=ALU.mult),
                  reads=[PB[b1], dc], writes=[a2])
            if i == 0:
                kb.op("dve", lambda e, a2=a2: e.memset(a2[0:1, :], 0.0), reads=[a2], writes=[a2])
            hsd = hsds[i % 2]
            kb.op("pool", lambda e, a1=a1, a2=a2: e.tensor_tensor(out=hsd[:, 0, :], in0=a1[:], in1=a2[:], op=ALU.add),
                  reads=[a1, a2], writes=[(hsd, 0)])
            kb.op("pool", lambda e, a1=a1, a2=a2: e.tensor_tensor(out=hsd[:, 1, :], in0=a1[:], in1=a2[:], op=ALU.subtract),
                  reads=[a1, a2], writes=[(hsd, 1)])
            kb.dma("pool", HSD.t.rearrange("s n c -> n s c")[i * 128:(i + 1) * 128, :, :], hsd[:], reads=[hsd])
        phase_begin(label="filter_s1")
        tw1 = load_tw1()
        s1b = s1_bufs()
        fft_s1(HSD[0], A1K.t[0], tw1, s1b)
        fft_s1(HSD[1], A1K.t[1], tw1, s1b)
        phase_begin(label="filter_s2")
        dftm = kb.sb([64, 3, 64], BF16)
        kb.dma("sp", dftm[:], C["dftm"][:, :, :], writes=[dftm])
        FC = min(4, HH)
        at_ = [[kb.sb([64, FC, DH], BF16) for _ in range(4)] for _ in range(2)]
        ko = [[kb.sb([64, FC, DH], BF16) for _ in range(2)] for _ in range(2)]
        for c_ in range(HH // FC):
            f0 = c_ * FC
            A = at_[c_ % 2]
            for q_, (sg_, ri_) in enumerate(((0, 0), (0, 1), (1, 0), (1, 1))):
                kb.dma("sp", A[q_][:], A1K.t[sg_, ri_].rearrange("f n c -> n f c")[:, f0:f0 + FC, :], writes=[A[q_]])
            kos = ko[c_ % 2]
            for fl in range(FC):
                br, bi = next_pb(), next_pb()
                kb.op("pe", lambda e: e.matmul(kb.ps(br)[0:64, :], lhsT=dftm[:, 0, :], rhs=A[0][:, fl, :], start=True, stop=False),
                      reads=[dftm, A[0]], writes=[PB[br]])
                kb.op("pe", lambda e: e.matmul(kb.ps(br)[0:64, :], lhsT=dftm[:, 1, :], rhs=A[1][:, fl, :], start=False, stop=True),
                      reads=[dftm, A[1]], writes=[PB[br]])
                kb.op("pe", lambda e: e.matmul(kb.ps(bi)[0:64, :], lhsT=dftm[:, 0, :], rhs=A[3][:, fl, :], start=True, stop=False),
                      reads=[dftm, A[3]], writes=[PB[bi]])
                kb.op("pe", lambda e: e.matmul(kb.ps(bi)[0:64, :], lhsT=dftm[:, 2, :], rhs=A[2][:, fl, :], start=False, stop=True),
                      reads=[dftm, A[2]], writes=[PB[bi]])
                kb.op("act", lambda e: e.copy(out=kos[0][:, fl, :], in_=kb.ps(br)[0:64, :]), reads=[PB[br]], writes=[(kos[0], fl)])
                kb.op("dve", lambda e: e.tensor_copy(out=kos[1][:, fl, :], in_=kb.ps(bi)[0:64, :]), reads=[PB[bi]], writes=[(kos[1], fl)])
            for ri_ in range(2):
                kb.dma("pool", KS.t[ri_, f0:f0 + FC].rearrange("f k c -> k f c"), kos[ri_][:], reads=[kos[ri_]])

    def load_tw1():
        tw1 = kb.sb([HH, 2, 64, HH], BF16)
        kb.dma("sp", tw1[:], C["tw1"][:, :, :, :], writes=[tw1])
        return tw1

    def s1_bufs():
        return ([kb.sb([HH, 8, DH], BF16) for _ in range(2)],
                [[kb.sb([HH, 8, DH], BF16) for _ in range(2)] for _ in range(2)])

    def fft_s1(src, dst, tw1, s1b):
        xv = src.rearrange("(a b) c -> a b c", b=64)
        if not hasattr(fft_s1, "bufs"):
            pass
        xt, ot = s1b
        ne = 0
        for g in range(8):
            x_ = xt[g % 2]
            kb.dma("sp", x_[:], xv[:, g * 8:(g + 1) * 8, :], writes=[x_])
            for ri_ in range(2):
                o_ = ot[g % 2][ri_]
                for nl in range(8):
                    bank = next_pb()
                    kb.op("pe", lambda e: e.matmul(kb.ps(bank)[0:HH, :], lhsT=tw1[:, ri_, g * 8 + nl, :], rhs=x_[:, nl, :],
                                                   start=True, stop=True), reads=[tw1, x_], writes=[PB[bank]])
                    if ne % 2 == 0:
                        kb.op("act", lambda e: e.copy(out=o_[:, nl, :], in_=kb.ps(bank)[0:HH, :]), reads=[PB[bank]], writes=[(o_, nl)])
                    else:
                        kb.op("dve", lambda e: e.tensor_copy(out=o_[:, nl, :], in_=kb.ps(bank)[0:HH, :]), reads=[PB[bank]], writes=[(o_, nl)])
                    ne += 1
                kb.dma("pool", dst[ri_, :, g * 8:(g + 1) * 8, :], o_[:], reads=[o_])

    def phase_norm0(xsrc):
        phase_begin()
        tmps = norm_tmp()
        xts = [kb.sb([128, D], F32) for _ in range(2)]
        for i in range(NT):
            xt = xts[i % 2]
            kb.dma("sp", xt[:], xsrc[i * 128:(i + 1) * 128, :], writes=[xt])
            norm_transpose(xt, i, tmps[i % 2])

    def phase_proj(li):
        phase_begin(0)
        gt = kb.sb([128, 8], F32)
        kb.dma("sp", gt[:], P["g_mix_pre"][li], writes=[gt])
        W = P["w_in"][li]
        ev = [kb.sb([128, 512], BF16) for _ in range(4)]
        evc = [0]

        def next_ev():
            evc[0] += 1
            return ev[evc[0] % 4]

        cw = kb.sb([128, 12, 3], F32)
        cb = kb.sb([128, 12], F32)
        kb.dma("sp", cw[:], P["hy_cw"][li], writes=[cw])
        kb.dma("sp", cb[:], P["hy_cb"][li], writes=[cb])
        raws = [kb.sb([128, L + 2], BF16) for _ in range(2)]
        for r in raws:
            kb.op("pool", lambda e, r=r: e.memset(r[:, 0:1], 0.0), writes=[(r, "h0")])
            kb.op("pool", lambda e, r=r: e.memset(r[:, L + 1:L + 2], 0.0), writes=[(r, "h1")])
        acc = kb.sb([128, L], F32)
        x1c = kb.sb([128, L], BF16)
        oc = [kb.sb([128, L], BF16) for _ in range(2)]
        tz = [kb.sb([128, 8, 128], BF16) for _ in range(2)]
        wfs = [kb.sb([128, 8, 128], F32) for _ in range(2)]
        wbs = [kb.sb([128, 8, 128], BF16) for _ in range(2)]
        wsrc = W.rearrange("(k p) c -> p k c", p=128)
        order = [(b, part) for b in range(4) for part in range(3)]

        def prep(n_):
            b_, part_ = order[n_]
            blk_ = part_ * 4 + b_
            s_ = n_ % 2
            kb.dma("sp", wfs[s_][:], wsrc[:, :, blk_ * 128:(blk_ + 1) * 128], writes=[wfs[s_]])
            kb.op("pool", lambda e: e.tensor_tensor(out=wbs[s_][:], in0=wfs[s_][:],
                                                    in1=gt[:, :, None].to_broadcast([128, 8, 128]), op=ALU.mult),
                  reads=[wfs[s_], gt], writes=[wbs[s_]])
            return wbs[s_]

        zvt_v = ZVT.t.rearrange("(i p) c -> p i c", p=128)
        TB = min(8, NT)

        def transposes(dst, b):
            for i0 in range(0, NT, TB):
                tzt = tz[(i0 // TB) % 2]
                for ii in range(TB):
                    kb.op("pe", lambda e, ii=ii: e.transpose(out=kb.ps(7, [128, 8, 128], BF16)[:, ii, :],
                                                             in_=dst[:, (i0 + ii) * 128:(i0 + ii + 1) * 128], identity=ident[:]),
                          reads=[dst, ident], writes=[PB[7]])
                kb.op("act", lambda e: e.copy(out=tzt[:, 0:TB, :], in_=kb.ps(7, [128, 8, 128], BF16)[:, 0:TB, :]),
                      reads=[PB[7]], writes=[tzt])
                kb.dma("pool", zvt_v[:, i0:i0 + TB, b * 128:(b + 1) * 128], tzt[:, 0:TB, :], reads=[tzt])

        wnext = prep(0)
        pending = None
        for n_, (b, part) in enumerate(order):
            blk = part * 4 + b
            raw = raws[n_ % 2]
            wbv = wnext
            if n_ + 1 < len(order):
                wnext = prep(n_ + 1)
            for j in range(NQ):
                bank = next_pb()
                gemm_fm(wbv, 8, 0, 128, HT, j, bank)
                kb.op("act", lambda e: e.copy(out=raw[:, 1 + j * 512:1 + (j + 1) * 512], in_=kb.ps(bank)),
                      reads=[PB[bank]], writes=[(raw, j)])
            if pending is not None:
                transposes(*pending)
                pending = None
            dst = x1c if part == 1 else oc[0 if part == 0 else 1]
            kb.op("dve", lambda e: e.tensor_scalar(out=acc[:], in0=raw[:, 0:L], scalar1=cw[:, blk, 0:1], scalar2=cb[:, blk:blk + 1],
                                                   op0=ALU.mult, op1=ALU.add), reads=[raw, cw, cb], writes=[acc])
            kb.op("dve", lambda e: e.scalar_tensor_tensor(out=acc[:], in0=raw[:, 1:L + 1], scalar=cw[:, blk, 1:2], in1=acc[:],
                                                          op0=ALU.mult, op1=ALU.add), reads=[raw, cw, acc], writes=[acc])
            kb.op("dve", lambda e: e.scalar_tensor_tensor(out=dst[:], in0=raw[:, 2:L + 2], scalar=cw[:, blk, 2:3], in1=acc[:],
                                                          op0=ALU.mult, op1=ALU.add), reads=[raw, cw, acc], writes=[dst])
            if part == 0:
                kb.dma("pool", X0fm[b * 128:(b + 1) * 128, :], dst[:], reads=[dst])
            elif part == 2:
                kb.op("dve", lambda e: e.tensor_tensor(out=dst[:], in0=dst[:], in1=x1c[:], op=ALU.mult),
                      reads=[dst, x1c], writes=[dst])
                kb.dma("pool", ZVfm[b * 128:(b + 1) * 128, :], dst[:], reads=[dst])
                pending = (dst, b)
        transposes(*pending)

        phase_begin(2)
        gt = kb.sb([128, 8], F32)
        kb.dma("sp", gt[:], P["g_mix_pre"][li], writes=[gt])
        ev = [kb.sb([128, 512], BF16) for _ in range(4)]
        def fm_group(col0, ncols, dest, func, scale=1.0):
            for c0 in range(0, ncols, 512):
                nc_ = min(512, ncols - c0)
                wbv = load_w(W, 8, col0 + c0, nc_, gt)
                for m in range(nc_ // 128):
                    for j in range(NQ):
                        bank = next_pb()
                        gemm_fm(wbv, 8, m * 128, 128, HT, j, bank)
                        o = next_ev()
                        kb.op("act", lambda e, o=o, bank=bank: e.activation(out=o[:], in_=kb.ps(bank), func=func, scale=scale),
                              reads=[PB[bank]], writes=[o])
                        r0 = c0 + m * 128
                        kb.dma("pool", dest[r0:r0 + 128, j * 512:(j + 1) * 512], o[:], reads=[o])

        def tok_group(col0, dest):
            wbv = load_w(W, 8, col0, 512, gt)
            for i in range(NT):
                bank = next_pb()
                gemm_tok(wbv, 8, 0, 512, HT, i, bank)
                o = next_ev()
                if i % 2 == 0:
                    kb.op("dve", lambda e, o=o, bank=bank: e.tensor_copy(out=o[:], in_=kb.ps(bank)), reads=[PB[bank]], writes=[o])
                else:
                    kb.op("act", lambda e, o=o, bank=bank: e.copy(out=o[:], in_=kb.ps(bank)), reads=[PB[bank]], writes=[o])
                kb.dma("pool", dest[i * 128:(i + 1) * 128, :], o[:], reads=[o])

        fm_group(1536, 512, Qfm, AF.Copy, 128.0 ** -0.5)
        fm_group(2048, 512, Kfm, AF.Copy)
        tok_group(2048, Ktok)
        tok_group(2560, Vtok)
        fm_group(3072, 512, Gfm, AF.Silu)
        wbv = load_w(W, 8, 3584, 32, gt)
        for j in range(NQ):
            bank = next_pb()
            gemm_fm(wbv, 8, 0, 32, HT, j, bank)
            kb.op("act", lambda e, j=j, bank=bank: e.copy(out=LRH[0:32, j * 512:(j + 1) * 512], in_=kb.ps(bank)[0:32, :]),
                  reads=[PB[bank]], writes=[(LRH, j)])
            kb.op("dve", lambda e, j=j, bank=bank: e.tensor_tensor(out=LRL[0:32, j * 512:(j + 1) * 512], in0=kb.ps(bank)[0:32, :],
                                                                   in1=LRH[0:32, j * 512:(j + 1) * 512], op=ALU.subtract),
                  reads=[PB[bank], (LRH, j)], writes=[(LRL, j)])
        fm_group(4128, 3 * D, GATES, AF.Sigmoid)

        phase_begin(1)
        gt = kb.sb([128, 8], F32)
        kb.dma("sp", gt[:], P["g_mix_pre"][li], writes=[gt])
        ev = [kb.sb([128, 512], BF16) for _ in range(4)]
        PW = 16
        ua = kb.sb([128, L + 2 * PW], F32)
        ub = kb.sb([128, L + 2 * PW], F32)
        uc = kb.sb([128, L + 2 * PW], F32)
        icn = kb.sb([128, L], F32)
        pwt = kb.sb([128, 4, 128], F32)
        pwb = kb.sb([128, 4, 128], BF16)
        psc = kb.sb([128, 4], F32)
        dbf = kb.sb([128, L], BF16)
        kb.dma("sp", pwt[:], P["pool_w"][li], writes=[pwt])
        kb.dma("sp", psc[:], P["pool_scale"][li], writes=[psc])
        kb.op("pool", lambda e: e.tensor_copy(out=pwb[:], in_=pwt[:]), reads=[pwt], writes=[pwb])
        for t_ in (ua, ub, uc):
            kb.op("pool", lambda e, t_=t_: e.memset(t_[:], 0.0), writes=[t_])
        for gi, wv in enumerate((2, 4, 8, 16)):
            wbv = load_w(W, 8, 3616 + gi * 128, 128, gt)
            kb.dma("sp", icn[:], C["invcnt"][gi], writes=[icn])
            for j in range(NQ):
                bank = next_pb()
                gemm_fm(wbv, 8, 0, 128, HT, j, bank)
                kb.op("act", lambda e, j=j, bank=bank: e.copy(out=ua[:, PW + j * 512:PW + (j + 1) * 512], in_=kb.ps(bank)),
                      reads=[PB[bank]], writes=[ua])
            src, dsts = ua, [ub, uc]
            lo, hi = -14, L + 14
            kb.op("dve", lambda e, lo=lo, hi=hi: e.tensor_tensor(
                out=ub[:, PW + lo:PW + hi], in0=ua[:, PW + lo - 1:PW + hi - 1], in1=ua[:, PW + lo:PW + hi], op=ALU.add),
                reads=[ua], writes=[ub])
            cur, oth = ub, uc
            sh = 1
            rng = [(-12, L + 12), (-8, L + 8), (0, L)]
            for si in range(int(math.log2(wv)) - 1):
                lo, hi = rng[si]
                kb.op("dve", lambda e, lo=lo, hi=hi, cur=cur, oth=oth, sh=sh: e.tensor_tensor(
                    out=oth[:, PW + lo:PW + hi], in0=cur[:, PW + lo - sh:PW + hi - sh],
                    in1=cur[:, PW + lo + sh:PW + hi + sh], op=ALU.add), reads=[cur], writes=[oth])
                cur, oth = oth, cur
                sh *= 2
            kb.op("dve", lambda e, cur=cur, oth=oth: e.tensor_tensor(out=oth[:, PW:PW + L], in0=cur[:, PW:PW + L], in1=icn[:], op=ALU.mult),
                  reads=[cur, icn], writes=[oth])
            kb.op("dve", lambda e, oth=oth: e.tensor_tensor(out=dbf[:], in0=oth[:, PW:PW + L], in1=ua[:, PW:PW + L], op=ALU.subtract),
                  reads=[oth, ua], writes=[dbf])
            for t_ in (ub, uc):
                kb.op("pool", lambda e, t_=t_: e.memset(t_[:, 0:PW], 0.0), reads=[t_], writes=[t_])
                kb.op("pool", lambda e, t_=t_: e.memset(t_[:, PW + L:PW + L + PW], 0.0), reads=[t_], writes=[t_])
            for j in range(NQ):
                bank = next_pb()
                kb.op("pe", lambda e, j=j, bank=bank, gi=gi: e.matmul(kb.ps(bank), lhsT=pwb[:, gi, :], rhs=dbf[:, j * 512:(j + 1) * 512],
                                                               start=True, stop=True), reads=[pwb, dbf], writes=[PB[bank]])
                o = next_ev()
                kb.op("dve", lambda e, o=o, bank=bank, gi=gi: e.tensor_scalar(out=o[:], in0=kb.ps(bank), scalar1=psc[:, gi:gi + 1],
                                                                       scalar2=None, op0=ALU.mult), reads=[PB[bank], psc], writes=[o])
                kb.dma("pool", YP[gi * 128:(gi + 1) * 128, j * 512:(j + 1) * 512], o[:], reads=[o])

    def phase_hyena(li):
        phase_begin(label="hy_s1")
        tw1 = load_tw1()
        fft_s1(ZVT.t, A1Z.t, tw1, s1_bufs())
        phase_begin(label="hy_s2")
        dftm = kb.sb([64, 3, 64], BF16)
        gtw = kb.sb([64, 3, HH, 64], BF16)
        kb.dma("sp", dftm[:], C["dftm"][:, :, :], writes=[dftm])
        kb.dma("sp", gtw[:], C["gtw"][:, :, :, :], writes=[gtw])
        FC = min(4, HH)
        at_ = [[kb.sb([64, FC, DH], BF16) for _ in range(2)] for _ in range(2)]
        kt_ = [[kb.sb([64, FC, DH], BF16) for _ in range(2)] for _ in range(2)]
        bo = [[kb.sb([64, FC, DH], BF16) for _ in range(2)] for _ in range(2)]
        m = [[kb.sb([64, DH], F32) for _ in range(4)] for _ in range(2)]
        yy = [[kb.sb([64, DH], BF16) for _ in range(2)] for _ in range(2)]
        def s2_prod(f1_):
            c_, fl = divmod(f1_, FC)
            f0 = c_ * FC
            A, K_ = at_[c_ % 2], kt_[c_ % 2]
            if fl == 0:
                for ri_ in range(2):
                    kb.dma("sp", A[ri_][:], A1Z.t[ri_].rearrange("f n c -> n f c")[:, f0:f0 + FC, :], writes=[A[ri_]])
                    kb.dma("sp", K_[ri_][:], KS.t[ri_, f0:f0 + FC].rearrange("f k c -> k f c"), writes=[K_[ri_]])
            mm_, y_ = m[f1_ % 2], yy[f1_ % 2]
            zr, zi = f1_ % 2, 2 + f1_ % 2
            kb.op("pe", lambda e: e.matmul(kb.ps(zr)[0:64, :], lhsT=dftm[:, 0, :], rhs=A[0][:, fl, :], start=True, stop=False),
                  reads=[dftm, A[0]], writes=[PB[zr]])
            kb.op("pe", lambda e: e.matmul(kb.ps(zr)[0:64, :], lhsT=dftm[:, 1, :], rhs=A[1][:, fl, :], start=False, stop=True),
                  reads=[dftm, A[1]], writes=[PB[zr]])
            kb.op("pe", lambda e: e.matmul(kb.ps(zi)[0:64, :], lhsT=dftm[:, 0, :], rhs=A[1][:, fl, :], start=True, stop=False),
                  reads=[dftm, A[1]], writes=[PB[zi]])
            kb.op("pe", lambda e: e.matmul(kb.ps(zi)[0:64, :], lhsT=dftm[:, 2, :], rhs=A[0][:, fl, :], start=False, stop=True),
                  reads=[dftm, A[0]], writes=[PB[zi]])
            kb.op("dve", lambda e: e.tensor_tensor(out=mm_[0][:], in0=kb.ps(zr)[0:64, :], in1=K_[0][:, fl, :], op=ALU.mult),
                  reads=[PB[zr], K_[0]], writes=[mm_[0]])
            kb.op("dve", lambda e: e.tensor_tensor(out=mm_[1][:], in0=kb.ps(zi)[0:64, :], in1=K_[1][:, fl, :], op=ALU.mult),
                  reads=[PB[zi], K_[1]], writes=[mm_[1]])
            kb.op("dve", lambda e: e.tensor_tensor(out=mm_[2][:], in0=kb.ps(zr)[0:64, :], in1=K_[1][:, fl, :], op=ALU.mult),
                  reads=[PB[zr], K_[1]], writes=[mm_[2]])
            kb.op("dve", lambda e: e.tensor_tensor(out=mm_[3][:], in0=kb.ps(zi)[0:64, :], in1=K_[0][:, fl, :], op=ALU.mult),
                  reads=[PB[zi], K_[0]], writes=[mm_[3]])
            kb.op("pool", lambda e: e.tensor_tensor(out=y_[0][:], in0=mm_[0][:], in1=mm_[1][:], op=ALU.subtract),
                  reads=[mm_[0], mm_[1]], writes=[y_[0]])
            kb.op("pool", lambda e: e.tensor_tensor(out=y_[1][:], in0=mm_[2][:], in1=mm_[3][:], op=ALU.add),
                  reads=[mm_[2], mm_[3]], writes=[y_[1]])

        def inv_a(f1_):
            c_, fl = divmod(f1_, FC)
            f0 = c_ * FC
            y_ = yy[f1_ % 2]
            bos = bo[c_ % 2]
            br, bi = 4 + f1_ % 2, 6 + f1_ % 2
            kb.op("pe", lambda e: e.matmul(kb.ps(br)[0:64, :], lhsT=gtw[:, 0, f1_, :], rhs=y_[0][:], start=True, stop=False),
                  reads=[gtw, y_[0]], writes=[PB[br]])
            kb.op("pe", lambda e: e.matmul(kb.ps(br)[0:64, :], lhsT=gtw[:, 2, f1_, :], rhs=y_[1][:], start=False, stop=True),
                  reads=[gtw, y_[1]], writes=[PB[br]])
            kb.op("pe", lambda e: e.matmul(kb.ps(bi)[0:64, :], lhsT=gtw[:, 1, f1_, :], rhs=y_[0][:], start=True, stop=False),
                  reads=[gtw, y_[0]], writes=[PB[bi]])
            kb.op("pe", lambda e: e.matmul(kb.ps(bi)[0:64, :], lhsT=gtw[:, 0, f1_, :], rhs=y_[1][:], start=False, stop=True),
                  reads=[gtw, y_[1]], writes=[PB[bi]])
            kb.op("act", lambda e: e.copy(out=bos[0][:, fl, :], in_=kb.ps(br)[0:64, :]), reads=[PB[br]], writes=[(bos[0], fl)])
            kb.op("act", lambda e: e.copy(out=bos[1][:, fl, :], in_=kb.ps(bi)[0:64, :]), reads=[PB[bi]], writes=[(bos[1], fl)])
            if fl == FC - 1:
                for ri_ in range(2):
                    kb.dma("pool", B1.t[ri_, :, f0:f0 + FC, :], bos[ri_][:], reads=[bos[ri_]])

        for k_ in range(HH + 1):
            if k_ < HH:
                s2_prod(k_)
            if k_ >= 1:
                inv_a(k_ - 1)
        phase_begin(label="hy_ib")
        hb = kb.sb([128, 4], F32)
        kb.dma("sp", hb[:], P["hy_bias"][li], writes=[hb])
        m4 = kb.sb([HH, 2, HH], BF16)
        kb.dma("sp", m4[:], C["m4"][:, :, :], writes=[m4])
        ysb = kb.sb([128, 4, L], BF16)
        bt_ = [[kb.sb([HH, 8, DH], BF16) for _ in range(2)] for _ in range(2)]
        for g in range(8):
            Bt = bt_[g % 2]
            for ri_ in range(2):
                kb.dma("sp", Bt[ri_][:], B1.t[ri_].rearrange("t f c -> f t c")[:, g * 8:(g + 1) * 8, :], writes=[Bt[ri_]])
            for cb_ in range(4):
                bank = next_pb()
                for tl in range(8):
                    kb.op("pe", lambda e: e.matmul(kb.ps(bank, [128, 8, HH])[:, tl, :], lhsT=Bt[0][:, tl, cb_ * 128:(cb_ + 1) * 128],
                                                   rhs=m4[:, 0, :], start=True, stop=False), reads=[Bt[0], m4], writes=[PB[bank]])
                    kb.op("pe", lambda e: e.matmul(kb.ps(bank, [128, 8, HH])[:, tl, :], lhsT=Bt[1][:, tl, cb_ * 128:(cb_ + 1) * 128],
                                                   rhs=m4[:, 1, :], start=False, stop=True), reads=[Bt[1], m4], writes=[PB[bank]])
                dst = ysb[:, cb_, :].rearrange("p (a b) -> p a b", b=64)[:, :, g * 8:(g + 1) * 8]
                src_ = kb.ps(bank, [128, 8, HH]).rearrange("p a b -> p b a")
                if (g * 4 + cb_) % 2 == 0:
                    kb.op("act", lambda e: e.copy(out=dst, in_=src_), reads=[PB[bank]], writes=[(ysb, (cb_, g))])
                else:
                    kb.op("dve", lambda e: e.tensor_copy(out=dst, in_=src_), reads=[PB[bank]], writes=[(ysb, (cb_, g))])
        x0t = [kb.sb([128, 512], BF16) for _ in range(2)]
        zvf = [kb.sb([128, 512], BF16) for _ in range(2)]
        tmp = [kb.sb([128, 512], F32) for _ in range(2)]
        yo = [kb.sb([128, 512], BF16) for _ in range(2)]
        for tt in range(NQ):
            for cb_ in range(4):
                k = (tt * 4 + cb_) % 2
                kb.dma("sp", x0t[k][:], X0fm[cb_ * 128:(cb_ + 1) * 128, tt * 512:(tt + 1) * 512], writes=[x0t[k]])
                kb.dma("sp", zvf[k][:], ZVfm[cb_ * 128:(cb_ + 1) * 128, tt * 512:(tt + 1) * 512], writes=[zvf[k]])
                kb.op("dve", lambda e: e.scalar_tensor_tensor(
                    out=tmp[k][:], in0=zvf[k][:], scalar=hb[:, cb_:cb_ + 1], in1=ysb[:, cb_, tt * 512:(tt + 1) * 512],
                    op0=ALU.mult, op1=ALU.add), reads=[zvf[k], hb, ysb], writes=[tmp[k]])
                kb.op("pool", lambda e: e.tensor_tensor(out=yo[k][:], in0=tmp[k][:], in1=x0t[k][:], op=ALU.mult),
                      reads=[tmp[k], x0t[k]], writes=[yo[k]])
                kb.dma("pool", YH[cb_ * 128:(cb_ + 1) * 128, tt * 512:(tt + 1) * 512], yo[k][:], reads=[yo[k]])
        st["pb"] = 0

    def phase_gla(li):
        phase_begin()
        w2x = kb.sb([33, 2, 512], F32)
        gn = kb.sb([128, 1], F32)
        kb.dma("sp", w2x[:], P["gla_w2x"][li], writes=[w2x])
        kb.dma("sp", gn[:], P["gla_norm"][li], writes=[gn])
        w2h = kb.sb([33, 2, 512], BF16)
        w2l = kb.sb([33, 2, 512], BF16)
        kb.op("dve", lambda e: e.tensor_copy(out=w2h[:], in_=w2x[:]), reads=[w2x], writes=[w2h])
        kb.op("dve", lambda e: e.tensor_tensor(out=w2l[:], in0=w2x[:], in1=w2h[:], op=ALU.subtract), reads=[w2x, w2h], writes=[w2l])
        OB = kb.sb([128, 4, L], BF16)
        S32 = [kb.sb([128, 4, 128], F32) for _ in range(2)]
        Sbf = [kb.sb([128, 4, 128], BF16) for _ in range(2)]
        for d_ in range(2):
            kb.op("pool", lambda e, d_=d_: e.memset(S32[d_][:], 0.0), writes=[S32[d_]])
            kb.op("pool", lambda e, d_=d_: e.memset(Sbf[d_][:], 0.0), writes=[Sbf[d_]])
        lah = [kb.sb([128, 512], BF16) for _ in range(2)]
        lal = [kb.sb([128, 512], BF16) for _ in range(2)]
        PD = lambda shape, dt: [kb.sb(shape, dt) for _ in range(2)]
        PS = lambda shape, dt: [[kb.sb(shape, dt) for _ in range(2)] for _ in range(2)]
        e1, la, edec = PD([128, 512], F32), PD([128, 512], F32), PD([128, 512], F32)
        EK = PD([128, 4, 128], F32)
        kt, qf, kf, ke = PD([128, 512], BF16), PD([128, 4, 128], BF16), PD([128, 4, 128], BF16), PD([128, 4, 128], BF16)
        EQ = PS([128, 4, 128], F32)
        vt, kdec = PS([128, 512], BF16), PS([128, 512], BF16)
        qe, msk = PS([128, 4, 128], BF16), PS([128, 4, 128], BF16)
        gf, sqb, yob = PD([128, 4, 128], BF16), PD([128, 4, 128], BF16), PD([128, 4, 128], BF16)
        o32, rsd = PD([128, 4, 128], F32), PD([128, 4, 128], F32)
        qv = Qfm.t.rearrange("(h d) t -> d h t", h=4)
        kv = Kfm.t.rearrange("(h d) t -> d h t", h=4)
        gv_ = Gfm.t.rearrange("(h d) t -> d h t", h=4)
        yv = YG.t.rearrange("(h d) t -> d h t", h=4)
        V3 = [128, 4, 128]

        def tile_of(step, d_):
            return step if d_ == 0 else NT - 1 - step

        def pre(step, d_, stage):
            i = tile_of(step, d_)
            sl = slice(i * 128, (i + 1) * 128)
            p = step % 2
            bA, bB = 4 * d_, 4 * d_ + 1
            tri_fm = 0 if d_ == 0 else 2
            tri_dec = 3 if d_ == 0 else 1
            if stage == 1:
                pre1(d_, p, sl, bA)
            elif stage == 2:
                pre2(d_, p, bA, bB, tri_fm, tri_dec)
            else:
                pre3(d_, p, bA)

        def pre1(d_, p, sl, bA):
            kb.dma("sp", kt[d_][:], Ktok[sl, :], writes=[kt[d_]])
            kb.dma("sp", vt[d_][p][:], Vtok[sl, :], writes=[vt[d_][p]])
            kb.dma("sp", qf[d_][:], qv[:, :, sl], writes=[qf[d_]])
            kb.dma("sp", kf[d_][:], kv[:, :, sl], writes=[kf[d_]])
            kb.op("pe", lambda e: e.matmul(kb.ps(bA), lhsT=LRH[:, sl], rhs=w2h[:, d_, :], start=True, stop=False),
                  reads=[LRH, w2h], writes=[PB[bA]])
            kb.op("pe", lambda e: e.matmul(kb.ps(bA), lhsT=LRL[:, sl], rhs=w2h[:, d_, :], start=False, stop=False),
                  reads=[LRL, w2h], writes=[PB[bA]])
            kb.op("pe", lambda e: e.matmul(kb.ps(bA), lhsT=LRH[:, sl], rhs=w2l[:, d_, :], start=False, stop=True),
                  reads=[LRH, w2l], writes=[PB[bA]])
            kb.op("act", lambda e: e.activation(out=e1[d_][:], in_=kb.ps(bA), func=AF.Exp, scale=-1.0),
                  reads=[PB[bA]], writes=[e1[d_]])
            kb.op("act", lambda e: e.activation(out=e1[d_][:], in_=e1[d_][:], func=AF.Ln, bias=1.0),
                  reads=[e1[d_]], writes=[e1[d_]])
            kb.op("dve", lambda e: e.tensor_scalar(out=la[d_][:], in0=e1[d_][:], scalar1=-1.0 / 16.0, scalar2=-1.0,
                                                   op0=ALU.mult, op1=ALU.max), reads=[e1[d_]], writes=[la[d_]])
            kb.op("act", lambda e: e.copy(out=lah[d_][:], in_=la[d_][:]), reads=[la[d_]], writes=[lah[d_]])
            kb.op("dve", lambda e: e.tensor_tensor(out=lal[d_][:], in0=la[d_][:], in1=lah[d_][:], op=ALU.subtract),
                  reads=[la[d_], lah[d_]], writes=[lal[d_]])

        def pre2(d_, p, bA, bB, tri_fm, tri_dec):
            kb.op("pe", lambda e: e.matmul(kb.ps(bA), lhsT=tri32[:, tri_dec, :], rhs=lah[d_][:], start=True, stop=False),
                  reads=[tri32, lah[d_]], writes=[PB[bA]])
            kb.op("pe", lambda e: e.matmul(kb.ps(bA), lhsT=tri32[:, tri_dec, :], rhs=lal[d_][:], start=False, stop=True),
                  reads=[tri32, lal[d_]], writes=[PB[bA]])
            for h in range(4):
                kb.op("pe", lambda e, h=h: e.matmul(kb.ps(bB, V3)[:, h, :], lhsT=lah[d_][:, h * 128:(h + 1) * 128],
                                                    rhs=tri32[:, tri_fm, :], start=True, stop=False),
                      reads=[tri32, lah[d_]], writes=[PB[bB]])
                kb.op("pe", lambda e, h=h: e.matmul(kb.ps(bB, V3)[:, h, :], lhsT=lal[d_][:, h * 128:(h + 1) * 128],
                                                    rhs=tri32[:, tri_fm, :], start=False, stop=True),
                      reads=[tri32, lal[d_]], writes=[PB[bB]])
            kb.op("act", lambda e: e.activation(out=edec[d_][:], in_=kb.ps(bA), func=AF.Exp), reads=[PB[bA]], writes=[edec[d_]])
            kb.op("act", lambda e: e.activation(out=EQ[d_][p][:], in_=kb.ps(bB, V3), func=AF.Exp), reads=[PB[bB]], writes=[EQ[d_][p]])
            kb.op("act", lambda e: e.activation(out=EK[d_][:], in_=kb.ps(bB, V3), func=AF.Exp, scale=-1.0),
                  reads=[PB[bB]], writes=[EK[d_]])
            kb.op("pool", lambda e: e.tensor_tensor(out=kdec[d_][p][:], in0=kt[d_][:], in1=edec[d_][:], op=ALU.mult),
                  reads=[kt[d_], edec[d_]], writes=[kdec[d_][p]])
            kb.op("dve", lambda e: e.tensor_tensor(out=qe[d_][p][:], in0=qf[d_][:], in1=EQ[d_][p][:], op=ALU.mult),
                  reads=[qf[d_], EQ[d_][p]], writes=[qe[d_][p]])
            kb.op("pool", lambda e: e.tensor_tensor(out=ke[d_][:], in0=kf[d_][:], in1=EK[d_][:], op=ALU.mult),
                  reads=[kf[d_], EK[d_]], writes=[ke[d_]])

        def pre3(d_, p, bA):
            for h in range(4):
                kb.op("pe", lambda e, h=h: e.matmul(kb.ps(bA, V3)[:, h, :], lhsT=ke[d_][:, h, :], rhs=qe[d_][p][:, h, :],
                                                    start=True, stop=True), reads=[ke[d_], qe[d_][p]], writes=[PB[bA]])
            kb.op("dve", lambda e: e.tensor_tensor(out=msk[d_][p][:], in0=kb.ps(bA, V3),
                                                   in1=tri32[:, 3 * d_:3 * d_ + 1, :].to_broadcast(V3), op=ALU.mult),
                  reads=[PB[bA], tri32], writes=[msk[d_][p]])

        def seq(step, d_):
            i = tile_of(step, d_)
            sl = slice(i * 128, (i + 1) * 128)
            p = step % 2
            bC, bD = 4 * d_ + 2, 4 * d_ + 3
            final = step >= NT // 2
            for h in range(4):
                hs = slice(h * 128, (h + 1) * 128)
                kb.op("pe", lambda e, h=h, hs=hs: e.matmul(kb.ps(bC, V3)[:, h, :], lhsT=vt[d_][p][:, hs], rhs=msk[d_][p][:, h, :],
                                                           start=True, stop=False), reads=[vt[d_][p], msk[d_][p]], writes=[PB[bC]])
                kb.op("pe", lambda e, h=h: e.matmul(kb.ps(bC, V3)[:, h, :], lhsT=Sbf[d_][:, h, :], rhs=qe[d_][p][:, h, :],
                                                    start=False, stop=True), reads=[Sbf[d_], qe[d_][p]], writes=[PB[bC]])
            for h in range(4):
                hs = slice(h * 128, (h + 1) * 128)
                kb.op("pe", lambda e, h=h, hs=hs: e.matmul(kb.ps(bD, V3)[:, h, :], lhsT=kdec[d_][p][:, hs], rhs=vt[d_][p][:, hs],
                                                           start=True, stop=True), reads=[kdec[d_][p], vt[d_][p]], writes=[PB[bD]])
            dcol = 127 if d_ == 0 else 0
            kb.op("dve", lambda e: e.tensor_tensor(out=S32[d_][:], in0=S32[d_][:],
                                                   in1=EQ[d_][p][:, :, dcol:dcol + 1].to_broadcast(V3), op=ALU.mult),
                  reads=[S32[d_], EQ[d_][p]], writes=[S32[d_]])
            kb.op("dve", lambda e: e.tensor_tensor(out=S32[d_][:], in0=S32[d_][:], in1=kb.ps(bD, V3), op=ALU.add),
                  reads=[S32[d_], PB[bD]], writes=[S32[d_]])
            kb.op("pool", lambda e: e.tensor_copy(out=Sbf[d_][:], in_=S32[d_][:]), reads=[S32[d_]], writes=[Sbf[d_]])
            if not final:
                kb.op("act", lambda e: e.copy(out=OB[:, :, sl], in_=kb.ps(bC, V3)), reads=[PB[bC]], writes=[(OB, i)])
            else:
                kb.dma("sp", gf[d_][:], gv_[:, :, sl], writes=[gf[d_]])
                kb.op("dve", lambda e: e.tensor_tensor(out=o32[d_][:], in0=kb.ps(bC, V3), in1=OB[:, :, sl], op=ALU.add),
                      reads=[PB[bC], (OB, i)], writes=[o32[d_]])
                kb.op("act", lambda e: e.activation(out=sqb[d_][:], in_=o32[d_][:], func=AF.Square), reads=[o32[d_]], writes=[sqb[d_]])
                kb.op("pe", lambda e: e.matmul(kb.ps(bD), lhsT=ones[:], rhs=sqb[d_][:].rearrange("p a b -> p (a b)"), start=True, stop=True),
                      reads=[ones, sqb[d_]], writes=[PB[bD]])
                kb.op("act", lambda e: e.activation(out=rsd[d_][:].rearrange("p a b -> p (a b)"), in_=kb.ps(bD), func=AF.Ln,
                                                    scale=1.0 / 128.0, bias=EPS), reads=[PB[bD]], writes=[rsd[d_]])
                kb.op("act", lambda e: e.activation(out=rsd[d_][:], in_=rsd[d_][:], func=AF.Exp, scale=-0.5), reads=[rsd[d_]], writes=[rsd[d_]])
                kb.op("pool", lambda e: e.tensor_tensor(out=o32[d_][:], in0=o32[d_][:], in1=rsd[d_][:], op=ALU.mult),
                      reads=[o32[d_], rsd[d_]], writes=[o32[d_]])
                kb.op("dve", lambda e: e.scalar_tensor_tensor(out=yob[d_][:], in0=o32[d_][:], scalar=gn[:, 0:1], in1=gf[d_][:],
                                                              op0=ALU.mult, op1=ALU.mult), reads=[o32[d_], gn, gf[d_]], writes=[yob[d_]])
                kb.dma("pool", yv[:, :, sl], yob[d_][:], reads=[yob[d_]])

        for step in range(NT + 1):
            if step < NT:
                pre(step, 0, 1)
                pre(step, 1, 1)
            if step >= 1:
                seq(step - 1, 0)
                seq(step - 1, 1)
            if step < NT:
                pre(step, 0, 2)
                pre(step, 1, 2)
                pre(step, 0, 3)
                pre(step, 1, 3)
        st["pb"] = 0

    def load_w_into(dst, wap, kblocks, c0, ncols, gt=None, dcol0=0):
        src = wap.rearrange("(k p) c -> p k c", p=128)
        WF = st["WF"]
        kper = max(1, 4096 // ncols)
        k0 = 0
        while k0 < kblocks:
            kn = min(kper, kblocks - k0)
            wfv = WF.t[:, 0:kn * ncols].rearrange("p (k c) -> p k c", k=kn)
            kb.dma("sp", wfv, src[:, k0:k0 + kn, c0:c0 + ncols], writes=[WF])
            if gt is None:
                kb.op("pool", lambda e, wfv=wfv, k0=k0, kn=kn: e.tensor_copy(out=dst.t[:, k0:k0 + kn, dcol0:dcol0 + ncols], in_=wfv),
                      reads=[WF], writes=[(dst, (k0, dcol0))])
            else:
                kb.op("pool", lambda e, wfv=wfv, k0=k0, kn=kn: e.tensor_tensor(
                    out=dst.t[:, k0:k0 + kn, dcol0:dcol0 + ncols], in0=wfv,
                    in1=gt.t[:, k0:k0 + kn, None].to_broadcast([128, kn, ncols]), op=ALU.mult),
                    reads=[WF, gt], writes=[(dst, (k0, dcol0))])
            k0 += kn

    def post_tiles():
        return {"yt": [kb.sb([128, D], F32) for _ in range(2)], "xo": [kb.sb([128, D], F32) for _ in range(2)],
                "sq": [kb.sb([128, 1], F32) for _ in range(2)], "rs": [kb.sb([128, 1], F32) for _ in range(2)],
                "junk": kb.sb([128, D], BF16), "nt": norm_tmp()}

    def post(i, ba, bb, xsrc, gp, xdst, pt, do_norm):
        k = i % 2
        yt, xo, sq, rs, junk = pt["yt"][k], pt["xo"][k], pt["sq"][k], pt["rs"][k], pt["junk"]
        kb.dma("sp", xo[:], xsrc[i * 128:(i + 1) * 128, :], writes=[xo])
        kb.op("act", lambda e: e.copy(out=yt[:, 0:512], in_=kb.ps(ba)), reads=[PB[ba]], writes=[(yt, 0)])
        kb.op("dve", lambda e: e.tensor_copy(out=yt[:, 512:1024], in_=kb.ps(bb)), reads=[PB[bb]], writes=[(yt, 1)])
        kb.op("act", lambda e: e.activation(out=junk[:], in_=yt[:], func=AF.Square, scale=1.0 / 32.0, accum_out=sq[:]),
              reads=[yt], writes=[junk, sq])
        kb.op("act", lambda e: e.activation(out=rs[:], in_=sq[:], func=AF.Ln, bias=EPS), reads=[sq], writes=[rs])
        kb.op("act", lambda e: e.activation(out=rs[:], in_=rs[:], func=AF.Exp, scale=-0.5), reads=[rs], writes=[rs])
        kb.op("dve", lambda e: e.scalar_tensor_tensor(out=yt[:], in0=yt[:], scalar=rs[:, 0:1], in1=gp[:], op0=ALU.mult, op1=ALU.mult),
              reads=[yt, rs, gp], writes=[yt])
        kb.op("pool", lambda e: e.tensor_tensor(out=xo[:], in0=xo[:], in1=yt[:], op=ALU.add), reads=[xo, yt], writes=[xo])
        kb.dma("pool", xdst[i * 128:(i + 1) * 128, :], xo[:], reads=[xo])
        if do_norm:
            norm_transpose(xo, i, pt["nt"][k])

    def phase_merge(li, xsrc, xdst):
        phase_begin(0, True)
        MG = HT
        wbr = [kb.sb([128, 4, D], BF16) for _ in range(3)]
        for b in range(3):
            for c0 in (0, 512):
                load_w_into(wbr[b], P["w_br"][li, b], 4, c0, 512, None, c0)
        ysrc = [Y_.t.rearrange("(k p) t -> p k t", p=128) for Y_ in (YH, YG, YP)]
        gsrc = GATES.t.rearrange("(b r) t -> r b t", b=3)
        yt_ = [[kb.sb([128, 4, 512], BF16) for _ in range(3)] for _ in range(2)]
        gt_ = [kb.sb([128, 3, 512], BF16) for _ in range(2)]
        mm = [[kb.sb([128, 512], F32) for _ in range(3)] for _ in range(2)]
        ng = 0
        for tt in range(NQ):
            ys = yt_[tt % 2]
            for b in range(3):
                kb.dma("sp", ys[b][:], ysrc[b][:, :, tt * 512:(tt + 1) * 512], writes=[ys[b]])
            for db in range(8):
                g = gt_[ng % 2]
                m_ = mm[ng % 2]
                ng += 1
                kb.dma("sp", g[:], gsrc[db * 128:(db + 1) * 128, :, tt * 512:(tt + 1) * 512], writes=[g])
                banks = [next_pb(), next_pb(), next_pb()]
                for b in range(3):
                    for k in range(4):
                        kb.op("pe", lambda e, b=b, k=k, db=db, ys=ys, banks=banks: e.matmul(
                            kb.ps(banks[b]), lhsT=wbr[b][:, k, db * 128:(db + 1) * 128], rhs=ys[b][:, k, :],
                            start=(k == 0), stop=(k == 3)), reads=[wbr[b], ys[b]], writes=[PB[banks[b]]])
                for b in range(3):
                    kb.op("dve", lambda e, b=b, g=g, m_=m_, banks=banks: e.tensor_tensor(
                        out=m_[b][:], in0=kb.ps(banks[b]), in1=g[:, b, :], op=ALU.mult), reads=[PB[banks[b]], g], writes=[m_[b]])
                kb.op("pool", lambda e, m_=m_: e.tensor_tensor(out=m_[0][:], in0=m_[0][:], in1=m_[1][:], op=ALU.add),
                      reads=[m_[0], m_[1]], writes=[m_[0]])
                kb.op("pool", lambda e, m_=m_, db=db, tt=tt: e.tensor_tensor(
                    out=MG[:, db, tt * 512:(tt + 1) * 512], in0=m_[0][:], in1=m_[2][:], op=ALU.add),
                    reads=[m_[0], m_[2]], writes=[(MG, 4 * tt), (MG, 4 * tt + 1), (MG, 4 * tt + 2), (MG, 4 * tt + 3)])
        phase_begin(0, True)
        wo = kb.sb([128, 8, D], BF16)
        gp = kb.sb([128, D], F32)
        kb.dma("sp", gp[:], P["g_mix_post"][li], writes=[gp])
        for c0 in (0, 512):
            load_w_into(wo, P["w_out"][li], 8, c0, 512, None, c0)
        pt = post_tiles()

        def mm(i):
            ba, bb = next_pb(), next_pb()
            gemm_tok(wo, 8, 0, 512, MG, i, ba, srckey=i)
            gemm_tok(wo, 8, 512, 512, MG, i, bb, srckey=i)
            return ba, bb

        cur = mm(0)
        for i in range(NT):
            nxt = mm(i + 1) if i + 1 < NT else None
            post(i, cur[0], cur[1], xsrc, gp, xdst, pt, True)
            cur = nxt

    def phase_ffn_up(li):
        phase_begin(0)
        gt = kb.sb([128, 8], F32)
        kb.dma("sp", gt[:], P["g_ffn_pre"][li], writes=[gt])
        cw = kb.sb([128, 22, 3], F32)
        cb = kb.sb([128, 22], F32)
        kb.dma("sp", cw[:], P["ffn_cw"][li], writes=[cw])
        kb.dma("sp", cb[:], P["ffn_cb"][li], writes=[cb])
        raws = [kb.sb([128, L + 2], BF16) for _ in range(2)]
        for r in raws:
            kb.op("pool", lambda e, r=r: e.memset(r[:, 0:1], 0.0), writes=[(r, "h0")])
            kb.op("pool", lambda e, r=r: e.memset(r[:, L + 1:L + 2], 0.0), writes=[(r, "h1")])
        bts = [kb.sb([128, L], BF16) for _ in range(2)]
        gas = [kb.sb([128, L], BF16) for _ in range(2)]
        acc = kb.sb([128, L], F32)
        wfs = [kb.sb([128, 8, 256], F32) for _ in range(2)]
        wbs = [kb.sb([128, 8, 256], BF16) for _ in range(2)]
        wsrc = P["ffn_up"][li].rearrange("(k p) c -> p k c", p=128)

        def prep(m):
            s_ = m % 2
            kb.dma("sp", wfs[s_][:, :, 0:128], wsrc[:, :, m * 128:(m + 1) * 128], writes=[(wfs[s_], 0)])
            kb.dma("sp", wfs[s_][:, :, 128:256], wsrc[:, :, DFF + m * 128:DFF + (m + 1) * 128], writes=[(wfs[s_], 1)])
            kb.op("pool", lambda e: e.tensor_tensor(out=wbs[s_][:], in0=wfs[s_][:],
                                                    in1=gt[:, :, None].to_broadcast([128, 8, 256]), op=ALU.mult),
                  reads=[wfs[s_], gt], writes=[wbs[s_]])
            return wbs[s_]

        wnext = prep(0)
        for m in range(22):
            raw, bt, ga = raws[m % 2], bts[m % 2], gas[m % 2]
            wcur = wnext
            if m + 1 < 22:
                wnext = prep(m + 1)
            for j in range(NQ):
                b1_, b2_ = next_pb(), next_pb()
                gemm_fm(wcur, 8, 0, 128, HT, j, b1_)
                gemm_fm(wcur, 8, 128, 128, HT, j, b2_)
                kb.op("act", lambda e: e.copy(out=raw[:, 1 + j * 512:1 + (j + 1) * 512], in_=kb.ps(b1_)),
                      reads=[PB[b1_]], writes=[(raw, j)])
                kb.op("act", lambda e: e.copy(out=bt[:, j * 512:(j + 1) * 512], in_=kb.ps(b2_)),
                      reads=[PB[b2_]], writes=[(bt, j)])
            kb.op("dve", lambda e: e.tensor_scalar(out=acc[:], in0=raw[:, 0:L], scalar1=cw[:, m, 0:1], scalar2=cb[:, m:m + 1],
                                                   op0=ALU.mult, op1=ALU.add), reads=[raw, cw, cb], writes=[acc])
            kb.op("dve", lambda e: e.scalar_tensor_tensor(out=acc[:], in0=raw[:, 1:L + 1], scalar=cw[:, m, 1:2], in1=acc[:],
                                                          op0=ALU.mult, op1=ALU.add), reads=[raw, cw, acc], writes=[acc])
            kb.op("dve", lambda e: e.scalar_tensor_tensor(out=acc[:], in0=raw[:, 2:L + 2], scalar=cw[:, m, 2:3], in1=acc[:],
                                                          op0=ALU.mult, op1=ALU.add), reads=[raw, cw, acc], writes=[acc])
            kb.op("act", lambda e: e.activation(out=ga[:], in_=acc[:], func=AF.Gelu), reads=[acc], writes=[ga])
            kb.op("dve", lambda e: e.tensor_tensor(out=ga[:], in0=ga[:], in1=bt[:], op=ALU.mult), reads=[ga, bt], writes=[ga])
            kb.dma("pool", ACTfm[m * 128:(m + 1) * 128, :], ga[:], reads=[ga])

    def phase_ffn_down(li, xsrc, xdst, do_norm):
        phase_begin(0, True)
        wd = kb.sb([128, 22, D], BF16)
        gp = kb.sb([128, D], F32)
        kb.dma("sp", gp[:], P["g_ffn_post"][li], writes=[gp])
        for c0 in range(0, D, 128):
            load_w_into(wd, P["ffn_down"][li], 22, c0, 128, None, c0)
        asrc = ACTfm.t.rearrange("(k p) t -> p k t", p=128)
        at = [kb.sb([128, 22, 512], BF16) for _ in range(1)]
        pt = post_tiles()
        def load_a(tt):
            a = at[0]
            kb.dma("sp", a[:, 0:11, :], asrc[:, 0:11, tt * 512:(tt + 1) * 512], writes=[(a, 0)])
            kb.dma("sp", a[:, 11:22, :], asrc[:, 11:22, tt * 512:(tt + 1) * 512], writes=[(a, 1)])

        def mm(i):
            if i % 4 == 0:
                load_a(i // 4)
            a, ii = at[0], i % 4
            ba, bb = next_pb(), next_pb()
            for c0, bank in ((0, ba), (512, bb)):
                for k in range(22):
                    kb.op("pe", lambda e, k=k, c0=c0, bank=bank: e.matmul(
                        kb.ps(bank), lhsT=a[:, k, ii * 128:(ii + 1) * 128], rhs=wd[:, k, c0:c0 + 512],
                        start=(k == 0), stop=(k == 21)), reads=[a, wd], writes=[PB[bank]])
            return ba, bb

        cur = mm(0)
        for i in range(NT):
            nxt = mm(i + 1) if i + 1 < NT else None
            post(i, cur[0], cur[1], xsrc, gp, xdst, pt, do_norm)
            cur = nxt

    phase_filter(0)
    phase_norm0(x_in)
    for li in range(depth):
        xs = x_in if li == 0 else XB
        phase_proj(li)
        phase_hyena(li)
        phase_gla(li)
        phase_merge(li, xs, XA)
        phase_ffn_up(li)
        lastl = (li == depth - 1)
        if not lastl:
            phase_filter(li + 1)
        phase_ffn_down(li, XA, out if lastl else XB, not lastl)
    nc = kb.finish()
    return nc, kb


_CACHE = {}


def _in_maps(inputs, L, depth, nb):
    consts = make_consts(L)
    params = relayout_params(inputs, depth)
    x = np.asarray(inputs["x"], np.float32)
    maps = []
    for b in range(nb):
        m = {"x": np.ascontiguousarray(x[b])}
        m.update(consts)
        m.update(params)
        maps.append(m)
    return maps


def kernel(**inputs):
    x = np.asarray(inputs["x"])
    B, L, _ = x.shape
    depth = int(np.asarray(inputs["w_in"]).shape[0])
    nc, _ = build(L, depth)
    maps = _in_maps(inputs, L, depth, B)
    res = run_bass_kernel_spmd(nc, maps, core_ids=list(range(B)))
    return np.stack([np.asarray(r["out"], np.float32) for r in res.results], 0)
```

```python
import contextlib
import math
import numpy as np
import ml_dtypes
import concourse.bass as bass
import concourse.mybir as mybir
from concourse.bass_utils import run_bass_kernel_spmd

F32 = mybir.dt.float32
BF16 = mybir.dt.bfloat16
AF = mybir.ActivationFunctionType
ALU = mybir.AluOpType
NDMA_SEM = 8

D = 1024
DH = 512
DIN = 7200
DFF = 2816
EPS = 1e-6


class T:
    def __init__(self, name, ap, parent=None):
        self.name = name
        self.t = ap
        if parent is None:
            self.w = {}
            self.r = {}
            self.root = self
        else:
            self.root = parent.root

    def __getitem__(self, idx):
        return self.t[idx]

    def view(self, ap):
        return T(self.name, ap, parent=self)


class Op:
    __slots__ = ("eng", "fn", "deps", "marked", "val", "sem", "isdma")

    def __init__(self, eng, fn):
        self.eng = eng
        self.fn = fn
        self.deps = []
        self.marked = False
        self.val = 0
        self.sem = None
        self.isdma = False


class _Rec:
    def __getattr__(self, name):
        def f(*a, **k):
            self.call = (name, a, k)
        return f


class KB:
    def __init__(self, sb_bytes):
        self.nc = bass.Bass("TRN2", target_bir_lowering=False)
        self.es = contextlib.ExitStack()
        self.ops = []
        nc = self.nc
        self.handles = {"pe": nc.tensor, "act": nc.scalar, "dve": nc.vector,
                        "pool": nc.gpsimd, "sp": nc.sync}
        self.sems = {e: self.es.enter_context(nc.semaphore("s_" + e)) for e in self.handles}
        self.dq = {}
        for q in ("sp", "act", "pool"):
            self.dq[q] = {"sems": [self.es.enter_context(nc.semaphore("d_%s%d" % (q, i)))
                                   for i in range(NDMA_SEM)],
                          "n": 0, "last": [None] * NDMA_SEM, "cnt": [0] * NDMA_SEM}
        self.last = {}
        self.arena = self.es.enter_context(nc.sbuf_tensor("arena", [128, sb_bytes // 2], BF16))
        self.sb_bytes = sb_bytes
        self.top = 0
        self.nbuf = 0
        self.pbanks = [self.es.enter_context(nc.psum_tensor("pb%d" % i, [128, 512], F32))
                       for i in range(8)]

    def sb(self, shape, dt, name=None):
        esz = 4 if dt == F32 else 2
        n = int(np.prod(shape[1:])) * esz
        n = (n + 63) // 64 * 64
        off = self.top
        self.top += n
        assert self.top <= self.sb_bytes, "SBUF arena overflow %d" % self.top
        ap = self.arena[:, off // 2:(off + n) // 2]
        if dt == F32:
            ap = ap.bitcast(F32)
        ap = ap[:, 0:int(np.prod(shape[1:]))]
        if len(shape) == 3:
            ap = ap.rearrange("p (a b) -> p a b", a=shape[1])
        elif len(shape) == 4:
            ap = ap.rearrange("p (a b c) -> p a b c", a=shape[1], b=shape[2])
        if shape[0] < 128:
            ap = ap[0:shape[0]]
        self.nbuf += 1
        return T(name or "sb%d" % self.nbuf, ap)

    def ps(self, bank, shape=None, dt=F32):
        ap = self.pbanks[bank][:]
        if dt == BF16:
            ap = ap.bitcast(BF16)
        if shape is not None and len(shape) == 3:
            ap = ap[:, 0:shape[1] * shape[2]].rearrange("p (a b) -> p a b", a=shape[1])
        elif shape is not None:
            ap = ap[:, 0:shape[1]]
        if shape is not None and shape[0] < 128:
            ap = ap[0:shape[0]]
        return ap

    def psT(self, bank):
        if not hasattr(self, "_pst"):
            self._pst = [T("pbank%d" % i, self.pbanks[i][:]) for i in range(8)]
        return self._pst[bank]

    def dram(self, name, shape, dt, kind="Internal"):
        return T(name, self.nc.dram_tensor(name, list(shape), dt, kind=kind).ap())

    @staticmethod
    def _norm(lst):
        out = []
        for x in lst:
            if isinstance(x, tuple):
                out.append((x[0].root, x[1]))
            else:
                out.append((x.root, None))
        return out

    def _hazards(self, op, reads, writes):
        deps = op.deps
        for t, key in reads:
            if key is None:
                deps.extend(t.w.values())
            else:
                for k in (key, None):
                    p = t.w.get(k)
                    if p is not None:
                        deps.append(p)
        for t, key in writes:
            if key is None:
                deps.extend(t.w.values())
                for l in t.r.values():
                    deps.extend(x for x in l if x.isdma or op.isdma or x.eng != op.eng or op.eng != 'pe')
            else:
                for k in (key, None):
                    p = t.w.get(k)
                    if p is not None:
                        deps.append(p)
                    deps.extend(x for x in t.r.get(k, ()) if x.isdma or op.isdma or x.eng != op.eng or op.eng != 'pe')
        for t, key in reads:
            l = t.r.setdefault(key, [])
            if not op.isdma:
                l[:] = [o for o in l if o.eng != op.eng or o.isdma]
            l.append(op)
        for t, key in writes:
            if key is None:
                t.w = {None: op}
                t.r = {}
            else:
                t.w[key] = op
                t.r[key] = []

    def op(self, eng, fn, reads=(), writes=()):
        rec = _Rec()
        fn(rec)
        name, a, k = rec.call
        o = Op(eng, lambda h: getattr(h, name)(*a, **k))
        self._hazards(o, self._norm(reads), self._norm(writes))
        if eng == "pe":
            o.deps = [d for d in o.deps if not (d.eng == "pe" and not d.isdma)]
        self.ops.append(o)
        self.last[eng] = o
        return o

    def dma(self, q, out, in_, reads=(), writes=()):
        o = Op(q, lambda e: e.dma_start(out=out, in_=in_))
        o.isdma = True
        dq = self.dq[q]
        i = dq["n"] % NDMA_SEM
        dq["n"] += 1
        if dq["last"][i] is not None:
            o.deps.append(dq["last"][i])
        dq["last"][i] = o
        dq["cnt"][i] += 16
        o.sem = dq["sems"][i]
        o.val = dq["cnt"][i]
        self._hazards(o, self._norm(reads), self._norm(writes))
        self.ops.append(o)
        return o

    def mark(self, label):
        o = Op("sp", None)
        o.sem = label
        o.marked = "label"
        self.ops.append(o)

    def barrier(self):
        deps = list(self.last.values())
        for q in self.dq.values():
            deps.extend(x for x in q["last"] if x is not None)
        for e in self.handles:
            o = Op(e, None)
            o.deps = list(deps)
            self.ops.append(o)

    def finish(self):
        self.barrier()
        for o in self.ops:
            for d in o.deps:
                if not d.isdma:
                    d.marked = True
        cnt = {e: 0 for e in self.handles}
        for o in self.ops:
            if not o.isdma and o.marked is True:
                cnt[o.eng] += 1
                o.val = cnt[o.eng]
                o.sem = self.sems[o.eng]
        seen = {e: {} for e in self.handles}
        nwait = 0
        self.marks = []
        for o in self.ops:
            if o.marked == "label":
                self.marks.append((o.sem, self.nc.get_next_instruction_name()))
                continue
            h = self.handles[o.eng]
            sn = seen[o.eng]
            for d in o.deps:
                k = id(d.sem)
                if sn.get(k, 0) >= d.val:
                    continue
                h.wait_ge(d.sem, d.val)
                nwait += 1
                sn[k] = d.val
            if o.fn is None:
                continue
            ins = o.fn(h)
            if o.isdma:
                ins.then_inc(o.sem, 16)
            elif o.marked is True:
                ins.then_inc(o.sem, 1)
        self.stats = {"ops": len(self.ops), "waits": nwait, "marked": dict(cnt)}
        return self.nc


def make_consts(L):
    bf = ml_dtypes.bfloat16
    c = {}
    c["ident"] = np.eye(128, dtype=np.float32).astype(bf)
    c["ones"] = np.ones((128, 128), np.float32).astype(bf)
    a = np.arange(128)
    uti = (a[:, None] <= a[None, :]).astype(np.float32)
    uts = (a[:, None] < a[None, :]).astype(np.float32)
    lti = (a[:, None] >= a[None, :]).astype(np.float32)
    lts = (a[:, None] > a[None, :]).astype(np.float32)
    c["tri32"] = np.stack([uti, uts, lti, lts], 1).astype(bf)
    t = np.linspace(0.0, 1.0, L, dtype=np.float32)
    bands = 16
    w = (2.0 * np.float32(math.pi) * np.arange(L, dtype=np.float32) / np.float32(L)).astype(np.float32)
    f = np.linspace(1e-4, bands - 1, bands, dtype=np.float32)
    ang = (f[None, :] * w[:, None]).astype(np.float32)
    z = np.concatenate([t[:, None], np.cos(ang), -np.sin(ang)], -1).astype(np.float32)
    c["zT"] = np.ascontiguousarray(z.T)
    c["tcol"] = np.ascontiguousarray(-t.reshape(L // 128, 128).T)
    max_decay = math.log(1e-2) / 0.3
    min_decay = math.log(1e-2) / 1.5
    deltas = np.abs(np.linspace(min_decay, max_decay, DH, dtype=np.float32))
    c["absd"] = np.ascontiguousarray(np.broadcast_to(deltas[None, :], (128, DH))).astype(np.float32)
    N = 2 * L
    N1 = N // 64
    H = N1 // 2
    n1 = np.arange(H, dtype=np.float64)[:, None, None]
    n2 = np.arange(64, dtype=np.float64)[None, :, None]
    f1 = np.arange(H, dtype=np.float64)[None, None, :]
    al = 2 * np.pi * ((f1 + 0.5) * n1 / N1 + (f1 + 0.5) * n2 / N)
    c["tw1"] = np.ascontiguousarray(np.stack([np.cos(al), -np.sin(al)], 1)).astype(bf)
    a64 = np.arange(64, dtype=np.float64)
    be = 2 * np.pi * np.outer(a64, a64) / 64
    c["dftm"] = np.ascontiguousarray(np.stack([np.cos(be), np.sin(be), -np.sin(be)], 1)).astype(bf)
    f2 = a64[:, None, None]
    f1b = np.arange(H, dtype=np.float64)[None, :, None]
    t2 = a64[None, None, :]
    ga = 2 * np.pi * (f2 * t2 / 64 + (f1b + 0.5) * t2 / N)
    c["gtw"] = np.ascontiguousarray(np.stack([np.cos(ga), np.sin(ga), -np.sin(ga)], 1)).astype(bf)
    f1c = np.arange(H, dtype=np.float64)[:, None]
    t1 = np.arange(H, dtype=np.float64)[None, :]
    ph = 2 * np.pi * (f1c + 0.5) * t1 / N1
    c["m4"] = np.ascontiguousarray(np.stack([(2.0 / N) * np.cos(ph), -(2.0 / N) * np.sin(ph)], 1)).astype(bf)
    pos = np.arange(L)
    inv = []
    for wv in (2, 4, 8, 16):
        half = wv // 2
        cntv = (np.minimum(pos + half, L) - np.maximum(pos - half, 0)).astype(np.float32)
        inv.append(1.0 / cntv)
    c["invcnt"] = np.ascontiguousarray(
        np.broadcast_to(np.stack(inv, 0)[:, None, :], (4, 128, L))).astype(np.float32)
    return c


def relayout_params(p, depth):
    f = np.float32
    o = {}

    def pk(v, nb):
        return np.ascontiguousarray(np.asarray(v, f).reshape(nb, 128).T)

    o["g_mix_pre"] = np.stack([pk(p["norm_mix_pre"][i], 8) for i in range(depth)])
    o["g_ffn_pre"] = np.stack([pk(p["norm_ffn_pre"][i], 8) for i in range(depth)])
    o["g_mix_post"] = np.ascontiguousarray(np.broadcast_to(
        np.asarray(p["norm_mix_post"], f)[:depth, None, :], (depth, 128, D)))
    o["g_ffn_post"] = np.ascontiguousarray(np.broadcast_to(
        np.asarray(p["norm_ffn_post"], f)[:depth, None, :], (depth, 128, D)))
    o["w_in"] = np.asarray(p["w_in"], f)[:depth]
    cw = np.asarray(p["hy_conv_w"], f)[:depth]
    o["hy_cw"] = np.ascontiguousarray(cw.reshape(depth, 3, 12, 128).transpose(0, 3, 2, 1))
    o["hy_cb"] = np.stack([pk(p["hy_conv_b"][i], 12) for i in range(depth)])
    o["hy_w1"] = np.asarray(p["hy_filt_w1"], f)[:depth]
    o["hy_w2"] = np.asarray(p["hy_filt_w2"], f)[:depth]
    o["hy_w3"] = np.asarray(p["hy_filt_w3"], f)[:depth]
    vec = np.stack([np.asarray(p["hy_filt_b1"], f)[:depth], np.asarray(p["hy_filt_freq1"], f)[:depth],
                    np.asarray(p["hy_filt_b2"], f)[:depth], np.asarray(p["hy_filt_freq2"], f)[:depth]], -1)
    o["hy_vec"] = np.ascontiguousarray(vec)
    o["hy_bias"] = np.stack([pk(p["hy_bias"][i], 4) for i in range(depth)])
    w2 = np.asarray(p["gla_gate_w2"], f)[:depth]
    gb = np.asarray(p["gla_gate_b"], f)[:depth]
    w2x = np.zeros((depth, 2, 33, 512), f)
    w2x[:, 0, 0:16] = w2[:, 0]
    w2x[:, 1, 16:32] = w2[:, 1]
    w2x[:, :, 32] = gb
    o["gla_w2x"] = np.ascontiguousarray(w2x.transpose(0, 2, 1, 3))
    o["gla_norm"] = np.asarray(p["gla_norm"], f)[:depth].reshape(depth, 128, 1)
    o["pool_w"] = np.ascontiguousarray(np.asarray(p["pool_w"], f)[:depth].transpose(0, 2, 1, 3))
    o["pool_scale"] = np.stack([pk(p["pool_scale"][i], 4) for i in range(depth)])
    o["w_br"] = np.ascontiguousarray(np.stack(
        [np.asarray(p["w_br_hyena"], f)[:depth], np.asarray(p["w_br_gla"], f)[:depth],
         np.asarray(p["w_br_pool"], f)[:depth]], 1))
    o["w_out"] = np.asarray(p["w_out"], f)[:depth]
    o["ffn_up"] = np.asarray(p["ffn_w_up"], f)[:depth]
    fw = np.asarray(p["ffn_conv_w"], f)[:depth]
    o["ffn_cw"] = np.ascontiguousarray(fw.reshape(depth, 3, 22, 128).transpose(0, 3, 2, 1))
    o["ffn_cb"] = np.stack([pk(p["ffn_conv_b"][i], 22) for i in range(depth)])
    o["ffn_down"] = np.asarray(p["ffn_w_down"], f)[:depth]
    return o


def build(L, depth, dbg=()):
    NT = L // 128
    NQ = L // 512
    NFB = 2 * NT
    HH = (2 * L // 64) // 2
    kb = KB(sb_bytes=200 * 1024)

    def din(name, shape, dt=F32):
        return kb.dram(name, shape, dt, kind="ExternalInput")

    def scr(name, shape, dt=BF16):
        return kb.dram(name, shape, dt, kind=("ExternalOutput" if name in dbg else "Internal"))

    x_in = din("x", [L, D])
    C = {k: din(k, list(v.shape), BF16 if v.dtype != np.float32 else F32)
         for k, v in make_consts(L if L <= 512 else 128 * 4).items()} if False else None
    cshapes = {"ident": ([128, 128], BF16), "ones": ([128, 128], BF16), "tri32": ([128, 4, 128], BF16), "zT": ([33, L], F32), "tcol": ([128, NT], F32),
               "absd": ([128, DH], F32), "tw1": ([HH, 2, 64, HH], BF16), "dftm": ([64, 3, 64], BF16),
               "gtw": ([64, 3, HH, 64], BF16), "m4": ([HH, 2, HH], BF16),
               "invcnt": ([4, 128, L], F32)}
    C = {k: din(k, s, dt) for k, (s, dt) in cshapes.items()}
    n = depth
    pshapes = {"g_mix_pre": [n, 128, 8], "g_ffn_pre": [n, 128, 8], "g_mix_post": [n, 128, D],
               "g_ffn_post": [n, 128, D], "w_in": [n, D, DIN], "hy_cw": [n, 128, 12, 3],
               "hy_cb": [n, 128, 12], "hy_w1": [n, 33, 64], "hy_w2": [n, 64, 64], "hy_w3": [n, 64, 1024],
               "hy_vec": [n, 64, 4], "hy_bias": [n, 128, 4], "gla_w2x": [n, 33, 2, 512],
               "gla_norm": [n, 128, 1], "pool_w": [n, 128, 4, 128], "pool_scale": [n, 128, 4],
               "w_br": [n, 3, DH, D], "w_out": [n, D, D], "ffn_up": [n, D, 2 * DFF],
               "ffn_cw": [n, 128, 22, 3], "ffn_cb": [n, 128, 22], "ffn_down": [n, DFF, D]}
    P = {k: din(k, s) for k, s in pshapes.items()}
    out = kb.dram("out", [L, D], F32, kind="ExternalOutput")

    XA = scr("XA", [L, D], F32)
    XB = scr("XB", [L, D], F32)
    X0fm = scr("X0fm", [DH, L])
    ZVfm = scr("ZVfm", [DH, L])
    ZVT = scr("ZVT", [L, DH])
    HSD = scr("HSD", [2, L, DH])
    A1Z = scr("A1Z", [2, HH, 64, DH])
    A1K = scr("A1K", [2, 2, HH, 64, DH])
    KS = scr("KS", [2, HH, 64, DH])
    B1 = scr("B1", [2, 64, HH, DH])
    Qfm = scr("Qfm", [DH, L])
    Kfm = scr("Kfm", [DH, L])
    Gfm = scr("Gfm", [DH, L])
    Ktok = scr("Ktok", [L, DH])
    Vtok = scr("Vtok", [L, DH])
    YH = scr("YH", [DH, L])
    YG = scr("YG", [DH, L])
    YP = scr("YP", [DH, L])
    GATES = scr("GATES", [3 * D, L])
    ACTfm = scr("ACTfm", [DFF, L])

    HT = kb.sb([128, 8, L], BF16, "HT")
    ident = kb.sb([128, 128], BF16, "ident")
    ones = kb.sb([128, 128], BF16, "ones")
    tri32 = kb.sb([128, 4, 128], BF16, "tri32")
    LRH = kb.sb([33, L], BF16, "LRH")
    LRL = kb.sb([33, L], BF16, "LRL")
    kb.dma("sp", ident[:], C["ident"][:, :], writes=[ident])
    kb.dma("sp", ones[:], C["ones"][:, :], writes=[ones])
    kb.dma("sp", tri32[:], C["tri32"][:, :, :], writes=[tri32])
    kb.op("dve", lambda e: e.memset(LRH[32:33, :], 1.0), writes=[LRH])
    kb.op("dve", lambda e: e.memset(LRL[32:33, :], 0.0), writes=[LRL])
    base_top = kb.top
    PB = [kb.psT(i) for i in range(8)]
    st = {"wb": 0, "pb": 0}

    def dump(name, t_, shape, dt=F32):
        if name in dbg:
            d_ = kb.dram(name, shape, dt, kind="ExternalOutput")
            kb.dma("sp", d_.t, t_.t, reads=[t_])

    def phase_begin(nwb=0, wf=False, label=None):
        kb.barrier()
        import inspect
        kb.mark(label or inspect.stack()[1].function + ":%d" % inspect.stack()[1].lineno)
        kb.top = base_top
        if wf or nwb:
            st["WF"] = kb.sb([128, 4096], F32, "WF")
        st["WB"] = [kb.sb([128, 6144], BF16, "WB%d" % i) for i in range(nwb)]

    def load_w(wap, kblocks, c0, ncols, gt=None):
        i = st["wb"] % len(st["WB"])
        st["wb"] += 1
        wb = st["WB"][i]
        WF = st["WF"]
        wbv = wb.view(wb.t[:, 0:kblocks * ncols].rearrange("p (k c) -> p k c", k=kblocks))
        kper = max(1, 4096 // ncols)
        src = wap.rearrange("(k p) c -> p k c", p=128)
        k0 = 0
        while k0 < kblocks:
            kn = min(kper, kblocks - k0)
            wfv = WF.t[:, 0:kn * ncols].rearrange("p (k c) -> p k c", k=kn)
            kb.dma("sp", wfv, src[:, k0:k0 + kn, c0:c0 + ncols], writes=[WF])
            if gt is None:
                kb.op("pool", lambda e, wfv=wfv, k0=k0, kn=kn: e.tensor_copy(out=wbv.t[:, k0:k0 + kn, :], in_=wfv),
                      reads=[WF], writes=[(wb, k0)])
            else:
                kb.op("pool", lambda e, wfv=wfv, k0=k0, kn=kn: e.tensor_tensor(
                    out=wbv.t[:, k0:k0 + kn, :], in0=wfv,
                    in1=gt.t[:, k0:k0 + kn, None].to_broadcast([128, kn, ncols]), op=ALU.mult),
                    reads=[WF, gt], writes=[(wb, k0)])
            k0 += kn
        return wbv

    def next_pb(nb=6):
        b = st["pb"] % nb
        st["pb"] += 1
        return b

    def gemm_fm(wbv, kblocks, mcol, mw, src, j, bank):
        for k in range(kblocks):
            kb.op("pe", lambda e, k=k: e.matmul(kb.ps(bank)[0:mw, :], lhsT=wbv.t[:, k, mcol:mcol + mw],
                                                rhs=src.t[:, k, j * 512:(j + 1) * 512],
                                                start=(k == 0), stop=(k == kblocks - 1)),
                  reads=[wbv, src], writes=[PB[bank]])

    def gemm_tok(wbv, kblocks, c0, ncols, src, i, bank, srckey=None):
        for k in range(kblocks):
            kb.op("pe", lambda e, k=k: e.matmul(kb.ps(bank)[:, 0:ncols], lhsT=src.t[:, k, i * 128:(i + 1) * 128],
                                                rhs=wbv.t[:, k, c0:c0 + ncols],
                                                start=(k == 0), stop=(k == kblocks - 1)),
                  reads=[wbv, (src, srckey) if srckey is not None else src], writes=[PB[bank]])

    def norm_transpose(xt, i, tmp):
        junk, sq, rs, hn = tmp["junk"], tmp["sq"], tmp["rs"], tmp["hn"]
        kb.op("act", lambda e: e.activation(out=junk[:], in_=xt[:], func=AF.Square, scale=1.0 / 32.0,
                                            accum_out=sq[:]), reads=[xt], writes=[junk, sq])
        kb.op("act", lambda e: e.activation(out=rs[:], in_=sq[:], func=AF.Ln, bias=EPS), reads=[sq], writes=[rs])
        kb.op("act", lambda e: e.activation(out=rs[:], in_=rs[:], func=AF.Exp, scale=-0.5), reads=[rs], writes=[rs])
        kb.op("dve", lambda e: e.tensor_scalar(out=hn[:], in0=xt[:], scalar1=rs[:], scalar2=None, op0=ALU.mult),
              reads=[xt, rs], writes=[hn])
        for k in range(8):
            kb.op("pe", lambda e, k=k: e.transpose(out=kb.ps(7, [128, 8, 128], BF16)[:, k, :],
                                                   in_=hn[:, k * 128:(k + 1) * 128], identity=ident[:]),
                  reads=[hn, ident], writes=[PB[7]])
        kb.op("act", lambda e: e.copy(out=HT[:, :, i * 128:(i + 1) * 128], in_=kb.ps(7, [128, 8, 128], BF16)),
              reads=[PB[7]], writes=[(HT, i)])

    def norm_tmp():
        return [{"junk": kb.sb([128, D], BF16), "sq": kb.sb([128, 1], F32), "rs": kb.sb([128, 1], F32),
                 "hn": kb.sb([128, D], BF16)} for _ in range(2)]

    TWO_PI = 2.0 * math.pi

    def phase_filter(li):
        phase_begin()
        HS = HT.view(HT.t[:, :, :].rearrange("p a b -> p (a b)")[:, 0:NT * 1024].rearrange(
            "p (n c) -> p n c", n=NT))
        w1 = kb.sb([33, 64], F32)
        w2 = kb.sb([64, 64], F32)
        w3 = kb.sb([64, 1024], F32)
        vec = kb.sb([64, 4], F32)
        pv = kb.sb([64, 2], F32)
        absd = kb.sb([128, DH], F32)
        tcol = kb.sb([128, NT], F32)
        H1 = kb.sb([64, L], F32)
        H2 = kb.sb([64, L], F32)
        kb.dma("sp", w1[:], P["hy_w1"][li], writes=[w1])
        kb.dma("sp", w2[:], P["hy_w2"][li], writes=[w2])
        kb.dma("sp", w3[:], P["hy_w3"][li], writes=[w3])
        kb.dma("sp", vec[:], P["hy_vec"][li], writes=[vec])
        kb.dma("sp", absd[:], C["absd"][:, :], writes=[absd])
        kb.dma("sp", tcol[:], C["tcol"][:, :], writes=[tcol])
        kb.op("dve", lambda e: e.tensor_tensor(out=pv[:, 0:1], in0=vec[:, 0:1], in1=vec[:, 1:2], op=ALU.mult),
              reads=[vec], writes=[pv])
        kb.op("dve", lambda e: e.tensor_tensor(out=pv[:, 1:2], in0=vec[:, 2:3], in1=vec[:, 3:4], op=ALU.mult),
              reads=[vec], writes=[pv])
        zt = [kb.sb([33, 512], F32) for _ in range(2)]
        arg = [kb.sb([64, 512], F32) for _ in range(2)]

        def sin_layer(wt, kdim, srcfn, dst, frcol, pvcol, j, bank):
            a = arg[j % 2]
            src, srcT = srcfn(j)
            kb.op("pe", lambda e: e.matmul(kb.ps(bank)[0:64, :], lhsT=wt[0:kdim, :], rhs=src,
                                           start=True, stop=True), reads=[wt, srcT], writes=[PB[bank]])
            kb.op("dve", lambda e: e.tensor_scalar(out=a[:], in0=kb.ps(bank)[0:64, :], scalar1=vec[:, frcol:frcol + 1],
                                                   scalar2=pv[:, pvcol:pvcol + 1], op0=ALU.mult, op1=ALU.add),
                  reads=[PB[bank], vec, pv], writes=[a])
            kb.op("dve", lambda e: e.tensor_scalar(out=ni[:], in0=a[:], scalar1=1.0 / TWO_PI, scalar2=None, op0=ALU.mult),
                  reads=[a], writes=[ni])
            kb.op("dve", lambda e: e.scalar_tensor_tensor(out=a[:], in0=ni[:], scalar=-TWO_PI, in1=a[:], op0=ALU.mult, op1=ALU.add),
                  reads=[ni, a], writes=[a])
            kb.op("dve", lambda e: e.tensor_scalar(out=m1[:], in0=a[:], scalar1=math.pi, scalar2=-TWO_PI, op0=ALU.is_gt, op1=ALU.mult),
                  reads=[a], writes=[m1])
            kb.op("dve", lambda e: e.tensor_scalar(out=m2[:], in0=a[:], scalar1=-math.pi, scalar2=TWO_PI, op0=ALU.is_lt, op1=ALU.mult),
                  reads=[a], writes=[m2])
            kb.op("dve", lambda e: e.tensor_tensor(out=a[:], in0=a[:], in1=m1[:], op=ALU.add), reads=[a, m1], writes=[a])
            kb.op("dve", lambda e: e.tensor_tensor(out=a[:], in0=a[:], in1=m2[:], op=ALU.add), reads=[a, m2], writes=[a])
            kb.op("dve", lambda e: e.tensor_scalar(out=a[:], in0=a[:], scalar1=math.pi, scalar2=-math.pi, op0=ALU.min, op1=ALU.max),
                  reads=[a], writes=[a])
            kb.op("act", lambda e: e.activation(out=dst[:, j * 512:(j + 1) * 512], in_=a[:], func=AF.Sin), reads=[a], writes=[(dst, j)])

        negpi = kb.sb([128, 1], F32)
        ni = kb.sb([64, 512], F32)
        ni = ni.view(ni.t.bitcast(mybir.dt.int32))
        m1 = kb.sb([64, 512], F32)
        m2 = kb.sb([64, 512], F32)
        kb.op("dve", lambda e: e.memset(negpi[:], -math.pi), writes=[negpi])
        for j in range(NQ):
            z = zt[j % 2]
            kb.dma("sp", z[:], C["zT"][:, j * 512:(j + 1) * 512], writes=[z])
            sin_layer(w1, 33, lambda j, z=z: (z[:], z), H1, 1, 0, j, next_pb())
        for j in range(NQ):
            sin_layer(w2, 64, lambda j: (H1[:, j * 512:(j + 1) * 512], H1), H2, 3, 1, j, next_pb())
        hsds = [kb.sb([128, 2, DH], BF16) for _ in range(2)]
        dec = [kb.sb([128, DH], F32) for _ in range(2)]
        t1 = [kb.sb([128, DH], F32) for _ in range(2)]
        t2 = [kb.sb([128, DH], F32) for _ in range(2)]
        for i in range(NT):
            dc, a1, a2 = dec[i % 2], t1[i % 2], t2[i % 2]
            b0, b1 = next_pb(), next_pb()
            for half, bank in ((0, b0), (1, b1)):
                kb.op("pe", lambda e, half=half, bank=bank: e.matmul(
                    kb.ps(bank), lhsT=H2[:, i * 128:(i + 1) * 128], rhs=w3[:, half * 512:(half + 1) * 512],
                    start=True, stop=True), reads=[H2, w3], writes=[PB[bank]])
            kb.op("act", lambda e, dc=dc: e.activation(out=dc[:], in_=absd[:], func=AF.Exp, scale=tcol[:, i:i + 1]),
                  reads=[absd, tcol], writes=[dc])
            kb.op("dve", lambda e, dc=dc, a1=a1, b0=b0: e.tensor_tensor(out=a1[:], in0=kb.ps(b0), in1=dc[:], op=ALU.mult),
                  reads=[PB[b0], dc], writes=[a1])
            kb.op("dve", lambda e, dc=dc, a2=a2, b1=b1: e.tensor_tensor(out=a2[:], in0=kb.ps(b1), in1=dc[:], op=ALU.mult),
                  reads=[PB[b1], dc], writes=[a2])
            if i == 0:
                kb.op("dve", lambda e, a2=a2: e.memset(a2[0:1, :], 0.0), reads=[a2], writes=[a2])
            hsd = hsds[i % 2]
            kb.op("pool", lambda e, a1=a1, a2=a2: e.tensor_tensor(out=hsd[:, 0, :], in0=a1[:], in1=a2[:], op=ALU.add),
                  reads=[a1, a2], writes=[(hsd, 0)])
            kb.op("pool", lambda e, a1=a1, a2=a2: e.tensor_tensor(out=hsd[:, 1, :], in0=a1[:], in1=a2[:], op=ALU.subtract),
                  reads=[a1, a2], writes=[(hsd, 1)])
            kb.dma("pool", HSD.t.rearrange("s n c -> n s c")[i * 128:(i + 1) * 128, :, :], hsd[:], reads=[hsd])
        phase_begin(label="filter_s1")
        tw1 = load_tw1()
        s1b = s1_bufs()
        fft_s1(HSD[0], A1K.t[0], tw1, s1b)
        fft_s1(HSD[1], A1K.t[1], tw1, s1b)
        phase_begin(label="filter_s2")
        dftm = kb.sb([64, 3, 64], BF16)
        kb.dma("sp", dftm[:], C["dftm"][:, :, :], writes=[dftm])
        FC = min(4, HH)
        at_ = [[kb.sb([64, FC, DH], BF16) for _ in range(4)] for _ in range(2)]
        ko = [[kb.sb([64, FC, DH], BF16) for _ in range(2)] for _ in range(2)]
        for c_ in range(HH // FC):
            f0 = c_ * FC
            A = at_[c_ % 2]
            for q_, (sg_, ri_) in enumerate(((0, 0), (0, 1), (1, 0), (1, 1))):
                kb.dma("sp", A[q_][:], A1K.t[sg_, ri_].rearrange("f n c -> n f c")[:, f0:f0 + FC, :], writes=[A[q_]])
            kos = ko[c_ % 2]
            for fl in range(FC):
                br, bi = next_pb(), next_pb()
                kb.op("pe", lambda e: e.matmul(kb.ps(br)[0:64, :], lhsT=dftm[:, 0, :], rhs=A[0][:, fl, :], start=True, stop=False),
                      reads=[dftm, A[0]], writes=[PB[br]])
                kb.op("pe", lambda e: e.matmul(kb.ps(br)[0:64, :], lhsT=dftm[:, 1, :], rhs=A[1][:, fl, :], start=False, stop=True),
                      reads=[dftm, A[1]], writes=[PB[br]])
                kb.op("pe", lambda e: e.matmul(kb.ps(bi)[0:64, :], lhsT=dftm[:, 0, :], rhs=A[3][:, fl, :], start=True, stop=False),
                      reads=[dftm, A[3]], writes=[PB[bi]])
                kb.op("pe", lambda e: e.matmul(kb.ps(bi)[0:64, :], lhsT=dftm[:, 2, :], rhs=A[2][:, fl, :], start=False, stop=True),
                      reads=[dftm, A[2]], writes=[PB[bi]])
                kb.op("act", lambda e: e.copy(out=kos[0][:, fl, :], in_=kb.ps(br)[0:64, :]), reads=[PB[br]], writes=[(kos[0], fl)])
                kb.op("dve", lambda e: e.tensor_copy(out=kos[1][:, fl, :], in_=kb.ps(bi)[0:64, :]), reads=[PB[bi]], writes=[(kos[1], fl)])
            for ri_ in range(2):
                kb.dma("pool", KS.t[ri_, f0:f0 + FC].rearrange("f k c -> k f c"), kos[ri_][:], reads=[kos[ri_]])

    def load_tw1():
        tw1 = kb.sb([HH, 2, 64, HH], BF16)
        kb.dma("sp", tw1[:], C["tw1"][:, :, :, :], writes=[tw1])
        return tw1

    def s1_bufs():
        return ([kb.sb([HH, 8, DH], BF16) for _ in range(2)],
                [[kb.sb([HH, 8, DH], BF16) for _ in range(2)] for _ in range(2)])

    def fft_s1(src, dst, tw1, s1b):
        xv = src.rearrange("(a b) c -> a b c", b=64)
        if not hasattr(fft_s1, "bufs"):
            pass
        xt, ot = s1b
        ne = 0
        for g in range(8):
            x_ = xt[g % 2]
            kb.dma("sp", x_[:], xv[:, g * 8:(g + 1) * 8, :], writes=[x_])
            for ri_ in range(2):
                o_ = ot[g % 2][ri_]
                for nl in range(8):
                    bank = next_pb()
                    kb.op("pe", lambda e: e.matmul(kb.ps(bank)[0:HH, :], lhsT=tw1[:, ri_, g * 8 + nl, :], rhs=x_[:, nl, :],
                                                   start=True, stop=True), reads=[tw1, x_], writes=[PB[bank]])
                    if ne % 2 == 0:
                        kb.op("act", lambda e: e.copy(out=o_[:, nl, :], in_=kb.ps(bank)[0:HH, :]), reads=[PB[bank]], writes=[(o_, nl)])
                    else:
                        kb.op("dve", lambda e: e.tensor_copy(out=o_[:, nl, :], in_=kb.ps(bank)[0:HH, :]), reads=[PB[bank]], writes=[(o_, nl)])
                    ne += 1
                kb.dma("pool", dst[ri_, :, g * 8:(g + 1) * 8, :], o_[:], reads=[o_])

    def phase_norm0(xsrc):
        phase_begin()
        tmps = norm_tmp()
        xts = [kb.sb([128, D], F32) for _ in range(2)]
        for i in range(NT):
            xt = xts[i % 2]
            kb.dma("sp", xt[:], xsrc[i * 128:(i + 1) * 128, :], writes=[xt])
            norm_transpose(xt, i, tmps[i % 2])

    def phase_proj(li):
        phase_begin(0)
        gt = kb.sb([128, 8], F32)
        kb.dma("sp", gt[:], P["g_mix_pre"][li], writes=[gt])
        W = P["w_in"][li]
        ev = [kb.sb([128, 512], BF16) for _ in range(4)]
        evc = [0]

        def next_ev():
            evc[0] += 1
            return ev[evc[0] % 4]

        cw = kb.sb([128, 12, 3], F32)
        cb = kb.sb([128, 12], F32)
        kb.dma("sp", cw[:], P["hy_cw"][li], writes=[cw])
        kb.dma("sp", cb[:], P["hy_cb"][li], writes=[cb])
        raws = [kb.sb([128, L + 2], BF16) for _ in range(2)]
        for r in raws:
            kb.op("pool", lambda e, r=r: e.memset(r[:, 0:1], 0.0), writes=[(r, "h0")])
            kb.op("pool", lambda e, r=r: e.memset(r[:, L + 1:L + 2], 0.0), writes=[(r, "h1")])
        acc = kb.sb([128, L], F32)
        x1c = kb.sb([128, L], BF16)
        oc = [kb.sb([128, L], BF16) for _ in range(2)]
        tz = [kb.sb([128, 8, 128], BF16) for _ in range(2)]
        wfs = [kb.sb([128, 8, 128], F32) for _ in range(2)]
        wbs = [kb.sb([128, 8, 128], BF16) for _ in range(2)]
        wsrc = W.rearrange("(k p) c -> p k c", p=128)
        order = [(b, part) for b in range(4) for part in range(3)]

        def prep(n_):
            b_, part_ = order[n_]
            blk_ = part_ * 4 + b_
            s_ = n_ % 2
            kb.dma("sp", wfs[s_][:], wsrc[:, :, blk_ * 128:(blk_ + 1) * 128], writes=[wfs[s_]])
            kb.op("pool", lambda e: e.tensor_tensor(out=wbs[s_][:], in0=wfs[s_][:],
                                                    in1=gt[:, :, None].to_broadcast([128, 8, 128]), op=ALU.mult),
                  reads=[wfs[s_], gt], writes=[wbs[s_]])
            return wbs[s_]

        zvt_v = ZVT.t.rearrange("(i p) c -> p i c", p=128)
        TB = min(8, NT)

        def transposes(dst, b):
            for i0 in range(0, NT, TB):
                tzt = tz[(i0 // TB) % 2]
                for ii in range(TB):
                    kb.op("pe", lambda e, ii=ii: e.transpose(out=kb.ps(7, [128, 8, 128], BF16)[:, ii, :],
                                                             in_=dst[:, (i0 + ii) * 128:(i0 + ii + 1) * 128], identity=ident[:]),
                          reads=[dst, ident], writes=[PB[7]])
                kb.op("act", lambda e: e.copy(out=tzt[:, 0:TB, :], in_=kb.ps(7, [128, 8, 128], BF16)[:, 0:TB, :]),
                      reads=[PB[7]], writes=[tzt])
                kb.dma("pool", zvt_v[:, i0:i0 + TB, b * 128:(b + 1) * 128], tzt[:, 0:TB, :], reads=[tzt])

        wnext = prep(0)
        pending = None
        for n_, (b, part) in enumerate(order):
            blk = part * 4 + b
            raw = raws[n_ % 2]
            wbv = wnext
            if n_ + 1 < len(order):
                wnext = prep(n_ + 1)
            for j in range(NQ):
                bank = next_pb()
                gemm_fm(wbv, 8, 0, 128, HT, j, bank)
                kb.op("act", lambda e: e.copy(out=raw[:, 1 + j * 512:1 + (j + 1) * 512], in_=kb.ps(bank)),
                      reads=[PB[bank]], writes=[(raw, j)])
            if pending is not None:
                transposes(*pending)
                pending = None
            dst = x1c if part == 1 else oc[0 if part == 0 else 1]
            kb.op("dve", lambda e: e.tensor_scalar(out=acc[:], in0=raw[:, 0:L], scalar1=cw[:, blk, 0:1], scalar2=cb[:, blk:blk + 1],
                                                   op0=ALU.mult, op1=ALU.add), reads=[raw, cw, cb], writes=[acc])
            kb.op("dve", lambda e: e.scalar_tensor_tensor(out=acc[:], in0=raw[:, 1:L + 1], scalar=cw[:, blk, 1:2], in1=acc[:],
                                                          op0=ALU.mult, op1=ALU.add), reads=[raw, cw, acc], writes=[acc])
            kb.op("dve", lambda e: e.scalar_tensor_tensor(out=dst[:], in0=raw[:, 2:L + 2], scalar=cw[:, blk, 2:3], in1=acc[:],
                                                          op0=ALU.mult, op1=ALU.add), reads=[raw, cw, acc], writes=[dst])
            if part == 0:
                kb.dma("pool", X0fm[b * 128:(b + 1) * 128, :], dst[:], reads=[dst])
            elif part == 2:
                kb.op("dve", lambda e: e.tensor_tensor(out=dst[:], in0=dst[:], in1=x1c[:], op=ALU.mult),
                      reads=[dst, x1c], writes=[dst])
                kb.dma("pool", ZVfm[b * 128:(b + 1) * 128, :], dst[:], reads=[dst])
                pending = (dst, b)
        transposes(*pending)

        phase_begin(2)
        gt = kb.sb([128, 8], F32)
        kb.dma("sp", gt[:], P["g_mix_pre"][li], writes=[gt])
        ev = [kb.sb([128, 512], BF16) for _ in range(4)]
        def fm_group(col0, ncols, dest, func, scale=1.0):
            for c0 in range(0, ncols, 512):
                nc_ = min(512, ncols - c0)
                wbv = load_w(W, 8, col0 + c0, nc_, gt)
                for m in range(nc_ // 128):
                    for j in range(NQ):
                        bank = next_pb()
                        gemm_fm(wbv, 8, m * 128, 128, HT, j, bank)
                        o = next_ev()
                        kb.op("act", lambda e, o=o, bank=bank: e.activation(out=o[:], in_=kb.ps(bank), func=func, scale=scale),
                              reads=[PB[bank]], writes=[o])
                        r0 = c0 + m * 128
                        kb.dma("pool", dest[r0:r0 + 128, j * 512:(j + 1) * 512], o[:], reads=[o])

        def tok_group(col0, dest):
            wbv = load_w(W, 8, col0, 512, gt)
            for i in range(NT):
                bank = next_pb()
                gemm_tok(wbv, 8, 0, 512, HT, i, bank)
                o = next_ev()
                if i % 2 == 0:
                    kb.op("dve", lambda e, o=o, bank=bank: e.tensor_copy(out=o[:], in_=kb.ps(bank)), reads=[PB[bank]], writes=[o])
                else:
                    kb.op("act", lambda e, o=o, bank=bank: e.copy(out=o[:], in_=kb.ps(bank)), reads=[PB[bank]], writes=[o])
                kb.dma("pool", dest[i * 128:(i + 1) * 128, :], o[:], reads=[o])

        fm_group(1536, 512, Qfm, AF.Copy, 128.0 ** -0.5)
        fm_group(2048, 512, Kfm, AF.Copy)
        tok_group(2048, Ktok)
        tok_group(2560, Vtok)
        fm_group(3072, 512, Gfm, AF.Silu)
        wbv = load_w(W, 8, 3584, 32, gt)
        for j in range(NQ):
            bank = next_pb()
            gemm_fm(wbv, 8, 0, 32, HT, j, bank)
            kb.op("act", lambda e, j=j, bank=bank: e.copy(out=LRH[0:32, j * 512:(j + 1) * 512], in_=kb.ps(bank)[0:32, :]),
                  reads=[PB[bank]], writes=[(LRH, j)])
            kb.op("dve", lambda e, j=j, bank=bank: e.tensor_tensor(out=LRL[0:32, j * 512:(j + 1) * 512], in0=kb.ps(bank)[0:32, :],
                                                                   in1=LRH[0:32, j * 512:(j + 1) * 512], op=ALU.subtract),
                  reads=[PB[bank], (LRH, j)], writes=[(LRL, j)])
        fm_group(4128, 3 * D, GATES, AF.Sigmoid)

        phase_begin(1)
        gt = kb.sb([128, 8], F32)
        kb.dma("sp", gt[:], P["g_mix_pre"][li], writes=[gt])
        ev = [kb.sb([128, 512], BF16) for _ in range(4)]
        PW = 16
        ua = kb.sb([128, L + 2 * PW], F32)
        ub = kb.sb([128, L + 2 * PW], F32)
        uc = kb.sb([128, L + 2 * PW], F32)
        icn = kb.sb([128, L], F32)
        pwt = kb.sb([128, 4, 128], F32)
        pwb = kb.sb([128, 4, 128], BF16)
        psc = kb.sb([128, 4], F32)
        dbf = kb.sb([128, L], BF16)
        kb.dma("sp", pwt[:], P["pool_w"][li], writes=[pwt])
        kb.dma("sp", psc[:], P["pool_scale"][li], writes=[psc])
        kb.op("pool", lambda e: e.tensor_copy(out=pwb[:], in_=pwt[:]), reads=[pwt], writes=[pwb])
        for t_ in (ua, ub, uc):
            kb.op("pool", lambda e, t_=t_: e.memset(t_[:], 0.0), writes=[t_])
        for gi, wv in enumerate((2, 4, 8, 16)):
            wbv = load_w(W, 8, 3616 + gi * 128, 128, gt)
            kb.dma("sp", icn[:], C["invcnt"][gi], writes=[icn])
            for j in range(NQ):
                bank = next_pb()
                gemm_fm(wbv, 8, 0, 128, HT, j, bank)
                kb.op("act", lambda e, j=j, bank=bank: e.copy(out=ua[:, PW + j * 512:PW + (j + 1) * 512], in_=kb.ps(bank)),
                      reads=[PB[bank]], writes=[ua])
            src, dsts = ua, [ub, uc]
            lo, hi = -14, L + 14
            kb.op("dve", lambda e, lo=lo, hi=hi: e.tensor_tensor(
                out=ub[:, PW + lo:PW + hi], in0=ua[:, PW + lo - 1:PW + hi - 1], in1=ua[:, PW + lo:PW + hi], op=ALU.add),
                reads=[ua], writes=[ub])
            cur, oth = ub, uc
            sh = 1
            rng = [(-12, L + 12), (-8, L + 8), (0, L)]
            for si in range(int(math.log2(wv)) - 1):
                lo, hi = rng[si]
                kb.op("dve", lambda e, lo=lo, hi=hi, cur=cur, oth=oth, sh=sh: e.tensor_tensor(
                    out=oth[:, PW + lo:PW + hi], in0=cur[:, PW + lo - sh:PW + hi - sh],
                    in1=cur[:, PW + lo + sh:PW + hi + sh], op=ALU.add), reads=[cur], writes=[oth])
                cur, oth = oth, cur
                sh *= 2
            kb.op("dve", lambda e, cur=cur, oth=oth: e.tensor_tensor(out=oth[:, PW:PW + L], in0=cur[:, PW:PW + L], in1=icn[:], op=ALU.mult),
                  reads=[cur, icn], writes=[oth])
            kb.op("dve", lambda e, oth=oth: e.tensor_tensor(out=dbf[:], in0=oth[:, PW:PW + L], in1=ua[:, PW:PW + L], op=ALU.subtract),
                  reads=[oth, ua], writes=[dbf])
            for t_ in (ub, uc):
                kb.op("pool", lambda e, t_=t_: e.memset(t_[:, 0:PW], 0.0), reads=[t_], writes=[t_])
                kb.op("pool", lambda e, t_=t_: e.memset(t_[:, PW + L:PW + L + PW], 0.0), reads=[t_], writes=[t_])
            for j in range(NQ):
                bank = next_pb()
                kb.op("pe", lambda e, j=j, bank=bank, gi=gi: e.matmul(kb.ps(bank), lhsT=pwb[:, gi, :], rhs=dbf[:, j * 512:(j + 1) * 512],
                                                               start=True, stop=True), reads=[pwb, dbf], writes=[PB[bank]])
                o = next_ev()
                kb.op("dve", lambda e, o=o, bank=bank, gi=gi: e.tensor_scalar(out=o[:], in0=kb.ps(bank), scalar1=psc[:, gi:gi + 1],
                                                                       scalar2=None, op0=ALU.mult), reads=[PB[bank], psc], writes=[o])
                kb.dma("pool", YP[gi * 128:(gi + 1) * 128, j * 512:(j + 1) * 512], o[:], reads=[o])

    def phase_hyena(li):
        phase_begin(label="hy_s1")
        tw1 = load_tw1()
        fft_s1(ZVT.t, A1Z.t, tw1, s1_bufs())
        phase_begin(label="hy_s2")
        dftm = kb.sb([64, 3, 64], BF16)
        gtw = kb.sb([64, 3, HH, 64], BF16)
        kb.dma("sp", dftm[:], C["dftm"][:, :, :], writes=[dftm])
        kb.dma("sp", gtw[:], C["gtw"][:, :, :, :], writes=[gtw])
        FC = min(4, HH)
        at_ = [[kb.sb([64, FC, DH], BF16) for _ in range(2)] for _ in range(2)]
        kt_ = [[kb.sb([64, FC, DH], BF16) for _ in range(2)] for _ in range(2)]
        bo = [[kb.sb([64, FC, DH], BF16) for _ in range(2)] for _ in range(2)]
        m = [[kb.sb([64, DH], F32) for _ in range(4)] for _ in range(2)]
        yy = [[kb.sb([64, DH], BF16) for _ in range(2)] for _ in range(2)]
        def s2_prod(f1_):
            c_, fl = divmod(f1_, FC)
            f0 = c_ * FC
            A, K_ = at_[c_ % 2], kt_[c_ % 2]
            if fl == 0:
                for ri_ in range(2):
                    kb.dma("sp", A[ri_][:], A1Z.t[ri_].rearrange("f n c -> n f c")[:, f0:f0 + FC, :], writes=[A[ri_]])
                    kb.dma("sp", K_[ri_][:], KS.t[ri_, f0:f0 + FC].rearrange("f k c -> k f c"), writes=[K_[ri_]])
            mm_, y_ = m[f1_ % 2], yy[f1_ % 2]
            zr, zi = f1_ % 2, 2 + f1_ % 2
            kb.op("pe", lambda e: e.matmul(kb.ps(zr)[0:64, :], lhsT=dftm[:, 0, :], rhs=A[0][:, fl, :], start=True, stop=False),
                  reads=[dftm, A[0]], writes=[PB[zr]])
            kb.op("pe", lambda e: e.matmul(kb.ps(zr)[0:64, :], lhsT=dftm[:, 1, :], rhs=A[1][:, fl, :], start=False, stop=True),
                  reads=[dftm, A[1]], writes=[PB[zr]])
            kb.op("pe", lambda e: e.matmul(kb.ps(zi)[0:64, :], lhsT=dftm[:, 0, :], rhs=A[1][:, fl, :], start=True, stop=False),
                  reads=[dftm, A[1]], writes=[PB[zi]])
            kb.op("pe", lambda e: e.matmul(kb.ps(zi)[0:64, :], lhsT=dftm[:, 2, :], rhs=A[0][:, fl, :], start=False, stop=True),
                  reads=[dftm, A[0]], writes=[PB[zi]])
            kb.op("dve", lambda e: e.tensor_tensor(out=mm_[0][:], in0=kb.ps(zr)[0:64, :], in1=K_[0][:, fl, :], op=ALU.mult),
                  reads=[PB[zr], K_[0]], writes=[mm_[0]])
            kb.op("dve", lambda e: e.tensor_tensor(out=mm_[1][:], in0=kb.ps(zi)[0:64, :], in1=K_[1][:, fl, :], op=ALU.mult),
                  reads=[PB[zi], K_[1]], writes=[mm_[1]])
            kb.op("dve", lambda e: e.tensor_tensor(out=mm_[2][:], in0=kb.ps(zr)[0:64, :], in1=K_[1][:, fl, :], op=ALU.mult),
                  reads=[PB[zr], K_[1]], writes=[mm_[2]])
            kb.op("dve", lambda e: e.tensor_tensor(out=mm_[3][:], in0=kb.ps(zi)[0:64, :], in1=K_[0][:, fl, :], op=ALU.mult),
                  reads=[PB[zi], K_[0]], writes=[mm_[3]])
            kb.op("pool", lambda e: e.tensor_tensor(out=y_[0][:], in0=mm_[0][:], in1=mm_[1][:], op=ALU.subtract),
                  reads=[mm_[0], mm_[1]], writes=[y_[0]])
            kb.op("pool", lambda e: e.tensor_tensor(out=y_[1][:], in0=mm_[2][:], in1=mm_[3][:], op=ALU.add),
                  reads=[mm_[2], mm_[3]], writes=[y_[1]])

        def inv_a(f1_):
            c_, fl = divmod(f1_, FC)
            f0 = c_ * FC
            y_ = yy[f1_ % 2]
            bos = bo[c_ % 2]
            br, bi = 4 + f1_ % 2, 6 + f1_ % 2
            kb.op("pe", lambda e: e.matmul(kb.ps(br)[0:64, :], lhsT=gtw[:, 0, f1_, :], rhs=y_[0][:], start=True, stop=False),
                  reads=[gtw, y_[0]], writes=[PB[br]])
            kb.op("pe", lambda e: e.matmul(kb.ps(br)[0:64, :], lhsT=gtw[:, 2, f1_, :], rhs=y_[1][:], start=False, stop=True),
                  reads=[gtw, y_[1]], writes=[PB[br]])
            kb.op("pe", lambda e: e.matmul(kb.ps(bi)[0:64, :], lhsT=gtw[:, 1, f1_, :], rhs=y_[0][:], start=True, stop=False),
                  reads=[gtw, y_[0]], writes=[PB[bi]])
            kb.op("pe", lambda e: e.matmul(kb.ps(bi)[0:64, :], lhsT=gtw[:, 0, f1_, :], rhs=y_[1][:], start=False, stop=True),
                  reads=[gtw, y_[1]], writes=[PB[bi]])
            kb.op("act", lambda e: e.copy(out=bos[0][:, fl, :], in_=kb.ps(br)[0:64, :]), reads=[PB[br]], writes=[(bos[0], fl)])
            kb.op("act", lambda e: e.copy(out=bos[1][:, fl, :], in_=kb.ps(bi)[0:64, :]), reads=[PB[bi]], writes=[(bos[1], fl)])
            if fl == FC - 1:
                for ri_ in range(2):
                    kb.dma("pool", B1.t[ri_, :, f0:f0 + FC, :], bos[ri_][:], reads=[bos[ri_]])

        for k_ in range(HH + 1):
            if k_ < HH:
                s2_prod(k_)
            if k_ >= 1:
                inv_a(k_ - 1)
        phase_begin(label="hy_ib")
        hb = kb.sb([128, 4], F32)
        kb.dma("sp", hb[:], P["hy_bias"][li], writes=[hb])
        m4 = kb.sb([HH, 2, HH], BF16)
        kb.dma("sp", m4[:], C["m4"][:, :, :], writes=[m4])
        ysb = kb.sb([128, 4, L], BF16)
        bt_ = [[kb.sb([HH, 8, DH], BF16) for _ in range(2)] for _ in range(2)]
        for g in range(8):
            Bt = bt_[g % 2]
            for ri_ in range(2):
                kb.dma("sp", Bt[ri_][:], B1.t[ri_].rearrange("t f c -> f t c")[:, g * 8:(g + 1) * 8, :], writes=[Bt[ri_]])
            for cb_ in range(4):
                bank = next_pb()
                for tl in range(8):
                    kb.op("pe", lambda e: e.matmul(kb.ps(bank, [128, 8, HH])[:, tl, :], lhsT=Bt[0][:, tl, cb_ * 128:(cb_ + 1) * 128],
                                                   rhs=m4[:, 0, :], start=True, stop=False), reads=[Bt[0], m4], writes=[PB[bank]])
                    kb.op("pe", lambda e: e.matmul(kb.ps(bank, [128, 8, HH])[:, tl, :], lhsT=Bt[1][:, tl, cb_ * 128:(cb_ + 1) * 128],
                                                   rhs=m4[:, 1, :], start=False, stop=True), reads=[Bt[1], m4], writes=[PB[bank]])
                dst = ysb[:, cb_, :].rearrange("p (a b) -> p a b", b=64)[:, :, g * 8:(g + 1) * 8]
                src_ = kb.ps(bank, [128, 8, HH]).rearrange("p a b -> p b a")
                if (g * 4 + cb_) % 2 == 0:
                    kb.op("act", lambda e: e.copy(out=dst, in_=src_), reads=[PB[bank]], writes=[(ysb, (cb_, g))])
                else:
                    kb.op("dve", lambda e: e.tensor_copy(out=dst, in_=src_), reads=[PB[bank]], writes=[(ysb, (cb_, g))])
        x0t = [kb.sb([128, 512], BF16) for _ in range(2)]
        zvf = [kb.sb([128, 512], BF16) for _ in range(2)]
        tmp = [kb.sb([128, 512], F32) for _ in range(2)]
        yo = [kb.sb([128, 512], BF16) for _ in range(2)]
        for tt in range(NQ):
            for cb_ in range(4):
                k = (tt * 4 + cb_) % 2
                kb.dma("sp", x0t[k][:], X0fm[cb_ * 128:(cb_ + 1) * 128, tt * 512:(tt + 1) * 512], writes=[x0t[k]])
                kb.dma("sp", zvf[k][:], ZVfm[cb_ * 128:(cb_ + 1) * 128, tt * 512:(tt + 1) * 512], writes=[zvf[k]])
                kb.op("dve", lambda e: e.scalar_tensor_tensor(
                    out=tmp[k][:], in0=zvf[k][:], scalar=hb[:, cb_:cb_ + 1], in1=ysb[:, cb_, tt * 512:(tt + 1) * 512],
                    op0=ALU.mult, op1=ALU.add), reads=[zvf[k], hb, ysb], writes=[tmp[k]])
                kb.op("pool", lambda e: e.tensor_tensor(out=yo[k][:], in0=tmp[k][:], in1=x0t[k][:], op=ALU.mult),
                      reads=[tmp[k], x0t[k]], writes=[yo[k]])
                kb.dma("pool", YH[cb_ * 128:(cb_ + 1) * 128, tt * 512:(tt + 1) * 512], yo[k][:], reads=[yo[k]])
        st["pb"] = 0

    def phase_gla(li):
        phase_begin()
        w2x = kb.sb([33, 2, 512], F32)
        gn = kb.sb([128, 1], F32)
        kb.dma("sp", w2x[:], P["gla_w2x"][li], writes=[w2x])
        kb.dma("sp", gn[:], P["gla_norm"][li], writes=[gn])
        w2h = kb.sb([33, 2, 512], BF16)
        w2l = kb.sb([33, 2, 512], BF16)
        kb.op("dve", lambda e: e.tensor_copy(out=w2h[:], in_=w2x[:]), reads=[w2x], writes=[w2h])
        kb.op("dve", lambda e: e.tensor_tensor(out=w2l[:], in0=w2x[:], in1=w2h[:], op=ALU.subtract), reads=[w2x, w2h], writes=[w2l])
        OB = kb.sb([128, 4, L], BF16)
        S32 = [kb.sb([128, 4, 128], F32) for _ in range(2)]
        Sbf = [kb.sb([128, 4, 128], BF16) for _ in range(2)]
        for d_ in range(2):
            kb.op("pool", lambda e, d_=d_: e.memset(S32[d_][:], 0.0), writes=[S32[d_]])
            kb.op("pool", lambda e, d_=d_: e.memset(Sbf[d_][:], 0.0), writes=[Sbf[d_]])
        lah = [kb.sb([128, 512], BF16) for _ in range(2)]
        lal = [kb.sb([128, 512], BF16) for _ in range(2)]
        PD = lambda shape, dt: [kb.sb(shape, dt) for _ in range(2)]
        PS = lambda shape, dt: [[kb.sb(shape, dt) for _ in range(2)] for _ in range(2)]
        e1, la, edec = PD([128, 512], F32), PD([128, 512], F32), PD([128, 512], BF16)
        EK = PD([128, 4, 128], BF16)
        kt, qf, kf, ke = PD([128, 512], BF16), PD([128, 4, 128], BF16), PD([128, 4, 128], BF16), PD([128, 4, 128], BF16)
        EQ = PS([128, 4, 128], BF16)
        DEC = PS([128, 4, 1], F32)
        vt, kdec = PS([128, 512], BF16), PS([128, 512], BF16)
        qe, msk = PS([128, 4, 128], BF16), PS([128, 4, 128], BF16)
        gf, sqb, yob = PD([128, 4, 128], BF16), PD([128, 4, 128], BF16), PD([128, 4, 128], BF16)
        o32, rsd = PD([128, 4, 128], F32), PD([128, 4, 128], F32)
        qv = Qfm.t.rearrange("(h d) t -> d h t", h=4)
        kv = Kfm.t.rearrange("(h d) t -> d h t", h=4)
        gv_ = Gfm.t.rearrange("(h d) t -> d h t", h=4)
        yv = YG.t.rearrange("(h d) t -> d h t", h=4)
        V3 = [128, 4, 128]

        def tile_of(step, d_):
            return step if d_ == 0 else NT - 1 - step

        def pre(step, d_, stage):
            i = tile_of(step, d_)
            sl = slice(i * 128, (i + 1) * 128)
            p = step % 2
            bA, bB = 4 * d_, 4 * d_ + 1
            tri_fm = 0 if d_ == 0 else 2
            tri_dec = 3 if d_ == 0 else 1
            if stage == 1:
                pre1(d_, p, sl, bA)
            elif stage == 2:
                pre2(d_, p, bA, bB, tri_fm, tri_dec)
            else:
                pre3(d_, p, bA)

        def pre1(d_, p, sl, bA):
            kb.dma("sp", kt[d_][:], Ktok[sl, :], writes=[kt[d_]])
            kb.dma("sp", vt[d_][p][:], Vtok[sl, :], writes=[vt[d_][p]])
            kb.dma("sp", qf[d_][:], qv[:, :, sl], writes=[qf[d_]])
            kb.dma("sp", kf[d_][:], kv[:, :, sl], writes=[kf[d_]])
            kb.op("pe", lambda e: e.matmul(kb.ps(bA), lhsT=LRH[:, sl], rhs=w2h[:, d_, :], start=True, stop=False),
                  reads=[LRH, w2h], writes=[PB[bA]])
            kb.op("pe", lambda e: e.matmul(kb.ps(bA), lhsT=LRL[:, sl], rhs=w2h[:, d_, :], start=False, stop=False),
                  reads=[LRL, w2h], writes=[PB[bA]])
            kb.op("pe", lambda e: e.matmul(kb.ps(bA), lhsT=LRH[:, sl], rhs=w2l[:, d_, :], start=False, stop=True),
                  reads=[LRH, w2l], writes=[PB[bA]])
            kb.op("act", lambda e: e.activation(out=e1[d_][:], in_=kb.ps(bA), func=AF.Exp, scale=-1.0),
                  reads=[PB[bA]], writes=[e1[d_]])
            kb.op("act", lambda e: e.activation(out=e1[d_][:], in_=e1[d_][:], func=AF.Ln, bias=1.0),
                  reads=[e1[d_]], writes=[e1[d_]])
            kb.op("dve", lambda e: e.tensor_scalar(out=la[d_][:], in0=e1[d_][:], scalar1=-1.0 / 16.0, scalar2=-1.0,
                                                   op0=ALU.mult, op1=ALU.max), reads=[e1[d_]], writes=[la[d_]])
            kb.op("act", lambda e: e.copy(out=lah[d_][:], in_=la[d_][:]), reads=[la[d_]], writes=[lah[d_]])
            kb.op("dve", lambda e: e.tensor_tensor(out=lal[d_][:], in0=la[d_][:], in1=lah[d_][:], op=ALU.subtract),
                  reads=[la[d_], lah[d_]], writes=[lal[d_]])

        def pre2(d_, p, bA, bB, tri_fm, tri_dec):
            kb.op("pe", lambda e: e.matmul(kb.ps(bA), lhsT=tri32[:, tri_dec, :], rhs=lah[d_][:], start=True, stop=False),
                  reads=[tri32, lah[d_]], writes=[PB[bA]])
            kb.op("pe", lambda e: e.matmul(kb.ps(bA), lhsT=tri32[:, tri_dec, :], rhs=lal[d_][:], start=False, stop=True),
                  reads=[tri32, lal[d_]], writes=[PB[bA]])
            for h in range(4):
                kb.op("pe", lambda e, h=h: e.matmul(kb.ps(bB, V3)[:, h, :], lhsT=lah[d_][:, h * 128:(h + 1) * 128],
                                                    rhs=tri32[:, tri_fm, :], start=True, stop=False),
                      reads=[tri32, lah[d_]], writes=[PB[bB]])
                kb.op("pe", lambda e, h=h: e.matmul(kb.ps(bB, V3)[:, h, :], lhsT=lal[d_][:, h * 128:(h + 1) * 128],
                                                    rhs=tri32[:, tri_fm, :], start=False, stop=True),
                      reads=[tri32, lal[d_]], writes=[PB[bB]])
            kb.op("act", lambda e: e.activation(out=edec[d_][:], in_=kb.ps(bA), func=AF.Exp), reads=[PB[bA]], writes=[edec[d_]])
            kb.op("act", lambda e: e.activation(out=EQ[d_][p][:], in_=kb.ps(bB, V3), func=AF.Exp), reads=[PB[bB]], writes=[EQ[d_][p]])
            kb.op("act", lambda e: e.activation(out=EK[d_][:], in_=kb.ps(bB, V3), func=AF.Exp, scale=-1.0),
                  reads=[PB[bB]], writes=[EK[d_]])
            dcol_ = 127 if d_ == 0 else 0
            kb.op("act", lambda e: e.activation(out=DEC[d_][p][:], in_=kb.ps(bB, V3)[:, :, dcol_:dcol_ + 1], func=AF.Exp),
                  reads=[PB[bB]], writes=[DEC[d_][p]])
            kb.op("dve", lambda e: e.tensor_tensor(out=kdec[d_][p][:], in0=kt[d_][:], in1=edec[d_][:], op=ALU.mult),
                  reads=[kt[d_], edec[d_]], writes=[kdec[d_][p]])
            kb.op("dve", lambda e: e.tensor_tensor(out=qe[d_][p][:], in0=qf[d_][:], in1=EQ[d_][p][:], op=ALU.mult),
                  reads=[qf[d_], EQ[d_][p]], writes=[qe[d_][p]])
            kb.op("dve", lambda e: e.tensor_tensor(out=ke[d_][:], in0=kf[d_][:], in1=EK[d_][:], op=ALU.mult),
                  reads=[kf[d_], EK[d_]], writes=[ke[d_]])

        def pre3(d_, p, bA):
            for h in range(4):
                kb.op("pe", lambda e, h=h: e.matmul(kb.ps(bA, V3)[:, h, :], lhsT=ke[d_][:, h, :], rhs=qe[d_][p][:, h, :],
                                                    start=True, stop=True), reads=[ke[d_], qe[d_][p]], writes=[PB[bA]])
            kb.op("dve", lambda e: e.tensor_tensor(out=msk[d_][p][:], in0=kb.ps(bA, V3),
                                                   in1=tri32[:, 3 * d_:3 * d_ + 1, :].to_broadcast(V3), op=ALU.mult),
                  reads=[PB[bA], tri32], writes=[msk[d_][p]])

        def seq(step, d_):
            i = tile_of(step, d_)
            sl = slice(i * 128, (i + 1) * 128)
            p = step % 2
            bC, bD = 4 * d_ + 2, 4 * d_ + 3
            final = step >= NT // 2
            for h in range(4):
                hs = slice(h * 128, (h + 1) * 128)
                kb.op("pe", lambda e, h=h, hs=hs: e.matmul(kb.ps(bC, V3)[:, h, :], lhsT=vt[d_][p][:, hs], rhs=msk[d_][p][:, h, :],
                                                           start=True, stop=False), reads=[vt[d_][p], msk[d_][p]], writes=[PB[bC]])
                kb.op("pe", lambda e, h=h: e.matmul(kb.ps(bC, V3)[:, h, :], lhsT=Sbf[d_][:, h, :], rhs=qe[d_][p][:, h, :],
                                                    start=False, stop=True), reads=[Sbf[d_], qe[d_][p]], writes=[PB[bC]])
            for h in range(4):
                hs = slice(h * 128, (h + 1) * 128)
                kb.op("pe", lambda e, h=h, hs=hs: e.matmul(kb.ps(bD, V3)[:, h, :], lhsT=kdec[d_][p][:, hs], rhs=vt[d_][p][:, hs],
                                                           start=True, stop=True), reads=[kdec[d_][p], vt[d_][p]], writes=[PB[bD]])
            kb.op("dve", lambda e: e.tensor_tensor(out=S32[d_][:], in0=S32[d_][:],
                                                   in1=DEC[d_][p][:, :, 0:1].to_broadcast(V3), op=ALU.mult),
                  reads=[S32[d_], DEC[d_][p]], writes=[S32[d_]])
            kb.op("dve", lambda e: e.tensor_tensor(out=S32[d_][:], in0=S32[d_][:], in1=kb.ps(bD, V3), op=ALU.add),
                  reads=[S32[d_], PB[bD]], writes=[S32[d_]])
            kb.op("act", lambda e: e.copy(out=Sbf[d_][:], in_=S32[d_][:]), reads=[S32[d_]], writes=[Sbf[d_]])
            if not final:
                kb.op("act", lambda e: e.copy(out=OB[:, :, sl], in_=kb.ps(bC, V3)), reads=[PB[bC]], writes=[(OB, i)])
            else:
                kb.dma("sp", gf[d_][:], gv_[:, :, sl], writes=[gf[d_]])
                kb.op("dve", lambda e: e.tensor_tensor(out=o32[d_][:], in0=kb.ps(bC, V3), in1=OB[:, :, sl], op=ALU.add),
                      reads=[PB[bC], (OB, i)], writes=[o32[d_]])
                kb.op("act", lambda e: e.activation(out=sqb[d_][:], in_=o32[d_][:], func=AF.Square), reads=[o32[d_]], writes=[sqb[d_]])
                kb.op("pe", lambda e: e.matmul(kb.ps(bD), lhsT=ones[:], rhs=sqb[d_][:].rearrange("p a b -> p (a b)"), start=True, stop=True),
                      reads=[ones, sqb[d_]], writes=[PB[bD]])
                kb.op("act", lambda e: e.activation(out=rsd[d_][:].rearrange("p a b -> p (a b)"), in_=kb.ps(bD), func=AF.Ln,
                                                    scale=1.0 / 128.0, bias=EPS), reads=[PB[bD]], writes=[rsd[d_]])
                kb.op("act", lambda e: e.activation(out=rsd[d_][:], in_=rsd[d_][:], func=AF.Exp, scale=-0.5), reads=[rsd[d_]], writes=[rsd[d_]])
                kb.op("dve", lambda e: e.tensor_tensor(out=o32[d_][:], in0=o32[d_][:], in1=rsd[d_][:], op=ALU.mult),
                      reads=[o32[d_], rsd[d_]], writes=[o32[d_]])
                kb.op("dve", lambda e: e.scalar_tensor_tensor(out=yob[d_][:], in0=o32[d_][:], scalar=gn[:, 0:1], in1=gf[d_][:],
                                                              op0=ALU.mult, op1=ALU.mult), reads=[o32[d_], gn, gf[d_]], writes=[yob[d_]])
                kb.dma("pool", yv[:, :, sl], yob[d_][:], reads=[yob[d_]])

        for step in range(NT + 1):
            if step < NT:
                pre(step, 0, 1)
                pre(step, 1, 1)
            if step >= 1:
                seq(step - 1, 0)
                seq(step - 1, 1)
            if step < NT:
                pre(step, 0, 2)
                pre(step, 1, 2)
                pre(step, 0, 3)
                pre(step, 1, 3)
        st["pb"] = 0

    def load_w_into(dst, wap, kblocks, c0, ncols, gt=None, dcol0=0):
        src = wap.rearrange("(k p) c -> p k c", p=128)
        WF = st["WF"]
        kper = max(1, 4096 // ncols)
        k0 = 0
        while k0 < kblocks:
            kn = min(kper, kblocks - k0)
            wfv = WF.t[:, 0:kn * ncols].rearrange("p (k c) -> p k c", k=kn)
            kb.dma("sp", wfv, src[:, k0:k0 + kn, c0:c0 + ncols], writes=[WF])
            if gt is None:
                kb.op("pool", lambda e, wfv=wfv, k0=k0, kn=kn: e.tensor_copy(out=dst.t[:, k0:k0 + kn, dcol0:dcol0 + ncols], in_=wfv),
                      reads=[WF], writes=[(dst, (k0, dcol0))])
            else:
                kb.op("pool", lambda e, wfv=wfv, k0=k0, kn=kn: e.tensor_tensor(
                    out=dst.t[:, k0:k0 + kn, dcol0:dcol0 + ncols], in0=wfv,
                    in1=gt.t[:, k0:k0 + kn, None].to_broadcast([128, kn, ncols]), op=ALU.mult),
                    reads=[WF, gt], writes=[(dst, (k0, dcol0))])
            k0 += kn

    def post_tiles():
        return {"yt": [kb.sb([128, D], F32) for _ in range(2)], "xo": [kb.sb([128, D], F32) for _ in range(2)],
                "sq": [kb.sb([128, 1], F32) for _ in range(2)], "rs": [kb.sb([128, 1], F32) for _ in range(2)],
                "junk": kb.sb([128, D], BF16), "nt": norm_tmp()}

    def post(i, ba, bb, xsrc, gp, xdst, pt, do_norm):
        k = i % 2
        yt, xo, sq, rs, junk = pt["yt"][k], pt["xo"][k], pt["sq"][k], pt["rs"][k], pt["junk"]
        kb.dma("sp", xo[:], xsrc[i * 128:(i + 1) * 128, :], writes=[xo])
        kb.op("act", lambda e: e.copy(out=yt[:, 0:512], in_=kb.ps(ba)), reads=[PB[ba]], writes=[(yt, 0)])
        kb.op("dve", lambda e: e.tensor_copy(out=yt[:, 512:1024], in_=kb.ps(bb)), reads=[PB[bb]], writes=[(yt, 1)])
        kb.op("act", lambda e: e.activation(out=junk[:], in_=yt[:], func=AF.Square, scale=1.0 / 32.0, accum_out=sq[:]),
              reads=[yt], writes=[junk, sq])
        kb.op("act", lambda e: e.activation(out=rs[:], in_=sq[:], func=AF.Ln, bias=EPS), reads=[sq], writes=[rs])
        kb.op("act", lambda e: e.activation(out=rs[:], in_=rs[:], func=AF.Exp, scale=-0.5), reads=[rs], writes=[rs])
        kb.op("dve", lambda e: e.scalar_tensor_tensor(out=yt[:], in0=yt[:], scalar=rs[:, 0:1], in1=gp[:], op0=ALU.mult, op1=ALU.mult),
              reads=[yt, rs, gp], writes=[yt])
        kb.op("dve", lambda e: e.tensor_tensor(out=xo[:], in0=xo[:], in1=yt[:], op=ALU.add), reads=[xo, yt], writes=[xo])
        kb.dma("pool", xdst[i * 128:(i + 1) * 128, :], xo[:], reads=[xo])
        if do_norm:
            norm_transpose(xo, i, pt["nt"][k])

    def phase_merge(li, xsrc, xdst):
        phase_begin(0, True)
        MG = HT
        wbr = [kb.sb([128, 4, D], BF16) for _ in range(3)]
        for b in range(3):
            for c0 in (0, 512):
                load_w_into(wbr[b], P["w_br"][li, b], 4, c0, 512, None, c0)
        ysrc = [Y_.t.rearrange("(k p) t -> p k t", p=128) for Y_ in (YH, YG, YP)]
        gsrc = GATES.t.rearrange("(b r) t -> r b t", b=3)
        yt_ = [[kb.sb([128, 4, 512], BF16) for _ in range(3)] for _ in range(2)]
        gt_ = [kb.sb([128, 3, 512], BF16) for _ in range(2)]
        mm = [[kb.sb([128, 512], BF16) for _ in range(3)] for _ in range(2)]
        ng = 0
        for tt in range(NQ):
            ys = yt_[tt % 2]
            for b in range(3):
                kb.dma("sp", ys[b][:], ysrc[b][:, :, tt * 512:(tt + 1) * 512], writes=[ys[b]])
            for db in range(8):
                g = gt_[ng % 2]
                m_ = mm[ng % 2]
                ng += 1
                kb.dma("sp", g[:], gsrc[db * 128:(db + 1) * 128, :, tt * 512:(tt + 1) * 512], writes=[g])
                banks = [next_pb(), next_pb(), next_pb()]
                for b in range(3):
                    for k in range(4):
                        kb.op("pe", lambda e, b=b, k=k, db=db, ys=ys, banks=banks: e.matmul(
                            kb.ps(banks[b]), lhsT=wbr[b][:, k, db * 128:(db + 1) * 128], rhs=ys[b][:, k, :],
                            start=(k == 0), stop=(k == 3)), reads=[wbr[b], ys[b]], writes=[PB[banks[b]]])
                for b in range(3):
                    kb.op("dve", lambda e, b=b, g=g, m_=m_, banks=banks: e.tensor_tensor(
                        out=m_[b][:], in0=kb.ps(banks[b]), in1=g[:, b, :], op=ALU.mult), reads=[PB[banks[b]], g], writes=[m_[b]])
                kb.op("dve", lambda e, m_=m_: e.tensor_tensor(out=m_[0][:], in0=m_[0][:], in1=m_[1][:], op=ALU.add),
                      reads=[m_[0], m_[1]], writes=[m_[0]])
                kb.op("dve", lambda e, m_=m_, db=db, tt=tt: e.tensor_tensor(
                    out=MG[:, db, tt * 512:(tt + 1) * 512], in0=m_[0][:], in1=m_[2][:], op=ALU.add),
                    reads=[m_[0], m_[2]], writes=[(MG, 4 * tt), (MG, 4 * tt + 1), (MG, 4 * tt + 2), (MG, 4 * tt + 3)])
        phase_begin(0, True)
        wo = kb.sb([128, 8, D], BF16)
        gp = kb.sb([128, D], F32)
        kb.dma("sp", gp[:], P["g_mix_post"][li], writes=[gp])
        for c0 in (0, 512):
            load_w_into(wo, P["w_out"][li], 8, c0, 512, None, c0)
        pt = post_tiles()

        def mm(i):
            ba, bb = next_pb(), next_pb()
            gemm_tok(wo, 8, 0, 512, MG, i, ba, srckey=i)
            gemm_tok(wo, 8, 512, 512, MG, i, bb, srckey=i)
            return ba, bb

        pend = [mm(0), mm(1)] if NT > 1 else [mm(0)]
        for i in range(NT):
            if i + 2 < NT:
                pend.append(mm(i + 2))
            cur = pend.pop(0)
            post(i, cur[0], cur[1], xsrc, gp, xdst, pt, True)

    def phase_ffn_up(li):
        phase_begin(0)
        gt = kb.sb([128, 8], F32)
        kb.dma("sp", gt[:], P["g_ffn_pre"][li], writes=[gt])
        cw = kb.sb([128, 22, 3], F32)
        cb = kb.sb([128, 22], F32)
        kb.dma("sp", cw[:], P["ffn_cw"][li], writes=[cw])
        kb.dma("sp", cb[:], P["ffn_cb"][li], writes=[cb])
        raws = [kb.sb([128, L + 2], BF16) for _ in range(2)]
        for r in raws:
            kb.op("pool", lambda e, r=r: e.memset(r[:, 0:1], 0.0), writes=[(r, "h0")])
            kb.op("pool", lambda e, r=r: e.memset(r[:, L + 1:L + 2], 0.0), writes=[(r, "h1")])
        bts = [kb.sb([128, L], BF16) for _ in range(2)]
        gas = [kb.sb([128, L], BF16) for _ in range(2)]
        acc = kb.sb([128, L], F32)
        wfs = [kb.sb([128, 8, 256], F32) for _ in range(2)]
        wbs = [kb.sb([128, 8, 256], BF16) for _ in range(2)]
        wsrc = P["ffn_up"][li].rearrange("(k p) c -> p k c", p=128)

        def prep(m):
            s_ = m % 2
            kb.dma("sp", wfs[s_][:, :, 0:128], wsrc[:, :, m * 128:(m + 1) * 128], writes=[(wfs[s_], 0)])
            kb.dma("sp", wfs[s_][:, :, 128:256], wsrc[:, :, DFF + m * 128:DFF + (m + 1) * 128], writes=[(wfs[s_], 1)])
            kb.op("pool", lambda e: e.tensor_tensor(out=wbs[s_][:], in0=wfs[s_][:],
                                                    in1=gt[:, :, None].to_broadcast([128, 8, 256]), op=ALU.mult),
                  reads=[wfs[s_], gt], writes=[wbs[s_]])
            return wbs[s_]

        wnext = prep(0)
        for m in range(22):
            raw, bt, ga = raws[m % 2], bts[m % 2], gas[m % 2]
            wcur = wnext
            if m + 1 < 22:
                wnext = prep(m + 1)
            for j in range(NQ):
                b1_, b2_ = next_pb(), next_pb()
                gemm_fm(wcur, 8, 0, 128, HT, j, b1_)
                gemm_fm(wcur, 8, 128, 128, HT, j, b2_)
                kb.op("act", lambda e: e.copy(out=raw[:, 1 + j * 512:1 + (j + 1) * 512], in_=kb.ps(b1_)),
                      reads=[PB[b1_]], writes=[(raw, j)])
                kb.op("act", lambda e: e.copy(out=bt[:, j * 512:(j + 1) * 512], in_=kb.ps(b2_)),
                      reads=[PB[b2_]], writes=[(bt, j)])
            kb.op("dve", lambda e: e.tensor_scalar(out=acc[:], in0=raw[:, 0:L], scalar1=cw[:, m, 0:1], scalar2=cb[:, m:m + 1],
                                                   op0=ALU.mult, op1=ALU.add), reads=[raw, cw, cb], writes=[acc])
            kb.op("dve", lambda e: e.scalar_tensor_tensor(out=acc[:], in0=raw[:, 1:L + 1], scalar=cw[:, m, 1:2], in1=acc[:],
                                                          op0=ALU.mult, op1=ALU.add), reads=[raw, cw, acc], writes=[acc])
            kb.op("dve", lambda e: e.scalar_tensor_tensor(out=acc[:], in0=raw[:, 2:L + 2], scalar=cw[:, m, 2:3], in1=acc[:],
                                                          op0=ALU.mult, op1=ALU.add), reads=[raw, cw, acc], writes=[acc])
            kb.op("act", lambda e: e.activation(out=ga[:], in_=acc[:], func=AF.Gelu), reads=[acc], writes=[ga])
            kb.op("dve", lambda e: e.tensor_tensor(out=ga[:], in0=ga[:], in1=bt[:], op=ALU.mult), reads=[ga, bt], writes=[ga])
            kb.dma("pool", ACTfm[m * 128:(m + 1) * 128, :], ga[:], reads=[ga])

    def phase_ffn_down(li, xsrc, xdst, do_norm):
        phase_begin(0, True)
        wd = kb.sb([128, 22, D], BF16)
        gp = kb.sb([128, D], F32)
        kb.dma("sp", gp[:], P["g_ffn_post"][li], writes=[gp])
        for c0 in range(0, D, 128):
            load_w_into(wd, P["ffn_down"][li], 22, c0, 128, None, c0)
        asrc = ACTfm.t.rearrange("(k p) t -> p k t", p=128)
        at = [kb.sb([128, 22, 512], BF16) for _ in range(1)]
        pt = post_tiles()
        def load_a(tt):
            a = at[0]
            kb.dma("sp", a[:, 0:11, :], asrc[:, 0:11, tt * 512:(tt + 1) * 512], writes=[(a, 0)])
            kb.dma("sp", a[:, 11:22, :], asrc[:, 11:22, tt * 512:(tt + 1) * 512], writes=[(a, 1)])

        def mm(i):
            if i % 4 == 0:
                load_a(i // 4)
            a, ii = at[0], i % 4
            ba, bb = next_pb(), next_pb()
            for c0, bank in ((0, ba), (512, bb)):
                for k in range(22):
                    kb.op("pe", lambda e, k=k, c0=c0, bank=bank: e.matmul(
                        kb.ps(bank), lhsT=a[:, k, ii * 128:(ii + 1) * 128], rhs=wd[:, k, c0:c0 + 512],
                        start=(k == 0), stop=(k == 21)), reads=[a, wd], writes=[PB[bank]])
            return ba, bb

        pend = [mm(0), mm(1)] if NT > 1 else [mm(0)]
        for i in range(NT):
            if i + 2 < NT:
                pend.append(mm(i + 2))
            cur = pend.pop(0)
            post(i, cur[0], cur[1], xsrc, gp, xdst, pt, do_norm)

    phase_filter(0)
    phase_norm0(x_in)
    for li in range(depth):
        xs = x_in if li == 0 else XB
        phase_proj(li)
        phase_hyena(li)
        phase_gla(li)
        phase_merge(li, xs, XA)
        phase_ffn_up(li)
        lastl = (li == depth - 1)
        if not lastl:
            phase_filter(li + 1)
        phase_ffn_down(li, XA, out if lastl else XB, not lastl)
    nc = kb.finish()
    return nc, kb


_CACHE = {}


def _in_maps(inputs, L, depth, nb):
    consts = make_consts(L)
    params = relayout_params(inputs, depth)
    x = np.asarray(inputs["x"], np.float32)
    maps = []
    for b in range(nb):
        m = {"x": np.ascontiguousarray(x[b])}
        m.update(consts)
        m.update(params)
        maps.append(m)
    return maps


def kernel(**inputs):
    x = np.asarray(inputs["x"])
    B, L, _ = x.shape
    depth = int(np.asarray(inputs["w_in"]).shape[0])
    nc, _ = build(L, depth)
    maps = _in_maps(inputs, L, depth, B)
    res = run_bass_kernel_spmd(nc, maps, core_ids=list(range(B)))
    return np.stack([np.asarray(r["out"], np.float32) for r in res.results], 0)
```

```python
import contextlib
import math
import numpy as np
import ml_dtypes
import concourse.bass as bass
import concourse.mybir as mybir
from concourse.bass_utils import run_bass_kernel_spmd

F32 = mybir.dt.float32
BF16 = mybir.dt.bfloat16
AF = mybir.ActivationFunctionType
ALU = mybir.AluOpType
NDMA_SEM = 8

D = 1024
DH = 512
DIN = 7200
DFF = 2816
EPS = 1e-6


class T:
    def __init__(self, name, ap, parent=None):
        self.name = name
        self.t = ap
        if parent is None:
            self.w = {}
            self.r = {}
            self.root = self
        else:
            self.root = parent.root

    def __getitem__(self, idx):
        return self.t[idx]

    def view(self, ap):
        return T(self.name, ap, parent=self)


class Op:
    __slots__ = ("eng", "fn", "deps", "marked", "val", "sem", "isdma")

    def __init__(self, eng, fn):
        self.eng = eng
        self.fn = fn
        self.deps = []
        self.marked = False
        self.val = 0
        self.sem = None
        self.isdma = False


class _Rec:
    def __getattr__(self, name):
        def f(*a, **k):
            self.call = (name, a, k)
        return f


class KB:
    def __init__(self, sb_bytes):
        self.nc = bass.Bass("TRN2", target_bir_lowering=False)
        self.es = contextlib.ExitStack()
        self.ops = []
        nc = self.nc
        self.handles = {"pe": nc.tensor, "act": nc.scalar, "dve": nc.vector,
                        "pool": nc.gpsimd, "sp": nc.sync}
        self.sems = {e: self.es.enter_context(nc.semaphore("s_" + e)) for e in self.handles}
        self.dq = {}
        for q in ("sp", "act", "pool"):
            self.dq[q] = {"sems": [self.es.enter_context(nc.semaphore("d_%s%d" % (q, i)))
                                   for i in range(NDMA_SEM)],
                          "n": 0, "last": [None] * NDMA_SEM, "cnt": [0] * NDMA_SEM}
        self.last = {}
        self.arena = self.es.enter_context(nc.sbuf_tensor("arena", [128, sb_bytes // 2], BF16))
        self.sb_bytes = sb_bytes
        self.top = 0
        self.nbuf = 0
        self.pbanks = [self.es.enter_context(nc.psum_tensor("pb%d" % i, [128, 512], F32))
                       for i in range(8)]

    def sb(self, shape, dt, name=None):
        esz = 4 if dt == F32 else 2
        n = int(np.prod(shape[1:])) * esz
        n = (n + 63) // 64 * 64
        off = self.top
        self.top += n
        assert self.top <= self.sb_bytes, "SBUF arena overflow %d" % self.top
        ap = self.arena[:, off // 2:(off + n) // 2]
        if dt == F32:
            ap = ap.bitcast(F32)
        ap = ap[:, 0:int(np.prod(shape[1:]))]
        if len(shape) == 3:
            ap = ap.rearrange("p (a b) -> p a b", a=shape[1])
        elif len(shape) == 4:
            ap = ap.rearrange("p (a b c) -> p a b c", a=shape[1], b=shape[2])
        if shape[0] < 128:
            ap = ap[0:shape[0]]
        self.nbuf += 1
        return T(name or "sb%d" % self.nbuf, ap)

    def ps(self, bank, shape=None, dt=F32):
        ap = self.pbanks[bank][:]
        if dt == BF16:
            ap = ap.bitcast(BF16)
        if shape is not None and len(shape) == 3:
            ap = ap[:, 0:shape[1] * shape[2]].rearrange("p (a b) -> p a b", a=shape[1])
        elif shape is not None:
            ap = ap[:, 0:shape[1]]
        if shape is not None and shape[0] < 128:
            ap = ap[0:shape[0]]
        return ap

    def psT(self, bank):
        if not hasattr(self, "_pst"):
            self._pst = [T("pbank%d" % i, self.pbanks[i][:]) for i in range(8)]
        return self._pst[bank]

    def dram(self, name, shape, dt, kind="Internal"):
        return T(name, self.nc.dram_tensor(name, list(shape), dt, kind=kind).ap())

    @staticmethod
    def _norm(lst):
        out = []
        for x in lst:
            if isinstance(x, tuple):
                out.append((x[0].root, x[1]))
            else:
                out.append((x.root, None))
        return out

    def _hazards(self, op, reads, writes):
        deps = op.deps
        for t, key in reads:
            if key is None:
                deps.extend(t.w.values())
            else:
                for k in (key, None):
                    p = t.w.get(k)
                    if p is not None:
                        deps.append(p)
        for t, key in writes:
            if key is None:
                deps.extend(t.w.values())
                for l in t.r.values():
                    deps.extend(x for x in l if x.isdma or op.isdma or x.eng != op.eng or op.eng != 'pe')
            else:
                for k in (key, None):
                    p = t.w.get(k)
                    if p is not None:
                        deps.append(p)
                    deps.extend(x for x in t.r.get(k, ()) if x.isdma or op.isdma or x.eng != op.eng or op.eng != 'pe')
        for t, key in reads:
            l = t.r.setdefault(key, [])
            if not op.isdma:
                l[:] = [o for o in l if o.eng != op.eng or o.isdma]
            l.append(op)
        for t, key in writes:
            if key is None:
                t.w = {None: op}
                t.r = {}
            else:
                t.w[key] = op
                t.r[key] = []

    def op(self, eng, fn, reads=(), writes=()):
        rec = _Rec()
        fn(rec)
        name, a, k = rec.call
        o = Op(eng, lambda h: getattr(h, name)(*a, **k))
        self._hazards(o, self._norm(reads), self._norm(writes))
        if eng == "pe":
            o.deps = [d for d in o.deps if not (d.eng == "pe" and not d.isdma)]
        self.ops.append(o)
        self.last[eng] = o
        return o

    def dma(self, q, out, in_, reads=(), writes=()):
        o = Op(q, lambda e: e.dma_start(out=out, in_=in_))
        o.isdma = True
        dq = self.dq[q]
        i = dq["n"] % NDMA_SEM
        dq["n"] += 1
        if dq["last"][i] is not None:
            o.deps.append(dq["last"][i])
        dq["last"][i] = o
        dq["cnt"][i] += 16
        o.sem = dq["sems"][i]
        o.val = dq["cnt"][i]
        self._hazards(o, self._norm(reads), self._norm(writes))
        self.ops.append(o)
        return o

    def mark(self, label):
        o = Op("sp", None)
        o.sem = label
        o.marked = "label"
        self.ops.append(o)

    def barrier(self):
        deps = list(self.last.values())
        for q in self.dq.values():
            deps.extend(x for x in q["last"] if x is not None)
        for e in self.handles:
            o = Op(e, None)
            o.deps = list(deps)
            self.ops.append(o)

    def finish(self):
        self.barrier()
        for o in self.ops:
            for d in o.deps:
                if not d.isdma:
                    d.marked = True
        cnt = {e: 0 for e in self.handles}
        for o in self.ops:
            if not o.isdma and o.marked is True:
                cnt[o.eng] += 1
                o.val = cnt[o.eng]
                o.sem = self.sems[o.eng]
        seen = {e: {} for e in self.handles}
        nwait = 0
        self.marks = []
        for o in self.ops:
            if o.marked == "label":
                self.marks.append((o.sem, self.nc.get_next_instruction_name()))
                continue
            h = self.handles[o.eng]
            sn = seen[o.eng]
            for d in o.deps:
                k = id(d.sem)
                if sn.get(k, 0) >= d.val:
                    continue
                h.wait_ge(d.sem, d.val)
                nwait += 1
                sn[k] = d.val
            if o.fn is None:
                continue
            ins = o.fn(h)
            if o.isdma:
                ins.then_inc(o.sem, 16)
            elif o.marked is True:
                ins.then_inc(o.sem, 1)
        self.stats = {"ops": len(self.ops), "waits": nwait, "marked": dict(cnt)}
        return self.nc


def make_consts(L):
    bf = ml_dtypes.bfloat16
    c = {}
    c["ident"] = np.eye(128, dtype=np.float32).astype(bf)
    c["ones"] = np.ones((128, 128), np.float32).astype(bf)
    a = np.arange(128)
    uti = (a[:, None] <= a[None, :]).astype(np.float32)
    uts = (a[:, None] < a[None, :]).astype(np.float32)
    lti = (a[:, None] >= a[None, :]).astype(np.float32)
    lts = (a[:, None] > a[None, :]).astype(np.float32)
    c["tri32"] = np.stack([uti, uts, lti, lts], 1).astype(bf)
    t = np.linspace(0.0, 1.0, L, dtype=np.float32)
    bands = 16
    w = (2.0 * np.float32(math.pi) * np.arange(L, dtype=np.float32) / np.float32(L)).astype(np.float32)
    f = np.linspace(1e-4, bands - 1, bands, dtype=np.float32)
    ang = (f[None, :] * w[:, None]).astype(np.float32)
    z = np.concatenate([t[:, None], np.cos(ang), -np.sin(ang)], -1).astype(np.float32)
    c["zT"] = np.ascontiguousarray(z.T)
    c["tcol"] = np.ascontiguousarray(-t.reshape(L // 128, 128).T)
    max_decay = math.log(1e-2) / 0.3
    min_decay = math.log(1e-2) / 1.5
    deltas = np.abs(np.linspace(min_decay, max_decay, DH, dtype=np.float32))
    c["absd"] = np.ascontiguousarray(np.broadcast_to(deltas[None, :], (128, DH))).astype(np.float32)
    N = 2 * L
    N1 = N // 64
    H = N1 // 2
    n1 = np.arange(H, dtype=np.float64)[:, None, None]
    n2 = np.arange(64, dtype=np.float64)[None, :, None]
    f1 = np.arange(H, dtype=np.float64)[None, None, :]
    al = 2 * np.pi * ((f1 + 0.5) * n1 / N1 + (f1 + 0.5) * n2 / N)
    c["tw1"] = np.ascontiguousarray(np.stack([np.cos(al), -np.sin(al)], 1)).astype(bf)
    a64 = np.arange(64, dtype=np.float64)
    be = 2 * np.pi * np.outer(a64, a64) / 64
    c["dftm"] = np.ascontiguousarray(np.stack([np.cos(be), np.sin(be), -np.sin(be)], 1)).astype(bf)
    f2 = a64[:, None, None]
    f1b = np.arange(H, dtype=np.float64)[None, :, None]
    t2 = a64[None, None, :]
    ga = 2 * np.pi * (f2 * t2 / 64 + (f1b + 0.5) * t2 / N)
    c["gtw"] = np.ascontiguousarray(np.stack([np.cos(ga), np.sin(ga), -np.sin(ga)], 1)).astype(bf)
    f1c = np.arange(H, dtype=np.float64)[:, None]
    t1 = np.arange(H, dtype=np.float64)[None, :]
    ph = 2 * np.pi * (f1c + 0.5) * t1 / N1
    c["m4"] = np.ascontiguousarray(np.stack([(2.0 / N) * np.cos(ph), -(2.0 / N) * np.sin(ph)], 1)).astype(bf)
    pos = np.arange(L)
    inv = []
    for wv in (2, 4, 8, 16):
        half = wv // 2
        cntv = (np.minimum(pos + half, L) - np.maximum(pos - half, 0)).astype(np.float32)
        inv.append(1.0 / cntv)
    c["invcnt"] = np.ascontiguousarray(
        np.broadcast_to(np.stack(inv, 0)[:, None, :], (4, 128, L))).astype(np.float32)
    return c


def relayout_params(p, depth):
    f = np.float32
    o = {}

    def pk(v, nb):
        return np.ascontiguousarray(np.asarray(v, f).reshape(nb, 128).T)

    o["g_mix_pre"] = np.stack([pk(p["norm_mix_pre"][i], 8) for i in range(depth)])
    o["g_ffn_pre"] = np.stack([pk(p["norm_ffn_pre"][i], 8) for i in range(depth)])
    o["g_mix_post"] = np.ascontiguousarray(np.broadcast_to(
        np.asarray(p["norm_mix_post"], f)[:depth, None, :], (depth, 128, D)))
    o["g_ffn_post"] = np.ascontiguousarray(np.broadcast_to(
        np.asarray(p["norm_ffn_post"], f)[:depth, None, :], (depth, 128, D)))
    o["w_in"] = np.asarray(p["w_in"], f)[:depth]
    cw = np.asarray(p["hy_conv_w"], f)[:depth]
    o["hy_cw"] = np.ascontiguousarray(cw.reshape(depth, 3, 12, 128).transpose(0, 3, 2, 1))
    o["hy_cb"] = np.stack([pk(p["hy_conv_b"][i], 12) for i in range(depth)])
    o["hy_w1"] = np.asarray(p["hy_filt_w1"], f)[:depth]
    o["hy_w2"] = np.asarray(p["hy_filt_w2"], f)[:depth]
    o["hy_w3"] = np.asarray(p["hy_filt_w3"], f)[:depth]
    vec = np.stack([np.asarray(p["hy_filt_b1"], f)[:depth], np.asarray(p["hy_filt_freq1"], f)[:depth],
                    np.asarray(p["hy_filt_b2"], f)[:depth], np.asarray(p["hy_filt_freq2"], f)[:depth]], -1)
    o["hy_vec"] = np.ascontiguousarray(vec)
    o["hy_bias"] = np.stack([pk(p["hy_bias"][i], 4) for i in range(depth)])
    w2 = np.asarray(p["gla_gate_w2"], f)[:depth]
    gb = np.asarray(p["gla_gate_b"], f)[:depth]
    w2x = np.zeros((depth, 2, 33, 512), f)
    w2x[:, 0, 0:16] = w2[:, 0]
    w2x[:, 1, 16:32] = w2[:, 1]
    w2x[:, :, 32] = gb
    o["gla_w2x"] = np.ascontiguousarray(w2x.transpose(0, 2, 1, 3))
    o["gla_norm"] = np.asarray(p["gla_norm"], f)[:depth].reshape(depth, 128, 1)
    o["pool_w"] = np.ascontiguousarray(np.asarray(p["pool_w"], f)[:depth].transpose(0, 2, 1, 3))
    o["pool_scale"] = np.stack([pk(p["pool_scale"][i], 4) for i in range(depth)])
    o["w_br"] = np.ascontiguousarray(np.stack(
        [np.asarray(p["w_br_hyena"], f)[:depth], np.asarray(p["w_br_gla"], f)[:depth],
         np.asarray(p["w_br_pool"], f)[:depth]], 1))
    o["w_out"] = np.asarray(p["w_out"], f)[:depth]
    o["ffn_up"] = np.asarray(p["ffn_w_up"], f)[:depth]
    fw = np.asarray(p["ffn_conv_w"], f)[:depth]
    o["ffn_cw"] = np.ascontiguousarray(fw.reshape(depth, 3, 22, 128).transpose(0, 3, 2, 1))
    o["ffn_cb"] = np.stack([pk(p["ffn_conv_b"][i], 22) for i in range(depth)])
    o["ffn_down"] = np.asarray(p["ffn_w_down"], f)[:depth]
    return o


def build(L, depth, dbg=()):
    NT = L // 128
    NQ = L // 512
    NFB = 2 * NT
    HH = (2 * L // 64) // 2
    kb = KB(sb_bytes=200 * 1024)

    def din(name, shape, dt=F32):
        return kb.dram(name, shape, dt, kind="ExternalInput")

    def scr(name, shape, dt=BF16):
        return kb.dram(name, shape, dt, kind=("ExternalOutput" if name in dbg else "Internal"))

    x_in = din("x", [L, D])
    C = {k: din(k, list(v.shape), BF16 if v.dtype != np.float32 else F32)
         for k, v in make_consts(L if L <= 512 else 128 * 4).items()} if False else None
    cshapes = {"ident": ([128, 128], BF16), "ones": ([128, 128], BF16), "tri32": ([128, 4, 128], BF16), "zT": ([33, L], F32), "tcol": ([128, NT], F32),
               "absd": ([128, DH], F32), "tw1": ([HH, 2, 64, HH], BF16), "dftm": ([64, 3, 64], BF16),
               "gtw": ([64, 3, HH, 64], BF16), "m4": ([HH, 2, HH], BF16),
               "invcnt": ([4, 128, L], F32)}
    C = {k: din(k, s, dt) for k, (s, dt) in cshapes.items()}
    n = depth
    pshapes = {"g_mix_pre": [n, 128, 8], "g_ffn_pre": [n, 128, 8], "g_mix_post": [n, 128, D],
               "g_ffn_post": [n, 128, D], "w_in": [n, D, DIN], "hy_cw": [n, 128, 12, 3],
               "hy_cb": [n, 128, 12], "hy_w1": [n, 33, 64], "hy_w2": [n, 64, 64], "hy_w3": [n, 64, 1024],
               "hy_vec": [n, 64, 4], "hy_bias": [n, 128, 4], "gla_w2x": [n, 33, 2, 512],
               "gla_norm": [n, 128, 1], "pool_w": [n, 128, 4, 128], "pool_scale": [n, 128, 4],
               "w_br": [n, 3, DH, D], "w_out": [n, D, D], "ffn_up": [n, D, 2 * DFF],
               "ffn_cw": [n, 128, 22, 3], "ffn_cb": [n, 128, 22], "ffn_down": [n, DFF, D]}
    P = {k: din(k, s) for k, s in pshapes.items()}
    out = kb.dram("out", [L, D], F32, kind="ExternalOutput")

    XA = scr("XA", [L, D], F32)
    XB = scr("XB", [L, D], F32)
    X0fm = scr("X0fm", [DH, L])
    ZVfm = scr("ZVfm", [DH, L])
    ZVT = scr("ZVT", [L, DH])
    HSD = scr("HSD", [2, L, DH])
    A1Z = scr("A1Z", [2, HH, 64, DH])
    A1K = scr("A1K", [2, 2, HH, 64, DH])
    KS = scr("KS", [2, HH, 64, DH])
    B1 = scr("B1", [2, 64, HH, DH])
    Qfm = scr("Qfm", [DH, L])
    Kfm = scr("Kfm", [DH, L])
    Gfm = scr("Gfm", [DH, L])
    Ktok = scr("Ktok", [L, DH])
    Vtok = scr("Vtok", [L, DH])
    YH = scr("YH", [DH, L])
    YG = scr("YG", [DH, L])
    YP = scr("YP", [DH, L])
    GATES = scr("GATES", [3 * D, L])
    ACTfm = scr("ACTfm", [DFF, L])

    HT = kb.sb([128, 8, L], BF16, "HT")
    ident = kb.sb([128, 128], BF16, "ident")
    ones = kb.sb([128, 128], BF16, "ones")
    tri32 = kb.sb([128, 4, 128], BF16, "tri32")
    LRH = kb.sb([33, L], BF16, "LRH")
    LRL = kb.sb([33, L], BF16, "LRL")
    kb.dma("sp", ident[:], C["ident"][:, :], writes=[ident])
    kb.dma("sp", ones[:], C["ones"][:, :], writes=[ones])
    kb.dma("sp", tri32[:], C["tri32"][:, :, :], writes=[tri32])
    kb.op("dve", lambda e: e.memset(LRH[32:33, :], 1.0), writes=[LRH])
    kb.op("dve", lambda e: e.memset(LRL[32:33, :], 0.0), writes=[LRL])
    base_top = kb.top
    PB = [kb.psT(i) for i in range(8)]
    st = {"wb": 0, "pb": 0}

    def dump(name, t_, shape, dt=F32):
        if name in dbg:
            d_ = kb.dram(name, shape, dt, kind="ExternalOutput")
            kb.dma("sp", d_.t, t_.t, reads=[t_])

    def phase_begin(nwb=0, wf=False, label=None):
        kb.barrier()
        import inspect
        kb.mark(label or inspect.stack()[1].function + ":%d" % inspect.stack()[1].lineno)
        kb.top = base_top
        if wf or nwb:
            st["WF"] = kb.sb([128, 4096], F32, "WF")
        st["WB"] = [kb.sb([128, 6144], BF16, "WB%d" % i) for i in range(nwb)]

    def load_w(wap, kblocks, c0, ncols, gt=None):
        i = st["wb"] % len(st["WB"])
        st["wb"] += 1
        wb = st["WB"][i]
        WF = st["WF"]
        wbv = wb.view(wb.t[:, 0:kblocks * ncols].rearrange("p (k c) -> p k c", k=kblocks))
        kper = max(1, 4096 // ncols)
        src = wap.rearrange("(k p) c -> p k c", p=128)
        k0 = 0
        while k0 < kblocks:
            kn = min(kper, kblocks - k0)
            wfv = WF.t[:, 0:kn * ncols].rearrange("p (k c) -> p k c", k=kn)
            kb.dma("sp", wfv, src[:, k0:k0 + kn, c0:c0 + ncols], writes=[WF])
            if gt is None:
                kb.op("pool", lambda e, wfv=wfv, k0=k0, kn=kn: e.tensor_copy(out=wbv.t[:, k0:k0 + kn, :], in_=wfv),
                      reads=[WF], writes=[(wb, k0)])
            else:
                kb.op("pool", lambda e, wfv=wfv, k0=k0, kn=kn: e.tensor_tensor(
                    out=wbv.t[:, k0:k0 + kn, :], in0=wfv,
                    in1=gt.t[:, k0:k0 + kn, None].to_broadcast([128, kn, ncols]), op=ALU.mult),
                    reads=[WF, gt], writes=[(wb, k0)])
            k0 += kn
        return wbv

    def next_pb(nb=6):
        b = st["pb"] % nb
        st["pb"] += 1
        return b

    def gemm_fm(wbv, kblocks, mcol, mw, src, j, bank):
        for k in range(kblocks):
            kb.op("pe", lambda e, k=k: e.matmul(kb.ps(bank)[0:mw, :], lhsT=wbv.t[:, k, mcol:mcol + mw],
                                                rhs=src.t[:, k, j * 512:(j + 1) * 512],
                                                start=(k == 0), stop=(k == kblocks - 1)),
                  reads=[wbv, src], writes=[PB[bank]])

    def gemm_tok(wbv, kblocks, c0, ncols, src, i, bank, srckey=None):
        for k in range(kblocks):
            kb.op("pe", lambda e, k=k: e.matmul(kb.ps(bank)[:, 0:ncols], lhsT=src.t[:, k, i * 128:(i + 1) * 128],
                                                rhs=wbv.t[:, k, c0:c0 + ncols],
                                                start=(k == 0), stop=(k == kblocks - 1)),
                  reads=[wbv, (src, srckey) if srckey is not None else src], writes=[PB[bank]])

    def nt_b(hn, i):
        for k in range(8):
            kb.op("pe", lambda e, k=k: e.transpose(out=kb.ps(7, [128, 8, 128], BF16)[:, k, :],
                                                   in_=hn[:, k * 128:(k + 1) * 128], identity=ident[:]),
                  reads=[hn, ident], writes=[PB[7]])
        kb.op("act", lambda e: e.copy(out=HT[:, :, i * 128:(i + 1) * 128], in_=kb.ps(7, [128, 8, 128], BF16)),
              reads=[PB[7]], writes=[(HT, i)])

    def norm_transpose(xt, i, tmp, defer=None):
        junk, sq, rs, hn = tmp["junk"], tmp["sq"], tmp["rs"], tmp["hn"]
        kb.op("act", lambda e: e.activation(out=junk[:], in_=xt[:], func=AF.Square, scale=1.0 / 32.0,
                                            accum_out=sq[:]), reads=[xt], writes=[junk, sq])
        kb.op("act", lambda e: e.activation(out=rs[:], in_=sq[:], func=AF.Ln, bias=EPS), reads=[sq], writes=[rs])
        kb.op("act", lambda e: e.activation(out=rs[:], in_=rs[:], func=AF.Exp, scale=-0.5), reads=[rs], writes=[rs])
        kb.op("dve", lambda e: e.tensor_scalar(out=hn[:], in0=xt[:], scalar1=rs[:], scalar2=None, op0=ALU.mult),
              reads=[xt, rs], writes=[hn])
        if defer is not None:
            defer.append((hn, i))
        else:
            nt_b(hn, i)

    def norm_tmp():
        return [{"junk": kb.sb([128, D], BF16), "sq": kb.sb([128, 1], F32), "rs": kb.sb([128, 1], F32),
                 "hn": kb.sb([128, D], BF16)} for _ in range(2)]

    TWO_PI = 2.0 * math.pi

    def phase_filter(li, preload_down=None):
        phase_begin()
        HS = HT.view(HT.t[:, :, :].rearrange("p a b -> p (a b)")[:, 0:NT * 1024].rearrange(
            "p (n c) -> p n c", n=NT))
        w1 = kb.sb([33, 64], F32)
        w2 = kb.sb([64, 64], F32)
        w3 = kb.sb([64, 1024], F32)
        vec = kb.sb([64, 4], F32)
        pv = kb.sb([64, 2], F32)
        absd = kb.sb([128, DH], F32)
        tcol = kb.sb([128, NT], F32)
        H1 = kb.sb([64, L], F32)
        H2 = kb.sb([64, L], F32)
        kb.dma("sp", w1[:], P["hy_w1"][li], writes=[w1])
        kb.dma("sp", w2[:], P["hy_w2"][li], writes=[w2])
        kb.dma("sp", w3[:], P["hy_w3"][li], writes=[w3])
        kb.dma("sp", vec[:], P["hy_vec"][li], writes=[vec])
        kb.dma("sp", absd[:], C["absd"][:, :], writes=[absd])
        kb.dma("sp", tcol[:], C["tcol"][:, :], writes=[tcol])
        kb.op("dve", lambda e: e.tensor_tensor(out=pv[:, 0:1], in0=vec[:, 0:1], in1=vec[:, 1:2], op=ALU.mult),
              reads=[vec], writes=[pv])
        kb.op("dve", lambda e: e.tensor_tensor(out=pv[:, 1:2], in0=vec[:, 2:3], in1=vec[:, 3:4], op=ALU.mult),
              reads=[vec], writes=[pv])
        zt = [kb.sb([33, 512], F32) for _ in range(2)]
        arg = [kb.sb([64, 512], F32) for _ in range(2)]

        def sin_layer(wt, kdim, srcfn, dst, frcol, pvcol, j, bank):
            a = arg[j % 2]
            src, srcT = srcfn(j)
            kb.op("pe", lambda e: e.matmul(kb.ps(bank)[0:64, :], lhsT=wt[0:kdim, :], rhs=src,
                                           start=True, stop=True), reads=[wt, srcT], writes=[PB[bank]])
            kb.op("dve", lambda e: e.tensor_scalar(out=a[:], in0=kb.ps(bank)[0:64, :], scalar1=vec[:, frcol:frcol + 1],
                                                   scalar2=pv[:, pvcol:pvcol + 1], op0=ALU.mult, op1=ALU.add),
                  reads=[PB[bank], vec, pv], writes=[a])
            kb.op("dve", lambda e: e.tensor_scalar(out=ni[:], in0=a[:], scalar1=1.0 / TWO_PI, scalar2=None, op0=ALU.mult),
                  reads=[a], writes=[ni])
            kb.op("dve", lambda e: e.scalar_tensor_tensor(out=a[:], in0=ni[:], scalar=-TWO_PI, in1=a[:], op0=ALU.mult, op1=ALU.add),
                  reads=[ni, a], writes=[a])
            kb.op("dve", lambda e: e.tensor_scalar(out=m1[:], in0=a[:], scalar1=math.pi, scalar2=-TWO_PI, op0=ALU.is_gt, op1=ALU.mult),
                  reads=[a], writes=[m1])
            kb.op("dve", lambda e: e.tensor_scalar(out=m2[:], in0=a[:], scalar1=-math.pi, scalar2=TWO_PI, op0=ALU.is_lt, op1=ALU.mult),
                  reads=[a], writes=[m2])
            kb.op("dve", lambda e: e.tensor_tensor(out=a[:], in0=a[:], in1=m1[:], op=ALU.add), reads=[a, m1], writes=[a])
            kb.op("dve", lambda e: e.tensor_tensor(out=a[:], in0=a[:], in1=m2[:], op=ALU.add), reads=[a, m2], writes=[a])
            kb.op("dve", lambda e: e.tensor_scalar(out=a[:], in0=a[:], scalar1=math.pi, scalar2=-math.pi, op0=ALU.min, op1=ALU.max),
                  reads=[a], writes=[a])
            kb.op("act", lambda e: e.activation(out=dst[:, j * 512:(j + 1) * 512], in_=a[:], func=AF.Sin), reads=[a], writes=[(dst, j)])

        negpi = kb.sb([128, 1], F32)
        ni = kb.sb([64, 512], F32)
        ni = ni.view(ni.t.bitcast(mybir.dt.int32))
        m1 = kb.sb([64, 512], F32)
        m2 = kb.sb([64, 512], F32)
        kb.op("dve", lambda e: e.memset(negpi[:], -math.pi), writes=[negpi])
        for j in range(NQ):
            z = zt[j % 2]
            kb.dma("sp", z[:], C["zT"][:, j * 512:(j + 1) * 512], writes=[z])
            sin_layer(w1, 33, lambda j, z=z: (z[:], z), H1, 1, 0, j, next_pb())
        for j in range(NQ):
            sin_layer(w2, 64, lambda j: (H1[:, j * 512:(j + 1) * 512], H1), H2, 3, 1, j, next_pb())
        hsds = [kb.sb([128, 2, DH], BF16) for _ in range(2)]
        dec = [kb.sb([128, DH], F32) for _ in range(2)]
        t1 = [kb.sb([128, DH], F32) for _ in range(2)]
        t2 = [kb.sb([128, DH], F32) for _ in range(2)]
        for i in range(NT):
            dc, a1, a2 = dec[i % 2], t1[i % 2], t2[i % 2]
            b0, b1 = next_pb(), next_pb()
            for half, bank in ((0, b0), (1, b1)):
                kb.op("pe", lambda e, half=half, bank=bank: e.matmul(
                    kb.ps(bank), lhsT=H2[:, i * 128:(i + 1) * 128], rhs=w3[:, half * 512:(half + 1) * 512],
                    start=True, stop=True), reads=[H2, w3], writes=[PB[bank]])
            kb.op("act", lambda e, dc=dc: e.activation(out=dc[:], in_=absd[:], func=AF.Exp, scale=tcol[:, i:i + 1]),
                  reads=[absd, tcol], writes=[dc])
            kb.op("dve", lambda e, dc=dc, a1=a1, b0=b0: e.tensor_tensor(out=a1[:], in0=kb.ps(b0), in1=dc[:], op=ALU.mult),
                  reads=[PB[b0], dc], writes=[a1])
            kb.op("dve", lambda e, dc=dc, a2=a2, b1=b1: e.tensor_tensor(out=a2[:], in0=kb.ps(b1), in1=dc[:], op=ALU.mult),
                  reads=[PB[b1], dc], writes=[a2])
            if i == 0:
                kb.op("dve", lambda e, a2=a2: e.memset(a2[0:1, :], 0.0), reads=[a2], writes=[a2])
            hsd = hsds[i % 2]
            kb.op("pool", lambda e, a1=a1, a2=a2: e.tensor_tensor(out=hsd[:, 0, :], in0=a1[:], in1=a2[:], op=ALU.add),
                  reads=[a1, a2], writes=[(hsd, 0)])
            kb.op("pool", lambda e, a1=a1, a2=a2: e.tensor_tensor(out=hsd[:, 1, :], in0=a1[:], in1=a2[:], op=ALU.subtract),
                  reads=[a1, a2], writes=[(hsd, 1)])
            kb.dma("pool", HSD.t.rearrange("s n c -> n s c")[i * 128:(i + 1) * 128, :, :], hsd[:], reads=[hsd])
        phase_begin(label="filter_s1")
        tw1 = load_tw1()
        s1b = s1_bufs()
        fft_s1(HSD[0], A1K.t[0], tw1, s1b)
        fft_s1(HSD[1], A1K.t[1], tw1, s1b)
        phase_begin(0, True, label="filter_s2")
        preload = []
        if preload_down is not None:
            wd_pre = kb.sb([128, 22, D], BF16)
            preload = [(lambda c0=c0: load_w_into(wd_pre, P["ffn_down"][preload_down], 22, c0, 128, None, c0))
                       for c0 in range(0, D, 128)]
        dftm = kb.sb([64, 3, 64], BF16)
        kb.dma("sp", dftm[:], C["dftm"][:, :, :], writes=[dftm])
        FC = min(4, HH)
        at_ = [[kb.sb([64, FC, DH], BF16) for _ in range(4)] for _ in range(2)]
        ko = [[kb.sb([64, FC, DH], BF16) for _ in range(2)] for _ in range(2)]
        nch = HH // FC
        for c_ in range(nch):
            if preload and (c_ % max(1, nch // 8) == 0 or c_ == nch - 1):
                preload.pop(0)()
                if c_ == nch - 1:
                    while preload:
                        preload.pop(0)()
            f0 = c_ * FC
            A = at_[c_ % 2]
            for q_, (sg_, ri_) in enumerate(((0, 0), (0, 1), (1, 0), (1, 1))):
                kb.dma("sp", A[q_][:], A1K.t[sg_, ri_].rearrange("f n c -> n f c")[:, f0:f0 + FC, :], writes=[A[q_]])
            kos = ko[c_ % 2]
            for fl in range(FC):
                br, bi = next_pb(), next_pb()
                kb.op("pe", lambda e: e.matmul(kb.ps(br)[0:64, :], lhsT=dftm[:, 0, :], rhs=A[0][:, fl, :], start=True, stop=False),
                      reads=[dftm, A[0]], writes=[PB[br]])
                kb.op("pe", lambda e: e.matmul(kb.ps(br)[0:64, :], lhsT=dftm[:, 1, :], rhs=A[1][:, fl, :], start=False, stop=True),
                      reads=[dftm, A[1]], writes=[PB[br]])
                kb.op("pe", lambda e: e.matmul(kb.ps(bi)[0:64, :], lhsT=dftm[:, 0, :], rhs=A[3][:, fl, :], start=True, stop=False),
                      reads=[dftm, A[3]], writes=[PB[bi]])
                kb.op("pe", lambda e: e.matmul(kb.ps(bi)[0:64, :], lhsT=dftm[:, 2, :], rhs=A[2][:, fl, :], start=False, stop=True),
                      reads=[dftm, A[2]], writes=[PB[bi]])
                kb.op("act", lambda e: e.copy(out=kos[0][:, fl, :], in_=kb.ps(br)[0:64, :]), reads=[PB[br]], writes=[(kos[0], fl)])
                kb.op("dve", lambda e: e.tensor_copy(out=kos[1][:, fl, :], in_=kb.ps(bi)[0:64, :]), reads=[PB[bi]], writes=[(kos[1], fl)])
            for ri_ in range(2):
                kb.dma("pool", KS.t[ri_, f0:f0 + FC].rearrange("f k c -> k f c"), kos[ri_][:], reads=[kos[ri_]])

    def load_tw1():
        tw1 = kb.sb([HH, 2, 64, HH], BF16)
        kb.dma("sp", tw1[:], C["tw1"][:, :, :, :], writes=[tw1])
        return tw1

    def s1_bufs():
        return ([kb.sb([HH, 8, DH], BF16) for _ in range(2)],
                [[kb.sb([HH, 8, DH], BF16) for _ in range(2)] for _ in range(2)])

    def fft_s1(src, dst, tw1, s1b):
        xv = src.rearrange("(a b) c -> a b c", b=64)
        if not hasattr(fft_s1, "bufs"):
            pass
        xt, ot = s1b
        ne = 0
        for g in range(8):
            x_ = xt[g % 2]
            kb.dma("sp", x_[:], xv[:, g * 8:(g + 1) * 8, :], writes=[x_])
            for ri_ in range(2):
                o_ = ot[g % 2][ri_]
                for nl in range(8):
                    bank = next_pb()
                    kb.op("pe", lambda e: e.matmul(kb.ps(bank)[0:HH, :], lhsT=tw1[:, ri_, g * 8 + nl, :], rhs=x_[:, nl, :],
                                                   start=True, stop=True), reads=[tw1, x_], writes=[PB[bank]])
                    if ne % 2 == 0:
                        kb.op("act", lambda e: e.copy(out=o_[:, nl, :], in_=kb.ps(bank)[0:HH, :]), reads=[PB[bank]], writes=[(o_, nl)])
                    else:
                        kb.op("dve", lambda e: e.tensor_copy(out=o_[:, nl, :], in_=kb.ps(bank)[0:HH, :]), reads=[PB[bank]], writes=[(o_, nl)])
                    ne += 1
                kb.dma("pool", dst[ri_, :, g * 8:(g + 1) * 8, :], o_[:], reads=[o_])

    def phase_norm0(xsrc):
        phase_begin()
        tmps = norm_tmp()
        xts = [kb.sb([128, D], F32) for _ in range(2)]
        for i in range(NT):
            xt = xts[i % 2]
            kb.dma("sp", xt[:], xsrc[i * 128:(i + 1) * 128, :], writes=[xt])
            norm_transpose(xt, i, tmps[i % 2])

    def phase_proj(li):
        phase_begin(0)
        gt = kb.sb([128, 8], F32)
        kb.dma("sp", gt[:], P["g_mix_pre"][li], writes=[gt])
        W = P["w_in"][li]
        ev = [kb.sb([128, 512], BF16) for _ in range(4)]
        evc = [0]

        def next_ev():
            evc[0] += 1
            return ev[evc[0] % 4]

        cw = kb.sb([128, 12, 3], F32)
        cb = kb.sb([128, 12], F32)
        kb.dma("sp", cw[:], P["hy_cw"][li], writes=[cw])
        kb.dma("sp", cb[:], P["hy_cb"][li], writes=[cb])
        raws = [kb.sb([128, L + 2], BF16) for _ in range(2)]
        for r in raws:
            kb.op("pool", lambda e, r=r: e.memset(r[:, 0:1], 0.0), writes=[(r, "h0")])
            kb.op("pool", lambda e, r=r: e.memset(r[:, L + 1:L + 2], 0.0), writes=[(r, "h1")])
        acc = kb.sb([128, L], F32)
        x1c = kb.sb([128, L], BF16)
        oc = [kb.sb([128, L], BF16) for _ in range(2)]
        tz = [kb.sb([128, 8, 128], BF16) for _ in range(2)]
        wfs = [kb.sb([128, 8, 128], F32) for _ in range(2)]
        wbs = [kb.sb([128, 8, 128], BF16) for _ in range(2)]
        wsrc = W.rearrange("(k p) c -> p k c", p=128)
        order = [(b, part) for b in range(4) for part in range(3)]

        def prep(n_):
            b_, part_ = order[n_]
            blk_ = part_ * 4 + b_
            s_ = n_ % 2
            kb.dma("sp", wfs[s_][:], wsrc[:, :, blk_ * 128:(blk_ + 1) * 128], writes=[wfs[s_]])
            kb.op("pool", lambda e: e.tensor_tensor(out=wbs[s_][:], in0=wfs[s_][:],
                                                    in1=gt[:, :, None].to_broadcast([128, 8, 128]), op=ALU.mult),
                  reads=[wfs[s_], gt], writes=[wbs[s_]])
            return wbs[s_]

        zvt_v = ZVT.t.rearrange("(i p) c -> p i c", p=128)
        TB = min(8, NT)

        def transposes(dst, b):
            for i0 in range(0, NT, TB):
                tzt = tz[(i0 // TB) % 2]
                for ii in range(TB):
                    kb.op("pe", lambda e, ii=ii: e.transpose(out=kb.ps(7, [128, 8, 128], BF16)[:, ii, :],
                                                             in_=dst[:, (i0 + ii) * 128:(i0 + ii + 1) * 128], identity=ident[:]),
                          reads=[dst, ident], writes=[PB[7]])
                kb.op("act", lambda e: e.copy(out=tzt[:, 0:TB, :], in_=kb.ps(7, [128, 8, 128], BF16)[:, 0:TB, :]),
                      reads=[PB[7]], writes=[tzt])
                kb.dma("pool", zvt_v[:, i0:i0 + TB, b * 128:(b + 1) * 128], tzt[:, 0:TB, :], reads=[tzt])

        wnext = prep(0)
        pending = None
        for n_, (b, part) in enumerate(order):
            blk = part * 4 + b
            raw = raws[n_ % 2]
            wbv = wnext
            if n_ + 1 < len(order):
                wnext = prep(n_ + 1)
            for j in range(NQ):
                bank = next_pb()
                gemm_fm(wbv, 8, 0, 128, HT, j, bank)
                kb.op("act", lambda e: e.copy(out=raw[:, 1 + j * 512:1 + (j + 1) * 512], in_=kb.ps(bank)),
                      reads=[PB[bank]], writes=[(raw, j)])
            if pending is not None:
                transposes(*pending)
                pending = None
            dst = x1c if part == 1 else oc[0 if part == 0 else 1]
            kb.op("dve", lambda e: e.tensor_scalar(out=acc[:], in0=raw[:, 0:L], scalar1=cw[:, blk, 0:1], scalar2=cb[:, blk:blk + 1],
                                                   op0=ALU.mult, op1=ALU.add), reads=[raw, cw, cb], writes=[acc])
            kb.op("dve", lambda e: e.scalar_tensor_tensor(out=acc[:], in0=raw[:, 1:L + 1], scalar=cw[:, blk, 1:2], in1=acc[:],
                                                          op0=ALU.mult, op1=ALU.add), reads=[raw, cw, acc], writes=[acc])
            kb.op("dve", lambda e: e.scalar_tensor_tensor(out=dst[:], in0=raw[:, 2:L + 2], scalar=cw[:, blk, 2:3], in1=acc[:],
                                                          op0=ALU.mult, op1=ALU.add), reads=[raw, cw, acc], writes=[dst])
            if part == 0:
                kb.dma("pool", X0fm[b * 128:(b + 1) * 128, :], dst[:], reads=[dst])
            elif part == 2:
                kb.op("dve", lambda e: e.tensor_tensor(out=dst[:], in0=dst[:], in1=x1c[:], op=ALU.mult),
                      reads=[dst, x1c], writes=[dst])
                kb.dma("pool", ZVfm[b * 128:(b + 1) * 128, :], dst[:], reads=[dst])
                pending = (dst, b)
        transposes(*pending)

        phase_begin(2)
        gt = kb.sb([128, 8], F32)
        kb.dma("sp", gt[:], P["g_mix_pre"][li], writes=[gt])
        ev = [kb.sb([128, 512], BF16) for _ in range(4)]
        def fm_group(col0, ncols, dest, func, scale=1.0):
            for c0 in range(0, ncols, 512):
                nc_ = min(512, ncols - c0)
                wbv = load_w(W, 8, col0 + c0, nc_, gt)
                for m in range(nc_ // 128):
                    for j in range(NQ):
                        bank = next_pb()
                        gemm_fm(wbv, 8, m * 128, 128, HT, j, bank)
                        o = next_ev()
                        kb.op("act", lambda e, o=o, bank=bank: e.activation(out=o[:], in_=kb.ps(bank), func=func, scale=scale),
                              reads=[PB[bank]], writes=[o])
                        r0 = c0 + m * 128
                        kb.dma("pool", dest[r0:r0 + 128, j * 512:(j + 1) * 512], o[:], reads=[o])

        def tok_group(col0, dest):
            wbv = load_w(W, 8, col0, 512, gt)
            for i in range(NT):
                bank = next_pb()
                gemm_tok(wbv, 8, 0, 512, HT, i, bank)
                o = next_ev()
                if i % 2 == 0:
                    kb.op("dve", lambda e, o=o, bank=bank: e.tensor_copy(out=o[:], in_=kb.ps(bank)), reads=[PB[bank]], writes=[o])
                else:
                    kb.op("act", lambda e, o=o, bank=bank: e.copy(out=o[:], in_=kb.ps(bank)), reads=[PB[bank]], writes=[o])
                kb.dma("pool", dest[i * 128:(i + 1) * 128, :], o[:], reads=[o])

        fm_group(1536, 512, Qfm, AF.Copy, 128.0 ** -0.5)
        fm_group(2048, 512, Kfm, AF.Copy)
        tok_group(2048, Ktok)
        tok_group(2560, Vtok)
        fm_group(3072, 512, Gfm, AF.Silu)
        wbv = load_w(W, 8, 3584, 32, gt)
        for j in range(NQ):
            bank = next_pb()
            gemm_fm(wbv, 8, 0, 32, HT, j, bank)
            kb.op("act", lambda e, j=j, bank=bank: e.copy(out=LRH[0:32, j * 512:(j + 1) * 512], in_=kb.ps(bank)[0:32, :]),
                  reads=[PB[bank]], writes=[(LRH, j)])
            kb.op("dve", lambda e, j=j, bank=bank: e.tensor_tensor(out=LRL[0:32, j * 512:(j + 1) * 512], in0=kb.ps(bank)[0:32, :],
                                                                   in1=LRH[0:32, j * 512:(j + 1) * 512], op=ALU.subtract),
                  reads=[PB[bank], (LRH, j)], writes=[(LRL, j)])
        fm_group(4128, 3 * D, GATES, AF.Sigmoid)

        phase_begin(1)
        gt = kb.sb([128, 8], F32)
        kb.dma("sp", gt[:], P["g_mix_pre"][li], writes=[gt])
        ev = [kb.sb([128, 512], BF16) for _ in range(4)]
        PW = 16
        ua = kb.sb([128, L + 2 * PW], F32)
        ub = kb.sb([128, L + 2 * PW], F32)
        uc = kb.sb([128, L + 2 * PW], F32)
        icn = kb.sb([128, L], F32)
        pwt = kb.sb([128, 4, 128], F32)
        pwb = kb.sb([128, 4, 128], BF16)
        psc = kb.sb([128, 4], F32)
        dbf = kb.sb([128, L], BF16)
        kb.dma("sp", pwt[:], P["pool_w"][li], writes=[pwt])
        kb.dma("sp", psc[:], P["pool_scale"][li], writes=[psc])
        kb.op("pool", lambda e: e.tensor_copy(out=pwb[:], in_=pwt[:]), reads=[pwt], writes=[pwb])
        for t_ in (ua, ub, uc):
            kb.op("pool", lambda e, t_=t_: e.memset(t_[:], 0.0), writes=[t_])
        for gi, wv in enumerate((2, 4, 8, 16)):
            wbv = load_w(W, 8, 3616 + gi * 128, 128, gt)
            kb.dma("sp", icn[:], C["invcnt"][gi], writes=[icn])
            for j in range(NQ):
                bank = next_pb()
                gemm_fm(wbv, 8, 0, 128, HT, j, bank)
                kb.op("act", lambda e, j=j, bank=bank: e.copy(out=ua[:, PW + j * 512:PW + (j + 1) * 512], in_=kb.ps(bank)),
                      reads=[PB[bank]], writes=[ua])
            src, dsts = ua, [ub, uc]
            lo, hi = -14, L + 14
            kb.op("dve", lambda e, lo=lo, hi=hi: e.tensor_tensor(
                out=ub[:, PW + lo:PW + hi], in0=ua[:, PW + lo - 1:PW + hi - 1], in1=ua[:, PW + lo:PW + hi], op=ALU.add),
                reads=[ua], writes=[ub])
            cur, oth = ub, uc
            sh = 1
            rng = [(-12, L + 12), (-8, L + 8), (0, L)]
            for si in range(int(math.log2(wv)) - 1):
                lo, hi = rng[si]
                kb.op("dve", lambda e, lo=lo, hi=hi, cur=cur, oth=oth, sh=sh: e.tensor_tensor(
                    out=oth[:, PW + lo:PW + hi], in0=cur[:, PW + lo - sh:PW + hi - sh],
                    in1=cur[:, PW + lo + sh:PW + hi + sh], op=ALU.add), reads=[cur], writes=[oth])
                cur, oth = oth, cur
                sh *= 2
            kb.op("dve", lambda e, cur=cur, oth=oth: e.tensor_tensor(out=oth[:, PW:PW + L], in0=cur[:, PW:PW + L], in1=icn[:], op=ALU.mult),
                  reads=[cur, icn], writes=[oth])
            kb.op("dve", lambda e, oth=oth: e.tensor_tensor(out=dbf[:], in0=oth[:, PW:PW + L], in1=ua[:, PW:PW + L], op=ALU.subtract),
                  reads=[oth, ua], writes=[dbf])
            for t_ in (ub, uc):
                kb.op("pool", lambda e, t_=t_: e.memset(t_[:, 0:PW], 0.0), reads=[t_], writes=[t_])
                kb.op("pool", lambda e, t_=t_: e.memset(t_[:, PW + L:PW + L + PW], 0.0), reads=[t_], writes=[t_])
            for j in range(NQ):
                bank = next_pb()
                kb.op("pe", lambda e, j=j, bank=bank, gi=gi: e.matmul(kb.ps(bank), lhsT=pwb[:, gi, :], rhs=dbf[:, j * 512:(j + 1) * 512],
                                                               start=True, stop=True), reads=[pwb, dbf], writes=[PB[bank]])
                o = next_ev()
                kb.op("dve", lambda e, o=o, bank=bank, gi=gi: e.tensor_scalar(out=o[:], in0=kb.ps(bank), scalar1=psc[:, gi:gi + 1],
                                                                       scalar2=None, op0=ALU.mult), reads=[PB[bank], psc], writes=[o])
                kb.dma("pool", YP[gi * 128:(gi + 1) * 128, j * 512:(j + 1) * 512], o[:], reads=[o])

    def phase_hyena(li):
        phase_begin(label="hy_s1")
        tw1 = load_tw1()
        fft_s1(ZVT.t, A1Z.t, tw1, s1_bufs())
        phase_begin(label="hy_s2")
        dftm = kb.sb([64, 3, 64], BF16)
        gtw = kb.sb([64, 3, HH, 64], BF16)
        kb.dma("sp", dftm[:], C["dftm"][:, :, :], writes=[dftm])
        kb.dma("sp", gtw[:], C["gtw"][:, :, :, :], writes=[gtw])
        FC = min(4, HH)
        at_ = [[kb.sb([64, FC, DH], BF16) for _ in range(2)] for _ in range(2)]
        kt_ = [[kb.sb([64, FC, DH], BF16) for _ in range(2)] for _ in range(2)]
        bo = [[kb.sb([64, FC, DH], BF16) for _ in range(2)] for _ in range(2)]
        m = [[kb.sb([64, DH], F32) for _ in range(4)] for _ in range(2)]
        yy = [[kb.sb([64, DH], BF16) for _ in range(2)] for _ in range(2)]
        def s2_prod(f1_):
            c_, fl = divmod(f1_, FC)
            f0 = c_ * FC
            A, K_ = at_[c_ % 2], kt_[c_ % 2]
            if fl == 0:
                for ri_ in range(2):
                    kb.dma("sp", A[ri_][:], A1Z.t[ri_].rearrange("f n c -> n f c")[:, f0:f0 + FC, :], writes=[A[ri_]])
                    kb.dma("sp", K_[ri_][:], KS.t[ri_, f0:f0 + FC].rearrange("f k c -> k f c"), writes=[K_[ri_]])
            mm_, y_ = m[f1_ % 2], yy[f1_ % 2]
            zr, zi = f1_ % 2, 2 + f1_ % 2
            kb.op("pe", lambda e: e.matmul(kb.ps(zr)[0:64, :], lhsT=dftm[:, 0, :], rhs=A[0][:, fl, :], start=True, stop=False),
                  reads=[dftm, A[0]], writes=[PB[zr]])
            kb.op("pe", lambda e: e.matmul(kb.ps(zr)[0:64, :], lhsT=dftm[:, 1, :], rhs=A[1][:, fl, :], start=False, stop=True),
                  reads=[dftm, A[1]], writes=[PB[zr]])
            kb.op("pe", lambda e: e.matmul(kb.ps(zi)[0:64, :], lhsT=dftm[:, 0, :], rhs=A[1][:, fl, :], start=True, stop=False),
                  reads=[dftm, A[1]], writes=[PB[zi]])
            kb.op("pe", lambda e: e.matmul(kb.ps(zi)[0:64, :], lhsT=dftm[:, 2, :], rhs=A[0][:, fl, :], start=False, stop=True),
                  reads=[dftm, A[0]], writes=[PB[zi]])
            kb.op("dve", lambda e: e.tensor_tensor(out=mm_[0][:], in0=kb.ps(zr)[0:64, :], in1=K_[0][:, fl, :], op=ALU.mult),
                  reads=[PB[zr], K_[0]], writes=[mm_[0]])
            kb.op("dve", lambda e: e.tensor_tensor(out=mm_[1][:], in0=kb.ps(zi)[0:64, :], in1=K_[1][:, fl, :], op=ALU.mult),
                  reads=[PB[zi], K_[1]], writes=[mm_[1]])
            kb.op("dve", lambda e: e.tensor_tensor(out=mm_[2][:], in0=kb.ps(zr)[0:64, :], in1=K_[1][:, fl, :], op=ALU.mult),
                  reads=[PB[zr], K_[1]], writes=[mm_[2]])
            kb.op("dve", lambda e: e.tensor_tensor(out=mm_[3][:], in0=kb.ps(zi)[0:64, :], in1=K_[0][:, fl, :], op=ALU.mult),
                  reads=[PB[zi], K_[0]], writes=[mm_[3]])
            kb.op("pool", lambda e: e.tensor_tensor(out=y_[0][:], in0=mm_[0][:], in1=mm_[1][:], op=ALU.subtract),
                  reads=[mm_[0], mm_[1]], writes=[y_[0]])
            kb.op("pool", lambda e: e.tensor_tensor(out=y_[1][:], in0=mm_[2][:], in1=mm_[3][:], op=ALU.add),
                  reads=[mm_[2], mm_[3]], writes=[y_[1]])

        def inv_a(f1_):
            c_, fl = divmod(f1_, FC)
            f0 = c_ * FC
            y_ = yy[f1_ % 2]
            bos = bo[c_ % 2]
            br, bi = 4 + f1_ % 2, 6 + f1_ % 2
            kb.op("pe", lambda e: e.matmul(kb.ps(br)[0:64, :], lhsT=gtw[:, 0, f1_, :], rhs=y_[0][:], start=True, stop=False),
                  reads=[gtw, y_[0]], writes=[PB[br]])
            kb.op("pe", lambda e: e.matmul(kb.ps(br)[0:64, :], lhsT=gtw[:, 2, f1_, :], rhs=y_[1][:], start=False, stop=True),
                  reads=[gtw, y_[1]], writes=[PB[br]])
            kb.op("pe", lambda e: e.matmul(kb.ps(bi)[0:64, :], lhsT=gtw[:, 1, f1_, :], rhs=y_[0][:], start=True, stop=False),
                  reads=[gtw, y_[0]], writes=[PB[bi]])
            kb.op("pe", lambda e: e.matmul(kb.ps(bi)[0:64, :], lhsT=gtw[:, 0, f1_, :], rhs=y_[1][:], start=False, stop=True),
                  reads=[gtw, y_[1]], writes=[PB[bi]])
            kb.op("act", lambda e: e.copy(out=bos[0][:, fl, :], in_=kb.ps(br)[0:64, :]), reads=[PB[br]], writes=[(bos[0], fl)])
            kb.op("act", lambda e: e.copy(out=bos[1][:, fl, :], in_=kb.ps(bi)[0:64, :]), reads=[PB[bi]], writes=[(bos[1], fl)])
            if fl == FC - 1:
                for ri_ in range(2):
                    kb.dma("pool", B1.t[ri_, :, f0:f0 + FC, :], bos[ri_][:], reads=[bos[ri_]])

        for k_ in range(HH + 1):
            if k_ < HH:
                s2_prod(k_)
            if k_ >= 1:
                inv_a(k_ - 1)
        phase_begin(label="hy_ib")
        hb = kb.sb([128, 4], F32)
        kb.dma("sp", hb[:], P["hy_bias"][li], writes=[hb])
        m4 = kb.sb([HH, 2, HH], BF16)
        kb.dma("sp", m4[:], C["m4"][:, :, :], writes=[m4])
        ysb = kb.sb([128, 4, L], BF16)
        bt_ = [[kb.sb([HH, 8, DH], BF16) for _ in range(2)] for _ in range(2)]
        for g in range(8):
            Bt = bt_[g % 2]
            for ri_ in range(2):
                kb.dma("sp", Bt[ri_][:], B1.t[ri_].rearrange("t f c -> f t c")[:, g * 8:(g + 1) * 8, :], writes=[Bt[ri_]])
            for cb_ in range(4):
                bank = next_pb()
                for tl in range(8):
                    kb.op("pe", lambda e: e.matmul(kb.ps(bank, [128, 8, HH])[:, tl, :], lhsT=Bt[0][:, tl, cb_ * 128:(cb_ + 1) * 128],
                                                   rhs=m4[:, 0, :], start=True, stop=False), reads=[Bt[0], m4], writes=[PB[bank]])
                    kb.op("pe", lambda e: e.matmul(kb.ps(bank, [128, 8, HH])[:, tl, :], lhsT=Bt[1][:, tl, cb_ * 128:(cb_ + 1) * 128],
                                                   rhs=m4[:, 1, :], start=False, stop=True), reads=[Bt[1], m4], writes=[PB[bank]])
                dst = ysb[:, cb_, :].rearrange("p (a b) -> p a b", b=64)[:, :, g * 8:(g + 1) * 8]
                src_ = kb.ps(bank, [128, 8, HH]).rearrange("p a b -> p b a")
                if (g * 4 + cb_) % 2 == 0:
                    kb.op("act", lambda e: e.copy(out=dst, in_=src_), reads=[PB[bank]], writes=[(ysb, (cb_, g))])
                else:
                    kb.op("dve", lambda e: e.tensor_copy(out=dst, in_=src_), reads=[PB[bank]], writes=[(ysb, (cb_, g))])
        x0t = [kb.sb([128, 512], BF16) for _ in range(2)]
        zvf = [kb.sb([128, 512], BF16) for _ in range(2)]
        tmp = [kb.sb([128, 512], F32) for _ in range(2)]
        yo = [kb.sb([128, 512], BF16) for _ in range(2)]
        for tt in range(NQ):
            for cb_ in range(4):
                k = (tt * 4 + cb_) % 2
                kb.dma("sp", x0t[k][:], X0fm[cb_ * 128:(cb_ + 1) * 128, tt * 512:(tt + 1) * 512], writes=[x0t[k]])
                kb.dma("sp", zvf[k][:], ZVfm[cb_ * 128:(cb_ + 1) * 128, tt * 512:(tt + 1) * 512], writes=[zvf[k]])
                kb.op("dve", lambda e: e.scalar_tensor_tensor(
                    out=tmp[k][:], in0=zvf[k][:], scalar=hb[:, cb_:cb_ + 1], in1=ysb[:, cb_, tt * 512:(tt + 1) * 512],
                    op0=ALU.mult, op1=ALU.add), reads=[zvf[k], hb, ysb], writes=[tmp[k]])
                kb.op("pool", lambda e: e.tensor_tensor(out=yo[k][:], in0=tmp[k][:], in1=x0t[k][:], op=ALU.mult),
                      reads=[tmp[k], x0t[k]], writes=[yo[k]])
                kb.dma("pool", YH[cb_ * 128:(cb_ + 1) * 128, tt * 512:(tt + 1) * 512], yo[k][:], reads=[yo[k]])
        st["pb"] = 0

    def phase_gla(li):
        phase_begin(0, True)
        wbr_pre = [kb.sb([128, 4, D], BF16) for _ in range(3)]
        preload = [(lambda b=b, c0=c0: load_w_into(wbr_pre[b], P["w_br"][li, b], 4, c0, 512, None, c0))
                   for b in range(3) for c0 in (0, 512)]
        w2x = kb.sb([33, 2, 512], F32)
        gn = kb.sb([128, 1], F32)
        kb.dma("sp", w2x[:], P["gla_w2x"][li], writes=[w2x])
        kb.dma("sp", gn[:], P["gla_norm"][li], writes=[gn])
        w2h = kb.sb([33, 2, 512], BF16)
        w2l = kb.sb([33, 2, 512], BF16)
        kb.op("dve", lambda e: e.tensor_copy(out=w2h[:], in_=w2x[:]), reads=[w2x], writes=[w2h])
        kb.op("dve", lambda e: e.tensor_tensor(out=w2l[:], in0=w2x[:], in1=w2h[:], op=ALU.subtract), reads=[w2x, w2h], writes=[w2l])
        OB = HT.view(HT.t[:, 0:4, :])
        S32 = [kb.sb([128, 4, 128], F32) for _ in range(2)]
        Sbf = [kb.sb([128, 4, 128], BF16) for _ in range(2)]
        for d_ in range(2):
            kb.op("pool", lambda e, d_=d_: e.memset(S32[d_][:], 0.0), writes=[S32[d_]])
            kb.op("pool", lambda e, d_=d_: e.memset(Sbf[d_][:], 0.0), writes=[Sbf[d_]])
        lah = [kb.sb([128, 512], BF16) for _ in range(2)]
        lal = [kb.sb([128, 512], BF16) for _ in range(2)]
        PD = lambda shape, dt: [kb.sb(shape, dt) for _ in range(2)]
        PS = lambda shape, dt: [[kb.sb(shape, dt) for _ in range(2)] for _ in range(2)]
        e1, la, edec = PD([128, 512], F32), PD([128, 512], F32), PD([128, 512], BF16)
        EK = PD([128, 4, 128], BF16)
        kt, qf, kf, ke = PD([128, 512], BF16), PD([128, 4, 128], BF16), PD([128, 4, 128], BF16), PD([128, 4, 128], BF16)
        EQ = PS([128, 4, 128], BF16)
        DEC = PS([128, 4, 1], F32)
        vt, kdec = PS([128, 512], BF16), PS([128, 512], BF16)
        qe, msk = PS([128, 4, 128], BF16), PS([128, 4, 128], BF16)
        gf, sqb, yob = PD([128, 4, 128], BF16), PD([128, 4, 128], BF16), PD([128, 4, 128], BF16)
        o32, rsd = PD([128, 4, 128], F32), PD([128, 4, 128], F32)
        qv = Qfm.t.rearrange("(h d) t -> d h t", h=4)
        kv = Kfm.t.rearrange("(h d) t -> d h t", h=4)
        gv_ = Gfm.t.rearrange("(h d) t -> d h t", h=4)
        yv = YG.t.rearrange("(h d) t -> d h t", h=4)
        V3 = [128, 4, 128]

        def tile_of(step, d_):
            return step if d_ == 0 else NT - 1 - step

        def pre(step, d_, stage):
            i = tile_of(step, d_)
            sl = slice(i * 128, (i + 1) * 128)
            p = step % 2
            bA, bB = 4 * d_, 4 * d_ + 1
            tri_fm = 0 if d_ == 0 else 2
            tri_dec = 3 if d_ == 0 else 1
            if stage == 1:
                pre1(d_, p, sl, bA)
            elif stage == 2:
                pre2(d_, p, bA, bB, tri_fm, tri_dec)
            else:
                pre3(d_, p, bA)

        def pre1(d_, p, sl, bA):
            kb.dma("sp", kt[d_][:], Ktok[sl, :], writes=[kt[d_]])
            kb.dma("sp", vt[d_][p][:], Vtok[sl, :], writes=[vt[d_][p]])
            kb.dma("sp", qf[d_][:], qv[:, :, sl], writes=[qf[d_]])
            kb.dma("sp", kf[d_][:], kv[:, :, sl], writes=[kf[d_]])
            kb.op("pe", lambda e: e.matmul(kb.ps(bA), lhsT=LRH[:, sl], rhs=w2h[:, d_, :], start=True, stop=False),
                  reads=[LRH, w2h], writes=[PB[bA]])
            kb.op("pe", lambda e: e.matmul(kb.ps(bA), lhsT=LRL[:, sl], rhs=w2h[:, d_, :], start=False, stop=False),
                  reads=[LRL, w2h], writes=[PB[bA]])
            kb.op("pe", lambda e: e.matmul(kb.ps(bA), lhsT=LRH[:, sl], rhs=w2l[:, d_, :], start=False, stop=True),
                  reads=[LRH, w2l], writes=[PB[bA]])
            kb.op("act", lambda e: e.activation(out=e1[d_][:], in_=kb.ps(bA), func=AF.Exp, scale=-1.0),
                  reads=[PB[bA]], writes=[e1[d_]])
            kb.op("act", lambda e: e.activation(out=e1[d_][:], in_=e1[d_][:], func=AF.Ln, bias=1.0),
                  reads=[e1[d_]], writes=[e1[d_]])
            kb.op("dve", lambda e: e.tensor_scalar(out=la[d_][:], in0=e1[d_][:], scalar1=-1.0 / 16.0, scalar2=-1.0,
                                                   op0=ALU.mult, op1=ALU.max), reads=[e1[d_]], writes=[la[d_]])
            kb.op("act", lambda e: e.copy(out=lah[d_][:], in_=la[d_][:]), reads=[la[d_]], writes=[lah[d_]])
            kb.op("dve", lambda e: e.tensor_tensor(out=lal[d_][:], in0=la[d_][:], in1=lah[d_][:], op=ALU.subtract),
                  reads=[la[d_], lah[d_]], writes=[lal[d_]])

        def pre2(d_, p, bA, bB, tri_fm, tri_dec):
            kb.op("pe", lambda e: e.matmul(kb.ps(bA), lhsT=tri32[:, tri_dec, :], rhs=lah[d_][:], start=True, stop=False),
                  reads=[tri32, lah[d_]], writes=[PB[bA]])
            kb.op("pe", lambda e: e.matmul(kb.ps(bA), lhsT=tri32[:, tri_dec, :], rhs=lal[d_][:], start=False, stop=True),
                  reads=[tri32, lal[d_]], writes=[PB[bA]])
            for h in range(4):
                kb.op("pe", lambda e, h=h: e.matmul(kb.ps(bB, V3)[:, h, :], lhsT=lah[d_][:, h * 128:(h + 1) * 128],
                                                    rhs=tri32[:, tri_fm, :], start=True, stop=False),
                      reads=[tri32, lah[d_]], writes=[PB[bB]])
                kb.op("pe", lambda e, h=h: e.matmul(kb.ps(bB, V3)[:, h, :], lhsT=lal[d_][:, h * 128:(h + 1) * 128],
                                                    rhs=tri32[:, tri_fm, :], start=False, stop=True),
                      reads=[tri32, lal[d_]], writes=[PB[bB]])
            kb.op("act", lambda e: e.activation(out=edec[d_][:], in_=kb.ps(bA), func=AF.Exp), reads=[PB[bA]], writes=[edec[d_]])
            kb.op("act", lambda e: e.activation(out=EQ[d_][p][:], in_=kb.ps(bB, V3), func=AF.Exp), reads=[PB[bB]], writes=[EQ[d_][p]])
            kb.op("act", lambda e: e.activation(out=EK[d_][:], in_=kb.ps(bB, V3), func=AF.Exp, scale=-1.0),
                  reads=[PB[bB]], writes=[EK[d_]])
            dcol_ = 127 if d_ == 0 else 0
            kb.op("act", lambda e: e.activation(out=DEC[d_][p][:], in_=kb.ps(bB, V3)[:, :, dcol_:dcol_ + 1], func=AF.Exp),
                  reads=[PB[bB]], writes=[DEC[d_][p]])
            kb.op("dve", lambda e: e.tensor_tensor(out=kdec[d_][p][:], in0=kt[d_][:], in1=edec[d_][:], op=ALU.mult),
                  reads=[kt[d_], edec[d_]], writes=[kdec[d_][p]])
            kb.op("dve", lambda e: e.tensor_tensor(out=qe[d_][p][:], in0=qf[d_][:], in1=EQ[d_][p][:], op=ALU.mult),
                  reads=[qf[d_], EQ[d_][p]], writes=[qe[d_][p]])
            kb.op("dve", lambda e: e.tensor_tensor(out=ke[d_][:], in0=kf[d_][:], in1=EK[d_][:], op=ALU.mult),
                  reads=[kf[d_], EK[d_]], writes=[ke[d_]])

        def pre3(d_, p, bA):
            for h in range(4):
                kb.op("pe", lambda e, h=h: e.matmul(kb.ps(bA, V3)[:, h, :], lhsT=ke[d_][:, h, :], rhs=qe[d_][p][:, h, :],
                                                    start=True, stop=True), reads=[ke[d_], qe[d_][p]], writes=[PB[bA]])
            kb.op("dve", lambda e: e.tensor_tensor(out=msk[d_][p][:], in0=kb.ps(bA, V3),
                                                   in1=tri32[:, 3 * d_:3 * d_ + 1, :].to_broadcast(V3), op=ALU.mult),
                  reads=[PB[bA], tri32], writes=[msk[d_][p]])

        def seq(step, d_):
            i = tile_of(step, d_)
            sl = slice(i * 128, (i + 1) * 128)
            p = step % 2
            bC, bD = 4 * d_ + 2, 4 * d_ + 3
            final = step >= NT // 2
            for h in range(4):
                hs = slice(h * 128, (h + 1) * 128)
                kb.op("pe", lambda e, h=h, hs=hs: e.matmul(kb.ps(bC, V3)[:, h, :], lhsT=vt[d_][p][:, hs], rhs=msk[d_][p][:, h, :],
                                                           start=True, stop=False), reads=[vt[d_][p], msk[d_][p]], writes=[PB[bC]])
                kb.op("pe", lambda e, h=h: e.matmul(kb.ps(bC, V3)[:, h, :], lhsT=Sbf[d_][:, h, :], rhs=qe[d_][p][:, h, :],
                                                    start=False, stop=True), reads=[Sbf[d_], qe[d_][p]], writes=[PB[bC]])
            for h in range(4):
                hs = slice(h * 128, (h + 1) * 128)
                kb.op("pe", lambda e, h=h, hs=hs: e.matmul(kb.ps(bD, V3)[:, h, :], lhsT=kdec[d_][p][:, hs], rhs=vt[d_][p][:, hs],
                                                           start=True, stop=True), reads=[kdec[d_][p], vt[d_][p]], writes=[PB[bD]])
            kb.op("dve", lambda e: e.tensor_tensor(out=S32[d_][:], in0=S32[d_][:],
                                                   in1=DEC[d_][p][:, :, 0:1].to_broadcast(V3), op=ALU.mult),
                  reads=[S32[d_], DEC[d_][p]], writes=[S32[d_]])
            kb.op("dve", lambda e: e.tensor_tensor(out=S32[d_][:], in0=S32[d_][:], in1=kb.ps(bD, V3), op=ALU.add),
                  reads=[S32[d_], PB[bD]], writes=[S32[d_]])
            kb.op("act", lambda e: e.copy(out=Sbf[d_][:], in_=S32[d_][:]), reads=[S32[d_]], writes=[Sbf[d_]])
            if not final:
                kb.op("act", lambda e: e.copy(out=OB[:, :, sl], in_=kb.ps(bC, V3)), reads=[PB[bC]], writes=[(OB, i)])
            else:
                kb.dma("sp", gf[d_][:], gv_[:, :, sl], writes=[gf[d_]])
                kb.op("dve", lambda e: e.tensor_tensor(out=o32[d_][:], in0=kb.ps(bC, V3), in1=OB[:, :, sl], op=ALU.add),
                      reads=[PB[bC], (OB, i)], writes=[o32[d_]])
                kb.op("act", lambda e: e.activation(out=sqb[d_][:], in_=o32[d_][:], func=AF.Square), reads=[o32[d_]], writes=[sqb[d_]])
                kb.op("pe", lambda e: e.matmul(kb.ps(bD), lhsT=ones[:], rhs=sqb[d_][:].rearrange("p a b -> p (a b)"), start=True, stop=True),
                      reads=[ones, sqb[d_]], writes=[PB[bD]])
                kb.op("act", lambda e: e.activation(out=rsd[d_][:].rearrange("p a b -> p (a b)"), in_=kb.ps(bD), func=AF.Ln,
                                                    scale=1.0 / 128.0, bias=EPS), reads=[PB[bD]], writes=[rsd[d_]])
                kb.op("act", lambda e: e.activation(out=rsd[d_][:], in_=rsd[d_][:], func=AF.Exp, scale=-0.5), reads=[rsd[d_]], writes=[rsd[d_]])
                kb.op("dve", lambda e: e.tensor_tensor(out=o32[d_][:], in0=o32[d_][:], in1=rsd[d_][:], op=ALU.mult),
                      reads=[o32[d_], rsd[d_]], writes=[o32[d_]])
                kb.op("dve", lambda e: e.scalar_tensor_tensor(out=yob[d_][:], in0=o32[d_][:], scalar=gn[:, 0:1], in1=gf[d_][:],
                                                              op0=ALU.mult, op1=ALU.mult), reads=[o32[d_], gn, gf[d_]], writes=[yob[d_]])
                kb.dma("pool", yv[:, :, sl], yob[d_][:], reads=[yob[d_]])

        for step in range(NT + 1):
            if preload and step >= 1 and (step % max(1, NT // 8) == 0 or step == NT):
                preload.pop(0)()
                if step == NT:
                    while preload:
                        preload.pop(0)()
            if step < NT:
                pre(step, 0, 1)
                pre(step, 1, 1)
            if step >= 1:
                seq(step - 1, 0)
                seq(step - 1, 1)
            if step < NT:
                pre(step, 0, 2)
                pre(step, 1, 2)
                pre(step, 0, 3)
                pre(step, 1, 3)
        st["pb"] = 0

    def load_w_into(dst, wap, kblocks, c0, ncols, gt=None, dcol0=0):
        src = wap.rearrange("(k p) c -> p k c", p=128)
        WF = st["WF"]
        kper = max(1, 4096 // ncols)
        k0 = 0
        while k0 < kblocks:
            kn = min(kper, kblocks - k0)
            wfv = WF.t[:, 0:kn * ncols].rearrange("p (k c) -> p k c", k=kn)
            kb.dma("sp", wfv, src[:, k0:k0 + kn, c0:c0 + ncols], writes=[WF])
            if gt is None:
                kb.op("pool", lambda e, wfv=wfv, k0=k0, kn=kn: e.tensor_copy(out=dst.t[:, k0:k0 + kn, dcol0:dcol0 + ncols], in_=wfv),
                      reads=[WF], writes=[(dst, (k0, dcol0))])
            else:
                kb.op("pool", lambda e, wfv=wfv, k0=k0, kn=kn: e.tensor_tensor(
                    out=dst.t[:, k0:k0 + kn, dcol0:dcol0 + ncols], in0=wfv,
                    in1=gt.t[:, k0:k0 + kn, None].to_broadcast([128, kn, ncols]), op=ALU.mult),
                    reads=[WF, gt], writes=[(dst, (k0, dcol0))])
            k0 += kn

    def post_tiles():
        return {"yt": [kb.sb([128, D], F32) for _ in range(2)], "xo": [kb.sb([128, D], F32) for _ in range(2)],
                "sq": [kb.sb([128, 1], F32) for _ in range(2)], "rs": [kb.sb([128, 1], F32) for _ in range(2)],
                "junk": kb.sb([128, D], BF16), "nt": norm_tmp()}

    def post(i, ba, bb, xsrc, gp, xdst, pt, do_norm, defer=None):
        k = i % 2
        yt, xo, sq, rs, junk = pt["yt"][k], pt["xo"][k], pt["sq"][k], pt["rs"][k], pt["junk"]
        kb.dma("sp", xo[:], xsrc[i * 128:(i + 1) * 128, :], writes=[xo])
        kb.op("act", lambda e: e.copy(out=yt[:, 0:512], in_=kb.ps(ba)), reads=[PB[ba]], writes=[(yt, 0)])
        kb.op("dve", lambda e: e.tensor_copy(out=yt[:, 512:1024], in_=kb.ps(bb)), reads=[PB[bb]], writes=[(yt, 1)])
        kb.op("act", lambda e: e.activation(out=junk[:], in_=yt[:], func=AF.Square, scale=1.0 / 32.0, accum_out=sq[:]),
              reads=[yt], writes=[junk, sq])
        kb.op("act", lambda e: e.activation(out=rs[:], in_=sq[:], func=AF.Ln, bias=EPS), reads=[sq], writes=[rs])
        kb.op("act", lambda e: e.activation(out=rs[:], in_=rs[:], func=AF.Exp, scale=-0.5), reads=[rs], writes=[rs])
        kb.op("dve", lambda e: e.scalar_tensor_tensor(out=yt[:], in0=yt[:], scalar=rs[:, 0:1], in1=gp[:], op0=ALU.mult, op1=ALU.mult),
              reads=[yt, rs, gp], writes=[yt])
        kb.op("dve", lambda e: e.tensor_tensor(out=xo[:], in0=xo[:], in1=yt[:], op=ALU.add), reads=[xo, yt], writes=[xo])
        kb.dma("pool", xdst[i * 128:(i + 1) * 128, :], xo[:], reads=[xo])
        if do_norm:
            norm_transpose(xo, i, pt["nt"][k], defer)

    def phase_merge(li, xsrc, xdst):
        phase_begin(0, True)
        MG = HT
        wbr = [kb.sb([128, 4, D], BF16) for _ in range(3)]
        wo_pre = kb.sb([128, 8, D], BF16)
        preload = [(lambda c0=c0: load_w_into(wo_pre, P["w_out"][li], 8, c0, 512, None, c0)) for c0 in (0, 512)]
        ysrc = [Y_.t.rearrange("(k p) t -> p k t", p=128) for Y_ in (YH, YG, YP)]
        gsrc = GATES.t.rearrange("(b r) t -> r b t", b=3)
        yt_ = [[kb.sb([128, 4, 512], BF16) for _ in range(3)] for _ in range(2)]
        gt_ = [kb.sb([128, 3, 512], BF16) for _ in range(2)]
        mm = [[kb.sb([128, 512], BF16) for _ in range(3)] for _ in range(2)]
        ng = 0
        for tt in range(NQ):
            if preload and (tt >= 1 or NQ == 1):
                preload.pop(0)()
                if tt == NQ - 1:
                    while preload:
                        preload.pop(0)()
            ys = yt_[tt % 2]
            for b in range(3):
                kb.dma("sp", ys[b][:], ysrc[b][:, :, tt * 512:(tt + 1) * 512], writes=[ys[b]])
            for db in range(8):
                g = gt_[ng % 2]
                m_ = mm[ng % 2]
                ng += 1
                kb.dma("sp", g[:], gsrc[db * 128:(db + 1) * 128, :, tt * 512:(tt + 1) * 512], writes=[g])
                banks = [next_pb(), next_pb(), next_pb()]
                for b in range(3):
                    for k in range(4):
                        kb.op("pe", lambda e, b=b, k=k, db=db, ys=ys, banks=banks: e.matmul(
                            kb.ps(banks[b]), lhsT=wbr[b][:, k, db * 128:(db + 1) * 128], rhs=ys[b][:, k, :],
                            start=(k == 0), stop=(k == 3)), reads=[wbr[b], ys[b]], writes=[PB[banks[b]]])
                for b in range(3):
                    kb.op("dve", lambda e, b=b, g=g, m_=m_, banks=banks: e.tensor_tensor(
                        out=m_[b][:], in0=kb.ps(banks[b]), in1=g[:, b, :], op=ALU.mult), reads=[PB[banks[b]], g], writes=[m_[b]])
                kb.op("dve", lambda e, m_=m_: e.tensor_tensor(out=m_[0][:], in0=m_[0][:], in1=m_[1][:], op=ALU.add),
                      reads=[m_[0], m_[1]], writes=[m_[0]])
                kb.op("dve", lambda e, m_=m_, db=db, tt=tt: e.tensor_tensor(
                    out=MG[:, db, tt * 512:(tt + 1) * 512], in0=m_[0][:], in1=m_[2][:], op=ALU.add),
                    reads=[m_[0], m_[2]], writes=[(MG, 4 * tt), (MG, 4 * tt + 1), (MG, 4 * tt + 2), (MG, 4 * tt + 3)])
        phase_begin(0, True)
        _pad = [kb.sb([128, 4, D], BF16) for _ in range(3)]
        wo = kb.sb([128, 8, D], BF16)
        gp = kb.sb([128, D], F32)
        kb.dma("sp", gp[:], P["g_mix_post"][li], writes=[gp])
        pt = post_tiles()

        def mm(i):
            ba, bb = next_pb(), next_pb()
            gemm_tok(wo, 8, 0, 512, MG, i, ba, srckey=i)
            gemm_tok(wo, 8, 512, 512, MG, i, bb, srckey=i)
            return ba, bb

        pend = [mm(0), mm(1)] if NT > 1 else [mm(0)]
        dq_ = []
        for i in range(NT):
            if i + 2 < NT:
                pend.append(mm(i + 2))
            while dq_:
                nt_b(*dq_.pop(0))
            cur = pend.pop(0)
            post(i, cur[0], cur[1], xsrc, gp, xdst, pt, True, dq_)
        while dq_:
            nt_b(*dq_.pop(0))

    def phase_ffn_up(li):
        phase_begin(0)
        gt = kb.sb([128, 8], F32)
        kb.dma("sp", gt[:], P["g_ffn_pre"][li], writes=[gt])
        cw = kb.sb([128, 22, 3], F32)
        cb = kb.sb([128, 22], F32)
        kb.dma("sp", cw[:], P["ffn_cw"][li], writes=[cw])
        kb.dma("sp", cb[:], P["ffn_cb"][li], writes=[cb])
        raws = [kb.sb([128, L + 2], BF16) for _ in range(2)]
        for r in raws:
            kb.op("pool", lambda e, r=r: e.memset(r[:, 0:1], 0.0), writes=[(r, "h0")])
            kb.op("pool", lambda e, r=r: e.memset(r[:, L + 1:L + 2], 0.0), writes=[(r, "h1")])
        bts = [kb.sb([128, L], BF16) for _ in range(2)]
        gas = [kb.sb([128, L], BF16) for _ in range(2)]
        acc = kb.sb([128, L], F32)
        wfs = [kb.sb([128, 8, 256], F32) for _ in range(2)]
        wbs = [kb.sb([128, 8, 256], BF16) for _ in range(2)]
        wsrc = P["ffn_up"][li].rearrange("(k p) c -> p k c", p=128)

        def prep(m):
            s_ = m % 2
            kb.dma("sp", wfs[s_][:, :, 0:128], wsrc[:, :, m * 128:(m + 1) * 128], writes=[(wfs[s_], 0)])
            kb.dma("sp", wfs[s_][:, :, 128:256], wsrc[:, :, DFF + m * 128:DFF + (m + 1) * 128], writes=[(wfs[s_], 1)])
            kb.op("pool", lambda e: e.tensor_tensor(out=wbs[s_][:], in0=wfs[s_][:],
                                                    in1=gt[:, :, None].to_broadcast([128, 8, 256]), op=ALU.mult),
                  reads=[wfs[s_], gt], writes=[wbs[s_]])
            return wbs[s_]

        wnext = prep(0)
        for m in range(22):
            raw, bt, ga = raws[m % 2], bts[m % 2], gas[m % 2]
            wcur = wnext
            if m + 1 < 22:
                wnext = prep(m + 1)
            for j in range(NQ):
                b1_, b2_ = next_pb(), next_pb()
                gemm_fm(wcur, 8, 0, 128, HT, j, b1_)
                gemm_fm(wcur, 8, 128, 128, HT, j, b2_)
                kb.op("act", lambda e: e.copy(out=raw[:, 1 + j * 512:1 + (j + 1) * 512], in_=kb.ps(b1_)),
                      reads=[PB[b1_]], writes=[(raw, j)])
                kb.op("act", lambda e: e.copy(out=bt[:, j * 512:(j + 1) * 512], in_=kb.ps(b2_)),
                      reads=[PB[b2_]], writes=[(bt, j)])
            kb.op("dve", lambda e: e.tensor_scalar(out=acc[:], in0=raw[:, 0:L], scalar1=cw[:, m, 0:1], scalar2=cb[:, m:m + 1],
                                                   op0=ALU.mult, op1=ALU.add), reads=[raw, cw, cb], writes=[acc])
            kb.op("dve", lambda e: e.scalar_tensor_tensor(out=acc[:], in0=raw[:, 1:L + 1], scalar=cw[:, m, 1:2], in1=acc[:],
                                                          op0=ALU.mult, op1=ALU.add), reads=[raw, cw, acc], writes=[acc])
            kb.op("dve", lambda e: e.scalar_tensor_tensor(out=acc[:], in0=raw[:, 2:L + 2], scalar=cw[:, m, 2:3], in1=acc[:],
                                                          op0=ALU.mult, op1=ALU.add), reads=[raw, cw, acc], writes=[acc])
            kb.op("act", lambda e: e.activation(out=ga[:], in_=acc[:], func=AF.Gelu), reads=[acc], writes=[ga])
            kb.op("dve", lambda e: e.tensor_tensor(out=ga[:], in0=ga[:], in1=bt[:], op=ALU.mult), reads=[ga, bt], writes=[ga])
            kb.dma("pool", ACTfm[m * 128:(m + 1) * 128, :], ga[:], reads=[ga])

    def phase_ffn_down(li, xsrc, xdst, do_norm, preloaded=False):
        phase_begin(0, True)
        wd = kb.sb([128, 22, D], BF16)
        gp = kb.sb([128, D], F32)
        kb.dma("sp", gp[:], P["g_ffn_post"][li], writes=[gp])
        if not preloaded:
            for c0 in range(0, D, 128):
                load_w_into(wd, P["ffn_down"][li], 22, c0, 128, None, c0)
        asrc = ACTfm.t.rearrange("(k p) t -> p k t", p=128)
        at = [kb.sb([128, 22, 512], BF16) for _ in range(1)]
        pt = post_tiles()
        def load_a(tt):
            a = at[0]
            kb.dma("sp", a[:, 0:11, :], asrc[:, 0:11, tt * 512:(tt + 1) * 512], writes=[(a, 0)])
            kb.dma("sp", a[:, 11:22, :], asrc[:, 11:22, tt * 512:(tt + 1) * 512], writes=[(a, 1)])

        def mm(i):
            if i % 4 == 0:
                load_a(i // 4)
            a, ii = at[0], i % 4
            ba, bb = next_pb(), next_pb()
            for c0, bank in ((0, ba), (512, bb)):
                for k in range(22):
                    kb.op("pe", lambda e, k=k, c0=c0, bank=bank: e.matmul(
                        kb.ps(bank), lhsT=a[:, k, ii * 128:(ii + 1) * 128], rhs=wd[:, k, c0:c0 + 512],
                        start=(k == 0), stop=(k == 21)), reads=[a, wd], writes=[PB[bank]])
            return ba, bb

        pend = [mm(0), mm(1)] if NT > 1 else [mm(0)]
        dq_ = []
        for i in range(NT):
            if i + 2 < NT:
                pend.append(mm(i + 2))
            while dq_:
                nt_b(*dq_.pop(0))
            cur = pend.pop(0)
            post(i, cur[0], cur[1], xsrc, gp, xdst, pt, do_norm, dq_)
        while dq_:
            nt_b(*dq_.pop(0))

    phase_filter(0)
    phase_norm0(x_in)
    for li in range(depth):
        xs = x_in if li == 0 else XB
        phase_proj(li)
        phase_hyena(li)
        phase_gla(li)
        phase_merge(li, xs, XA)
        phase_ffn_up(li)
        lastl = (li == depth - 1)
        if not lastl:
            phase_filter(li + 1, preload_down=li)
        phase_ffn_down(li, XA, out if lastl else XB, not lastl, preloaded=not lastl)
    nc = kb.finish()
    return nc, kb


_CACHE = {}


def _in_maps(inputs, L, depth, nb):
    consts = make_consts(L)
    params = relayout_params(inputs, depth)
    x = np.asarray(inputs["x"], np.float32)
    maps = []
    for b in range(nb):
        m = {"x": np.ascontiguousarray(x[b])}
        m.update(consts)
        m.update(params)
        maps.append(m)
    return maps


def kernel(**inputs):
    x = np.asarray(inputs["x"])
    B, L, _ = x.shape
    depth = int(np.asarray(inputs["w_in"]).shape[0])
    nc, _ = build(L, depth)
    maps = _in_maps(inputs, L, depth, B)
    res = run_bass_kernel_spmd(nc, maps, core_ids=list(range(B)))
    return np.stack([np.asarray(r["out"], np.float32) for r in res.results], 0)
```

```python
import contextlib
import math
import numpy as np
import ml_dtypes
import concourse.bass as bass
import concourse.mybir as mybir
from concourse.bass_utils import run_bass_kernel_spmd

F32 = mybir.dt.float32
BF16 = mybir.dt.bfloat16
AF = mybir.ActivationFunctionType
ALU = mybir.AluOpType
NDMA_SEM = 8

D = 1024
DH = 512
DIN = 7200
DFF = 2816
EPS = 1e-6


class T:
    def __init__(self, name, ap, parent=None):
        self.name = name
        self.t = ap
        if parent is None:
            self.w = {}
            self.r = {}
            self.root = self
        else:
            self.root = parent.root

    def __getitem__(self, idx):
        return self.t[idx]

    def view(self, ap):
        return T(self.name, ap, parent=self)


class Op:
    __slots__ = ("eng", "fn", "deps", "marked", "val", "sem", "isdma")

    def __init__(self, eng, fn):
        self.eng = eng
        self.fn = fn
        self.deps = []
        self.marked = False
        self.val = 0
        self.sem = None
        self.isdma = False


class _Rec:
    def __getattr__(self, name):
        def f(*a, **k):
            self.call = (name, a, k)
        return f


class KB:
    def __init__(self, sb_bytes):
        self.nc = bass.Bass("TRN2", target_bir_lowering=False)
        self.es = contextlib.ExitStack()
        self.ops = []
        nc = self.nc
        self.handles = {"pe": nc.tensor, "act": nc.scalar, "dve": nc.vector,
                        "pool": nc.gpsimd, "sp": nc.sync}
        self.sems = {e: self.es.enter_context(nc.semaphore("s_" + e)) for e in self.handles}
        self.dq = {}
        for q in ("sp", "act", "pool"):
            self.dq[q] = {"sems": [self.es.enter_context(nc.semaphore("d_%s%d" % (q, i)))
                                   for i in range(NDMA_SEM)],
                          "n": 0, "last": [None] * NDMA_SEM, "cnt": [0] * NDMA_SEM}
        self.last = {}
        self.arena = self.es.enter_context(nc.sbuf_tensor("arena", [128, sb_bytes // 2], BF16))
        self.sb_bytes = sb_bytes
        self.top = 0
        self.nbuf = 0
        self.pbanks = [self.es.enter_context(nc.psum_tensor("pb%d" % i, [128, 512], F32))
                       for i in range(8)]

    def sb(self, shape, dt, name=None):
        esz = 4 if dt == F32 else 2
        n = int(np.prod(shape[1:])) * esz
        n = (n + 63) // 64 * 64
        off = self.top
        self.top += n
        assert self.top <= self.sb_bytes, "SBUF arena overflow %d" % self.top
        ap = self.arena[:, off // 2:(off + n) // 2]
        if dt == F32:
            ap = ap.bitcast(F32)
        ap = ap[:, 0:int(np.prod(shape[1:]))]
        if len(shape) == 3:
            ap = ap.rearrange("p (a b) -> p a b", a=shape[1])
        elif len(shape) == 4:
            ap = ap.rearrange("p (a b c) -> p a b c", a=shape[1], b=shape[2])
        if shape[0] < 128:
            ap = ap[0:shape[0]]
        self.nbuf += 1
        return T(name or "sb%d" % self.nbuf, ap)

    def ps(self, bank, shape=None, dt=F32):
        ap = self.pbanks[bank][:]
        if dt == BF16:
            ap = ap.bitcast(BF16)
        if shape is not None and len(shape) == 3:
            ap = ap[:, 0:shape[1] * shape[2]].rearrange("p (a b) -> p a b", a=shape[1])
        elif shape is not None:
            ap = ap[:, 0:shape[1]]
        if shape is not None and shape[0] < 128:
            ap = ap[0:shape[0]]
        return ap

    def psT(self, bank):
        if not hasattr(self, "_pst"):
            self._pst = [T("pbank%d" % i, self.pbanks[i][:]) for i in range(8)]
        return self._pst[bank]

    def dram(self, name, shape, dt, kind="Internal"):
        return T(name, self.nc.dram_tensor(name, list(shape), dt, kind=kind).ap())

    @staticmethod
    def _norm(lst):
        out = []
        for x in lst:
            if isinstance(x, tuple):
                out.append((x[0].root, x[1]))
            else:
                out.append((x.root, None))
        return out

    def _hazards(self, op, reads, writes):
        deps = op.deps
        for t, key in reads:
            if key is None:
                deps.extend(t.w.values())
            else:
                for k in (key, None):
                    p = t.w.get(k)
                    if p is not None:
                        deps.append(p)
        for t, key in writes:
            if key is None:
                deps.extend(t.w.values())
                for l in t.r.values():
                    deps.extend(x for x in l if x.isdma or op.isdma or x.eng != op.eng or op.eng != 'pe')
            else:
                for k in (key, None):
                    p = t.w.get(k)
                    if p is not None:
                        deps.append(p)
                    deps.extend(x for x in t.r.get(k, ()) if x.isdma or op.isdma or x.eng != op.eng or op.eng != 'pe')
        for t, key in reads:
            l = t.r.setdefault(key, [])
            if not op.isdma:
                l[:] = [o for o in l if o.eng != op.eng or o.isdma]
            l.append(op)
        for t, key in writes:
            if key is None:
                t.w = {None: op}
                t.r = {}
            else:
                t.w[key] = op
                t.r[key] = []

    def op(self, eng, fn, reads=(), writes=()):
        rec = _Rec()
        fn(rec)
        name, a, k = rec.call
        o = Op(eng, lambda h: getattr(h, name)(*a, **k))
        self._hazards(o, self._norm(reads), self._norm(writes))
        if eng == "pe":
            o.deps = [d for d in o.deps if not (d.eng == "pe" and not d.isdma)]
        self.ops.append(o)
        self.last[eng] = o
        return o

    def dma(self, q, out, in_, reads=(), writes=()):
        o = Op(q, lambda e: e.dma_start(out=out, in_=in_))
        o.isdma = True
        dq = self.dq[q]
        i = dq["n"] % NDMA_SEM
        dq["n"] += 1
        if dq["last"][i] is not None:
            o.deps.append(dq["last"][i])
        dq["last"][i] = o
        dq["cnt"][i] += 16
        o.sem = dq["sems"][i]
        o.val = dq["cnt"][i]
        self._hazards(o, self._norm(reads), self._norm(writes))
        self.ops.append(o)
        return o

    def mark(self, label):
        o = Op("sp", None)
        o.sem = label
        o.marked = "label"
        self.ops.append(o)

    def barrier(self):
        deps = list(self.last.values())
        for q in self.dq.values():
            deps.extend(x for x in q["last"] if x is not None)
        for e in self.handles:
            o = Op(e, None)
            o.deps = list(deps)
            self.ops.append(o)

    def finish(self):
        self.barrier()
        for o in self.ops:
            for d in o.deps:
                if not d.isdma:
                    d.marked = True
        cnt = {e: 0 for e in self.handles}
        for o in self.ops:
            if not o.isdma and o.marked is True:
                cnt[o.eng] += 1
                o.val = cnt[o.eng]
                o.sem = self.sems[o.eng]
        seen = {e: {} for e in self.handles}
        nwait = 0
        self.marks = []
        for o in self.ops:
            if o.marked == "label":
                self.marks.append((o.sem, self.nc.get_next_instruction_name()))
                continue
            h = self.handles[o.eng]
            sn = seen[o.eng]
            for d in o.deps:
                k = id(d.sem)
                if sn.get(k, 0) >= d.val:
                    continue
                h.wait_ge(d.sem, d.val)
                nwait += 1
                sn[k] = d.val
            if o.fn is None:
                continue
            ins = o.fn(h)
            if o.isdma:
                ins.then_inc(o.sem, 16)
            elif o.marked is True:
                ins.then_inc(o.sem, 1)
        self.stats = {"ops": len(self.ops), "waits": nwait, "marked": dict(cnt)}
        return self.nc


def make_consts(L):
    bf = ml_dtypes.bfloat16
    c = {}
    c["ident"] = np.eye(128, dtype=np.float32).astype(bf)
    c["ones"] = np.ones((128, 128), np.float32).astype(bf)
    a = np.arange(128)
    uti = (a[:, None] <= a[None, :]).astype(np.float32)
    uts = (a[:, None] < a[None, :]).astype(np.float32)
    lti = (a[:, None] >= a[None, :]).astype(np.float32)
    lts = (a[:, None] > a[None, :]).astype(np.float32)
    c["tri32"] = np.stack([uti, uts, lti, lts], 1).astype(bf)
    t = np.linspace(0.0, 1.0, L, dtype=np.float32)
    bands = 16
    w = (2.0 * np.float32(math.pi) * np.arange(L, dtype=np.float32) / np.float32(L)).astype(np.float32)
    f = np.linspace(1e-4, bands - 1, bands, dtype=np.float32)
    ang = (f[None, :] * w[:, None]).astype(np.float32)
    z = np.concatenate([t[:, None], np.cos(ang), -np.sin(ang)], -1).astype(np.float32)
    c["zT"] = np.ascontiguousarray(z.T)
    c["tcol"] = np.ascontiguousarray(-t.reshape(L // 128, 128).T)
    max_decay = math.log(1e-2) / 0.3
    min_decay = math.log(1e-2) / 1.5
    deltas = np.abs(np.linspace(min_decay, max_decay, DH, dtype=np.float32))
    c["absd"] = np.ascontiguousarray(np.broadcast_to(deltas[None, :], (128, DH))).astype(np.float32)
    N = 2 * L
    N1 = N // 64
    H = N1 // 2
    n1 = np.arange(H, dtype=np.float64)[:, None, None]
    n2 = np.arange(64, dtype=np.float64)[None, :, None]
    f1 = np.arange(H, dtype=np.float64)[None, None, :]
    al = 2 * np.pi * ((f1 + 0.5) * n1 / N1 + (f1 + 0.5) * n2 / N)
    c["tw1"] = np.ascontiguousarray(np.stack([np.cos(al), -np.sin(al)], 1)).astype(bf)
    a64 = np.arange(64, dtype=np.float64)
    be = 2 * np.pi * np.outer(a64, a64) / 64
    c["dftm"] = np.ascontiguousarray(np.stack([np.cos(be), np.sin(be), -np.sin(be)], 1)).astype(bf)
    f2 = a64[:, None, None]
    f1b = np.arange(H, dtype=np.float64)[None, :, None]
    t2 = a64[None, None, :]
    ga = 2 * np.pi * (f2 * t2 / 64 + (f1b + 0.5) * t2 / N)
    c["gtw"] = np.ascontiguousarray(np.stack([np.cos(ga), np.sin(ga), -np.sin(ga)], 1)).astype(bf)
    f1c = np.arange(H, dtype=np.float64)[:, None]
    t1 = np.arange(H, dtype=np.float64)[None, :]
    ph = 2 * np.pi * (f1c + 0.5) * t1 / N1
    c["m4"] = np.ascontiguousarray(np.stack([(2.0 / N) * np.cos(ph), -(2.0 / N) * np.sin(ph)], 1)).astype(bf)
    pos = np.arange(L)
    inv = []
    for wv in (2, 4, 8, 16):
        half = wv // 2
        cntv = (np.minimum(pos + half, L) - np.maximum(pos - half, 0)).astype(np.float32)
        inv.append(1.0 / cntv)
    c["invcnt"] = np.ascontiguousarray(
        np.broadcast_to(np.stack(inv, 0)[:, None, :], (4, 128, L))).astype(np.float32)
    return c


def relayout_params(p, depth):
    f = np.float32
    o = {}

    def pk(v, nb):
        return np.ascontiguousarray(np.asarray(v, f).reshape(nb, 128).T)

    o["g_mix_pre"] = np.stack([pk(p["norm_mix_pre"][i], 8) for i in range(depth)])
    o["g_ffn_pre"] = np.stack([pk(p["norm_ffn_pre"][i], 8) for i in range(depth)])
    o["g_mix_post"] = np.ascontiguousarray(np.broadcast_to(
        np.asarray(p["norm_mix_post"], f)[:depth, None, :], (depth, 128, D)))
    o["g_ffn_post"] = np.ascontiguousarray(np.broadcast_to(
        np.asarray(p["norm_ffn_post"], f)[:depth, None, :], (depth, 128, D)))
    o["w_in"] = np.asarray(p["w_in"], f)[:depth]
    cw = np.asarray(p["hy_conv_w"], f)[:depth]
    o["hy_cw"] = np.ascontiguousarray(cw.reshape(depth, 3, 12, 128).transpose(0, 3, 2, 1))
    o["hy_cb"] = np.stack([pk(p["hy_conv_b"][i], 12) for i in range(depth)])
    o["hy_w1"] = np.asarray(p["hy_filt_w1"], f)[:depth]
    o["hy_w2"] = np.asarray(p["hy_filt_w2"], f)[:depth]
    o["hy_w3"] = np.asarray(p["hy_filt_w3"], f)[:depth]
    vec = np.stack([np.asarray(p["hy_filt_b1"], f)[:depth], np.asarray(p["hy_filt_freq1"], f)[:depth],
                    np.asarray(p["hy_filt_b2"], f)[:depth], np.asarray(p["hy_filt_freq2"], f)[:depth]], -1)
    o["hy_vec"] = np.ascontiguousarray(vec)
    o["hy_bias"] = np.stack([pk(p["hy_bias"][i], 4) for i in range(depth)])
    w2 = np.asarray(p["gla_gate_w2"], f)[:depth]
    gb = np.asarray(p["gla_gate_b"], f)[:depth]
    w2x = np.zeros((depth, 2, 33, 512), f)
    w2x[:, 0, 0:16] = w2[:, 0]
    w2x[:, 1, 16:32] = w2[:, 1]
    w2x[:, :, 32] = gb
    o["gla_w2x"] = np.ascontiguousarray(w2x.transpose(0, 2, 1, 3))
    o["gla_norm"] = np.asarray(p["gla_norm"], f)[:depth].reshape(depth, 128, 1)
    o["pool_w"] = np.ascontiguousarray(np.asarray(p["pool_w"], f)[:depth].transpose(0, 2, 1, 3))
    o["pool_scale"] = np.stack([pk(p["pool_scale"][i], 4) for i in range(depth)])
    o["w_br"] = np.ascontiguousarray(np.stack(
        [np.asarray(p["w_br_hyena"], f)[:depth], np.asarray(p["w_br_gla"], f)[:depth],
         np.asarray(p["w_br_pool"], f)[:depth]], 1))
    o["w_out"] = np.asarray(p["w_out"], f)[:depth]
    o["ffn_up"] = np.asarray(p["ffn_w_up"], f)[:depth]
    fw = np.asarray(p["ffn_conv_w"], f)[:depth]
    o["ffn_cw"] = np.ascontiguousarray(fw.reshape(depth, 3, 22, 128).transpose(0, 3, 2, 1))
    o["ffn_cb"] = np.stack([pk(p["ffn_conv_b"][i], 22) for i in range(depth)])
    o["ffn_down"] = np.asarray(p["ffn_w_down"], f)[:depth]
    return o


def build(L, depth, dbg=()):
    NT = L // 128
    NQ = L // 512
    NFB = 2 * NT
    HH = (2 * L // 64) // 2
    kb = KB(sb_bytes=200 * 1024)

    def din(name, shape, dt=F32):
        return kb.dram(name, shape, dt, kind="ExternalInput")

    def scr(name, shape, dt=BF16):
        return kb.dram(name, shape, dt, kind=("ExternalOutput" if name in dbg else "Internal"))

    x_in = din("x", [L, D])
    C = {k: din(k, list(v.shape), BF16 if v.dtype != np.float32 else F32)
         for k, v in make_consts(L if L <= 512 else 128 * 4).items()} if False else None
    cshapes = {"ident": ([128, 128], BF16), "ones": ([128, 128], BF16), "tri32": ([128, 4, 128], BF16), "zT": ([33, L], F32), "tcol": ([128, NT], F32),
               "absd": ([128, DH], F32), "tw1": ([HH, 2, 64, HH], BF16), "dftm": ([64, 3, 64], BF16),
               "gtw": ([64, 3, HH, 64], BF16), "m4": ([HH, 2, HH], BF16),
               "invcnt": ([4, 128, L], F32)}
    C = {k: din(k, s, dt) for k, (s, dt) in cshapes.items()}
    n = depth
    pshapes = {"g_mix_pre": [n, 128, 8], "g_ffn_pre": [n, 128, 8], "g_mix_post": [n, 128, D],
               "g_ffn_post": [n, 128, D], "w_in": [n, D, DIN], "hy_cw": [n, 128, 12, 3],
               "hy_cb": [n, 128, 12], "hy_w1": [n, 33, 64], "hy_w2": [n, 64, 64], "hy_w3": [n, 64, 1024],
               "hy_vec": [n, 64, 4], "hy_bias": [n, 128, 4], "gla_w2x": [n, 33, 2, 512],
               "gla_norm": [n, 128, 1], "pool_w": [n, 128, 4, 128], "pool_scale": [n, 128, 4],
               "w_br": [n, 3, DH, D], "w_out": [n, D, D], "ffn_up": [n, D, 2 * DFF],
               "ffn_cw": [n, 128, 22, 3], "ffn_cb": [n, 128, 22], "ffn_down": [n, DFF, D]}
    P = {k: din(k, s) for k, s in pshapes.items()}
    out = kb.dram("out", [L, D], F32, kind="ExternalOutput")

    XA = scr("XA", [L, D], F32)
    XB = scr("XB", [L, D], F32)
    X0fm = scr("X0fm", [DH, L])
    ZVfm = scr("ZVfm", [DH, L])
    ZVT = scr("ZVT", [L, DH])
    HSD = scr("HSD", [2, L, DH])
    A1Z = scr("A1Z", [2, HH, 64, DH])
    A1K = scr("A1K", [2, 2, HH, 64, DH])
    KS = scr("KS", [2, HH, 64, DH])
    B1 = scr("B1", [2, 64, HH, DH])
    Qfm = scr("Qfm", [DH, L])
    Kfm = scr("Kfm", [DH, L])
    Gfm = scr("Gfm", [DH, L])
    Ktok = scr("Ktok", [L, DH])
    Vtok = scr("Vtok", [L, DH])
    YH = scr("YH", [DH, L])
    YG = scr("YG", [DH, L])
    YP = scr("YP", [DH, L])
    GATES = scr("GATES", [3 * D, L])
    ACTfm = scr("ACTfm", [DFF, L])

    HT = kb.sb([128, 8, L], BF16, "HT")
    ident = kb.sb([128, 128], BF16, "ident")
    ones = kb.sb([128, 128], BF16, "ones")
    tri32 = kb.sb([128, 4, 128], BF16, "tri32")
    LRH = kb.sb([33, L], BF16, "LRH")
    LRL = kb.sb([33, L], BF16, "LRL")
    kb.dma("sp", ident[:], C["ident"][:, :], writes=[ident])
    kb.dma("sp", ones[:], C["ones"][:, :], writes=[ones])
    kb.dma("sp", tri32[:], C["tri32"][:, :, :], writes=[tri32])
    kb.op("dve", lambda e: e.memset(LRH[32:33, :], 1.0), writes=[LRH])
    kb.op("dve", lambda e: e.memset(LRL[32:33, :], 0.0), writes=[LRL])
    base_top = kb.top
    PB = [kb.psT(i) for i in range(8)]
    st = {"wb": 0, "pb": 0}

    def dump(name, t_, shape, dt=F32):
        if name in dbg:
            d_ = kb.dram(name, shape, dt, kind="ExternalOutput")
            kb.dma("sp", d_.t, t_.t, reads=[t_])

    def phase_begin(nwb=0, wf=False, label=None):
        kb.barrier()
        import inspect
        kb.mark(label or inspect.stack()[1].function + ":%d" % inspect.stack()[1].lineno)
        kb.top = base_top
        if wf or nwb:
            st["WF"] = kb.sb([128, 4096], F32, "WF")
        st["WB"] = [kb.sb([128, 6144], BF16, "WB%d" % i) for i in range(nwb)]

    def load_w(wap, kblocks, c0, ncols, gt=None):
        i = st["wb"] % len(st["WB"])
        st["wb"] += 1
        wb = st["WB"][i]
        WF = st["WF"]
        wbv = wb.view(wb.t[:, 0:kblocks * ncols].rearrange("p (k c) -> p k c", k=kblocks))
        kper = max(1, 4096 // ncols)
        src = wap.rearrange("(k p) c -> p k c", p=128)
        k0 = 0
        while k0 < kblocks:
            kn = min(kper, kblocks - k0)
            wfv = WF.t[:, 0:kn * ncols].rearrange("p (k c) -> p k c", k=kn)
            kb.dma("sp", wfv, src[:, k0:k0 + kn, c0:c0 + ncols], writes=[WF])
            if gt is None:
                kb.op("pool", lambda e, wfv=wfv, k0=k0, kn=kn: e.tensor_copy(out=wbv.t[:, k0:k0 + kn, :], in_=wfv),
                      reads=[WF], writes=[(wb, k0)])
            else:
                kb.op("pool", lambda e, wfv=wfv, k0=k0, kn=kn: e.tensor_tensor(
                    out=wbv.t[:, k0:k0 + kn, :], in0=wfv,
                    in1=gt.t[:, k0:k0 + kn, None].to_broadcast([128, kn, ncols]), op=ALU.mult),
                    reads=[WF, gt], writes=[(wb, k0)])
            k0 += kn
        return wbv

    def next_pb(nb=6):
        b = st["pb"] % nb
        st["pb"] += 1
        return b

    def gemm_fm(wbv, kblocks, mcol, mw, src, j, bank):
        for k in range(kblocks):
            kb.op("pe", lambda e, k=k: e.matmul(kb.ps(bank)[0:mw, :], lhsT=wbv.t[:, k, mcol:mcol + mw],
                                                rhs=src.t[:, k, j * 512:(j + 1) * 512],
                                                start=(k == 0), stop=(k == kblocks - 1)),
                  reads=[wbv, src], writes=[PB[bank]])

    def gemm_tok(wbv, kblocks, c0, ncols, src, i, bank, srckey=None):
        for k in range(kblocks):
            kb.op("pe", lambda e, k=k: e.matmul(kb.ps(bank)[:, 0:ncols], lhsT=src.t[:, k, i * 128:(i + 1) * 128],
                                                rhs=wbv.t[:, k, c0:c0 + ncols],
                                                start=(k == 0), stop=(k == kblocks - 1)),
                  reads=[wbv, (src, srckey) if srckey is not None else src], writes=[PB[bank]])

    def nt_b(hn, i):
        for k in range(8):
            kb.op("pe", lambda e, k=k: e.transpose(out=kb.ps(7, [128, 8, 128], BF16)[:, k, :],
                                                   in_=hn[:, k * 128:(k + 1) * 128], identity=ident[:]),
                  reads=[hn, ident], writes=[PB[7]])
        kb.op("act", lambda e: e.copy(out=HT[:, :, i * 128:(i + 1) * 128], in_=kb.ps(7, [128, 8, 128], BF16)),
              reads=[PB[7]], writes=[(HT, i)])

    def norm_transpose(xt, i, tmp, defer=None):
        junk, sq, rs, hn = tmp["junk"], tmp["sq"], tmp["rs"], tmp["hn"]
        kb.op("act", lambda e: e.activation(out=junk[:], in_=xt[:], func=AF.Square, scale=1.0 / 32.0,
                                            accum_out=sq[:]), reads=[xt], writes=[junk, sq])
        kb.op("act", lambda e: e.activation(out=rs[:], in_=sq[:], func=AF.Ln, bias=EPS), reads=[sq], writes=[rs])
        kb.op("act", lambda e: e.activation(out=rs[:], in_=rs[:], func=AF.Exp, scale=-0.5), reads=[rs], writes=[rs])
        kb.op("dve", lambda e: e.tensor_scalar(out=hn[:], in0=xt[:], scalar1=rs[:], scalar2=None, op0=ALU.mult),
              reads=[xt, rs], writes=[hn])
        if defer is not None:
            defer.append((hn, i))
        else:
            nt_b(hn, i)

    def norm_tmp(depth=2):
        junk = kb.sb([128, D], BF16)
        return [{"junk": junk, "sq": kb.sb([128, 1], F32), "rs": kb.sb([128, 1], F32),
                 "hn": kb.sb([128, D], BF16)} for _ in range(depth)]

    TWO_PI = 2.0 * math.pi

    def phase_filter(li, preload_down=None):
        phase_begin()
        HS = HT.view(HT.t[:, :, :].rearrange("p a b -> p (a b)")[:, 0:NT * 1024].rearrange(
            "p (n c) -> p n c", n=NT))
        w1 = kb.sb([33, 64], F32)
        w2 = kb.sb([64, 64], F32)
        w3 = kb.sb([64, 1024], F32)
        vec = kb.sb([64, 4], F32)
        pv = kb.sb([64, 2], F32)
        absd = kb.sb([128, DH], F32)
        tcol = kb.sb([128, NT], F32)
        H1 = kb.sb([64, L], F32)
        H2 = kb.sb([64, L], F32)
        kb.dma("sp", w1[:], P["hy_w1"][li], writes=[w1])
        kb.dma("sp", w2[:], P["hy_w2"][li], writes=[w2])
        kb.dma("sp", w3[:], P["hy_w3"][li], writes=[w3])
        kb.dma("sp", vec[:], P["hy_vec"][li], writes=[vec])
        kb.dma("sp", absd[:], C["absd"][:, :], writes=[absd])
        kb.dma("sp", tcol[:], C["tcol"][:, :], writes=[tcol])
        kb.op("dve", lambda e: e.tensor_tensor(out=pv[:, 0:1], in0=vec[:, 0:1], in1=vec[:, 1:2], op=ALU.mult),
              reads=[vec], writes=[pv])
        kb.op("dve", lambda e: e.tensor_tensor(out=pv[:, 1:2], in0=vec[:, 2:3], in1=vec[:, 3:4], op=ALU.mult),
              reads=[vec], writes=[pv])
        zt = [kb.sb([33, 512], F32) for _ in range(2)]
        arg = [kb.sb([64, 512], F32) for _ in range(2)]

        def sin_layer(wt, kdim, srcfn, dst, frcol, pvcol, j, bank):
            a = arg[j % 2]
            src, srcT = srcfn(j)
            kb.op("pe", lambda e: e.matmul(kb.ps(bank)[0:64, :], lhsT=wt[0:kdim, :], rhs=src,
                                           start=True, stop=True), reads=[wt, srcT], writes=[PB[bank]])
            kb.op("dve", lambda e: e.tensor_scalar(out=a[:], in0=kb.ps(bank)[0:64, :], scalar1=vec[:, frcol:frcol + 1],
                                                   scalar2=pv[:, pvcol:pvcol + 1], op0=ALU.mult, op1=ALU.add),
                  reads=[PB[bank], vec, pv], writes=[a])
            kb.op("dve", lambda e: e.tensor_scalar(out=ni[:], in0=a[:], scalar1=1.0 / TWO_PI, scalar2=None, op0=ALU.mult),
                  reads=[a], writes=[ni])
            kb.op("dve", lambda e: e.scalar_tensor_tensor(out=a[:], in0=ni[:], scalar=-TWO_PI, in1=a[:], op0=ALU.mult, op1=ALU.add),
                  reads=[ni, a], writes=[a])
            kb.op("dve", lambda e: e.tensor_scalar(out=m1[:], in0=a[:], scalar1=math.pi, scalar2=-TWO_PI, op0=ALU.is_gt, op1=ALU.mult),
                  reads=[a], writes=[m1])
            kb.op("dve", lambda e: e.tensor_scalar(out=m2[:], in0=a[:], scalar1=-math.pi, scalar2=TWO_PI, op0=ALU.is_lt, op1=ALU.mult),
                  reads=[a], writes=[m2])
            kb.op("dve", lambda e: e.tensor_tensor(out=a[:], in0=a[:], in1=m1[:], op=ALU.add), reads=[a, m1], writes=[a])
            kb.op("dve", lambda e: e.tensor_tensor(out=a[:], in0=a[:], in1=m2[:], op=ALU.add), reads=[a, m2], writes=[a])
            kb.op("dve", lambda e: e.tensor_scalar(out=a[:], in0=a[:], scalar1=math.pi, scalar2=-math.pi, op0=ALU.min, op1=ALU.max),
                  reads=[a], writes=[a])
            kb.op("act", lambda e: e.activation(out=dst[:, j * 512:(j + 1) * 512], in_=a[:], func=AF.Sin), reads=[a], writes=[(dst, j)])

        negpi = kb.sb([128, 1], F32)
        ni = kb.sb([64, 512], F32)
        ni = ni.view(ni.t.bitcast(mybir.dt.int32))
        m1 = kb.sb([64, 512], F32)
        m2 = kb.sb([64, 512], F32)
        kb.op("dve", lambda e: e.memset(negpi[:], -math.pi), writes=[negpi])
        for j in range(NQ):
            z = zt[j % 2]
            kb.dma("sp", z[:], C["zT"][:, j * 512:(j + 1) * 512], writes=[z])
            sin_layer(w1, 33, lambda j, z=z: (z[:], z), H1, 1, 0, j, next_pb())
        for j in range(NQ):
            sin_layer(w2, 64, lambda j: (H1[:, j * 512:(j + 1) * 512], H1), H2, 3, 1, j, next_pb())
        hsds = [kb.sb([128, 2, DH], BF16) for _ in range(2)]
        dec = [kb.sb([128, DH], F32) for _ in range(2)]
        t1 = [kb.sb([128, DH], F32) for _ in range(2)]
        t2 = [kb.sb([128, DH], F32) for _ in range(2)]
        for i in range(NT):
            dc, a1, a2 = dec[i % 2], t1[i % 2], t2[i % 2]
            b0, b1 = next_pb(), next_pb()
            for half, bank in ((0, b0), (1, b1)):
                kb.op("pe", lambda e, half=half, bank=bank: e.matmul(
                    kb.ps(bank), lhsT=H2[:, i * 128:(i + 1) * 128], rhs=w3[:, half * 512:(half + 1) * 512],
                    start=True, stop=True), reads=[H2, w3], writes=[PB[bank]])
            kb.op("act", lambda e, dc=dc: e.activation(out=dc[:], in_=absd[:], func=AF.Exp, scale=tcol[:, i:i + 1]),
                  reads=[absd, tcol], writes=[dc])
            kb.op("dve", lambda e, dc=dc, a1=a1, b0=b0: e.tensor_tensor(out=a1[:], in0=kb.ps(b0), in1=dc[:], op=ALU.mult),
                  reads=[PB[b0], dc], writes=[a1])
            kb.op("dve", lambda e, dc=dc, a2=a2, b1=b1: e.tensor_tensor(out=a2[:], in0=kb.ps(b1), in1=dc[:], op=ALU.mult),
                  reads=[PB[b1], dc], writes=[a2])
            if i == 0:
                kb.op("dve", lambda e, a2=a2: e.memset(a2[0:1, :], 0.0), reads=[a2], writes=[a2])
            hsd = hsds[i % 2]
            kb.op("pool", lambda e, a1=a1, a2=a2: e.tensor_tensor(out=hsd[:, 0, :], in0=a1[:], in1=a2[:], op=ALU.add),
                  reads=[a1, a2], writes=[(hsd, 0)])
            kb.op("pool", lambda e, a1=a1, a2=a2: e.tensor_tensor(out=hsd[:, 1, :], in0=a1[:], in1=a2[:], op=ALU.subtract),
                  reads=[a1, a2], writes=[(hsd, 1)])
            kb.dma("pool", HSD.t.rearrange("s n c -> n s c")[i * 128:(i + 1) * 128, :, :], hsd[:], reads=[hsd])
        phase_begin(label="filter_s1")
        tw1 = load_tw1()
        s1b = s1_bufs()
        fft_s1(HSD[0], A1K.t[0], tw1, s1b)
        fft_s1(HSD[1], A1K.t[1], tw1, s1b)
        phase_begin(0, True, label="filter_s2")
        preload = []
        if preload_down is not None:
            wd_pre = kb.sb([128, 22, D], BF16)
            preload = [(lambda c0=c0: load_w_into(wd_pre, P["ffn_down"][preload_down], 22, c0, 128, None, c0))
                       for c0 in range(0, D, 128)]
        dftm = kb.sb([64, 3, 64], BF16)
        kb.dma("sp", dftm[:], C["dftm"][:, :, :], writes=[dftm])
        FC = min(4, HH)
        at_ = [[kb.sb([64, FC, DH], BF16) for _ in range(4)] for _ in range(2)]
        ko = [[kb.sb([64, FC, DH], BF16) for _ in range(2)] for _ in range(2)]
        nch = HH // FC
        for c_ in range(nch):
            if preload and (c_ % max(1, nch // 8) == 0 or c_ == nch - 1):
                preload.pop(0)()
                if c_ == nch - 1:
                    while preload:
                        preload.pop(0)()
            f0 = c_ * FC
            A = at_[c_ % 2]
            for q_, (sg_, ri_) in enumerate(((0, 0), (0, 1), (1, 0), (1, 1))):
                kb.dma("sp", A[q_][:], A1K.t[sg_, ri_].rearrange("f n c -> n f c")[:, f0:f0 + FC, :], writes=[A[q_]])
            kos = ko[c_ % 2]
            for fl in range(FC):
                br, bi = next_pb(), next_pb()
                kb.op("pe", lambda e: e.matmul(kb.ps(br)[0:64, :], lhsT=dftm[:, 0, :], rhs=A[0][:, fl, :], start=True, stop=False),
                      reads=[dftm, A[0]], writes=[PB[br]])
                kb.op("pe", lambda e: e.matmul(kb.ps(br)[0:64, :], lhsT=dftm[:, 1, :], rhs=A[1][:, fl, :], start=False, stop=True),
                      reads=[dftm, A[1]], writes=[PB[br]])
                kb.op("pe", lambda e: e.matmul(kb.ps(bi)[0:64, :], lhsT=dftm[:, 0, :], rhs=A[3][:, fl, :], start=True, stop=False),
                      reads=[dftm, A[3]], writes=[PB[bi]])
                kb.op("pe", lambda e: e.matmul(kb.ps(bi)[0:64, :], lhsT=dftm[:, 2, :], rhs=A[2][:, fl, :], start=False, stop=True),
                      reads=[dftm, A[2]], writes=[PB[bi]])
                kb.op("act", lambda e: e.copy(out=kos[0][:, fl, :], in_=kb.ps(br)[0:64, :]), reads=[PB[br]], writes=[(kos[0], fl)])
                kb.op("dve", lambda e: e.tensor_copy(out=kos[1][:, fl, :], in_=kb.ps(bi)[0:64, :]), reads=[PB[bi]], writes=[(kos[1], fl)])
            for ri_ in range(2):
                kb.dma("pool", KS.t[ri_, f0:f0 + FC].rearrange("f k c -> k f c"), kos[ri_][:], reads=[kos[ri_]])

    def load_tw1():
        tw1 = kb.sb([HH, 2, 64, HH], BF16)
        kb.dma("sp", tw1[:], C["tw1"][:, :, :, :], writes=[tw1])
        return tw1

    def s1_bufs():
        return ([kb.sb([HH, 8, DH], BF16) for _ in range(2)],
                [[kb.sb([HH, 8, DH], BF16) for _ in range(2)] for _ in range(2)])

    def fft_s1(src, dst, tw1, s1b):
        xv = src.rearrange("(a b) c -> a b c", b=64)
        if not hasattr(fft_s1, "bufs"):
            pass
        xt, ot = s1b
        ne = 0
        for g in range(8):
            x_ = xt[g % 2]
            kb.dma("sp", x_[:], xv[:, g * 8:(g + 1) * 8, :], writes=[x_])
            for ri_ in range(2):
                o_ = ot[g % 2][ri_]
                for nl in range(8):
                    bank = next_pb()
                    kb.op("pe", lambda e: e.matmul(kb.ps(bank)[0:HH, :], lhsT=tw1[:, ri_, g * 8 + nl, :], rhs=x_[:, nl, :],
                                                   start=True, stop=True), reads=[tw1, x_], writes=[PB[bank]])
                    if ne % 2 == 0:
                        kb.op("act", lambda e: e.copy(out=o_[:, nl, :], in_=kb.ps(bank)[0:HH, :]), reads=[PB[bank]], writes=[(o_, nl)])
                    else:
                        kb.op("dve", lambda e: e.tensor_copy(out=o_[:, nl, :], in_=kb.ps(bank)[0:HH, :]), reads=[PB[bank]], writes=[(o_, nl)])
                    ne += 1
                kb.dma("pool", dst[ri_, :, g * 8:(g + 1) * 8, :], o_[:], reads=[o_])

    def phase_norm0(xsrc):
        phase_begin()
        tmps = norm_tmp()
        xts = [kb.sb([128, D], F32) for _ in range(2)]
        for i in range(NT):
            xt = xts[i % 2]
            kb.dma("sp", xt[:], xsrc[i * 128:(i + 1) * 128, :], writes=[xt])
            norm_transpose(xt, i, tmps[i % 2])

    def phase_proj(li):
        phase_begin(0)
        gt = kb.sb([128, 8], F32)
        kb.dma("sp", gt[:], P["g_mix_pre"][li], writes=[gt])
        W = P["w_in"][li]
        ev = [kb.sb([128, 512], BF16) for _ in range(4)]
        evc = [0]

        def next_ev():
            evc[0] += 1
            return ev[evc[0] % 4]

        cw = kb.sb([128, 12, 3], F32)
        cb = kb.sb([128, 12], F32)
        kb.dma("sp", cw[:], P["hy_cw"][li], writes=[cw])
        kb.dma("sp", cb[:], P["hy_cb"][li], writes=[cb])
        raws = [kb.sb([128, L + 2], BF16) for _ in range(2)]
        for r in raws:
            kb.op("pool", lambda e, r=r: e.memset(r[:, 0:1], 0.0), writes=[(r, "h0")])
            kb.op("pool", lambda e, r=r: e.memset(r[:, L + 1:L + 2], 0.0), writes=[(r, "h1")])
        acc = kb.sb([128, L], F32)
        x1c = kb.sb([128, L], BF16)
        oc = [kb.sb([128, L], BF16) for _ in range(2)]
        tz = [kb.sb([128, 8, 128], BF16) for _ in range(2)]
        wfs = [kb.sb([128, 8, 128], F32) for _ in range(2)]
        wbs = [kb.sb([128, 8, 128], BF16) for _ in range(2)]
        wsrc = W.rearrange("(k p) c -> p k c", p=128)
        order = [(b, part) for b in range(4) for part in range(3)]

        def prep(n_):
            b_, part_ = order[n_]
            blk_ = part_ * 4 + b_
            s_ = n_ % 2
            kb.dma("sp", wfs[s_][:], wsrc[:, :, blk_ * 128:(blk_ + 1) * 128], writes=[wfs[s_]])
            kb.op("pool", lambda e: e.tensor_tensor(out=wbs[s_][:], in0=wfs[s_][:],
                                                    in1=gt[:, :, None].to_broadcast([128, 8, 128]), op=ALU.mult),
                  reads=[wfs[s_], gt], writes=[wbs[s_]])
            return wbs[s_]

        zvt_v = ZVT.t.rearrange("(i p) c -> p i c", p=128)
        TB = min(8, NT)

        def transposes(dst, b):
            for i0 in range(0, NT, TB):
                tzt = tz[(i0 // TB) % 2]
                for ii in range(TB):
                    kb.op("pe", lambda e, ii=ii: e.transpose(out=kb.ps(7, [128, 8, 128], BF16)[:, ii, :],
                                                             in_=dst[:, (i0 + ii) * 128:(i0 + ii + 1) * 128], identity=ident[:]),
                          reads=[dst, ident], writes=[PB[7]])
                kb.op("act", lambda e: e.copy(out=tzt[:, 0:TB, :], in_=kb.ps(7, [128, 8, 128], BF16)[:, 0:TB, :]),
                      reads=[PB[7]], writes=[tzt])
                kb.dma("pool", zvt_v[:, i0:i0 + TB, b * 128:(b + 1) * 128], tzt[:, 0:TB, :], reads=[tzt])

        wnext = prep(0)
        pending = None
        for n_, (b, part) in enumerate(order):
            blk = part * 4 + b
            raw = raws[n_ % 2]
            wbv = wnext
            if n_ + 1 < len(order):
                wnext = prep(n_ + 1)
            for j in range(NQ):
                bank = next_pb()
                gemm_fm(wbv, 8, 0, 128, HT, j, bank)
                kb.op("act", lambda e: e.copy(out=raw[:, 1 + j * 512:1 + (j + 1) * 512], in_=kb.ps(bank)),
                      reads=[PB[bank]], writes=[(raw, j)])
            if pending is not None:
                transposes(*pending)
                pending = None
            dst = x1c if part == 1 else oc[0 if part == 0 else 1]
            kb.op("dve", lambda e: e.tensor_scalar(out=acc[:], in0=raw[:, 0:L], scalar1=cw[:, blk, 0:1], scalar2=cb[:, blk:blk + 1],
                                                   op0=ALU.mult, op1=ALU.add), reads=[raw, cw, cb], writes=[acc])
            kb.op("dve", lambda e: e.scalar_tensor_tensor(out=acc[:], in0=raw[:, 1:L + 1], scalar=cw[:, blk, 1:2], in1=acc[:],
                                                          op0=ALU.mult, op1=ALU.add), reads=[raw, cw, acc], writes=[acc])
            kb.op("dve", lambda e: e.scalar_tensor_tensor(out=dst[:], in0=raw[:, 2:L + 2], scalar=cw[:, blk, 2:3], in1=acc[:],
                                                          op0=ALU.mult, op1=ALU.add), reads=[raw, cw, acc], writes=[dst])
            if part == 0:
                kb.dma("pool", X0fm[b * 128:(b + 1) * 128, :], dst[:], reads=[dst])
            elif part == 2:
                kb.op("dve", lambda e: e.tensor_tensor(out=dst[:], in0=dst[:], in1=x1c[:], op=ALU.mult),
                      reads=[dst, x1c], writes=[dst])
                kb.dma("pool", ZVfm[b * 128:(b + 1) * 128, :], dst[:], reads=[dst])
                pending = (dst, b)
        transposes(*pending)

        phase_begin(2)
        gt = kb.sb([128, 8], F32)
        kb.dma("sp", gt[:], P["g_mix_pre"][li], writes=[gt])
        ev = [kb.sb([128, 512], BF16) for _ in range(4)]
        def fm_group(col0, ncols, dest, func, scale=1.0):
            for c0 in range(0, ncols, 512):
                nc_ = min(512, ncols - c0)
                wbv = load_w(W, 8, col0 + c0, nc_, gt)
                for m in range(nc_ // 128):
                    for j in range(NQ):
                        bank = next_pb()
                        gemm_fm(wbv, 8, m * 128, 128, HT, j, bank)
                        o = next_ev()
                        kb.op("act", lambda e, o=o, bank=bank: e.activation(out=o[:], in_=kb.ps(bank), func=func, scale=scale),
                              reads=[PB[bank]], writes=[o])
                        r0 = c0 + m * 128
                        kb.dma("pool", dest[r0:r0 + 128, j * 512:(j + 1) * 512], o[:], reads=[o])

        def tok_group(col0, dest):
            wbv = load_w(W, 8, col0, 512, gt)
            for i in range(NT):
                bank = next_pb()
                gemm_tok(wbv, 8, 0, 512, HT, i, bank)
                o = next_ev()
                if i % 2 == 0:
                    kb.op("dve", lambda e, o=o, bank=bank: e.tensor_copy(out=o[:], in_=kb.ps(bank)), reads=[PB[bank]], writes=[o])
                else:
                    kb.op("act", lambda e, o=o, bank=bank: e.copy(out=o[:], in_=kb.ps(bank)), reads=[PB[bank]], writes=[o])
                kb.dma("pool", dest[i * 128:(i + 1) * 128, :], o[:], reads=[o])

        fm_group(1536, 512, Qfm, AF.Copy, 128.0 ** -0.5)
        fm_group(2048, 512, Kfm, AF.Copy)
        tok_group(2048, Ktok)
        tok_group(2560, Vtok)
        fm_group(3072, 512, Gfm, AF.Silu)
        wbv = load_w(W, 8, 3584, 32, gt)
        for j in range(NQ):
            bank = next_pb()
            gemm_fm(wbv, 8, 0, 32, HT, j, bank)
            kb.op("act", lambda e, j=j, bank=bank: e.copy(out=LRH[0:32, j * 512:(j + 1) * 512], in_=kb.ps(bank)[0:32, :]),
                  reads=[PB[bank]], writes=[(LRH, j)])
            kb.op("dve", lambda e, j=j, bank=bank: e.tensor_tensor(out=LRL[0:32, j * 512:(j + 1) * 512], in0=kb.ps(bank)[0:32, :],
                                                                   in1=LRH[0:32, j * 512:(j + 1) * 512], op=ALU.subtract),
                  reads=[PB[bank], (LRH, j)], writes=[(LRL, j)])
        fm_group(4128, 3 * D, GATES, AF.Sigmoid)

        phase_begin(1)
        gt = kb.sb([128, 8], F32)
        kb.dma("sp", gt[:], P["g_mix_pre"][li], writes=[gt])
        ev = [kb.sb([128, 512], BF16) for _ in range(4)]
        PW = 16
        ua = kb.sb([128, L + 2 * PW], F32)
        ub = kb.sb([128, L + 2 * PW], F32)
        uc = kb.sb([128, L + 2 * PW], F32)
        icn = kb.sb([128, L], F32)
        pwt = kb.sb([128, 4, 128], F32)
        pwb = kb.sb([128, 4, 128], BF16)
        psc = kb.sb([128, 4], F32)
        dbf = kb.sb([128, L], BF16)
        kb.dma("sp", pwt[:], P["pool_w"][li], writes=[pwt])
        kb.dma("sp", psc[:], P["pool_scale"][li], writes=[psc])
        kb.op("pool", lambda e: e.tensor_copy(out=pwb[:], in_=pwt[:]), reads=[pwt], writes=[pwb])
        for t_ in (ua, ub, uc):
            kb.op("pool", lambda e, t_=t_: e.memset(t_[:], 0.0), writes=[t_])
        for gi, wv in enumerate((2, 4, 8, 16)):
            wbv = load_w(W, 8, 3616 + gi * 128, 128, gt)
            kb.dma("sp", icn[:], C["invcnt"][gi], writes=[icn])
            for j in range(NQ):
                bank = next_pb()
                gemm_fm(wbv, 8, 0, 128, HT, j, bank)
                kb.op("act", lambda e, j=j, bank=bank: e.copy(out=ua[:, PW + j * 512:PW + (j + 1) * 512], in_=kb.ps(bank)),
                      reads=[PB[bank]], writes=[ua])
            src, dsts = ua, [ub, uc]
            lo, hi = -14, L + 14
            kb.op("dve", lambda e, lo=lo, hi=hi: e.tensor_tensor(
                out=ub[:, PW + lo:PW + hi], in0=ua[:, PW + lo - 1:PW + hi - 1], in1=ua[:, PW + lo:PW + hi], op=ALU.add),
                reads=[ua], writes=[ub])
            cur, oth = ub, uc
            sh = 1
            rng = [(-12, L + 12), (-8, L + 8), (0, L)]
            for si in range(int(math.log2(wv)) - 1):
                lo, hi = rng[si]
                kb.op("dve", lambda e, lo=lo, hi=hi, cur=cur, oth=oth, sh=sh: e.tensor_tensor(
                    out=oth[:, PW + lo:PW + hi], in0=cur[:, PW + lo - sh:PW + hi - sh],
                    in1=cur[:, PW + lo + sh:PW + hi + sh], op=ALU.add), reads=[cur], writes=[oth])
                cur, oth = oth, cur
                sh *= 2
            kb.op("dve", lambda e, cur=cur, oth=oth: e.tensor_tensor(out=oth[:, PW:PW + L], in0=cur[:, PW:PW + L], in1=icn[:], op=ALU.mult),
                  reads=[cur, icn], writes=[oth])
            kb.op("dve", lambda e, oth=oth: e.tensor_tensor(out=dbf[:], in0=oth[:, PW:PW + L], in1=ua[:, PW:PW + L], op=ALU.subtract),
                  reads=[oth, ua], writes=[dbf])
            for t_ in (ub, uc):
                kb.op("pool", lambda e, t_=t_: e.memset(t_[:, 0:PW], 0.0), reads=[t_], writes=[t_])
                kb.op("pool", lambda e, t_=t_: e.memset(t_[:, PW + L:PW + L + PW], 0.0), reads=[t_], writes=[t_])
            for j in range(NQ):
                bank = next_pb()
                kb.op("pe", lambda e, j=j, bank=bank, gi=gi: e.matmul(kb.ps(bank), lhsT=pwb[:, gi, :], rhs=dbf[:, j * 512:(j + 1) * 512],
                                                               start=True, stop=True), reads=[pwb, dbf], writes=[PB[bank]])
                o = next_ev()
                kb.op("dve", lambda e, o=o, bank=bank, gi=gi: e.tensor_scalar(out=o[:], in0=kb.ps(bank), scalar1=psc[:, gi:gi + 1],
                                                                       scalar2=None, op0=ALU.mult), reads=[PB[bank], psc], writes=[o])
                kb.dma("pool", YP[gi * 128:(gi + 1) * 128, j * 512:(j + 1) * 512], o[:], reads=[o])

    def phase_hyena(li):
        phase_begin(label="hy_s1")
        tw1 = load_tw1()
        fft_s1(ZVT.t, A1Z.t, tw1, s1_bufs())
        phase_begin(label="hy_s2")
        dftm = kb.sb([64, 3, 64], BF16)
        gtw = kb.sb([64, 3, HH, 64], BF16)
        kb.dma("sp", dftm[:], C["dftm"][:, :, :], writes=[dftm])
        kb.dma("sp", gtw[:], C["gtw"][:, :, :, :], writes=[gtw])
        FC = min(4, HH)
        at_ = [[kb.sb([64, FC, DH], BF16) for _ in range(2)] for _ in range(2)]
        kt_ = [[kb.sb([64, FC, DH], BF16) for _ in range(2)] for _ in range(2)]
        bo = [[kb.sb([64, FC, DH], BF16) for _ in range(2)] for _ in range(2)]
        m = [[kb.sb([64, DH], F32) for _ in range(4)] for _ in range(2)]
        yy = [[kb.sb([64, DH], BF16) for _ in range(2)] for _ in range(2)]
        def s2_prod(f1_):
            c_, fl = divmod(f1_, FC)
            f0 = c_ * FC
            A, K_ = at_[c_ % 2], kt_[c_ % 2]
            if fl == 0:
                for ri_ in range(2):
                    kb.dma("sp", A[ri_][:], A1Z.t[ri_].rearrange("f n c -> n f c")[:, f0:f0 + FC, :], writes=[A[ri_]])
                    kb.dma("sp", K_[ri_][:], KS.t[ri_, f0:f0 + FC].rearrange("f k c -> k f c"), writes=[K_[ri_]])
            mm_, y_ = m[f1_ % 2], yy[f1_ % 2]
            zr, zi = f1_ % 2, 2 + f1_ % 2
            kb.op("pe", lambda e: e.matmul(kb.ps(zr)[0:64, :], lhsT=dftm[:, 0, :], rhs=A[0][:, fl, :], start=True, stop=False),
                  reads=[dftm, A[0]], writes=[PB[zr]])
            kb.op("pe", lambda e: e.matmul(kb.ps(zr)[0:64, :], lhsT=dftm[:, 1, :], rhs=A[1][:, fl, :], start=False, stop=True),
                  reads=[dftm, A[1]], writes=[PB[zr]])
            kb.op("pe", lambda e: e.matmul(kb.ps(zi)[0:64, :], lhsT=dftm[:, 0, :], rhs=A[1][:, fl, :], start=True, stop=False),
                  reads=[dftm, A[1]], writes=[PB[zi]])
            kb.op("pe", lambda e: e.matmul(kb.ps(zi)[0:64, :], lhsT=dftm[:, 2, :], rhs=A[0][:, fl, :], start=False, stop=True),
                  reads=[dftm, A[0]], writes=[PB[zi]])
            kb.op("dve", lambda e: e.tensor_tensor(out=mm_[0][:], in0=kb.ps(zr)[0:64, :], in1=K_[0][:, fl, :], op=ALU.mult),
                  reads=[PB[zr], K_[0]], writes=[mm_[0]])
            kb.op("dve", lambda e: e.tensor_tensor(out=mm_[1][:], in0=kb.ps(zi)[0:64, :], in1=K_[1][:, fl, :], op=ALU.mult),
                  reads=[PB[zi], K_[1]], writes=[mm_[1]])
            kb.op("dve", lambda e: e.tensor_tensor(out=mm_[2][:], in0=kb.ps(zr)[0:64, :], in1=K_[1][:, fl, :], op=ALU.mult),
                  reads=[PB[zr], K_[1]], writes=[mm_[2]])
            kb.op("dve", lambda e: e.tensor_tensor(out=mm_[3][:], in0=kb.ps(zi)[0:64, :], in1=K_[0][:, fl, :], op=ALU.mult),
                  reads=[PB[zi], K_[0]], writes=[mm_[3]])
            kb.op("pool", lambda e: e.tensor_tensor(out=y_[0][:], in0=mm_[0][:], in1=mm_[1][:], op=ALU.subtract),
                  reads=[mm_[0], mm_[1]], writes=[y_[0]])
            kb.op("pool", lambda e: e.tensor_tensor(out=y_[1][:], in0=mm_[2][:], in1=mm_[3][:], op=ALU.add),
                  reads=[mm_[2], mm_[3]], writes=[y_[1]])

        def inv_a(f1_):
            c_, fl = divmod(f1_, FC)
            f0 = c_ * FC
            y_ = yy[f1_ % 2]
            bos = bo[c_ % 2]
            br, bi = 4 + f1_ % 2, 6 + f1_ % 2
            kb.op("pe", lambda e: e.matmul(kb.ps(br)[0:64, :], lhsT=gtw[:, 0, f1_, :], rhs=y_[0][:], start=True, stop=False),
                  reads=[gtw, y_[0]], writes=[PB[br]])
            kb.op("pe", lambda e: e.matmul(kb.ps(br)[0:64, :], lhsT=gtw[:, 2, f1_, :], rhs=y_[1][:], start=False, stop=True),
                  reads=[gtw, y_[1]], writes=[PB[br]])
            kb.op("pe", lambda e: e.matmul(kb.ps(bi)[0:64, :], lhsT=gtw[:, 1, f1_, :], rhs=y_[0][:], start=True, stop=False),
                  reads=[gtw, y_[0]], writes=[PB[bi]])
            kb.op("pe", lambda e: e.matmul(kb.ps(bi)[0:64, :], lhsT=gtw[:, 0, f1_, :], rhs=y_[1][:], start=False, stop=True),
                  reads=[gtw, y_[1]], writes=[PB[bi]])
            kb.op("act", lambda e: e.copy(out=bos[0][:, fl, :], in_=kb.ps(br)[0:64, :]), reads=[PB[br]], writes=[(bos[0], fl)])
            kb.op("act", lambda e: e.copy(out=bos[1][:, fl, :], in_=kb.ps(bi)[0:64, :]), reads=[PB[bi]], writes=[(bos[1], fl)])
            if fl == FC - 1:
                for ri_ in range(2):
                    kb.dma("pool", B1.t[ri_, :, f0:f0 + FC, :], bos[ri_][:], reads=[bos[ri_]])

        for k_ in range(HH + 1):
            if k_ < HH:
                s2_prod(k_)
            if k_ >= 1:
                inv_a(k_ - 1)
        phase_begin(label="hy_ib")
        hb = kb.sb([128, 4], F32)
        kb.dma("sp", hb[:], P["hy_bias"][li], writes=[hb])
        m4 = kb.sb([HH, 2, HH], BF16)
        kb.dma("sp", m4[:], C["m4"][:, :, :], writes=[m4])
        ysb = kb.sb([128, 4, L], BF16)
        bt_ = [[kb.sb([HH, 8, DH], BF16) for _ in range(2)] for _ in range(2)]
        for g in range(8):
            Bt = bt_[g % 2]
            for ri_ in range(2):
                kb.dma("sp", Bt[ri_][:], B1.t[ri_].rearrange("t f c -> f t c")[:, g * 8:(g + 1) * 8, :], writes=[Bt[ri_]])
            for cb_ in range(4):
                bank = next_pb()
                for tl in range(8):
                    kb.op("pe", lambda e: e.matmul(kb.ps(bank, [128, 8, HH])[:, tl, :], lhsT=Bt[0][:, tl, cb_ * 128:(cb_ + 1) * 128],
                                                   rhs=m4[:, 0, :], start=True, stop=False), reads=[Bt[0], m4], writes=[PB[bank]])
                    kb.op("pe", lambda e: e.matmul(kb.ps(bank, [128, 8, HH])[:, tl, :], lhsT=Bt[1][:, tl, cb_ * 128:(cb_ + 1) * 128],
                                                   rhs=m4[:, 1, :], start=False, stop=True), reads=[Bt[1], m4], writes=[PB[bank]])
                dst = ysb[:, cb_, :].rearrange("p (a b) -> p a b", b=64)[:, :, g * 8:(g + 1) * 8]
                src_ = kb.ps(bank, [128, 8, HH]).rearrange("p a b -> p b a")
                if (g * 4 + cb_) % 2 == 0:
                    kb.op("act", lambda e: e.copy(out=dst, in_=src_), reads=[PB[bank]], writes=[(ysb, (cb_, g))])
                else:
                    kb.op("dve", lambda e: e.tensor_copy(out=dst, in_=src_), reads=[PB[bank]], writes=[(ysb, (cb_, g))])
        x0t = [kb.sb([128, 512], BF16) for _ in range(2)]
        zvf = [kb.sb([128, 512], BF16) for _ in range(2)]
        tmp = [kb.sb([128, 512], F32) for _ in range(2)]
        yo = [kb.sb([128, 512], BF16) for _ in range(2)]
        for tt in range(NQ):
            for cb_ in range(4):
                k = (tt * 4 + cb_) % 2
                kb.dma("sp", x0t[k][:], X0fm[cb_ * 128:(cb_ + 1) * 128, tt * 512:(tt + 1) * 512], writes=[x0t[k]])
                kb.dma("sp", zvf[k][:], ZVfm[cb_ * 128:(cb_ + 1) * 128, tt * 512:(tt + 1) * 512], writes=[zvf[k]])
                kb.op("dve", lambda e: e.scalar_tensor_tensor(
                    out=tmp[k][:], in0=zvf[k][:], scalar=hb[:, cb_:cb_ + 1], in1=ysb[:, cb_, tt * 512:(tt + 1) * 512],
                    op0=ALU.mult, op1=ALU.add), reads=[zvf[k], hb, ysb], writes=[tmp[k]])
                kb.op("pool", lambda e: e.tensor_tensor(out=yo[k][:], in0=tmp[k][:], in1=x0t[k][:], op=ALU.mult),
                      reads=[tmp[k], x0t[k]], writes=[yo[k]])
                kb.dma("pool", YH[cb_ * 128:(cb_ + 1) * 128, tt * 512:(tt + 1) * 512], yo[k][:], reads=[yo[k]])
        st["pb"] = 0

    def phase_gla(li):
        phase_begin(0, True)
        wbr_pre = [kb.sb([128, 4, D], BF16) for _ in range(3)]
        preload = [(lambda b=b, c0=c0: load_w_into(wbr_pre[b], P["w_br"][li, b], 4, c0, 512, None, c0))
                   for b in range(3) for c0 in (0, 512)]
        w2x = kb.sb([33, 2, 512], F32)
        gn = kb.sb([128, 1], F32)
        kb.dma("sp", w2x[:], P["gla_w2x"][li], writes=[w2x])
        kb.dma("sp", gn[:], P["gla_norm"][li], writes=[gn])
        w2h = kb.sb([33, 2, 512], BF16)
        w2l = kb.sb([33, 2, 512], BF16)
        kb.op("dve", lambda e: e.tensor_copy(out=w2h[:], in_=w2x[:]), reads=[w2x], writes=[w2h])
        kb.op("dve", lambda e: e.tensor_tensor(out=w2l[:], in0=w2x[:], in1=w2h[:], op=ALU.subtract), reads=[w2x, w2h], writes=[w2l])
        OB = HT.view(HT.t[:, 0:4, :])
        S32 = [kb.sb([128, 4, 128], F32) for _ in range(2)]
        Sbf = [kb.sb([128, 4, 128], BF16) for _ in range(2)]
        for d_ in range(2):
            kb.op("pool", lambda e, d_=d_: e.memset(S32[d_][:], 0.0), writes=[S32[d_]])
            kb.op("pool", lambda e, d_=d_: e.memset(Sbf[d_][:], 0.0), writes=[Sbf[d_]])
        lah = [kb.sb([128, 512], BF16) for _ in range(2)]
        lal = [kb.sb([128, 512], BF16) for _ in range(2)]
        PD = lambda shape, dt: [kb.sb(shape, dt) for _ in range(2)]
        PS = lambda shape, dt: [[kb.sb(shape, dt) for _ in range(2)] for _ in range(2)]
        e1, la, edec = PD([128, 512], F32), PD([128, 512], F32), PD([128, 512], BF16)
        EK = PD([128, 4, 128], BF16)
        kt, qf, kf, ke = PD([128, 512], BF16), PD([128, 4, 128], BF16), PD([128, 4, 128], BF16), PD([128, 4, 128], BF16)
        EQ = PS([128, 4, 128], BF16)
        DEC = PS([128, 4, 1], F32)
        vt, kdec = PS([128, 512], BF16), PS([128, 512], BF16)
        qe, msk = PS([128, 4, 128], BF16), PS([128, 4, 128], BF16)
        gf, sqb, yob = PD([128, 4, 128], BF16), PD([128, 4, 128], BF16), PD([128, 4, 128], BF16)
        o32, rsd = PD([128, 4, 128], F32), PD([128, 4, 128], F32)
        qv = Qfm.t.rearrange("(h d) t -> d h t", h=4)
        kv = Kfm.t.rearrange("(h d) t -> d h t", h=4)
        gv_ = Gfm.t.rearrange("(h d) t -> d h t", h=4)
        yv = YG.t.rearrange("(h d) t -> d h t", h=4)
        V3 = [128, 4, 128]

        def tile_of(step, d_):
            return step if d_ == 0 else NT - 1 - step

        def pre(step, d_, stage):
            i = tile_of(step, d_)
            sl = slice(i * 128, (i + 1) * 128)
            p = step % 2
            bA, bB = 4 * d_, 4 * d_ + 1
            tri_fm = 0 if d_ == 0 else 2
            tri_dec = 3 if d_ == 0 else 1
            if stage == 1:
                pre1(d_, p, sl, bA)
            elif stage == 2:
                pre2(d_, p, bA, bB, tri_fm, tri_dec)
            else:
                pre3(d_, p, bA)

        def pre1(d_, p, sl, bA):
            kb.dma("sp", kt[d_][:], Ktok[sl, :], writes=[kt[d_]])
            kb.dma("sp", vt[d_][p][:], Vtok[sl, :], writes=[vt[d_][p]])
            kb.dma("sp", qf[d_][:], qv[:, :, sl], writes=[qf[d_]])
            kb.dma("sp", kf[d_][:], kv[:, :, sl], writes=[kf[d_]])
            kb.op("pe", lambda e: e.matmul(kb.ps(bA), lhsT=LRH[:, sl], rhs=w2h[:, d_, :], start=True, stop=False),
                  reads=[LRH, w2h], writes=[PB[bA]])
            kb.op("pe", lambda e: e.matmul(kb.ps(bA), lhsT=LRL[:, sl], rhs=w2h[:, d_, :], start=False, stop=False),
                  reads=[LRL, w2h], writes=[PB[bA]])
            kb.op("pe", lambda e: e.matmul(kb.ps(bA), lhsT=LRH[:, sl], rhs=w2l[:, d_, :], start=False, stop=True),
                  reads=[LRH, w2l], writes=[PB[bA]])
            kb.op("act", lambda e: e.activation(out=e1[d_][:], in_=kb.ps(bA), func=AF.Exp, scale=-1.0),
                  reads=[PB[bA]], writes=[e1[d_]])
            kb.op("act", lambda e: e.activation(out=e1[d_][:], in_=e1[d_][:], func=AF.Ln, bias=1.0),
                  reads=[e1[d_]], writes=[e1[d_]])
            kb.op("dve", lambda e: e.tensor_scalar(out=la[d_][:], in0=e1[d_][:], scalar1=-1.0 / 16.0, scalar2=-1.0,
                                                   op0=ALU.mult, op1=ALU.max), reads=[e1[d_]], writes=[la[d_]])
            kb.op("act", lambda e: e.copy(out=lah[d_][:], in_=la[d_][:]), reads=[la[d_]], writes=[lah[d_]])
            kb.op("dve", lambda e: e.tensor_tensor(out=lal[d_][:], in0=la[d_][:], in1=lah[d_][:], op=ALU.subtract),
                  reads=[la[d_], lah[d_]], writes=[lal[d_]])

        def pre2(d_, p, bA, bB, tri_fm, tri_dec):
            kb.op("pe", lambda e: e.matmul(kb.ps(bA), lhsT=tri32[:, tri_dec, :], rhs=lah[d_][:], start=True, stop=False),
                  reads=[tri32, lah[d_]], writes=[PB[bA]])
            kb.op("pe", lambda e: e.matmul(kb.ps(bA), lhsT=tri32[:, tri_dec, :], rhs=lal[d_][:], start=False, stop=True),
                  reads=[tri32, lal[d_]], writes=[PB[bA]])
            for h in range(4):
                kb.op("pe", lambda e, h=h: e.matmul(kb.ps(bB, V3)[:, h, :], lhsT=lah[d_][:, h * 128:(h + 1) * 128],
                                                    rhs=tri32[:, tri_fm, :], start=True, stop=False),
                      reads=[tri32, lah[d_]], writes=[PB[bB]])
                kb.op("pe", lambda e, h=h: e.matmul(kb.ps(bB, V3)[:, h, :], lhsT=lal[d_][:, h * 128:(h + 1) * 128],
                                                    rhs=tri32[:, tri_fm, :], start=False, stop=True),
                      reads=[tri32, lal[d_]], writes=[PB[bB]])
            kb.op("act", lambda e: e.activation(out=edec[d_][:], in_=kb.ps(bA), func=AF.Exp), reads=[PB[bA]], writes=[edec[d_]])
            kb.op("act", lambda e: e.activation(out=EQ[d_][p][:], in_=kb.ps(bB, V3), func=AF.Exp), reads=[PB[bB]], writes=[EQ[d_][p]])
            kb.op("act", lambda e: e.activation(out=EK[d_][:], in_=kb.ps(bB, V3), func=AF.Exp, scale=-1.0),
                  reads=[PB[bB]], writes=[EK[d_]])
            dcol_ = 127 if d_ == 0 else 0
            kb.op("act", lambda e: e.activation(out=DEC[d_][p][:], in_=kb.ps(bB, V3)[:, :, dcol_:dcol_ + 1], func=AF.Exp),
                  reads=[PB[bB]], writes=[DEC[d_][p]])
            kb.op("dve", lambda e: e.tensor_tensor(out=kdec[d_][p][:], in0=kt[d_][:], in1=edec[d_][:], op=ALU.mult),
                  reads=[kt[d_], edec[d_]], writes=[kdec[d_][p]])
            kb.op("dve", lambda e: e.tensor_tensor(out=qe[d_][p][:], in0=qf[d_][:], in1=EQ[d_][p][:], op=ALU.mult),
                  reads=[qf[d_], EQ[d_][p]], writes=[qe[d_][p]])
            kb.op("dve", lambda e: e.tensor_tensor(out=ke[d_][:], in0=kf[d_][:], in1=EK[d_][:], op=ALU.mult),
                  reads=[kf[d_], EK[d_]], writes=[ke[d_]])

        def pre3(d_, p, bA):
            for h in range(4):
                kb.op("pe", lambda e, h=h: e.matmul(kb.ps(bA, V3)[:, h, :], lhsT=ke[d_][:, h, :], rhs=qe[d_][p][:, h, :],
                                                    start=True, stop=True), reads=[ke[d_], qe[d_][p]], writes=[PB[bA]])
            kb.op("dve", lambda e: e.tensor_tensor(out=msk[d_][p][:], in0=kb.ps(bA, V3),
                                                   in1=tri32[:, 3 * d_:3 * d_ + 1, :].to_broadcast(V3), op=ALU.mult),
                  reads=[PB[bA], tri32], writes=[msk[d_][p]])

        def seq(step, d_):
            i = tile_of(step, d_)
            sl = slice(i * 128, (i + 1) * 128)
            p = step % 2
            bC, bD = 4 * d_ + 2, 4 * d_ + 3
            final = step >= NT // 2
            for h in range(4):
                hs = slice(h * 128, (h + 1) * 128)
                kb.op("pe", lambda e, h=h, hs=hs: e.matmul(kb.ps(bC, V3)[:, h, :], lhsT=vt[d_][p][:, hs], rhs=msk[d_][p][:, h, :],
                                                           start=True, stop=False), reads=[vt[d_][p], msk[d_][p]], writes=[PB[bC]])
                kb.op("pe", lambda e, h=h: e.matmul(kb.ps(bC, V3)[:, h, :], lhsT=Sbf[d_][:, h, :], rhs=qe[d_][p][:, h, :],
                                                    start=False, stop=True), reads=[Sbf[d_], qe[d_][p]], writes=[PB[bC]])
            for h in range(4):
                hs = slice(h * 128, (h + 1) * 128)
                kb.op("pe", lambda e, h=h, hs=hs: e.matmul(kb.ps(bD, V3)[:, h, :], lhsT=kdec[d_][p][:, hs], rhs=vt[d_][p][:, hs],
                                                           start=True, stop=True), reads=[kdec[d_][p], vt[d_][p]], writes=[PB[bD]])
            kb.op("dve", lambda e: e.tensor_tensor(out=S32[d_][:], in0=S32[d_][:],
                                                   in1=DEC[d_][p][:, :, 0:1].to_broadcast(V3), op=ALU.mult),
                  reads=[S32[d_], DEC[d_][p]], writes=[S32[d_]])
            kb.op("dve", lambda e: e.tensor_tensor(out=S32[d_][:], in0=S32[d_][:], in1=kb.ps(bD, V3), op=ALU.add),
                  reads=[S32[d_], PB[bD]], writes=[S32[d_]])
            kb.op("act", lambda e: e.copy(out=Sbf[d_][:], in_=S32[d_][:]), reads=[S32[d_]], writes=[Sbf[d_]])
            if not final:
                kb.op("act", lambda e: e.copy(out=OB[:, :, sl], in_=kb.ps(bC, V3)), reads=[PB[bC]], writes=[(OB, i)])
            else:
                kb.dma("sp", gf[d_][:], gv_[:, :, sl], writes=[gf[d_]])
                kb.op("dve", lambda e: e.tensor_tensor(out=o32[d_][:], in0=kb.ps(bC, V3), in1=OB[:, :, sl], op=ALU.add),
                      reads=[PB[bC], (OB, i)], writes=[o32[d_]])
                kb.op("act", lambda e: e.activation(out=sqb[d_][:], in_=o32[d_][:], func=AF.Square), reads=[o32[d_]], writes=[sqb[d_]])
                kb.op("pe", lambda e: e.matmul(kb.ps(bD), lhsT=ones[:], rhs=sqb[d_][:].rearrange("p a b -> p (a b)"), start=True, stop=True),
                      reads=[ones, sqb[d_]], writes=[PB[bD]])
                kb.op("act", lambda e: e.activation(out=rsd[d_][:].rearrange("p a b -> p (a b)"), in_=kb.ps(bD), func=AF.Ln,
                                                    scale=1.0 / 128.0, bias=EPS), reads=[PB[bD]], writes=[rsd[d_]])
                kb.op("act", lambda e: e.activation(out=rsd[d_][:], in_=rsd[d_][:], func=AF.Exp, scale=-0.5), reads=[rsd[d_]], writes=[rsd[d_]])
                kb.op("dve", lambda e: e.tensor_tensor(out=o32[d_][:], in0=o32[d_][:], in1=rsd[d_][:], op=ALU.mult),
                      reads=[o32[d_], rsd[d_]], writes=[o32[d_]])
                kb.op("dve", lambda e: e.scalar_tensor_tensor(out=yob[d_][:], in0=o32[d_][:], scalar=gn[:, 0:1], in1=gf[d_][:],
                                                              op0=ALU.mult, op1=ALU.mult), reads=[o32[d_], gn, gf[d_]], writes=[yob[d_]])
                kb.dma("pool", yv[:, :, sl], yob[d_][:], reads=[yob[d_]])

        for step in range(NT + 1):
            if preload and step >= 1 and (step % max(1, NT // 8) == 0 or step == NT):
                preload.pop(0)()
                if step == NT:
                    while preload:
                        preload.pop(0)()
            if step < NT:
                pre(step, 0, 1)
                pre(step, 1, 1)
            if step >= 1:
                seq(step - 1, 0)
                seq(step - 1, 1)
            if step < NT:
                pre(step, 0, 2)
                pre(step, 1, 2)
                pre(step, 0, 3)
                pre(step, 1, 3)
        st["pb"] = 0

    def load_w_into(dst, wap, kblocks, c0, ncols, gt=None, dcol0=0):
        src = wap.rearrange("(k p) c -> p k c", p=128)
        WF = st["WF"]
        kper = max(1, 4096 // ncols)
        k0 = 0
        while k0 < kblocks:
            kn = min(kper, kblocks - k0)
            wfv = WF.t[:, 0:kn * ncols].rearrange("p (k c) -> p k c", k=kn)
            kb.dma("sp", wfv, src[:, k0:k0 + kn, c0:c0 + ncols], writes=[WF])
            if gt is None:
                kb.op("pool", lambda e, wfv=wfv, k0=k0, kn=kn: e.tensor_copy(out=dst.t[:, k0:k0 + kn, dcol0:dcol0 + ncols], in_=wfv),
                      reads=[WF], writes=[(dst, (k0, dcol0))])
            else:
                kb.op("pool", lambda e, wfv=wfv, k0=k0, kn=kn: e.tensor_tensor(
                    out=dst.t[:, k0:k0 + kn, dcol0:dcol0 + ncols], in0=wfv,
                    in1=gt.t[:, k0:k0 + kn, None].to_broadcast([128, kn, ncols]), op=ALU.mult),
                    reads=[WF, gt], writes=[(dst, (k0, dcol0))])
            k0 += kn

    def post_tiles(depth=2):
        return {"yt": [kb.sb([128, D], F32) for _ in range(depth)], "xo": [kb.sb([128, D], F32) for _ in range(depth)],
                "sq": [kb.sb([128, 1], F32) for _ in range(depth)], "rs": [kb.sb([128, 1], F32) for _ in range(depth)],
                "junk": kb.sb([128, D], BF16), "nt": norm_tmp(depth), "depth": depth}

    def post(i, ba, bb, xsrc, gp, xdst, pt, do_norm, defer=None):
        k = i % pt["depth"]
        yt, xo, sq, rs, junk = pt["yt"][k], pt["xo"][k], pt["sq"][k], pt["rs"][k], pt["junk"]
        kb.dma("sp", xo[:], xsrc[i * 128:(i + 1) * 128, :], writes=[xo])
        kb.op("act", lambda e: e.copy(out=yt[:, 0:512], in_=kb.ps(ba)), reads=[PB[ba]], writes=[(yt, 0)])
        kb.op("dve", lambda e: e.tensor_copy(out=yt[:, 512:1024], in_=kb.ps(bb)), reads=[PB[bb]], writes=[(yt, 1)])
        kb.op("act", lambda e: e.activation(out=junk[:], in_=yt[:], func=AF.Square, scale=1.0 / 32.0, accum_out=sq[:]),
              reads=[yt], writes=[junk, sq])
        kb.op("act", lambda e: e.activation(out=rs[:], in_=sq[:], func=AF.Ln, bias=EPS), reads=[sq], writes=[rs])
        kb.op("act", lambda e: e.activation(out=rs[:], in_=rs[:], func=AF.Exp, scale=-0.5), reads=[rs], writes=[rs])
        kb.op("dve", lambda e: e.scalar_tensor_tensor(out=yt[:], in0=yt[:], scalar=rs[:, 0:1], in1=gp[:], op0=ALU.mult, op1=ALU.mult),
              reads=[yt, rs, gp], writes=[yt])
        kb.op("dve", lambda e: e.tensor_tensor(out=xo[:], in0=xo[:], in1=yt[:], op=ALU.add), reads=[xo, yt], writes=[xo])
        kb.dma("pool", xdst[i * 128:(i + 1) * 128, :], xo[:], reads=[xo])
        if do_norm:
            norm_transpose(xo, i, pt["nt"][k], defer)

    def phase_merge(li, xsrc, xdst):
        phase_begin(0, True)
        MG = HT
        wbr = [kb.sb([128, 4, D], BF16) for _ in range(3)]
        wo_pre = kb.sb([128, 8, D], BF16)
        preload = [(lambda c0=c0: load_w_into(wo_pre, P["w_out"][li], 8, c0, 512, None, c0)) for c0 in (0, 512)]
        ysrc = [Y_.t.rearrange("(k p) t -> p k t", p=128) for Y_ in (YH, YG, YP)]
        gsrc = GATES.t.rearrange("(b r) t -> r b t", b=3)
        yt_ = [[kb.sb([128, 4, 512], BF16) for _ in range(3)] for _ in range(2)]
        gt_ = [kb.sb([128, 3, 512], BF16) for _ in range(2)]
        mm = [[kb.sb([128, 512], BF16) for _ in range(3)] for _ in range(2)]
        ng = 0
        for tt in range(NQ):
            if preload and (tt >= 1 or NQ == 1):
                preload.pop(0)()
                if tt == NQ - 1:
                    while preload:
                        preload.pop(0)()
            ys = yt_[tt % 2]
            for b in range(3):
                kb.dma("sp", ys[b][:], ysrc[b][:, :, tt * 512:(tt + 1) * 512], writes=[ys[b]])
            for db in range(8):
                g = gt_[ng % 2]
                m_ = mm[ng % 2]
                ng += 1
                kb.dma("sp", g[:], gsrc[db * 128:(db + 1) * 128, :, tt * 512:(tt + 1) * 512], writes=[g])
                banks = [next_pb(), next_pb(), next_pb()]
                for b in range(3):
                    for k in range(4):
                        kb.op("pe", lambda e, b=b, k=k, db=db, ys=ys, banks=banks: e.matmul(
                            kb.ps(banks[b]), lhsT=wbr[b][:, k, db * 128:(db + 1) * 128], rhs=ys[b][:, k, :],
                            start=(k == 0), stop=(k == 3)), reads=[wbr[b], ys[b]], writes=[PB[banks[b]]])
                for b in range(3):
                    kb.op("dve", lambda e, b=b, g=g, m_=m_, banks=banks: e.tensor_tensor(
                        out=m_[b][:], in0=kb.ps(banks[b]), in1=g[:, b, :], op=ALU.mult), reads=[PB[banks[b]], g], writes=[m_[b]])
                kb.op("dve", lambda e, m_=m_: e.tensor_tensor(out=m_[0][:], in0=m_[0][:], in1=m_[1][:], op=ALU.add),
                      reads=[m_[0], m_[1]], writes=[m_[0]])
                kb.op("dve", lambda e, m_=m_, db=db, tt=tt: e.tensor_tensor(
                    out=MG[:, db, tt * 512:(tt + 1) * 512], in0=m_[0][:], in1=m_[2][:], op=ALU.add),
                    reads=[m_[0], m_[2]], writes=[(MG, 4 * tt), (MG, 4 * tt + 1), (MG, 4 * tt + 2), (MG, 4 * tt + 3)])
        phase_begin(0, True)
        _pad = [kb.sb([128, 4, D], BF16) for _ in range(3)]
        wo = kb.sb([128, 8, D], BF16)
        gp = kb.sb([128, D], F32)
        kb.dma("sp", gp[:], P["g_mix_post"][li], writes=[gp])
        pt = post_tiles(4)

        def mm(i):
            ba, bb = next_pb(), next_pb()
            gemm_tok(wo, 8, 0, 512, MG, i, ba, srckey=i)
            gemm_tok(wo, 8, 512, 512, MG, i, bb, srckey=i)
            return ba, bb

        pend = [mm(0), mm(1)] if NT > 1 else [mm(0)]
        dq_ = []
        for i in range(NT):
            if i + 2 < NT:
                pend.append(mm(i + 2))
            while dq_:
                nt_b(*dq_.pop(0))
            cur = pend.pop(0)
            post(i, cur[0], cur[1], xsrc, gp, xdst, pt, True, dq_)
        while dq_:
            nt_b(*dq_.pop(0))

    def phase_ffn_up(li):
        phase_begin(0)
        gt = kb.sb([128, 8], F32)
        kb.dma("sp", gt[:], P["g_ffn_pre"][li], writes=[gt])
        cw = kb.sb([128, 22, 3], F32)
        cb = kb.sb([128, 22], F32)
        kb.dma("sp", cw[:], P["ffn_cw"][li], writes=[cw])
        kb.dma("sp", cb[:], P["ffn_cb"][li], writes=[cb])
        raws = [kb.sb([128, L + 2], BF16) for _ in range(2)]
        for r in raws:
            kb.op("pool", lambda e, r=r: e.memset(r[:, 0:1], 0.0), writes=[(r, "h0")])
            kb.op("pool", lambda e, r=r: e.memset(r[:, L + 1:L + 2], 0.0), writes=[(r, "h1")])
        bts = [kb.sb([128, L], BF16) for _ in range(2)]
        gas = [kb.sb([128, L], BF16) for _ in range(2)]
        acc = kb.sb([128, L], F32)
        wfs = [kb.sb([128, 8, 256], F32) for _ in range(2)]
        wbs = [kb.sb([128, 8, 256], BF16) for _ in range(2)]
        wsrc = P["ffn_up"][li].rearrange("(k p) c -> p k c", p=128)

        def prep(m):
            s_ = m % 2
            kb.dma("sp", wfs[s_][:, :, 0:128], wsrc[:, :, m * 128:(m + 1) * 128], writes=[(wfs[s_], 0)])
            kb.dma("sp", wfs[s_][:, :, 128:256], wsrc[:, :, DFF + m * 128:DFF + (m + 1) * 128], writes=[(wfs[s_], 1)])
            kb.op("pool", lambda e: e.tensor_tensor(out=wbs[s_][:], in0=wfs[s_][:],
                                                    in1=gt[:, :, None].to_broadcast([128, 8, 256]), op=ALU.mult),
                  reads=[wfs[s_], gt], writes=[wbs[s_]])
            return wbs[s_]

        wnext = prep(0)
        for m in range(22):
            raw, bt, ga = raws[m % 2], bts[m % 2], gas[m % 2]
            wcur = wnext
            if m + 1 < 22:
                wnext = prep(m + 1)
            for j in range(NQ):
                b1_, b2_ = next_pb(), next_pb()
                gemm_fm(wcur, 8, 0, 128, HT, j, b1_)
                gemm_fm(wcur, 8, 128, 128, HT, j, b2_)
                kb.op("act", lambda e: e.copy(out=raw[:, 1 + j * 512:1 + (j + 1) * 512], in_=kb.ps(b1_)),
                      reads=[PB[b1_]], writes=[(raw, j)])
                kb.op("act", lambda e: e.copy(out=bt[:, j * 512:(j + 1) * 512], in_=kb.ps(b2_)),
                      reads=[PB[b2_]], writes=[(bt, j)])
            kb.op("dve", lambda e: e.tensor_scalar(out=acc[:], in0=raw[:, 0:L], scalar1=cw[:, m, 0:1], scalar2=cb[:, m:m + 1],
                                                   op0=ALU.mult, op1=ALU.add), reads=[raw, cw, cb], writes=[acc])
            kb.op("dve", lambda e: e.scalar_tensor_tensor(out=acc[:], in0=raw[:, 1:L + 1], scalar=cw[:, m, 1:2], in1=acc[:],
                                                          op0=ALU.mult, op1=ALU.add), reads=[raw, cw, acc], writes=[acc])
            kb.op("dve", lambda e: e.scalar_tensor_tensor(out=acc[:], in0=raw[:, 2:L + 2], scalar=cw[:, m, 2:3], in1=acc[:],
                                                          op0=ALU.mult, op1=ALU.add), reads=[raw, cw, acc], writes=[acc])
            kb.op("act", lambda e: e.activation(out=ga[:], in_=acc[:], func=AF.Gelu), reads=[acc], writes=[ga])
            kb.op("dve", lambda e: e.tensor_tensor(out=ga[:], in0=ga[:], in1=bt[:], op=ALU.mult), reads=[ga, bt], writes=[ga])
            kb.dma("pool", ACTfm[m * 128:(m + 1) * 128, :], ga[:], reads=[ga])

    def phase_ffn_down(li, xsrc, xdst, do_norm, preloaded=False):
        phase_begin(0, True)
        wd = kb.sb([128, 22, D], BF16)
        gp = kb.sb([128, D], F32)
        kb.dma("sp", gp[:], P["g_ffn_post"][li], writes=[gp])
        if not preloaded:
            for c0 in range(0, D, 128):
                load_w_into(wd, P["ffn_down"][li], 22, c0, 128, None, c0)
        asrc = ACTfm.t.rearrange("(k p) t -> p k t", p=128)
        at = [kb.sb([128, 22, 256], BF16) for _ in range(2)]
        pt = post_tiles()
        def load_a(c_):
            a = at[c_ % 2]
            kb.dma("sp", a[:, 0:11, :], asrc[:, 0:11, c_ * 256:(c_ + 1) * 256], writes=[(a, 0)])
            kb.dma("sp", a[:, 11:22, :], asrc[:, 11:22, c_ * 256:(c_ + 1) * 256], writes=[(a, 1)])

        load_a(0)

        def mm(i):
            if i % 2 == 0 and i // 2 + 1 < NT // 2:
                load_a(i // 2 + 1)
            a, ii = at[(i // 2) % 2], i % 2
            ba, bb = next_pb(), next_pb()
            for c0, bank in ((0, ba), (512, bb)):
                for k in range(22):
                    kb.op("pe", lambda e, k=k, c0=c0, bank=bank: e.matmul(
                        kb.ps(bank), lhsT=a[:, k, ii * 128:(ii + 1) * 128], rhs=wd[:, k, c0:c0 + 512],
                        start=(k == 0), stop=(k == 21)), reads=[a, wd], writes=[PB[bank]])
            return ba, bb

        pend = [mm(0), mm(1)] if NT > 1 else [mm(0)]
        dq_ = []
        for i in range(NT):
            if i + 2 < NT:
                pend.append(mm(i + 2))
            while dq_:
                nt_b(*dq_.pop(0))
            cur = pend.pop(0)
            post(i, cur[0], cur[1], xsrc, gp, xdst, pt, do_norm, dq_)
        while dq_:
            nt_b(*dq_.pop(0))

    phase_filter(0)
    phase_norm0(x_in)
    for li in range(depth):
        xs = x_in if li == 0 else XB
        phase_proj(li)
        phase_hyena(li)
        phase_gla(li)
        phase_merge(li, xs, XA)
        phase_ffn_up(li)
        lastl = (li == depth - 1)
        if not lastl:
            phase_filter(li + 1, preload_down=li)
        phase_ffn_down(li, XA, out if lastl else XB, not lastl, preloaded=not lastl)
    nc = kb.finish()
    return nc, kb


_CACHE = {}


def _in_maps(inputs, L, depth, nb):
    consts = make_consts(L)
    params = relayout_params(inputs, depth)
    x = np.asarray(inputs["x"], np.float32)
    maps = []
    for b in range(nb):
        m = {"x": np.ascontiguousarray(x[b])}
        m.update(consts)
        m.update(params)
        maps.append(m)
    return maps


def kernel(**inputs):
    x = np.asarray(inputs["x"])
    B, L, _ = x.shape
    depth = int(np.asarray(inputs["w_in"]).shape[0])
    nc, _ = build(L, depth)
    maps = _in_maps(inputs, L, depth, B)
    res = run_bass_kernel_spmd(nc, maps, core_ids=list(range(B)))
    return np.stack([np.asarray(r["out"], np.float32) for r in res.results], 0)
```

```python
import contextlib
import math
import numpy as np
import ml_dtypes
import concourse.bass as bass
import concourse.mybir as mybir
from concourse.bass_utils import run_bass_kernel_spmd

F32 = mybir.dt.float32
BF16 = mybir.dt.bfloat16
AF = mybir.ActivationFunctionType
ALU = mybir.AluOpType
NDMA_SEM = 8

D = 1024
DH = 512
DIN = 7200
DFF = 2816
EPS = 1e-6


class T:
    def __init__(self, name, ap, parent=None):
        self.name = name
        self.t = ap
        if parent is None:
            self.w = {}
            self.r = {}
            self.root = self
        else:
            self.root = parent.root

    def __getitem__(self, idx):
        return self.t[idx]

    def view(self, ap):
        return T(self.name, ap, parent=self)


class Op:
    __slots__ = ("eng", "fn", "deps", "marked", "val", "sem", "isdma")

    def __init__(self, eng, fn):
        self.eng = eng
        self.fn = fn
        self.deps = []
        self.marked = False
        self.val = 0
        self.sem = None
        self.isdma = False


class _Rec:
    def __getattr__(self, name):
        def f(*a, **k):
            self.call = (name, a, k)
        return f


class KB:
    def __init__(self, sb_bytes):
        self.nc = bass.Bass("TRN2", target_bir_lowering=False)
        self.es = contextlib.ExitStack()
        self.ops = []
        nc = self.nc
        self.handles = {"pe": nc.tensor, "act": nc.scalar, "dve": nc.vector,
                        "pool": nc.gpsimd, "sp": nc.sync}
        self.sems = {e: self.es.enter_context(nc.semaphore("s_" + e)) for e in self.handles}
        self.dq = {}
        for q in ("sp", "act", "pool"):
            self.dq[q] = {"sems": [self.es.enter_context(nc.semaphore("d_%s%d" % (q, i)))
                                   for i in range(NDMA_SEM)],
                          "n": 0, "last": [None] * NDMA_SEM, "cnt": [0] * NDMA_SEM}
        self.last = {}
        self.arena = self.es.enter_context(nc.sbuf_tensor("arena", [128, sb_bytes // 2], BF16))
        self.sb_bytes = sb_bytes
        self.top = 0
        self.nbuf = 0
        self.pbanks = [self.es.enter_context(nc.psum_tensor("pb%d" % i, [128, 512], F32))
                       for i in range(8)]

    def sb(self, shape, dt, name=None):
        esz = 4 if dt == F32 else 2
        n = int(np.prod(shape[1:])) * esz
        n = (n + 63) // 64 * 64
        off = self.top
        self.top += n
        assert self.top <= self.sb_bytes, "SBUF arena overflow %d" % self.top
        ap = self.arena[:, off // 2:(off + n) // 2]
        if dt == F32:
            ap = ap.bitcast(F32)
        ap = ap[:, 0:int(np.prod(shape[1:]))]
        if len(shape) == 3:
            ap = ap.rearrange("p (a b) -> p a b", a=shape[1])
        elif len(shape) == 4:
            ap = ap.rearrange("p (a b c) -> p a b c", a=shape[1], b=shape[2])
        if shape[0] < 128:
            ap = ap[0:shape[0]]
        self.nbuf += 1
        return T(name or "sb%d" % self.nbuf, ap)

    def ps(self, bank, shape=None, dt=F32):
        ap = self.pbanks[bank][:]
        if dt == BF16:
            ap = ap.bitcast(BF16)
        if shape is not None and len(shape) == 3:
            ap = ap[:, 0:shape[1] * shape[2]].rearrange("p (a b) -> p a b", a=shape[1])
        elif shape is not None:
            ap = ap[:, 0:shape[1]]
        if shape is not None and shape[0] < 128:
            ap = ap[0:shape[0]]
        return ap

    def psT(self, bank):
        if not hasattr(self, "_pst"):
            self._pst = [T("pbank%d" % i, self.pbanks[i][:]) for i in range(8)]
        return self._pst[bank]

    def dram(self, name, shape, dt, kind="Internal"):
        return T(name, self.nc.dram_tensor(name, list(shape), dt, kind=kind).ap())

    @staticmethod
    def _norm(lst):
        out = []
        for x in lst:
            if isinstance(x, tuple):
                out.append((x[0].root, x[1]))
            else:
                out.append((x.root, None))
        return out

    def _hazards(self, op, reads, writes):
        deps = op.deps
        for t, key in reads:
            if key is None:
                deps.extend(t.w.values())
            else:
                for k in (key, None):
                    p = t.w.get(k)
                    if p is not None:
                        deps.append(p)
        for t, key in writes:
            if key is None:
                deps.extend(t.w.values())
                for l in t.r.values():
                    deps.extend(x for x in l if x.isdma or op.isdma or x.eng != op.eng or op.eng != 'pe')
            else:
                for k in (key, None):
                    p = t.w.get(k)
                    if p is not None:
                        deps.append(p)
                    deps.extend(x for x in t.r.get(k, ()) if x.isdma or op.isdma or x.eng != op.eng or op.eng != 'pe')
        for t, key in reads:
            l = t.r.setdefault(key, [])
            if not op.isdma:
                l[:] = [o for o in l if o.eng != op.eng or o.isdma]
            l.append(op)
        for t, key in writes:
            if key is None:
                t.w = {None: op}
                t.r = {}
            else:
                t.w[key] = op
                t.r[key] = []

    def op(self, eng, fn, reads=(), writes=()):
        rec = _Rec()
        fn(rec)
        name, a, k = rec.call
        o = Op(eng, lambda h: getattr(h, name)(*a, **k))
        self._hazards(o, self._norm(reads), self._norm(writes))
        if eng == "pe":
            o.deps = [d for d in o.deps if not (d.eng == "pe" and not d.isdma)]
        self.ops.append(o)
        self.last[eng] = o
        return o

    def dma(self, q, out, in_, reads=(), writes=()):
        o = Op(q, lambda e: e.dma_start(out=out, in_=in_))
        o.isdma = True
        dq = self.dq[q]
        i = dq["n"] % NDMA_SEM
        dq["n"] += 1
        if dq["last"][i] is not None:
            o.deps.append(dq["last"][i])
        dq["last"][i] = o
        dq["cnt"][i] += 16
        o.sem = dq["sems"][i]
        o.val = dq["cnt"][i]
        self._hazards(o, self._norm(reads), self._norm(writes))
        self.ops.append(o)
        return o

    def mark(self, label):
        o = Op("sp", None)
        o.sem = label
        o.marked = "label"
        self.ops.append(o)

    def barrier(self):
        deps = list(self.last.values())
        for q in self.dq.values():
            deps.extend(x for x in q["last"] if x is not None)
        for e in self.handles:
            o = Op(e, None)
            o.deps = list(deps)
            self.ops.append(o)

    def finish(self):
        self.barrier()
        for o in self.ops:
            for d in o.deps:
                if not d.isdma:
                    d.marked = True
        cnt = {e: 0 for e in self.handles}
        for o in self.ops:
            if not o.isdma and o.marked is True:
                cnt[o.eng] += 1
                o.val = cnt[o.eng]
                o.sem = self.sems[o.eng]
        seen = {e: {} for e in self.handles}
        nwait = 0
        self.marks = []
        for o in self.ops:
            if o.marked == "label":
                self.marks.append((o.sem, self.nc.get_next_instruction_name()))
                continue
            h = self.handles[o.eng]
            sn = seen[o.eng]
            for d in o.deps:
                k = id(d.sem)
                if sn.get(k, 0) >= d.val:
                    continue
                h.wait_ge(d.sem, d.val)
                nwait += 1
                sn[k] = d.val
            if o.fn is None:
                continue
            ins = o.fn(h)
            if o.isdma:
                ins.then_inc(o.sem, 16)
            elif o.marked is True:
                ins.then_inc(o.sem, 1)
        self.stats = {"ops": len(self.ops), "waits": nwait, "marked": dict(cnt)}
        return self.nc


def make_consts(L):
    bf = ml_dtypes.bfloat16
    c = {}
    c["ident"] = np.eye(128, dtype=np.float32).astype(bf)
    c["ones"] = np.ones((128, 128), np.float32).astype(bf)
    a = np.arange(128)
    uti = (a[:, None] <= a[None, :]).astype(np.float32)
    uts = (a[:, None] < a[None, :]).astype(np.float32)
    lti = (a[:, None] >= a[None, :]).astype(np.float32)
    lts = (a[:, None] > a[None, :]).astype(np.float32)
    c["tri32"] = np.stack([uti, uts, lti, lts], 1).astype(bf)
    t = np.linspace(0.0, 1.0, L, dtype=np.float32)
    bands = 16
    w = (2.0 * np.float32(math.pi) * np.arange(L, dtype=np.float32) / np.float32(L)).astype(np.float32)
    f = np.linspace(1e-4, bands - 1, bands, dtype=np.float32)
    ang = (f[None, :] * w[:, None]).astype(np.float32)
    z = np.concatenate([t[:, None], np.cos(ang), -np.sin(ang)], -1).astype(np.float32)
    c["zT"] = np.ascontiguousarray(z.T)
    c["tcol"] = np.ascontiguousarray(-t.reshape(L // 128, 128).T)
    max_decay = math.log(1e-2) / 0.3
    min_decay = math.log(1e-2) / 1.5
    deltas = np.abs(np.linspace(min_decay, max_decay, DH, dtype=np.float32))
    c["absd"] = np.ascontiguousarray(np.broadcast_to(deltas[None, :], (128, DH))).astype(np.float32)
    N = 2 * L
    N1 = N // 64
    H = N1 // 2
    n1 = np.arange(H, dtype=np.float64)[:, None, None]
    n2 = np.arange(64, dtype=np.float64)[None, :, None]
    f1 = np.arange(H, dtype=np.float64)[None, None, :]
    al = 2 * np.pi * ((f1 + 0.5) * n1 / N1 + (f1 + 0.5) * n2 / N)
    c["tw1"] = np.ascontiguousarray(np.stack([np.cos(al), -np.sin(al)], 1)).astype(bf)
    a64 = np.arange(64, dtype=np.float64)
    be = 2 * np.pi * np.outer(a64, a64) / 64
    c["dftm"] = np.ascontiguousarray(np.stack([np.cos(be), np.sin(be), -np.sin(be)], 1)).astype(bf)
    f2 = a64[:, None, None]
    f1b = np.arange(H, dtype=np.float64)[None, :, None]
    t2 = a64[None, None, :]
    ga = 2 * np.pi * (f2 * t2 / 64 + (f1b + 0.5) * t2 / N)
    c["gtw"] = np.ascontiguousarray(np.stack([np.cos(ga), np.sin(ga), -np.sin(ga)], 1)).astype(bf)
    f1c = np.arange(H, dtype=np.float64)[:, None]
    t1 = np.arange(H, dtype=np.float64)[None, :]
    ph = 2 * np.pi * (f1c + 0.5) * t1 / N1
    c["m4"] = np.ascontiguousarray(np.stack([(2.0 / N) * np.cos(ph), -(2.0 / N) * np.sin(ph)], 1)).astype(bf)
    pos = np.arange(L)
    inv = []
    for wv in (2, 4, 8, 16):
        half = wv // 2
        cntv = (np.minimum(pos + half, L) - np.maximum(pos - half, 0)).astype(np.float32)
        inv.append(1.0 / cntv)
    c["invcnt"] = np.ascontiguousarray(
        np.broadcast_to(np.stack(inv, 0)[:, None, :], (4, 128, L))).astype(np.float32)
    return c


def relayout_params(p, depth):
    f = np.float32
    o = {}

    def pk(v, nb):
        return np.ascontiguousarray(np.asarray(v, f).reshape(nb, 128).T)

    o["g_mix_pre"] = np.stack([pk(p["norm_mix_pre"][i], 8) for i in range(depth)])
    o["g_ffn_pre"] = np.stack([pk(p["norm_ffn_pre"][i], 8) for i in range(depth)])
    o["g_mix_post"] = np.ascontiguousarray(np.broadcast_to(
        np.asarray(p["norm_mix_post"], f)[:depth, None, :], (depth, 128, D)))
    o["g_ffn_post"] = np.ascontiguousarray(np.broadcast_to(
        np.asarray(p["norm_ffn_post"], f)[:depth, None, :], (depth, 128, D)))
    o["w_in"] = np.asarray(p["w_in"], f)[:depth]
    cw = np.asarray(p["hy_conv_w"], f)[:depth]
    o["hy_cw"] = np.ascontiguousarray(cw.reshape(depth, 3, 12, 128).transpose(0, 3, 2, 1))
    o["hy_cb"] = np.stack([pk(p["hy_conv_b"][i], 12) for i in range(depth)])
    o["hy_w1"] = np.asarray(p["hy_filt_w1"], f)[:depth]
    o["hy_w2"] = np.asarray(p["hy_filt_w2"], f)[:depth]
    o["hy_w3"] = np.asarray(p["hy_filt_w3"], f)[:depth]
    vec = np.stack([np.asarray(p["hy_filt_b1"], f)[:depth], np.asarray(p["hy_filt_freq1"], f)[:depth],
                    np.asarray(p["hy_filt_b2"], f)[:depth], np.asarray(p["hy_filt_freq2"], f)[:depth]], -1)
    o["hy_vec"] = np.ascontiguousarray(vec)
    o["hy_bias"] = np.stack([pk(p["hy_bias"][i], 4) for i in range(depth)])
    w2 = np.asarray(p["gla_gate_w2"], f)[:depth]
    gb = np.asarray(p["gla_gate_b"], f)[:depth]
    w2x = np.zeros((depth, 2, 33, 512), f)
    w2x[:, 0, 0:16] = w2[:, 0]
    w2x[:, 1, 16:32] = w2[:, 1]
    w2x[:, :, 32] = gb
    o["gla_w2x"] = np.ascontiguousarray(w2x.transpose(0, 2, 1, 3))
    o["gla_norm"] = np.asarray(p["gla_norm"], f)[:depth].reshape(depth, 128, 1)
    o["pool_w"] = np.ascontiguousarray(np.asarray(p["pool_w"], f)[:depth].transpose(0, 2, 1, 3))
    o["pool_scale"] = np.stack([pk(p["pool_scale"][i], 4) for i in range(depth)])
    o["w_br"] = np.ascontiguousarray(np.stack(
        [np.asarray(p["w_br_hyena"], f)[:depth], np.asarray(p["w_br_gla"], f)[:depth],
         np.asarray(p["w_br_pool"], f)[:depth]], 1))
    o["w_out"] = np.asarray(p["w_out"], f)[:depth]
    o["ffn_up"] = np.asarray(p["ffn_w_up"], f)[:depth]
    fw = np.asarray(p["ffn_conv_w"], f)[:depth]
    o["ffn_cw"] = np.ascontiguousarray(fw.reshape(depth, 3, 22, 128).transpose(0, 3, 2, 1))
    o["ffn_cb"] = np.stack([pk(p["ffn_conv_b"][i], 22) for i in range(depth)])
    o["ffn_down"] = np.asarray(p["ffn_w_down"], f)[:depth]
    return o


def build(L, depth, dbg=()):
    NT = L // 128
    NQ = L // 512
    NFB = 2 * NT
    HH = (2 * L // 64) // 2
    kb = KB(sb_bytes=200 * 1024)

    def din(name, shape, dt=F32):
        return kb.dram(name, shape, dt, kind="ExternalInput")

    def scr(name, shape, dt=BF16):
        return kb.dram(name, shape, dt, kind=("ExternalOutput" if name in dbg else "Internal"))

    x_in = din("x", [L, D])
    C = {k: din(k, list(v.shape), BF16 if v.dtype != np.float32 else F32)
         for k, v in make_consts(L if L <= 512 else 128 * 4).items()} if False else None
    cshapes = {"ident": ([128, 128], BF16), "ones": ([128, 128], BF16), "tri32": ([128, 4, 128], BF16), "zT": ([33, L], F32), "tcol": ([128, NT], F32),
               "absd": ([128, DH], F32), "tw1": ([HH, 2, 64, HH], BF16), "dftm": ([64, 3, 64], BF16),
               "gtw": ([64, 3, HH, 64], BF16), "m4": ([HH, 2, HH], BF16),
               "invcnt": ([4, 128, L], F32)}
    C = {k: din(k, s, dt) for k, (s, dt) in cshapes.items()}
    n = depth
    pshapes = {"g_mix_pre": [n, 128, 8], "g_ffn_pre": [n, 128, 8], "g_mix_post": [n, 128, D],
               "g_ffn_post": [n, 128, D], "w_in": [n, D, DIN], "hy_cw": [n, 128, 12, 3],
               "hy_cb": [n, 128, 12], "hy_w1": [n, 33, 64], "hy_w2": [n, 64, 64], "hy_w3": [n, 64, 1024],
               "hy_vec": [n, 64, 4], "hy_bias": [n, 128, 4], "gla_w2x": [n, 33, 2, 512],
               "gla_norm": [n, 128, 1], "pool_w": [n, 128, 4, 128], "pool_scale": [n, 128, 4],
               "w_br": [n, 3, DH, D], "w_out": [n, D, D], "ffn_up": [n, D, 2 * DFF],
               "ffn_cw": [n, 128, 22, 3], "ffn_cb": [n, 128, 22], "ffn_down": [n, DFF, D]}
    P = {k: din(k, s) for k, s in pshapes.items()}
    out = kb.dram("out", [L, D], F32, kind="ExternalOutput")

    XA = scr("XA", [L, D], F32)
    XB = scr("XB", [L, D], F32)
    X0fm = scr("X0fm", [DH, L])
    ZVfm = scr("ZVfm", [DH, L])
    ZVT = scr("ZVT", [L, DH])
    HSD = scr("HSD", [2, L, DH])
    A1Z = scr("A1Z", [2, HH, 64, DH])
    A1K = scr("A1K", [2, 2, HH, 64, DH])
    KS = scr("KS", [2, HH, 64, DH])
    B1 = scr("B1", [2, 64, HH, DH])
    Qfm = scr("Qfm", [DH, L])
    Kfm = scr("Kfm", [DH, L])
    Gfm = scr("Gfm", [DH, L])
    Ktok = scr("Ktok", [L, DH])
    Vtok = scr("Vtok", [L, DH])
    YH = scr("YH", [DH, L])
    YG = scr("YG", [DH, L])
    YP = scr("YP", [DH, L])
    GATES = scr("GATES", [3 * D, L])
    ACTfm = scr("ACTfm", [DFF, L])

    HT = kb.sb([128, 8, L], BF16, "HT")
    ident = kb.sb([128, 128], BF16, "ident")
    ones = kb.sb([128, 128], BF16, "ones")
    tri32 = kb.sb([128, 4, 128], BF16, "tri32")
    LRH = kb.sb([33, L], BF16, "LRH")
    LRL = kb.sb([33, L], BF16, "LRL")
    kb.dma("sp", ident[:], C["ident"][:, :], writes=[ident])
    kb.dma("sp", ones[:], C["ones"][:, :], writes=[ones])
    kb.dma("sp", tri32[:], C["tri32"][:, :, :], writes=[tri32])
    kb.op("dve", lambda e: e.memset(LRH[32:33, :], 1.0), writes=[LRH])
    kb.op("dve", lambda e: e.memset(LRL[32:33, :], 0.0), writes=[LRL])
    base_top = kb.top
    PB = [kb.psT(i) for i in range(8)]
    st = {"wb": 0, "pb": 0}

    def dump(name, t_, shape, dt=F32):
        if name in dbg:
            d_ = kb.dram(name, shape, dt, kind="ExternalOutput")
            kb.dma("sp", d_.t, t_.t, reads=[t_])

    def phase_begin(nwb=0, wf=False, label=None):
        kb.barrier()
        import inspect
        kb.mark(label or inspect.stack()[1].function + ":%d" % inspect.stack()[1].lineno)
        kb.top = base_top
        if wf or nwb:
            st["WF"] = kb.sb([128, 4096], F32, "WF")
        st["WB"] = [kb.sb([128, 6144], BF16, "WB%d" % i) for i in range(nwb)]

    def load_w(wap, kblocks, c0, ncols, gt=None):
        i = st["wb"] % len(st["WB"])
        st["wb"] += 1
        wb = st["WB"][i]
        WF = st["WF"]
        wbv = wb.view(wb.t[:, 0:kblocks * ncols].rearrange("p (k c) -> p k c", k=kblocks))
        kper = max(1, 4096 // ncols)
        src = wap.rearrange("(k p) c -> p k c", p=128)
        k0 = 0
        while k0 < kblocks:
            kn = min(kper, kblocks - k0)
            wfv = WF.t[:, 0:kn * ncols].rearrange("p (k c) -> p k c", k=kn)
            kb.dma("sp", wfv, src[:, k0:k0 + kn, c0:c0 + ncols], writes=[WF])
            if gt is None:
                kb.op("pool", lambda e, wfv=wfv, k0=k0, kn=kn: e.tensor_copy(out=wbv.t[:, k0:k0 + kn, :], in_=wfv),
                      reads=[WF], writes=[(wb, k0)])
            else:
                kb.op("pool", lambda e, wfv=wfv, k0=k0, kn=kn: e.tensor_tensor(
                    out=wbv.t[:, k0:k0 + kn, :], in0=wfv,
                    in1=gt.t[:, k0:k0 + kn, None].to_broadcast([128, kn, ncols]), op=ALU.mult),
                    reads=[WF, gt], writes=[(wb, k0)])
            k0 += kn
        return wbv

    def next_pb(nb=6):
        b = st["pb"] % nb
        st["pb"] += 1
        return b

    def gemm_fm(wbv, kblocks, mcol, mw, src, j, bank):
        for k in range(kblocks):
            kb.op("pe", lambda e, k=k: e.matmul(kb.ps(bank)[0:mw, :], lhsT=wbv.t[:, k, mcol:mcol + mw],
                                                rhs=src.t[:, k, j * 512:(j + 1) * 512],
                                                start=(k == 0), stop=(k == kblocks - 1)),
                  reads=[wbv, src], writes=[PB[bank]])

    def gemm_tok(wbv, kblocks, c0, ncols, src, i, bank, srckey=None):
        for k in range(kblocks):
            kb.op("pe", lambda e, k=k: e.matmul(kb.ps(bank)[:, 0:ncols], lhsT=src.t[:, k, i * 128:(i + 1) * 128],
                                                rhs=wbv.t[:, k, c0:c0 + ncols],
                                                start=(k == 0), stop=(k == kblocks - 1)),
                  reads=[wbv, (src, srckey) if srckey is not None else src], writes=[PB[bank]])

    def nt_b(hn, i):
        for k in range(8):
            kb.op("pe", lambda e, k=k: e.transpose(out=kb.ps(7, [128, 8, 128], BF16)[:, k, :],
                                                   in_=hn[:, k * 128:(k + 1) * 128], identity=ident[:]),
                  reads=[hn, ident], writes=[PB[7]])
        kb.op("act", lambda e: e.copy(out=HT[:, :, i * 128:(i + 1) * 128], in_=kb.ps(7, [128, 8, 128], BF16)),
              reads=[PB[7]], writes=[(HT, i)])

    def norm_transpose(xt, i, tmp, defer=None):
        junk, sq, rs, hn = tmp["junk"], tmp["sq"], tmp["rs"], tmp["hn"]
        kb.op("act", lambda e: e.activation(out=junk[:], in_=xt[:], func=AF.Square, scale=1.0 / 32.0,
                                            accum_out=sq[:]), reads=[xt], writes=[junk, sq])
        kb.op("act", lambda e: e.activation(out=rs[:], in_=sq[:], func=AF.Ln, bias=EPS), reads=[sq], writes=[rs])
        kb.op("act", lambda e: e.activation(out=rs[:], in_=rs[:], func=AF.Exp, scale=-0.5), reads=[rs], writes=[rs])
        kb.op("dve", lambda e: e.tensor_scalar(out=hn[:], in0=xt[:], scalar1=rs[:], scalar2=None, op0=ALU.mult),
              reads=[xt, rs], writes=[hn])
        if defer is not None:
            defer.append((hn, i))
        else:
            nt_b(hn, i)

    def norm_tmp(depth=2):
        junk = kb.sb([128, D], BF16)
        return [{"junk": junk, "sq": kb.sb([128, 1], F32), "rs": kb.sb([128, 1], F32),
                 "hn": kb.sb([128, D], BF16)} for _ in range(depth)]

    TWO_PI = 2.0 * math.pi

    def phase_filter(li, preload_down=None):
        phase_begin()
        HS = HT.view(HT.t[:, :, :].rearrange("p a b -> p (a b)")[:, 0:NT * 1024].rearrange(
            "p (n c) -> p n c", n=NT))
        w1 = kb.sb([33, 64], F32)
        w2 = kb.sb([64, 64], F32)
        w3 = kb.sb([64, 1024], F32)
        vec = kb.sb([64, 4], F32)
        pv = kb.sb([64, 2], F32)
        absd = kb.sb([128, DH], F32)
        tcol = kb.sb([128, NT], F32)
        H1 = kb.sb([64, L], F32)
        H2 = kb.sb([64, L], F32)
        kb.dma("sp", w1[:], P["hy_w1"][li], writes=[w1])
        kb.dma("sp", w2[:], P["hy_w2"][li], writes=[w2])
        kb.dma("sp", w3[:], P["hy_w3"][li], writes=[w3])
        kb.dma("sp", vec[:], P["hy_vec"][li], writes=[vec])
        kb.dma("sp", absd[:], C["absd"][:, :], writes=[absd])
        kb.dma("sp", tcol[:], C["tcol"][:, :], writes=[tcol])
        kb.op("dve", lambda e: e.tensor_tensor(out=pv[:, 0:1], in0=vec[:, 0:1], in1=vec[:, 1:2], op=ALU.mult),
              reads=[vec], writes=[pv])
        kb.op("dve", lambda e: e.tensor_tensor(out=pv[:, 1:2], in0=vec[:, 2:3], in1=vec[:, 3:4], op=ALU.mult),
              reads=[vec], writes=[pv])
        zt = [kb.sb([33, 512], F32) for _ in range(2)]
        arg = [kb.sb([64, 512], F32) for _ in range(2)]

        def sin_layer(wt, kdim, srcfn, dst, frcol, pvcol, j, bank):
            a = arg[j % 2]
            src, srcT = srcfn(j)
            kb.op("pe", lambda e: e.matmul(kb.ps(bank)[0:64, :], lhsT=wt[0:kdim, :], rhs=src,
                                           start=True, stop=True), reads=[wt, srcT], writes=[PB[bank]])
            kb.op("dve", lambda e: e.tensor_scalar(out=a[:], in0=kb.ps(bank)[0:64, :], scalar1=vec[:, frcol:frcol + 1],
                                                   scalar2=pv[:, pvcol:pvcol + 1], op0=ALU.mult, op1=ALU.add),
                  reads=[PB[bank], vec, pv], writes=[a])
            kb.op("dve", lambda e: e.tensor_scalar(out=ni[:], in0=a[:], scalar1=1.0 / TWO_PI, scalar2=None, op0=ALU.mult),
                  reads=[a], writes=[ni])
            kb.op("dve", lambda e: e.scalar_tensor_tensor(out=a[:], in0=ni[:], scalar=-TWO_PI, in1=a[:], op0=ALU.mult, op1=ALU.add),
                  reads=[ni, a], writes=[a])
            kb.op("dve", lambda e: e.tensor_scalar(out=m1[:], in0=a[:], scalar1=math.pi, scalar2=-TWO_PI, op0=ALU.is_gt, op1=ALU.mult),
                  reads=[a], writes=[m1])
            kb.op("dve", lambda e: e.tensor_scalar(out=m2[:], in0=a[:], scalar1=-math.pi, scalar2=TWO_PI, op0=ALU.is_lt, op1=ALU.mult),
                  reads=[a], writes=[m2])
            kb.op("dve", lambda e: e.tensor_tensor(out=a[:], in0=a[:], in1=m1[:], op=ALU.add), reads=[a, m1], writes=[a])
            kb.op("dve", lambda e: e.tensor_tensor(out=a[:], in0=a[:], in1=m2[:], op=ALU.add), reads=[a, m2], writes=[a])
            kb.op("dve", lambda e: e.tensor_scalar(out=a[:], in0=a[:], scalar1=math.pi, scalar2=-math.pi, op0=ALU.min, op1=ALU.max),
                  reads=[a], writes=[a])
            kb.op("act", lambda e: e.activation(out=dst[:, j * 512:(j + 1) * 512], in_=a[:], func=AF.Sin), reads=[a], writes=[(dst, j)])

        negpi = kb.sb([128, 1], F32)
        ni = kb.sb([64, 512], F32)
        ni = ni.view(ni.t.bitcast(mybir.dt.int32))
        m1 = kb.sb([64, 512], F32)
        m2 = kb.sb([64, 512], F32)
        kb.op("dve", lambda e: e.memset(negpi[:], -math.pi), writes=[negpi])
        for j in range(NQ):
            z = zt[j % 2]
            kb.dma("sp", z[:], C["zT"][:, j * 512:(j + 1) * 512], writes=[z])
            sin_layer(w1, 33, lambda j, z=z: (z[:], z), H1, 1, 0, j, next_pb())
        for j in range(NQ):
            sin_layer(w2, 64, lambda j: (H1[:, j * 512:(j + 1) * 512], H1), H2, 3, 1, j, next_pb())
        hsds = [kb.sb([128, 2, DH], BF16) for _ in range(2)]
        dec = [kb.sb([128, DH], F32) for _ in range(2)]
        t1 = [kb.sb([128, DH], F32) for _ in range(2)]
        t2 = [kb.sb([128, DH], F32) for _ in range(2)]
        for i in range(NT):
            dc, a1, a2 = dec[i % 2], t1[i % 2], t2[i % 2]
            b0, b1 = next_pb(), next_pb()
            for half, bank in ((0, b0), (1, b1)):
                kb.op("pe", lambda e, half=half, bank=bank: e.matmul(
                    kb.ps(bank), lhsT=H2[:, i * 128:(i + 1) * 128], rhs=w3[:, half * 512:(half + 1) * 512],
                    start=True, stop=True), reads=[H2, w3], writes=[PB[bank]])
            kb.op("act", lambda e, dc=dc: e.activation(out=dc[:], in_=absd[:], func=AF.Exp, scale=tcol[:, i:i + 1]),
                  reads=[absd, tcol], writes=[dc])
            kb.op("dve", lambda e, dc=dc, a1=a1, b0=b0: e.tensor_tensor(out=a1[:], in0=kb.ps(b0), in1=dc[:], op=ALU.mult),
                  reads=[PB[b0], dc], writes=[a1])
            kb.op("dve", lambda e, dc=dc, a2=a2, b1=b1: e.tensor_tensor(out=a2[:], in0=kb.ps(b1), in1=dc[:], op=ALU.mult),
                  reads=[PB[b1], dc], writes=[a2])
            if i == 0:
                kb.op("dve", lambda e, a2=a2: e.memset(a2[0:1, :], 0.0), reads=[a2], writes=[a2])
            hsd = hsds[i % 2]
            kb.op("pool", lambda e, a1=a1, a2=a2: e.tensor_tensor(out=hsd[:, 0, :], in0=a1[:], in1=a2[:], op=ALU.add),
                  reads=[a1, a2], writes=[(hsd, 0)])
            kb.op("pool", lambda e, a1=a1, a2=a2: e.tensor_tensor(out=hsd[:, 1, :], in0=a1[:], in1=a2[:], op=ALU.subtract),
                  reads=[a1, a2], writes=[(hsd, 1)])
            kb.dma("pool", HSD.t.rearrange("s n c -> n s c")[i * 128:(i + 1) * 128, :, :], hsd[:], reads=[hsd])
        phase_begin(label="filter_s1")
        tw1 = load_tw1()
        s1b = s1_bufs()
        fft_s1(HSD[0], A1K.t[0], tw1, s1b)
        fft_s1(HSD[1], A1K.t[1], tw1, s1b)
        phase_begin(0, True, label="filter_s2")
        preload = []
        if preload_down is not None:
            wd_pre = kb.sb([128, 22, D], BF16)
            preload = [(lambda c0=c0: load_w_into(wd_pre, P["ffn_down"][preload_down], 22, c0, 128, None, c0))
                       for c0 in range(0, D, 128)]
        dftm = kb.sb([64, 3, 64], BF16)
        kb.dma("sp", dftm[:], C["dftm"][:, :, :], writes=[dftm])
        FC = min(4, HH)
        at_ = [[kb.sb([64, FC, DH], BF16) for _ in range(4)] for _ in range(2)]
        ko = [[kb.sb([64, FC, DH], BF16) for _ in range(2)] for _ in range(2)]
        nch = HH // FC
        for c_ in range(nch):
            if preload and (c_ % max(1, nch // 8) == 0 or c_ == nch - 1):
                preload.pop(0)()
                if c_ == nch - 1:
                    while preload:
                        preload.pop(0)()
            f0 = c_ * FC
            A = at_[c_ % 2]
            for q_, (sg_, ri_) in enumerate(((0, 0), (0, 1), (1, 0), (1, 1))):
                kb.dma("sp", A[q_][:], A1K.t[sg_, ri_].rearrange("f n c -> n f c")[:, f0:f0 + FC, :], writes=[A[q_]])
            kos = ko[c_ % 2]
            for fl in range(FC):
                br, bi = next_pb(), next_pb()
                kb.op("pe", lambda e: e.matmul(kb.ps(br)[0:64, :], lhsT=dftm[:, 0, :], rhs=A[0][:, fl, :], start=True, stop=False),
                      reads=[dftm, A[0]], writes=[PB[br]])
                kb.op("pe", lambda e: e.matmul(kb.ps(br)[0:64, :], lhsT=dftm[:, 1, :], rhs=A[1][:, fl, :], start=False, stop=True),
                      reads=[dftm, A[1]], writes=[PB[br]])
                kb.op("pe", lambda e: e.matmul(kb.ps(bi)[0:64, :], lhsT=dftm[:, 0, :], rhs=A[3][:, fl, :], start=True, stop=False),
                      reads=[dftm, A[3]], writes=[PB[bi]])
                kb.op("pe", lambda e: e.matmul(kb.ps(bi)[0:64, :], lhsT=dftm[:, 2, :], rhs=A[2][:, fl, :], start=False, stop=True),
                      reads=[dftm, A[2]], writes=[PB[bi]])
                kb.op("act", lambda e: e.copy(out=kos[0][:, fl, :], in_=kb.ps(br)[0:64, :]), reads=[PB[br]], writes=[(kos[0], fl)])
                kb.op("dve", lambda e: e.tensor_copy(out=kos[1][:, fl, :], in_=kb.ps(bi)[0:64, :]), reads=[PB[bi]], writes=[(kos[1], fl)])
            for ri_ in range(2):
                kb.dma("pool", KS.t[ri_, f0:f0 + FC].rearrange("f k c -> k f c"), kos[ri_][:], reads=[kos[ri_]])

    def load_tw1():
        tw1 = kb.sb([HH, 2, 64, HH], BF16)
        kb.dma("sp", tw1[:], C["tw1"][:, :, :, :], writes=[tw1])
        return tw1

    def s1_bufs():
        return ([kb.sb([HH, 8, DH], BF16) for _ in range(2)],
                [[kb.sb([HH, 8, DH], BF16) for _ in range(2)] for _ in range(2)])

    def fft_s1(src, dst, tw1, s1b):
        xv = src.rearrange("(a b) c -> a b c", b=64)
        if not hasattr(fft_s1, "bufs"):
            pass
        xt, ot = s1b
        ne = 0
        for g in range(8):
            x_ = xt[g % 2]
            kb.dma("sp", x_[:], xv[:, g * 8:(g + 1) * 8, :], writes=[x_])
            for ri_ in range(2):
                o_ = ot[g % 2][ri_]
                for nl in range(8):
                    bank = next_pb()
                    kb.op("pe", lambda e: e.matmul(kb.ps(bank)[0:HH, :], lhsT=tw1[:, ri_, g * 8 + nl, :], rhs=x_[:, nl, :],
                                                   start=True, stop=True), reads=[tw1, x_], writes=[PB[bank]])
                    if ne % 2 == 0:
                        kb.op("act", lambda e: e.copy(out=o_[:, nl, :], in_=kb.ps(bank)[0:HH, :]), reads=[PB[bank]], writes=[(o_, nl)])
                    else:
                        kb.op("dve", lambda e: e.tensor_copy(out=o_[:, nl, :], in_=kb.ps(bank)[0:HH, :]), reads=[PB[bank]], writes=[(o_, nl)])
                    ne += 1
                kb.dma("pool", dst[ri_, :, g * 8:(g + 1) * 8, :], o_[:], reads=[o_])

    def phase_norm0(xsrc):
        phase_begin()
        tmps = norm_tmp()
        xts = [kb.sb([128, D], F32) for _ in range(2)]
        for i in range(NT):
            xt = xts[i % 2]
            kb.dma("sp", xt[:], xsrc[i * 128:(i + 1) * 128, :], writes=[xt])
            norm_transpose(xt, i, tmps[i % 2])

    def phase_proj(li):
        phase_begin(0)
        gt = kb.sb([128, 8], F32)
        kb.dma("sp", gt[:], P["g_mix_pre"][li], writes=[gt])
        W = P["w_in"][li]
        ev = [kb.sb([128, 512], BF16) for _ in range(4)]
        evc = [0]

        def next_ev():
            evc[0] += 1
            return ev[evc[0] % 4]

        cw = kb.sb([128, 12, 3], F32)
        cb = kb.sb([128, 12], F32)
        kb.dma("sp", cw[:], P["hy_cw"][li], writes=[cw])
        kb.dma("sp", cb[:], P["hy_cb"][li], writes=[cb])
        raws = [kb.sb([128, L + 2], BF16) for _ in range(2)]
        for r in raws:
            kb.op("pool", lambda e, r=r: e.memset(r[:, 0:1], 0.0), writes=[(r, "h0")])
            kb.op("pool", lambda e, r=r: e.memset(r[:, L + 1:L + 2], 0.0), writes=[(r, "h1")])
        acc = kb.sb([128, L], F32)
        x1c = kb.sb([128, L], BF16)
        oc = [kb.sb([128, L], BF16) for _ in range(2)]
        tz = [kb.sb([128, 8, 128], BF16) for _ in range(2)]
        wfs = [kb.sb([128, 8, 128], F32) for _ in range(2)]
        wbs = [kb.sb([128, 8, 128], BF16) for _ in range(2)]
        wsrc = W.rearrange("(k p) c -> p k c", p=128)
        order = [(b, part) for b in range(4) for part in range(3)]

        def prep(n_):
            b_, part_ = order[n_]
            blk_ = part_ * 4 + b_
            s_ = n_ % 2
            kb.dma("sp", wfs[s_][:], wsrc[:, :, blk_ * 128:(blk_ + 1) * 128], writes=[wfs[s_]])
            kb.op("pool", lambda e: e.tensor_tensor(out=wbs[s_][:], in0=wfs[s_][:],
                                                    in1=gt[:, :, None].to_broadcast([128, 8, 128]), op=ALU.mult),
                  reads=[wfs[s_], gt], writes=[wbs[s_]])
            return wbs[s_]

        zvt_v = ZVT.t.rearrange("(i p) c -> p i c", p=128)
        TB = min(8, NT)

        def transposes(dst, b):
            for i0 in range(0, NT, TB):
                tzt = tz[(i0 // TB) % 2]
                for ii in range(TB):
                    kb.op("pe", lambda e, ii=ii: e.transpose(out=kb.ps(7, [128, 8, 128], BF16)[:, ii, :],
                                                             in_=dst[:, (i0 + ii) * 128:(i0 + ii + 1) * 128], identity=ident[:]),
                          reads=[dst, ident], writes=[PB[7]])
                kb.op("act", lambda e: e.copy(out=tzt[:, 0:TB, :], in_=kb.ps(7, [128, 8, 128], BF16)[:, 0:TB, :]),
                      reads=[PB[7]], writes=[tzt])
                kb.dma("pool", zvt_v[:, i0:i0 + TB, b * 128:(b + 1) * 128], tzt[:, 0:TB, :], reads=[tzt])

        wnext = prep(0)
        pending = None
        for n_, (b, part) in enumerate(order):
            blk = part * 4 + b
            raw = raws[n_ % 2]
            wbv = wnext
            if n_ + 1 < len(order):
                wnext = prep(n_ + 1)
            for j in range(NQ):
                bank = next_pb()
                gemm_fm(wbv, 8, 0, 128, HT, j, bank)
                kb.op("act", lambda e: e.copy(out=raw[:, 1 + j * 512:1 + (j + 1) * 512], in_=kb.ps(bank)),
                      reads=[PB[bank]], writes=[(raw, j)])
            if pending is not None:
                transposes(*pending)
                pending = None
            dst = x1c if part == 1 else oc[0 if part == 0 else 1]
            kb.op("dve", lambda e: e.tensor_scalar(out=acc[:], in0=raw[:, 0:L], scalar1=cw[:, blk, 0:1], scalar2=cb[:, blk:blk + 1],
                                                   op0=ALU.mult, op1=ALU.add), reads=[raw, cw, cb], writes=[acc])
            kb.op("dve", lambda e: e.scalar_tensor_tensor(out=acc[:], in0=raw[:, 1:L + 1], scalar=cw[:, blk, 1:2], in1=acc[:],
                                                          op0=ALU.mult, op1=ALU.add), reads=[raw, cw, acc], writes=[acc])
            kb.op("dve", lambda e: e.scalar_tensor_tensor(out=dst[:], in0=raw[:, 2:L + 2], scalar=cw[:, blk, 2:3], in1=acc[:],
                                                          op0=ALU.mult, op1=ALU.add), reads=[raw, cw, acc], writes=[dst])
            if part == 0:
                kb.dma("pool", X0fm[b * 128:(b + 1) * 128, :], dst[:], reads=[dst])
            elif part == 2:
                kb.op("dve", lambda e: e.tensor_tensor(out=dst[:], in0=dst[:], in1=x1c[:], op=ALU.mult),
                      reads=[dst, x1c], writes=[dst])
                kb.dma("pool", ZVfm[b * 128:(b + 1) * 128, :], dst[:], reads=[dst])
                pending = (dst, b)
        transposes(*pending)

        phase_begin(2)
        gt = kb.sb([128, 8], F32)
        kb.dma("sp", gt[:], P["g_mix_pre"][li], writes=[gt])
        ev = [kb.sb([128, 512], BF16) for _ in range(4)]
        def fm_group(col0, ncols, dest, func, scale=1.0):
            for c0 in range(0, ncols, 512):
                nc_ = min(512, ncols - c0)
                wbv = load_w(W, 8, col0 + c0, nc_, gt)
                for m in range(nc_ // 128):
                    for j in range(NQ):
                        bank = next_pb()
                        gemm_fm(wbv, 8, m * 128, 128, HT, j, bank)
                        o = next_ev()
                        kb.op("act", lambda e, o=o, bank=bank: e.activation(out=o[:], in_=kb.ps(bank), func=func, scale=scale),
                              reads=[PB[bank]], writes=[o])
                        r0 = c0 + m * 128
                        kb.dma("pool", dest[r0:r0 + 128, j * 512:(j + 1) * 512], o[:], reads=[o])

        def tok_group(col0, dest):
            wbv = load_w(W, 8, col0, 512, gt)
            for i in range(NT):
                bank = next_pb()
                gemm_tok(wbv, 8, 0, 512, HT, i, bank)
                o = next_ev()
                if i % 2 == 0:
                    kb.op("dve", lambda e, o=o, bank=bank: e.tensor_copy(out=o[:], in_=kb.ps(bank)), reads=[PB[bank]], writes=[o])
                else:
                    kb.op("act", lambda e, o=o, bank=bank: e.copy(out=o[:], in_=kb.ps(bank)), reads=[PB[bank]], writes=[o])
                kb.dma("pool", dest[i * 128:(i + 1) * 128, :], o[:], reads=[o])

        fm_group(1536, 512, Qfm, AF.Copy, 128.0 ** -0.5)
        fm_group(2048, 512, Kfm, AF.Copy)
        tok_group(2048, Ktok)
        tok_group(2560, Vtok)
        fm_group(3072, 512, Gfm, AF.Silu)
        wbv = load_w(W, 8, 3584, 32, gt)
        for j in range(NQ):
            bank = next_pb()
            gemm_fm(wbv, 8, 0, 32, HT, j, bank)
            kb.op("act", lambda e, j=j, bank=bank: e.copy(out=LRH[0:32, j * 512:(j + 1) * 512], in_=kb.ps(bank)[0:32, :]),
                  reads=[PB[bank]], writes=[(LRH, j)])
            kb.op("dve", lambda e, j=j, bank=bank: e.tensor_tensor(out=LRL[0:32, j * 512:(j + 1) * 512], in0=kb.ps(bank)[0:32, :],
                                                                   in1=LRH[0:32, j * 512:(j + 1) * 512], op=ALU.subtract),
                  reads=[PB[bank], (LRH, j)], writes=[(LRL, j)])
        fm_group(4128, 3 * D, GATES, AF.Sigmoid)

        phase_begin(1)
        gt = kb.sb([128, 8], F32)
        kb.dma("sp", gt[:], P["g_mix_pre"][li], writes=[gt])
        ev = [kb.sb([128, 512], BF16) for _ in range(4)]
        PW = 16
        ua = kb.sb([128, L + 2 * PW], F32)
        ub = kb.sb([128, L + 2 * PW], F32)
        uc = kb.sb([128, L + 2 * PW], F32)
        icn = kb.sb([128, L], F32)
        pwt = kb.sb([128, 4, 128], F32)
        pwb = kb.sb([128, 4, 128], BF16)
        psc = kb.sb([128, 4], F32)
        dbf = kb.sb([128, L], BF16)
        kb.dma("sp", pwt[:], P["pool_w"][li], writes=[pwt])
        kb.dma("sp", psc[:], P["pool_scale"][li], writes=[psc])
        kb.op("pool", lambda e: e.tensor_copy(out=pwb[:], in_=pwt[:]), reads=[pwt], writes=[pwb])
        for t_ in (ua, ub, uc):
            kb.op("pool", lambda e, t_=t_: e.memset(t_[:], 0.0), writes=[t_])
        for gi, wv in enumerate((2, 4, 8, 16)):
            wbv = load_w(W, 8, 3616 + gi * 128, 128, gt)
            kb.dma("sp", icn[:], C["invcnt"][gi], writes=[icn])
            for j in range(NQ):
                bank = next_pb()
                gemm_fm(wbv, 8, 0, 128, HT, j, bank)
                kb.op("act", lambda e, j=j, bank=bank: e.copy(out=ua[:, PW + j * 512:PW + (j + 1) * 512], in_=kb.ps(bank)),
                      reads=[PB[bank]], writes=[ua])
            src, dsts = ua, [ub, uc]
            lo, hi = -14, L + 14
            kb.op("dve", lambda e, lo=lo, hi=hi: e.tensor_tensor(
                out=ub[:, PW + lo:PW + hi], in0=ua[:, PW + lo - 1:PW + hi - 1], in1=ua[:, PW + lo:PW + hi], op=ALU.add),
                reads=[ua], writes=[ub])
            cur, oth = ub, uc
            sh = 1
            rng = [(-12, L + 12), (-8, L + 8), (0, L)]
            for si in range(int(math.log2(wv)) - 1):
                lo, hi = rng[si]
                kb.op("dve", lambda e, lo=lo, hi=hi, cur=cur, oth=oth, sh=sh: e.tensor_tensor(
                    out=oth[:, PW + lo:PW + hi], in0=cur[:, PW + lo - sh:PW + hi - sh],
                    in1=cur[:, PW + lo + sh:PW + hi + sh], op=ALU.add), reads=[cur], writes=[oth])
                cur, oth = oth, cur
                sh *= 2
            kb.op("dve", lambda e, cur=cur, oth=oth: e.tensor_tensor(out=oth[:, PW:PW + L], in0=cur[:, PW:PW + L], in1=icn[:], op=ALU.mult),
                  reads=[cur, icn], writes=[oth])
            kb.op("dve", lambda e, oth=oth: e.tensor_tensor(out=dbf[:], in0=oth[:, PW:PW + L], in1=ua[:, PW:PW + L], op=ALU.subtract),
                  reads=[oth, ua], writes=[dbf])
            for t_ in (ub, uc):
                kb.op("pool", lambda e, t_=t_: e.memset(t_[:, 0:PW], 0.0), reads=[t_], writes=[t_])
                kb.op("pool", lambda e, t_=t_: e.memset(t_[:, PW + L:PW + L + PW], 0.0), reads=[t_], writes=[t_])
            for j in range(NQ):
                bank = next_pb()
                kb.op("pe", lambda e, j=j, bank=bank, gi=gi: e.matmul(kb.ps(bank), lhsT=pwb[:, gi, :], rhs=dbf[:, j * 512:(j + 1) * 512],
                                                               start=True, stop=True), reads=[pwb, dbf], writes=[PB[bank]])
                o = next_ev()
                kb.op("dve", lambda e, o=o, bank=bank, gi=gi: e.tensor_scalar(out=o[:], in0=kb.ps(bank), scalar1=psc[:, gi:gi + 1],
                                                                       scalar2=None, op0=ALU.mult), reads=[PB[bank], psc], writes=[o])
                kb.dma("pool", YP[gi * 128:(gi + 1) * 128, j * 512:(j + 1) * 512], o[:], reads=[o])

    def phase_hyena(li):
        phase_begin(label="hy_s1")
        tw1 = load_tw1()
        fft_s1(ZVT.t, A1Z.t, tw1, s1_bufs())
        phase_begin(label="hy_s2")
        dftm = kb.sb([64, 3, 64], BF16)
        gtw = kb.sb([64, 3, HH, 64], BF16)
        kb.dma("sp", dftm[:], C["dftm"][:, :, :], writes=[dftm])
        kb.dma("sp", gtw[:], C["gtw"][:, :, :, :], writes=[gtw])
        FC = min(4, HH)
        at_ = [[kb.sb([64, FC, DH], BF16) for _ in range(2)] for _ in range(2)]
        kt_ = [[kb.sb([64, FC, DH], BF16) for _ in range(2)] for _ in range(2)]
        bo = [[kb.sb([64, FC, DH], BF16) for _ in range(2)] for _ in range(2)]
        m = [[kb.sb([64, DH], F32) for _ in range(4)] for _ in range(2)]
        yy = [[kb.sb([64, DH], BF16) for _ in range(2)] for _ in range(2)]
        def s2_prod(f1_):
            c_, fl = divmod(f1_, FC)
            f0 = c_ * FC
            A, K_ = at_[c_ % 2], kt_[c_ % 2]
            if fl == 0:
                for ri_ in range(2):
                    kb.dma("sp", A[ri_][:], A1Z.t[ri_].rearrange("f n c -> n f c")[:, f0:f0 + FC, :], writes=[A[ri_]])
                    kb.dma("sp", K_[ri_][:], KS.t[ri_, f0:f0 + FC].rearrange("f k c -> k f c"), writes=[K_[ri_]])
            mm_, y_ = m[f1_ % 2], yy[f1_ % 2]
            zr, zi = f1_ % 2, 2 + f1_ % 2
            kb.op("pe", lambda e: e.matmul(kb.ps(zr)[0:64, :], lhsT=dftm[:, 0, :], rhs=A[0][:, fl, :], start=True, stop=False),
                  reads=[dftm, A[0]], writes=[PB[zr]])
            kb.op("pe", lambda e: e.matmul(kb.ps(zr)[0:64, :], lhsT=dftm[:, 1, :], rhs=A[1][:, fl, :], start=False, stop=True),
                  reads=[dftm, A[1]], writes=[PB[zr]])
            kb.op("pe", lambda e: e.matmul(kb.ps(zi)[0:64, :], lhsT=dftm[:, 0, :], rhs=A[1][:, fl, :], start=True, stop=False),
                  reads=[dftm, A[1]], writes=[PB[zi]])
            kb.op("pe", lambda e: e.matmul(kb.ps(zi)[0:64, :], lhsT=dftm[:, 2, :], rhs=A[0][:, fl, :], start=False, stop=True),
                  reads=[dftm, A[0]], writes=[PB[zi]])
            kb.op("dve", lambda e: e.tensor_tensor(out=mm_[0][:], in0=kb.ps(zr)[0:64, :], in1=K_[0][:, fl, :], op=ALU.mult),
                  reads=[PB[zr], K_[0]], writes=[mm_[0]])
            kb.op("dve", lambda e: e.tensor_tensor(out=mm_[1][:], in0=kb.ps(zi)[0:64, :], in1=K_[1][:, fl, :], op=ALU.mult),
                  reads=[PB[zi], K_[1]], writes=[mm_[1]])
            kb.op("dve", lambda e: e.tensor_tensor(out=mm_[2][:], in0=kb.ps(zr)[0:64, :], in1=K_[1][:, fl, :], op=ALU.mult),
                  reads=[PB[zr], K_[1]], writes=[mm_[2]])
            kb.op("dve", lambda e: e.tensor_tensor(out=mm_[3][:], in0=kb.ps(zi)[0:64, :], in1=K_[0][:, fl, :], op=ALU.mult),
                  reads=[PB[zi], K_[0]], writes=[mm_[3]])
            kb.op("pool", lambda e: e.tensor_tensor(out=y_[0][:], in0=mm_[0][:], in1=mm_[1][:], op=ALU.subtract),
                  reads=[mm_[0], mm_[1]], writes=[y_[0]])
            kb.op("pool", lambda e: e.tensor_tensor(out=y_[1][:], in0=mm_[2][:], in1=mm_[3][:], op=ALU.add),
                  reads=[mm_[2], mm_[3]], writes=[y_[1]])

        def inv_a(f1_):
            c_, fl = divmod(f1_, FC)
            f0 = c_ * FC
            y_ = yy[f1_ % 2]
            bos = bo[c_ % 2]
            br, bi = 4 + f1_ % 2, 6 + f1_ % 2
            kb.op("pe", lambda e: e.matmul(kb.ps(br)[0:64, :], lhsT=gtw[:, 0, f1_, :], rhs=y_[0][:], start=True, stop=False),
                  reads=[gtw, y_[0]], writes=[PB[br]])
            kb.op("pe", lambda e: e.matmul(kb.ps(br)[0:64, :], lhsT=gtw[:, 2, f1_, :], rhs=y_[1][:], start=False, stop=True),
                  reads=[gtw, y_[1]], writes=[PB[br]])
            kb.op("pe", lambda e: e.matmul(kb.ps(bi)[0:64, :], lhsT=gtw[:, 1, f1_, :], rhs=y_[0][:], start=True, stop=False),
                  reads=[gtw, y_[0]], writes=[PB[bi]])
            kb.op("pe", lambda e: e.matmul(kb.ps(bi)[0:64, :], lhsT=gtw[:, 0, f1_, :], rhs=y_[1][:], start=False, stop=True),
                  reads=[gtw, y_[1]], writes=[PB[bi]])
            kb.op("act", lambda e: e.copy(out=bos[0][:, fl, :], in_=kb.ps(br)[0:64, :]), reads=[PB[br]], writes=[(bos[0], fl)])
            kb.op("act", lambda e: e.copy(out=bos[1][:, fl, :], in_=kb.ps(bi)[0:64, :]), reads=[PB[bi]], writes=[(bos[1], fl)])
            if fl == FC - 1:
                for ri_ in range(2):
                    kb.dma("pool", B1.t[ri_, :, f0:f0 + FC, :], bos[ri_][:], reads=[bos[ri_]])

        for k_ in range(HH + 1):
            if k_ < HH:
                s2_prod(k_)
            if k_ >= 1:
                inv_a(k_ - 1)
        phase_begin(label="hy_ib")
        hb = kb.sb([128, 4], F32)
        kb.dma("sp", hb[:], P["hy_bias"][li], writes=[hb])
        m4 = kb.sb([HH, 2, HH], BF16)
        kb.dma("sp", m4[:], C["m4"][:, :, :], writes=[m4])
        ysb = kb.sb([128, 4, L], BF16)
        bt_ = [[kb.sb([HH, 8, DH], BF16) for _ in range(2)] for _ in range(2)]
        for g in range(8):
            Bt = bt_[g % 2]
            for ri_ in range(2):
                kb.dma("sp", Bt[ri_][:], B1.t[ri_].rearrange("t f c -> f t c")[:, g * 8:(g + 1) * 8, :], writes=[Bt[ri_]])
            for cb_ in range(4):
                bank = next_pb()
                for tl in range(8):
                    kb.op("pe", lambda e: e.matmul(kb.ps(bank, [128, 8, HH])[:, tl, :], lhsT=Bt[0][:, tl, cb_ * 128:(cb_ + 1) * 128],
                                                   rhs=m4[:, 0, :], start=True, stop=False), reads=[Bt[0], m4], writes=[PB[bank]])
                    kb.op("pe", lambda e: e.matmul(kb.ps(bank, [128, 8, HH])[:, tl, :], lhsT=Bt[1][:, tl, cb_ * 128:(cb_ + 1) * 128],
                                                   rhs=m4[:, 1, :], start=False, stop=True), reads=[Bt[1], m4], writes=[PB[bank]])
                dst = ysb[:, cb_, :].rearrange("p (a b) -> p a b", b=64)[:, :, g * 8:(g + 1) * 8]
                src_ = kb.ps(bank, [128, 8, HH]).rearrange("p a b -> p b a")
                if (g * 4 + cb_) % 2 == 0:
                    kb.op("act", lambda e: e.copy(out=dst, in_=src_), reads=[PB[bank]], writes=[(ysb, (cb_, g))])
                else:
                    kb.op("dve", lambda e: e.tensor_copy(out=dst, in_=src_), reads=[PB[bank]], writes=[(ysb, (cb_, g))])
        x0t = [kb.sb([128, 512], BF16) for _ in range(2)]
        zvf = [kb.sb([128, 512], BF16) for _ in range(2)]
        tmp = [kb.sb([128, 512], F32) for _ in range(2)]
        yo = [kb.sb([128, 512], BF16) for _ in range(2)]
        for tt in range(NQ):
            for cb_ in range(4):
                k = (tt * 4 + cb_) % 2
                kb.dma("sp", x0t[k][:], X0fm[cb_ * 128:(cb_ + 1) * 128, tt * 512:(tt + 1) * 512], writes=[x0t[k]])
                kb.dma("sp", zvf[k][:], ZVfm[cb_ * 128:(cb_ + 1) * 128, tt * 512:(tt + 1) * 512], writes=[zvf[k]])
                kb.op("dve", lambda e: e.scalar_tensor_tensor(
                    out=tmp[k][:], in0=zvf[k][:], scalar=hb[:, cb_:cb_ + 1], in1=ysb[:, cb_, tt * 512:(tt + 1) * 512],
                    op0=ALU.mult, op1=ALU.add), reads=[zvf[k], hb, ysb], writes=[tmp[k]])
                kb.op("pool", lambda e: e.tensor_tensor(out=yo[k][:], in0=tmp[k][:], in1=x0t[k][:], op=ALU.mult),
                      reads=[tmp[k], x0t[k]], writes=[yo[k]])
                kb.dma("pool", YH[cb_ * 128:(cb_ + 1) * 128, tt * 512:(tt + 1) * 512], yo[k][:], reads=[yo[k]])
        st["pb"] = 0

    def phase_gla(li):
        phase_begin(0, True)
        wbr_pre = [kb.sb([128, 4, D], BF16) for _ in range(3)]
        preload = [(lambda b=b, c0=c0: load_w_into(wbr_pre[b], P["w_br"][li, b], 4, c0, 512, None, c0))
                   for b in range(3) for c0 in (0, 512)]
        w2x = kb.sb([33, 2, 512], F32)
        gn = kb.sb([128, 1], F32)
        kb.dma("sp", w2x[:], P["gla_w2x"][li], writes=[w2x])
        kb.dma("sp", gn[:], P["gla_norm"][li], writes=[gn])
        w2h = kb.sb([33, 2, 512], BF16)
        w2l = kb.sb([33, 2, 512], BF16)
        kb.op("dve", lambda e: e.tensor_copy(out=w2h[:], in_=w2x[:]), reads=[w2x], writes=[w2h])
        kb.op("dve", lambda e: e.tensor_tensor(out=w2l[:], in0=w2x[:], in1=w2h[:], op=ALU.subtract), reads=[w2x, w2h], writes=[w2l])
        OB = HT.view(HT.t[:, 0:4, :])
        S32 = [kb.sb([128, 4, 128], F32) for _ in range(2)]
        Sbf = [kb.sb([128, 4, 128], BF16) for _ in range(2)]
        for d_ in range(2):
            kb.op("pool", lambda e, d_=d_: e.memset(S32[d_][:], 0.0), writes=[S32[d_]])
            kb.op("pool", lambda e, d_=d_: e.memset(Sbf[d_][:], 0.0), writes=[Sbf[d_]])
        lah = [kb.sb([128, 512], BF16) for _ in range(2)]
        lal = [kb.sb([128, 512], BF16) for _ in range(2)]
        PD = lambda shape, dt: [kb.sb(shape, dt) for _ in range(2)]
        PS = lambda shape, dt: [[kb.sb(shape, dt) for _ in range(2)] for _ in range(2)]
        e1, la, edec = PD([128, 512], F32), PD([128, 512], F32), PD([128, 512], BF16)
        EK = PD([128, 4, 128], BF16)
        kt, qf, kf, ke = PD([128, 512], BF16), PD([128, 4, 128], BF16), PD([128, 4, 128], BF16), PD([128, 4, 128], BF16)
        EQ = PS([128, 4, 128], BF16)
        DEC = PS([128, 4, 1], F32)
        vt, kdec = PS([128, 512], BF16), PS([128, 512], BF16)
        qe, msk = PS([128, 4, 128], BF16), PS([128, 4, 128], BF16)
        gf, sqb, yob = PD([128, 4, 128], BF16), PD([128, 4, 128], BF16), PD([128, 4, 128], BF16)
        o32, rsd = PD([128, 4, 128], F32), PD([128, 4, 128], F32)
        qv = Qfm.t.rearrange("(h d) t -> d h t", h=4)
        kv = Kfm.t.rearrange("(h d) t -> d h t", h=4)
        gv_ = Gfm.t.rearrange("(h d) t -> d h t", h=4)
        yv = YG.t.rearrange("(h d) t -> d h t", h=4)
        V3 = [128, 4, 128]

        def tile_of(step, d_):
            return step if d_ == 0 else NT - 1 - step

        def pre(step, d_, stage):
            i = tile_of(step, d_)
            sl = slice(i * 128, (i + 1) * 128)
            p = step % 2
            bA, bB = 4 * d_, 4 * d_ + 1
            tri_fm = 0 if d_ == 0 else 2
            tri_dec = 3 if d_ == 0 else 1
            if stage == 1:
                pre1(d_, p, sl, bA)
            elif stage == 2:
                pre2(d_, p, bA, bB, tri_fm, tri_dec)
            else:
                pre3(d_, p, bA)

        def pre1(d_, p, sl, bA):
            kb.dma("sp", kt[d_][:], Ktok[sl, :], writes=[kt[d_]])
            kb.dma("sp", vt[d_][p][:], Vtok[sl, :], writes=[vt[d_][p]])
            kb.dma("sp", qf[d_][:], qv[:, :, sl], writes=[qf[d_]])
            kb.dma("sp", kf[d_][:], kv[:, :, sl], writes=[kf[d_]])
            kb.op("pe", lambda e: e.matmul(kb.ps(bA), lhsT=LRH[:, sl], rhs=w2h[:, d_, :], start=True, stop=False),
                  reads=[LRH, w2h], writes=[PB[bA]])
            kb.op("pe", lambda e: e.matmul(kb.ps(bA), lhsT=LRL[:, sl], rhs=w2h[:, d_, :], start=False, stop=False),
                  reads=[LRL, w2h], writes=[PB[bA]])
            kb.op("pe", lambda e: e.matmul(kb.ps(bA), lhsT=LRH[:, sl], rhs=w2l[:, d_, :], start=False, stop=True),
                  reads=[LRH, w2l], writes=[PB[bA]])
            kb.op("act", lambda e: e.activation(out=e1[d_][:], in_=kb.ps(bA), func=AF.Exp, scale=-1.0),
                  reads=[PB[bA]], writes=[e1[d_]])
            kb.op("act", lambda e: e.activation(out=e1[d_][:], in_=e1[d_][:], func=AF.Ln, bias=1.0),
                  reads=[e1[d_]], writes=[e1[d_]])
            kb.op("dve", lambda e: e.tensor_scalar(out=la[d_][:], in0=e1[d_][:], scalar1=-1.0 / 16.0, scalar2=-1.0,
                                                   op0=ALU.mult, op1=ALU.max), reads=[e1[d_]], writes=[la[d_]])
            kb.op("act", lambda e: e.copy(out=lah[d_][:], in_=la[d_][:]), reads=[la[d_]], writes=[lah[d_]])
            kb.op("dve", lambda e: e.tensor_tensor(out=lal[d_][:], in0=la[d_][:], in1=lah[d_][:], op=ALU.subtract),
                  reads=[la[d_], lah[d_]], writes=[lal[d_]])

        def pre2(d_, p, bA, bB, tri_fm, tri_dec):
            kb.op("pe", lambda e: e.matmul(kb.ps(bA), lhsT=tri32[:, tri_dec, :], rhs=lah[d_][:], start=True, stop=False),
                  reads=[tri32, lah[d_]], writes=[PB[bA]])
            kb.op("pe", lambda e: e.matmul(kb.ps(bA), lhsT=tri32[:, tri_dec, :], rhs=lal[d_][:], start=False, stop=True),
                  reads=[tri32, lal[d_]], writes=[PB[bA]])
            for h in range(4):
                kb.op("pe", lambda e, h=h: e.matmul(kb.ps(bB, V3)[:, h, :], lhsT=lah[d_][:, h * 128:(h + 1) * 128],
                                                    rhs=tri32[:, tri_fm, :], start=True, stop=False),
                      reads=[tri32, lah[d_]], writes=[PB[bB]])
                kb.op("pe", lambda e, h=h: e.matmul(kb.ps(bB, V3)[:, h, :], lhsT=lal[d_][:, h * 128:(h + 1) * 128],
                                                    rhs=tri32[:, tri_fm, :], start=False, stop=True),
                      reads=[tri32, lal[d_]], writes=[PB[bB]])
            kb.op("act", lambda e: e.activation(out=edec[d_][:], in_=kb.ps(bA), func=AF.Exp), reads=[PB[bA]], writes=[edec[d_]])
            kb.op("act", lambda e: e.activation(out=EQ[d_][p][:], in_=kb.ps(bB, V3), func=AF.Exp), reads=[PB[bB]], writes=[EQ[d_][p]])
            kb.op("act", lambda e: e.activation(out=EK[d_][:], in_=kb.ps(bB, V3), func=AF.Exp, scale=-1.0),
                  reads=[PB[bB]], writes=[EK[d_]])
            dcol_ = 127 if d_ == 0 else 0
            kb.op("act", lambda e: e.activation(out=DEC[d_][p][:], in_=kb.ps(bB, V3)[:, :, dcol_:dcol_ + 1], func=AF.Exp),
                  reads=[PB[bB]], writes=[DEC[d_][p]])
            kb.op("dve", lambda e: e.tensor_tensor(out=kdec[d_][p][:], in0=kt[d_][:], in1=edec[d_][:], op=ALU.mult),
                  reads=[kt[d_], edec[d_]], writes=[kdec[d_][p]])
            kb.op("dve", lambda e: e.tensor_tensor(out=qe[d_][p][:], in0=qf[d_][:], in1=EQ[d_][p][:], op=ALU.mult),
                  reads=[qf[d_], EQ[d_][p]], writes=[qe[d_][p]])
            kb.op("dve", lambda e: e.tensor_tensor(out=ke[d_][:], in0=kf[d_][:], in1=EK[d_][:], op=ALU.mult),
                  reads=[kf[d_], EK[d_]], writes=[ke[d_]])

        def pre3(d_, p, bA):
            for h in range(4):
                kb.op("pe", lambda e, h=h: e.matmul(kb.ps(bA, V3)[:, h, :], lhsT=ke[d_][:, h, :], rhs=qe[d_][p][:, h, :],
                                                    start=True, stop=True), reads=[ke[d_], qe[d_][p]], writes=[PB[bA]])
            kb.op("dve", lambda e: e.tensor_tensor(out=msk[d_][p][:], in0=kb.ps(bA, V3),
                                                   in1=tri32[:, 3 * d_:3 * d_ + 1, :].to_broadcast(V3), op=ALU.mult),
                  reads=[PB[bA], tri32], writes=[msk[d_][p]])

        def seq(step, d_):
            i = tile_of(step, d_)
            sl = slice(i * 128, (i + 1) * 128)
            p = step % 2
            bC, bD = 4 * d_ + 2, 4 * d_ + 3
            final = step >= NT // 2
            for h in range(4):
                hs = slice(h * 128, (h + 1) * 128)
                kb.op("pe", lambda e, h=h, hs=hs: e.matmul(kb.ps(bC, V3)[:, h, :], lhsT=vt[d_][p][:, hs], rhs=msk[d_][p][:, h, :],
                                                           start=True, stop=False), reads=[vt[d_][p], msk[d_][p]], writes=[PB[bC]])
                kb.op("pe", lambda e, h=h: e.matmul(kb.ps(bC, V3)[:, h, :], lhsT=Sbf[d_][:, h, :], rhs=qe[d_][p][:, h, :],
                                                    start=False, stop=True), reads=[Sbf[d_], qe[d_][p]], writes=[PB[bC]])
            for h in range(4):
                hs = slice(h * 128, (h + 1) * 128)
                kb.op("pe", lambda e, h=h, hs=hs: e.matmul(kb.ps(bD, V3)[:, h, :], lhsT=kdec[d_][p][:, hs], rhs=vt[d_][p][:, hs],
                                                           start=True, stop=True), reads=[kdec[d_][p], vt[d_][p]], writes=[PB[bD]])
            kb.op("dve", lambda e: e.tensor_tensor(out=S32[d_][:], in0=S32[d_][:],
                                                   in1=DEC[d_][p][:, :, 0:1].to_broadcast(V3), op=ALU.mult),
                  reads=[S32[d_], DEC[d_][p]], writes=[S32[d_]])
            kb.op("dve", lambda e: e.tensor_tensor(out=S32[d_][:], in0=S32[d_][:], in1=kb.ps(bD, V3), op=ALU.add),
                  reads=[S32[d_], PB[bD]], writes=[S32[d_]])
            kb.op("act", lambda e: e.copy(out=Sbf[d_][:], in_=S32[d_][:]), reads=[S32[d_]], writes=[Sbf[d_]])
            if not final:
                kb.op("act", lambda e: e.copy(out=OB[:, :, sl], in_=kb.ps(bC, V3)), reads=[PB[bC]], writes=[(OB, i)])
            else:
                kb.dma("sp", gf[d_][:], gv_[:, :, sl], writes=[gf[d_]])
                kb.op("dve", lambda e: e.tensor_tensor(out=o32[d_][:], in0=kb.ps(bC, V3), in1=OB[:, :, sl], op=ALU.add),
                      reads=[PB[bC], (OB, i)], writes=[o32[d_]])
                kb.op("act", lambda e: e.activation(out=sqb[d_][:], in_=o32[d_][:], func=AF.Square), reads=[o32[d_]], writes=[sqb[d_]])
                kb.op("pe", lambda e: e.matmul(kb.ps(bD), lhsT=ones[:], rhs=sqb[d_][:].rearrange("p a b -> p (a b)"), start=True, stop=True),
                      reads=[ones, sqb[d_]], writes=[PB[bD]])
                kb.op("act", lambda e: e.activation(out=rsd[d_][:].rearrange("p a b -> p (a b)"), in_=kb.ps(bD), func=AF.Ln,
                                                    scale=1.0 / 128.0, bias=EPS), reads=[PB[bD]], writes=[rsd[d_]])
                kb.op("act", lambda e: e.activation(out=rsd[d_][:], in_=rsd[d_][:], func=AF.Exp, scale=-0.5), reads=[rsd[d_]], writes=[rsd[d_]])
                kb.op("dve", lambda e: e.tensor_tensor(out=o32[d_][:], in0=o32[d_][:], in1=rsd[d_][:], op=ALU.mult),
                      reads=[o32[d_], rsd[d_]], writes=[o32[d_]])
                kb.op("dve", lambda e: e.scalar_tensor_tensor(out=yob[d_][:], in0=o32[d_][:], scalar=gn[:, 0:1], in1=gf[d_][:],
                                                              op0=ALU.mult, op1=ALU.mult), reads=[o32[d_], gn, gf[d_]], writes=[yob[d_]])
                kb.dma("pool", yv[:, :, sl], yob[d_][:], reads=[yob[d_]])

        for step in range(NT + 1):
            if preload and step >= 1 and (step % max(1, NT // 8) == 0 or step == NT):
                preload.pop(0)()
                if step == NT:
                    while preload:
                        preload.pop(0)()
            if step < NT:
                pre(step, 0, 1)
                pre(step, 1, 1)
            if step >= 1:
                seq(step - 1, 0)
                seq(step - 1, 1)
            if step < NT:
                pre(step, 0, 2)
                pre(step, 1, 2)
                pre(step, 0, 3)
                pre(step, 1, 3)
        st["pb"] = 0

    def load_w_into(dst, wap, kblocks, c0, ncols, gt=None, dcol0=0):
        src = wap.rearrange("(k p) c -> p k c", p=128)
        WF = st["WF"]
        kper = max(1, 4096 // ncols)
        k0 = 0
        while k0 < kblocks:
            kn = min(kper, kblocks - k0)
            wfv = WF.t[:, 0:kn * ncols].rearrange("p (k c) -> p k c", k=kn)
            kb.dma("sp", wfv, src[:, k0:k0 + kn, c0:c0 + ncols], writes=[WF])
            if gt is None:
                kb.op("pool", lambda e, wfv=wfv, k0=k0, kn=kn: e.tensor_copy(out=dst.t[:, k0:k0 + kn, dcol0:dcol0 + ncols], in_=wfv),
                      reads=[WF], writes=[(dst, (k0, dcol0))])
            else:
                kb.op("pool", lambda e, wfv=wfv, k0=k0, kn=kn: e.tensor_tensor(
                    out=dst.t[:, k0:k0 + kn, dcol0:dcol0 + ncols], in0=wfv,
                    in1=gt.t[:, k0:k0 + kn, None].to_broadcast([128, kn, ncols]), op=ALU.mult),
                    reads=[WF, gt], writes=[(dst, (k0, dcol0))])
            k0 += kn

    def post_tiles(depth=2):
        return {"yt": [kb.sb([128, D], F32) for _ in range(depth)], "xo": [kb.sb([128, D], F32) for _ in range(depth)],
                "sq": [kb.sb([128, 1], F32) for _ in range(depth)], "rs": [kb.sb([128, 1], F32) for _ in range(depth)],
                "junk": kb.sb([128, D], BF16), "nt": norm_tmp(depth), "depth": depth}

    def post_stage(stage, i, banks, xsrc, gp, xdst, pt, do_norm):
        k = i % pt["depth"]
        yt, xo, sq, rs, junk = pt["yt"][k], pt["xo"][k], pt["sq"][k], pt["rs"][k], pt["junk"]
        nt = pt["nt"][k]
        if stage == "A":
            ba, bb = banks
            kb.dma("sp", xo[:], xsrc[i * 128:(i + 1) * 128, :], writes=[xo])
            kb.op("act", lambda e: e.copy(out=yt[:, 0:512], in_=kb.ps(ba)), reads=[PB[ba]], writes=[(yt, 0)])
            kb.op("dve", lambda e: e.tensor_copy(out=yt[:, 512:1024], in_=kb.ps(bb)), reads=[PB[bb]], writes=[(yt, 1)])
            kb.op("act", lambda e: e.activation(out=junk[:], in_=yt[:], func=AF.Square, scale=1.0 / 32.0, accum_out=sq[:]),
                  reads=[yt], writes=[junk, sq])
            kb.op("act", lambda e: e.activation(out=rs[:], in_=sq[:], func=AF.Ln, bias=EPS), reads=[sq], writes=[rs])
            kb.op("act", lambda e: e.activation(out=rs[:], in_=rs[:], func=AF.Exp, scale=-0.5), reads=[rs], writes=[rs])
        elif stage == "B":
            kb.op("dve", lambda e: e.scalar_tensor_tensor(out=yt[:], in0=yt[:], scalar=rs[:, 0:1], in1=gp[:], op0=ALU.mult, op1=ALU.mult),
                  reads=[yt, rs, gp], writes=[yt])
            kb.op("dve", lambda e: e.tensor_tensor(out=xo[:], in0=xo[:], in1=yt[:], op=ALU.add), reads=[xo, yt], writes=[xo])
            kb.dma("pool", xdst[i * 128:(i + 1) * 128, :], xo[:], reads=[xo])
        elif stage == "C":
            if do_norm:
                njunk, nsq, nrs, hn = nt["junk"], nt["sq"], nt["rs"], nt["hn"]
                kb.op("act", lambda e: e.activation(out=njunk[:], in_=xo[:], func=AF.Square, scale=1.0 / 32.0, accum_out=nsq[:]),
                      reads=[xo], writes=[njunk, nsq])
                kb.op("act", lambda e: e.activation(out=nrs[:], in_=nsq[:], func=AF.Ln, bias=EPS), reads=[nsq], writes=[nrs])
                kb.op("act", lambda e: e.activation(out=nrs[:], in_=nrs[:], func=AF.Exp, scale=-0.5), reads=[nrs], writes=[nrs])
                kb.op("dve", lambda e: e.tensor_scalar(out=hn[:], in0=xo[:], scalar1=nrs[:], scalar2=None, op0=ALU.mult),
                      reads=[xo, nrs], writes=[hn])
        elif stage == "D":
            if do_norm:
                nt_b(nt["hn"], i)

    def post_loop(mm, xsrc, gp, xdst, pt, do_norm, offs):
        banks = {}
        for j in range(min(2, NT)):
            banks[j] = mm(j)
        maxo = max(offs.values())
        for it in range(NT + maxo):
            if it + 2 < NT:
                banks[it + 2] = mm(it + 2)
            for stg in ("A", "B", "C", "D"):
                i = it - offs[stg]
                if 0 <= i < NT:
                    post_stage(stg, i, banks.get(i), xsrc, gp, xdst, pt, do_norm)

    def phase_merge(li, xsrc, xdst):
        phase_begin(0, True)
        MG = HT
        wbr = [kb.sb([128, 4, D], BF16) for _ in range(3)]
        wo_pre = kb.sb([128, 8, D], BF16)
        preload = [(lambda c0=c0: load_w_into(wo_pre, P["w_out"][li], 8, c0, 512, None, c0)) for c0 in (0, 512)]
        ysrc = [Y_.t.rearrange("(k p) t -> p k t", p=128) for Y_ in (YH, YG, YP)]
        gsrc = GATES.t.rearrange("(b r) t -> r b t", b=3)
        yt_ = [[kb.sb([128, 4, 512], BF16) for _ in range(3)] for _ in range(2)]
        gt_ = [kb.sb([128, 3, 512], BF16) for _ in range(2)]
        mm = [[kb.sb([128, 512], BF16) for _ in range(3)] for _ in range(2)]
        ng = 0
        for tt in range(NQ):
            if preload and (tt >= 1 or NQ == 1):
                preload.pop(0)()
                if tt == NQ - 1:
                    while preload:
                        preload.pop(0)()
            ys = yt_[tt % 2]
            for b in range(3):
                kb.dma("sp", ys[b][:], ysrc[b][:, :, tt * 512:(tt + 1) * 512], writes=[ys[b]])
            for db in range(8):
                g = gt_[ng % 2]
                m_ = mm[ng % 2]
                ng += 1
                kb.dma("sp", g[:], gsrc[db * 128:(db + 1) * 128, :, tt * 512:(tt + 1) * 512], writes=[g])
                banks = [next_pb(), next_pb(), next_pb()]
                for b in range(3):
                    for k in range(4):
                        kb.op("pe", lambda e, b=b, k=k, db=db, ys=ys, banks=banks: e.matmul(
                            kb.ps(banks[b]), lhsT=wbr[b][:, k, db * 128:(db + 1) * 128], rhs=ys[b][:, k, :],
                            start=(k == 0), stop=(k == 3)), reads=[wbr[b], ys[b]], writes=[PB[banks[b]]])
                for b in range(3):
                    kb.op("dve", lambda e, b=b, g=g, m_=m_, banks=banks: e.tensor_tensor(
                        out=m_[b][:], in0=kb.ps(banks[b]), in1=g[:, b, :], op=ALU.mult), reads=[PB[banks[b]], g], writes=[m_[b]])
                kb.op("dve", lambda e, m_=m_: e.tensor_tensor(out=m_[0][:], in0=m_[0][:], in1=m_[1][:], op=ALU.add),
                      reads=[m_[0], m_[1]], writes=[m_[0]])
                kb.op("dve", lambda e, m_=m_, db=db, tt=tt: e.tensor_tensor(
                    out=MG[:, db, tt * 512:(tt + 1) * 512], in0=m_[0][:], in1=m_[2][:], op=ALU.add),
                    reads=[m_[0], m_[2]], writes=[(MG, 4 * tt), (MG, 4 * tt + 1), (MG, 4 * tt + 2), (MG, 4 * tt + 3)])
        phase_begin(0, True)
        _pad = [kb.sb([128, 4, D], BF16) for _ in range(3)]
        wo = kb.sb([128, 8, D], BF16)
        gp = kb.sb([128, D], F32)
        kb.dma("sp", gp[:], P["g_mix_post"][li], writes=[gp])
        pt = post_tiles(4)

        def mm(i):
            ba, bb = next_pb(), next_pb()
            gemm_tok(wo, 8, 0, 512, MG, i, ba, srckey=i)
            gemm_tok(wo, 8, 512, 512, MG, i, bb, srckey=i)
            return ba, bb

        post_loop(mm, xsrc, gp, xdst, pt, True, {"A": 0, "B": 1, "C": 2, "D": 3})

    def phase_ffn_up(li):
        phase_begin(0)
        gt = kb.sb([128, 8], F32)
        kb.dma("sp", gt[:], P["g_ffn_pre"][li], writes=[gt])
        cw = kb.sb([128, 22, 3], F32)
        cb = kb.sb([128, 22], F32)
        kb.dma("sp", cw[:], P["ffn_cw"][li], writes=[cw])
        kb.dma("sp", cb[:], P["ffn_cb"][li], writes=[cb])
        raws = [kb.sb([128, L + 2], BF16) for _ in range(2)]
        for r in raws:
            kb.op("pool", lambda e, r=r: e.memset(r[:, 0:1], 0.0), writes=[(r, "h0")])
            kb.op("pool", lambda e, r=r: e.memset(r[:, L + 1:L + 2], 0.0), writes=[(r, "h1")])
        bts = [kb.sb([128, L], BF16) for _ in range(2)]
        gas = [kb.sb([128, L], BF16) for _ in range(2)]
        acc = kb.sb([128, L], F32)
        wfs = [kb.sb([128, 8, 256], F32) for _ in range(2)]
        wbs = [kb.sb([128, 8, 256], BF16) for _ in range(2)]
        wsrc = P["ffn_up"][li].rearrange("(k p) c -> p k c", p=128)

        def prep(m):
            s_ = m % 2
            kb.dma("sp", wfs[s_][:, :, 0:128], wsrc[:, :, m * 128:(m + 1) * 128], writes=[(wfs[s_], 0)])
            kb.dma("sp", wfs[s_][:, :, 128:256], wsrc[:, :, DFF + m * 128:DFF + (m + 1) * 128], writes=[(wfs[s_], 1)])
            kb.op("pool", lambda e: e.tensor_tensor(out=wbs[s_][:], in0=wfs[s_][:],
                                                    in1=gt[:, :, None].to_broadcast([128, 8, 256]), op=ALU.mult),
                  reads=[wfs[s_], gt], writes=[wbs[s_]])
            return wbs[s_]

        wnext = prep(0)
        for m in range(22):
            raw, bt, ga = raws[m % 2], bts[m % 2], gas[m % 2]
            wcur = wnext
            if m + 1 < 22:
                wnext = prep(m + 1)
            for j in range(NQ):
                b1_, b2_ = next_pb(), next_pb()
                gemm_fm(wcur, 8, 0, 128, HT, j, b1_)
                gemm_fm(wcur, 8, 128, 128, HT, j, b2_)
                kb.op("act", lambda e: e.copy(out=raw[:, 1 + j * 512:1 + (j + 1) * 512], in_=kb.ps(b1_)),
                      reads=[PB[b1_]], writes=[(raw, j)])
                kb.op("act", lambda e: e.copy(out=bt[:, j * 512:(j + 1) * 512], in_=kb.ps(b2_)),
                      reads=[PB[b2_]], writes=[(bt, j)])
            kb.op("dve", lambda e: e.tensor_scalar(out=acc[:], in0=raw[:, 0:L], scalar1=cw[:, m, 0:1], scalar2=cb[:, m:m + 1],
                                                   op0=ALU.mult, op1=ALU.add), reads=[raw, cw, cb], writes=[acc])
            kb.op("dve", lambda e: e.scalar_tensor_tensor(out=acc[:], in0=raw[:, 1:L + 1], scalar=cw[:, m, 1:2], in1=acc[:],
                                                          op0=ALU.mult, op1=ALU.add), reads=[raw, cw, acc], writes=[acc])
            kb.op("dve", lambda e: e.scalar_tensor_tensor(out=acc[:], in0=raw[:, 2:L + 2], scalar=cw[:, m, 2:3], in1=acc[:],
                                                          op0=ALU.mult, op1=ALU.add), reads=[raw, cw, acc], writes=[acc])
            kb.op("act", lambda e: e.activation(out=ga[:], in_=acc[:], func=AF.Gelu), reads=[acc], writes=[ga])
            kb.op("dve", lambda e: e.tensor_tensor(out=ga[:], in0=ga[:], in1=bt[:], op=ALU.mult), reads=[ga, bt], writes=[ga])
            kb.dma("pool", ACTfm[m * 128:(m + 1) * 128, :], ga[:], reads=[ga])

    def phase_ffn_down(li, xsrc, xdst, do_norm, preloaded=False):
        phase_begin(0, True)
        wd = kb.sb([128, 22, D], BF16)
        gp = kb.sb([128, D], F32)
        kb.dma("sp", gp[:], P["g_ffn_post"][li], writes=[gp])
        if not preloaded:
            for c0 in range(0, D, 128):
                load_w_into(wd, P["ffn_down"][li], 22, c0, 128, None, c0)
        asrc = ACTfm.t.rearrange("(k p) t -> p k t", p=128)
        at = [kb.sb([128, 22, 256], BF16) for _ in range(2)]
        pt = post_tiles()
        def load_a(c_):
            a = at[c_ % 2]
            kb.dma("sp", a[:, 0:11, :], asrc[:, 0:11, c_ * 256:(c_ + 1) * 256], writes=[(a, 0)])
            kb.dma("sp", a[:, 11:22, :], asrc[:, 11:22, c_ * 256:(c_ + 1) * 256], writes=[(a, 1)])

        load_a(0)

        def mm(i):
            if i % 2 == 0 and i // 2 + 1 < NT // 2:
                load_a(i // 2 + 1)
            a, ii = at[(i // 2) % 2], i % 2
            ba, bb = next_pb(), next_pb()
            for c0, bank in ((0, ba), (512, bb)):
                for k in range(22):
                    kb.op("pe", lambda e, k=k, c0=c0, bank=bank: e.matmul(
                        kb.ps(bank), lhsT=a[:, k, ii * 128:(ii + 1) * 128], rhs=wd[:, k, c0:c0 + 512],
                        start=(k == 0), stop=(k == 21)), reads=[a, wd], writes=[PB[bank]])
            return ba, bb

        post_loop(mm, xsrc, gp, xdst, pt, do_norm, {"A": 0, "B": 0, "C": 1, "D": 2})

    phase_filter(0)
    phase_norm0(x_in)
    for li in range(depth):
        xs = x_in if li == 0 else XB
        phase_proj(li)
        phase_hyena(li)
        phase_gla(li)
        phase_merge(li, xs, XA)
        phase_ffn_up(li)
        lastl = (li == depth - 1)
        if not lastl:
            phase_filter(li + 1, preload_down=li)
        phase_ffn_down(li, XA, out if lastl else XB, not lastl, preloaded=not lastl)
    nc = kb.finish()
    return nc, kb


_CACHE = {}


def _in_maps(inputs, L, depth, nb):
    consts = make_consts(L)
    params = relayout_params(inputs, depth)
    x = np.asarray(inputs["x"], np.float32)
    maps = []
    for b in range(nb):
        m = {"x": np.ascontiguousarray(x[b])}
        m.update(consts)
        m.update(params)
        maps.append(m)
    return maps


def kernel(**inputs):
    x = np.asarray(inputs["x"])
    B, L, _ = x.shape
    depth = int(np.asarray(inputs["w_in"]).shape[0])
    nc, _ = build(L, depth)
    maps = _in_maps(inputs, L, depth, B)
    res = run_bass_kernel_spmd(nc, maps, core_ids=list(range(B)))
    return np.stack([np.asarray(r["out"], np.float32) for r in res.results], 0)
```

```python
import contextlib
import math
import numpy as np
import ml_dtypes
import concourse.bass as bass
import concourse.mybir as mybir
from concourse.bass_utils import run_bass_kernel_spmd

F32 = mybir.dt.float32
BF16 = mybir.dt.bfloat16
AF = mybir.ActivationFunctionType
ALU = mybir.AluOpType
NDMA_SEM = 8

D = 1024
DH = 512
DIN = 7200
DFF = 2816
EPS = 1e-6


class T:
    def __init__(self, name, ap, parent=None):
        self.name = name
        self.t = ap
        if parent is None:
            self.w = {}
            self.r = {}
            self.root = self
        else:
            self.root = parent.root

    def __getitem__(self, idx):
        return self.t[idx]

    def view(self, ap):
        return T(self.name, ap, parent=self)


class Op:
    __slots__ = ("eng", "fn", "deps", "marked", "val", "sem", "isdma")

    def __init__(self, eng, fn):
        self.eng = eng
        self.fn = fn
        self.deps = []
        self.marked = False
        self.val = 0
        self.sem = None
        self.isdma = False


class _Rec:
    def __getattr__(self, name):
        def f(*a, **k):
            self.call = (name, a, k)
        return f


class KB:
    def __init__(self, sb_bytes):
        self.nc = bass.Bass("TRN2", target_bir_lowering=False)
        self.es = contextlib.ExitStack()
        self.ops = []
        nc = self.nc
        self.handles = {"pe": nc.tensor, "act": nc.scalar, "dve": nc.vector,
                        "pool": nc.gpsimd, "sp": nc.sync}
        self.sems = {e: self.es.enter_context(nc.semaphore("s_" + e)) for e in self.handles}
        self.dq = {}
        for q in ("sp", "act", "pool"):
            self.dq[q] = {"sems": [self.es.enter_context(nc.semaphore("d_%s%d" % (q, i)))
                                   for i in range(NDMA_SEM)],
                          "n": 0, "last": [None] * NDMA_SEM, "cnt": [0] * NDMA_SEM}
        self.last = {}
        self.arena = self.es.enter_context(nc.sbuf_tensor("arena", [128, sb_bytes // 2], BF16))
        self.sb_bytes = sb_bytes
        self.top = 0
        self.nbuf = 0
        self.pbanks = [self.es.enter_context(nc.psum_tensor("pb%d" % i, [128, 512], F32))
                       for i in range(8)]

    def sb(self, shape, dt, name=None):
        esz = 4 if dt == F32 else 2
        n = int(np.prod(shape[1:])) * esz
        n = (n + 63) // 64 * 64
        off = self.top
        self.top += n
        assert self.top <= self.sb_bytes, "SBUF arena overflow %d" % self.top
        ap = self.arena[:, off // 2:(off + n) // 2]
        if dt == F32:
            ap = ap.bitcast(F32)
        ap = ap[:, 0:int(np.prod(shape[1:]))]
        if len(shape) == 3:
            ap = ap.rearrange("p (a b) -> p a b", a=shape[1])
        elif len(shape) == 4:
            ap = ap.rearrange("p (a b c) -> p a b c", a=shape[1], b=shape[2])
        if shape[0] < 128:
            ap = ap[0:shape[0]]
        self.nbuf += 1
        return T(name or "sb%d" % self.nbuf, ap)

    def ps(self, bank, shape=None, dt=F32):
        ap = self.pbanks[bank][:]
        if dt == BF16:
            ap = ap.bitcast(BF16)
        if shape is not None and len(shape) == 3:
            ap = ap[:, 0:shape[1] * shape[2]].rearrange("p (a b) -> p a b", a=shape[1])
        elif shape is not None:
            ap = ap[:, 0:shape[1]]
        if shape is not None and shape[0] < 128:
            ap = ap[0:shape[0]]
        return ap

    def psT(self, bank):
        if not hasattr(self, "_pst"):
            self._pst = [T("pbank%d" % i, self.pbanks[i][:]) for i in range(8)]
        return self._pst[bank]

    def dram(self, name, shape, dt, kind="Internal"):
        return T(name, self.nc.dram_tensor(name, list(shape), dt, kind=kind).ap())

    @staticmethod
    def _norm(lst):
        out = []
        for x in lst:
            if isinstance(x, tuple):
                out.append((x[0].root, x[1]))
            else:
                out.append((x.root, None))
        return out

    def _hazards(self, op, reads, writes):
        deps = op.deps
        for t, key in reads:
            if key is None:
                deps.extend(t.w.values())
            else:
                for k in (key, None):
                    p = t.w.get(k)
                    if p is not None:
                        deps.append(p)
        for t, key in writes:
            if key is None:
                deps.extend(t.w.values())
                for l in t.r.values():
                    deps.extend(x for x in l if x.isdma or op.isdma or x.eng != op.eng or op.eng != 'pe')
            else:
                for k in (key, None):
                    p = t.w.get(k)
                    if p is not None:
                        deps.append(p)
                    deps.extend(x for x in t.r.get(k, ()) if x.isdma or op.isdma or x.eng != op.eng or op.eng != 'pe')
        for t, key in reads:
            l = t.r.setdefault(key, [])
            if not op.isdma:
                l[:] = [o for o in l if o.eng != op.eng or o.isdma]
            l.append(op)
        for t, key in writes:
            if key is None:
                t.w = {None: op}
                t.r = {}
            else:
                t.w[key] = op
                t.r[key] = []

    def op(self, eng, fn, reads=(), writes=()):
        rec = _Rec()
        fn(rec)
        name, a, k = rec.call
        o = Op(eng, lambda h: getattr(h, name)(*a, **k))
        self._hazards(o, self._norm(reads), self._norm(writes))
        if eng == "pe":
            o.deps = [d for d in o.deps if not (d.eng == "pe" and not d.isdma)]
        self.ops.append(o)
        self.last[eng] = o
        return o

    def dma(self, q, out, in_, reads=(), writes=()):
        o = Op(q, lambda e: e.dma_start(out=out, in_=in_))
        o.isdma = True
        dq = self.dq[q]
        i = dq["n"] % NDMA_SEM
        dq["n"] += 1
        if dq["last"][i] is not None:
            o.deps.append(dq["last"][i])
        dq["last"][i] = o
        dq["cnt"][i] += 16
        o.sem = dq["sems"][i]
        o.val = dq["cnt"][i]
        self._hazards(o, self._norm(reads), self._norm(writes))
        self.ops.append(o)
        return o

    def mark(self, label):
        o = Op("sp", None)
        o.sem = label
        o.marked = "label"
        self.ops.append(o)

    def barrier(self):
        deps = list(self.last.values())
        for q in self.dq.values():
            deps.extend(x for x in q["last"] if x is not None)
        for e in self.handles:
            o = Op(e, None)
            o.deps = list(deps)
            self.ops.append(o)

    def finish(self):
        self.barrier()
        for o in self.ops:
            for d in o.deps:
                if not d.isdma:
                    d.marked = True
        cnt = {e: 0 for e in self.handles}
        for o in self.ops:
            if not o.isdma and o.marked is True:
                cnt[o.eng] += 1
                o.val = cnt[o.eng]
                o.sem = self.sems[o.eng]
        seen = {e: {} for e in self.handles}
        nwait = 0
        self.marks = []
        for o in self.ops:
            if o.marked == "label":
                self.marks.append((o.sem, self.nc.get_next_instruction_name()))
                continue
            h = self.handles[o.eng]
            sn = seen[o.eng]
            for d in o.deps:
                k = id(d.sem)
                if sn.get(k, 0) >= d.val:
                    continue
                h.wait_ge(d.sem, d.val)
                nwait += 1
                sn[k] = d.val
            if o.fn is None:
                continue
            ins = o.fn(h)
            if o.isdma:
                ins.then_inc(o.sem, 16)
            elif o.marked is True:
                ins.then_inc(o.sem, 1)
        self.stats = {"ops": len(self.ops), "waits": nwait, "marked": dict(cnt)}
        return self.nc


def make_consts(L):
    bf = ml_dtypes.bfloat16
    c = {}
    c["ident"] = np.eye(128, dtype=np.float32).astype(bf)
    c["ones"] = np.ones((128, 128), np.float32).astype(bf)
    a = np.arange(128)
    uti = (a[:, None] <= a[None, :]).astype(np.float32)
    uts = (a[:, None] < a[None, :]).astype(np.float32)
    lti = (a[:, None] >= a[None, :]).astype(np.float32)
    lts = (a[:, None] > a[None, :]).astype(np.float32)
    c["tri32"] = np.stack([uti, uts, lti, lts], 1).astype(bf)
    t = np.linspace(0.0, 1.0, L, dtype=np.float32)
    bands = 16
    w = (2.0 * np.float32(math.pi) * np.arange(L, dtype=np.float32) / np.float32(L)).astype(np.float32)
    f = np.linspace(1e-4, bands - 1, bands, dtype=np.float32)
    ang = (f[None, :] * w[:, None]).astype(np.float32)
    z = np.concatenate([t[:, None], np.cos(ang), -np.sin(ang)], -1).astype(np.float32)
    c["zT"] = np.ascontiguousarray(z.T)
    c["tcol"] = np.ascontiguousarray(-t.reshape(L // 128, 128).T)
    max_decay = math.log(1e-2) / 0.3
    min_decay = math.log(1e-2) / 1.5
    deltas = np.abs(np.linspace(min_decay, max_decay, DH, dtype=np.float32))
    c["absd"] = np.ascontiguousarray(np.broadcast_to(deltas[None, :], (128, DH))).astype(np.float32)
    N = 2 * L
    N1 = N // 64
    H = N1 // 2
    n1 = np.arange(H, dtype=np.float64)[:, None, None]
    n2 = np.arange(64, dtype=np.float64)[None, :, None]
    f1 = np.arange(H, dtype=np.float64)[None, None, :]
    al = 2 * np.pi * ((f1 + 0.5) * n1 / N1 + (f1 + 0.5) * n2 / N)
    c["tw1"] = np.ascontiguousarray(np.stack([np.cos(al), -np.sin(al)], 1)).astype(bf)
    a64 = np.arange(64, dtype=np.float64)
    be = 2 * np.pi * np.outer(a64, a64) / 64
    c["dftm"] = np.ascontiguousarray(np.stack([np.cos(be), np.sin(be), -np.sin(be)], 1)).astype(bf)
    f2 = a64[:, None, None]
    f1b = np.arange(H, dtype=np.float64)[None, :, None]
    t2 = a64[None, None, :]
    ga = 2 * np.pi * (f2 * t2 / 64 + (f1b + 0.5) * t2 / N)
    c["gtw"] = np.ascontiguousarray(np.stack([np.cos(ga), np.sin(ga), -np.sin(ga)], 1)).astype(bf)
    f1c = np.arange(H, dtype=np.float64)[:, None]
    t1 = np.arange(H, dtype=np.float64)[None, :]
    ph = 2 * np.pi * (f1c + 0.5) * t1 / N1
    c["m4"] = np.ascontiguousarray(np.stack([(2.0 / N) * np.cos(ph), -(2.0 / N) * np.sin(ph)], 1)).astype(bf)
    pos = np.arange(L)
    inv = []
    for wv in (2, 4, 8, 16):
        half = wv // 2
        cntv = (np.minimum(pos + half, L) - np.maximum(pos - half, 0)).astype(np.float32)
        inv.append(1.0 / cntv)
    c["invcnt"] = np.ascontiguousarray(
        np.broadcast_to(np.stack(inv, 0)[:, None, :], (4, 128, L))).astype(np.float32)
    return c


def relayout_params(p, depth):
    f = np.float32
    o = {}

    def pk(v, nb):
        return np.ascontiguousarray(np.asarray(v, f).reshape(nb, 128).T)

    o["g_mix_pre"] = np.stack([pk(p["norm_mix_pre"][i], 8) for i in range(depth)])
    o["g_ffn_pre"] = np.stack([pk(p["norm_ffn_pre"][i], 8) for i in range(depth)])
    o["g_mix_post"] = np.ascontiguousarray(np.broadcast_to(
        np.asarray(p["norm_mix_post"], f)[:depth, None, :], (depth, 128, D)))
    o["g_ffn_post"] = np.ascontiguousarray(np.broadcast_to(
        np.asarray(p["norm_ffn_post"], f)[:depth, None, :], (depth, 128, D)))
    o["w_in"] = np.asarray(p["w_in"], f)[:depth]
    cw = np.asarray(p["hy_conv_w"], f)[:depth]
    o["hy_cw"] = np.ascontiguousarray(cw.reshape(depth, 3, 12, 128).transpose(0, 3, 2, 1))
    o["hy_cb"] = np.stack([pk(p["hy_conv_b"][i], 12) for i in range(depth)])
    o["hy_w1"] = np.asarray(p["hy_filt_w1"], f)[:depth]
    o["hy_w2"] = np.asarray(p["hy_filt_w2"], f)[:depth]
    o["hy_w3"] = np.asarray(p["hy_filt_w3"], f)[:depth]
    vec = np.stack([np.asarray(p["hy_filt_b1"], f)[:depth], np.asarray(p["hy_filt_freq1"], f)[:depth],
                    np.asarray(p["hy_filt_b2"], f)[:depth], np.asarray(p["hy_filt_freq2"], f)[:depth]], -1)
    o["hy_vec"] = np.ascontiguousarray(vec)
    o["hy_bias"] = np.stack([pk(p["hy_bias"][i], 4) for i in range(depth)])
    w2 = np.asarray(p["gla_gate_w2"], f)[:depth]
    gb = np.asarray(p["gla_gate_b"], f)[:depth]
    w2x = np.zeros((depth, 2, 33, 512), f)
    w2x[:, 0, 0:16] = w2[:, 0]
    w2x[:, 1, 16:32] = w2[:, 1]
    w2x[:, :, 32] = gb
    o["gla_w2x"] = np.ascontiguousarray(w2x.transpose(0, 2, 1, 3))
    o["gla_norm"] = np.asarray(p["gla_norm"], f)[:depth].reshape(depth, 128, 1)
    o["pool_w"] = np.ascontiguousarray(np.asarray(p["pool_w"], f)[:depth].transpose(0, 2, 1, 3))
    o["pool_scale"] = np.stack([pk(p["pool_scale"][i], 4) for i in range(depth)])
    o["w_br"] = np.ascontiguousarray(np.stack(
        [np.asarray(p["w_br_hyena"], f)[:depth], np.asarray(p["w_br_gla"], f)[:depth],
         np.asarray(p["w_br_pool"], f)[:depth]], 1))
    o["w_out"] = np.asarray(p["w_out"], f)[:depth]
    o["ffn_up"] = np.asarray(p["ffn_w_up"], f)[:depth]
    fw = np.asarray(p["ffn_conv_w"], f)[:depth]
    o["ffn_cw"] = np.ascontiguousarray(fw.reshape(depth, 3, 22, 128).transpose(0, 3, 2, 1))
    o["ffn_cb"] = np.stack([pk(p["ffn_conv_b"][i], 22) for i in range(depth)])
    o["ffn_down"] = np.asarray(p["ffn_w_down"], f)[:depth]
    return o


def build(L, depth, dbg=()):
    NT = L // 128
    NQ = L // 512
    NFB = 2 * NT
    HH = (2 * L // 64) // 2
    kb = KB(sb_bytes=200 * 1024)

    def din(name, shape, dt=F32):
        return kb.dram(name, shape, dt, kind="ExternalInput")

    def scr(name, shape, dt=BF16):
        return kb.dram(name, shape, dt, kind=("ExternalOutput" if name in dbg else "Internal"))

    x_in = din("x", [L, D])
    C = {k: din(k, list(v.shape), BF16 if v.dtype != np.float32 else F32)
         for k, v in make_consts(L if L <= 512 else 128 * 4).items()} if False else None
    cshapes = {"ident": ([128, 128], BF16), "ones": ([128, 128], BF16), "tri32": ([128, 4, 128], BF16), "zT": ([33, L], F32), "tcol": ([128, NT], F32),
               "absd": ([128, DH], F32), "tw1": ([HH, 2, 64, HH], BF16), "dftm": ([64, 3, 64], BF16),
               "gtw": ([64, 3, HH, 64], BF16), "m4": ([HH, 2, HH], BF16),
               "invcnt": ([4, 128, L], F32)}
    C = {k: din(k, s, dt) for k, (s, dt) in cshapes.items()}
    n = depth
    pshapes = {"g_mix_pre": [n, 128, 8], "g_ffn_pre": [n, 128, 8], "g_mix_post": [n, 128, D],
               "g_ffn_post": [n, 128, D], "w_in": [n, D, DIN], "hy_cw": [n, 128, 12, 3],
               "hy_cb": [n, 128, 12], "hy_w1": [n, 33, 64], "hy_w2": [n, 64, 64], "hy_w3": [n, 64, 1024],
               "hy_vec": [n, 64, 4], "hy_bias": [n, 128, 4], "gla_w2x": [n, 33, 2, 512],
               "gla_norm": [n, 128, 1], "pool_w": [n, 128, 4, 128], "pool_scale": [n, 128, 4],
               "w_br": [n, 3, DH, D], "w_out": [n, D, D], "ffn_up": [n, D, 2 * DFF],
               "ffn_cw": [n, 128, 22, 3], "ffn_cb": [n, 128, 22], "ffn_down": [n, DFF, D]}
    P = {k: din(k, s) for k, s in pshapes.items()}
    out = kb.dram("out", [L, D], F32, kind="ExternalOutput")

    XA = scr("XA", [L, D], F32)
    XB = scr("XB", [L, D], F32)
    X0fm = scr("X0fm", [DH, L])
    ZVfm = scr("ZVfm", [DH, L])
    ZVT = scr("ZVT", [L, DH])
    HSD = scr("HSD", [2, L, DH])
    A1Z = scr("A1Z", [2, HH, 64, DH])
    A1K = scr("A1K", [2, 2, HH, 64, DH])
    KS = scr("KS", [2, HH, 64, DH])
    B1 = scr("B1", [2, 64, HH, DH])
    Qfm = scr("Qfm", [DH, L])
    Kfm = scr("Kfm", [DH, L])
    Gfm = scr("Gfm", [DH, L])
    Ktok = scr("Ktok", [L, DH])
    Vtok = scr("Vtok", [L, DH])
    YH = scr("YH", [DH, L])
    YG = scr("YG", [DH, L])
    YP = scr("YP", [DH, L])
    GATES = scr("GATES", [3 * D, L])
    ACTfm = scr("ACTfm", [DFF, L])

    HT = kb.sb([128, 8, L], BF16, "HT")
    ident = kb.sb([128, 128], BF16, "ident")
    ones = kb.sb([128, 128], BF16, "ones")
    tri32 = kb.sb([128, 4, 128], BF16, "tri32")
    LRH = kb.sb([33, L], BF16, "LRH")
    LRL = kb.sb([33, L], BF16, "LRL")
    kb.dma("sp", ident[:], C["ident"][:, :], writes=[ident])
    kb.dma("sp", ones[:], C["ones"][:, :], writes=[ones])
    kb.dma("sp", tri32[:], C["tri32"][:, :, :], writes=[tri32])
    kb.op("dve", lambda e: e.memset(LRH[32:33, :], 1.0), writes=[LRH])
    kb.op("dve", lambda e: e.memset(LRL[32:33, :], 0.0), writes=[LRL])
    base_top = kb.top
    PB = [kb.psT(i) for i in range(8)]
    st = {"wb": 0, "pb": 0}

    def dump(name, t_, shape, dt=F32):
        if name in dbg:
            d_ = kb.dram(name, shape, dt, kind="ExternalOutput")
            kb.dma("sp", d_.t, t_.t, reads=[t_])

    def phase_begin(nwb=0, wf=False, label=None):
        kb.barrier()
        import inspect
        kb.mark(label or inspect.stack()[1].function + ":%d" % inspect.stack()[1].lineno)
        kb.top = base_top
        if wf or nwb:
            st["WF"] = kb.sb([128, 4096], F32, "WF")
        st["WB"] = [kb.sb([128, 6144], BF16, "WB%d" % i) for i in range(nwb)]

    def load_w(wap, kblocks, c0, ncols, gt=None):
        i = st["wb"] % len(st["WB"])
        st["wb"] += 1
        wb = st["WB"][i]
        WF = st["WF"]
        wbv = wb.view(wb.t[:, 0:kblocks * ncols].rearrange("p (k c) -> p k c", k=kblocks))
        kper = max(1, 4096 // ncols)
        src = wap.rearrange("(k p) c -> p k c", p=128)
        k0 = 0
        while k0 < kblocks:
            kn = min(kper, kblocks - k0)
            wfv = WF.t[:, 0:kn * ncols].rearrange("p (k c) -> p k c", k=kn)
            kb.dma("sp", wfv, src[:, k0:k0 + kn, c0:c0 + ncols], writes=[WF])
            if gt is None:
                kb.op("pool", lambda e, wfv=wfv, k0=k0, kn=kn: e.tensor_copy(out=wbv.t[:, k0:k0 + kn, :], in_=wfv),
                      reads=[WF], writes=[(wb, k0)])
            else:
                kb.op("pool", lambda e, wfv=wfv, k0=k0, kn=kn: e.tensor_tensor(
                    out=wbv.t[:, k0:k0 + kn, :], in0=wfv,
                    in1=gt.t[:, k0:k0 + kn, None].to_broadcast([128, kn, ncols]), op=ALU.mult),
                    reads=[WF, gt], writes=[(wb, k0)])
            k0 += kn
        return wbv

    def next_pb(nb=6):
        b = st["pb"] % nb
        st["pb"] += 1
        return b

    def gemm_fm(wbv, kblocks, mcol, mw, src, j, bank):
        for k in range(kblocks):
            kb.op("pe", lambda e, k=k: e.matmul(kb.ps(bank)[0:mw, :], lhsT=wbv.t[:, k, mcol:mcol + mw],
                                                rhs=src.t[:, k, j * 512:(j + 1) * 512],
                                                start=(k == 0), stop=(k == kblocks - 1)),
                  reads=[wbv, src], writes=[PB[bank]])

    def gemm_tok(wbv, kblocks, c0, ncols, src, i, bank, srckey=None):
        for k in range(kblocks):
            kb.op("pe", lambda e, k=k: e.matmul(kb.ps(bank)[:, 0:ncols], lhsT=src.t[:, k, i * 128:(i + 1) * 128],
                                                rhs=wbv.t[:, k, c0:c0 + ncols],
                                                start=(k == 0), stop=(k == kblocks - 1)),
                  reads=[wbv, (src, srckey) if srckey is not None else src], writes=[PB[bank]])

    def nt_b(hn, i):
        for k in range(8):
            kb.op("pe", lambda e, k=k: e.transpose(out=kb.ps(7, [128, 8, 128], BF16)[:, k, :],
                                                   in_=hn[:, k * 128:(k + 1) * 128], identity=ident[:]),
                  reads=[hn, ident], writes=[PB[7]])
        kb.op("act", lambda e: e.copy(out=HT[:, :, i * 128:(i + 1) * 128], in_=kb.ps(7, [128, 8, 128], BF16)),
              reads=[PB[7]], writes=[(HT, i)])

    def norm_transpose(xt, i, tmp, defer=None):
        junk, sq, rs, hn = tmp["junk"], tmp["sq"], tmp["rs"], tmp["hn"]
        kb.op("act", lambda e: e.activation(out=junk[:], in_=xt[:], func=AF.Square, scale=1.0 / 32.0,
                                            accum_out=sq[:]), reads=[xt], writes=[junk, sq])
        kb.op("act", lambda e: e.activation(out=rs[:], in_=sq[:], func=AF.Ln, bias=EPS), reads=[sq], writes=[rs])
        kb.op("act", lambda e: e.activation(out=rs[:], in_=rs[:], func=AF.Exp, scale=-0.5), reads=[rs], writes=[rs])
        kb.op("dve", lambda e: e.tensor_scalar(out=hn[:], in0=xt[:], scalar1=rs[:], scalar2=None, op0=ALU.mult),
              reads=[xt, rs], writes=[hn])
        if defer is not None:
            defer.append((hn, i))
        else:
            nt_b(hn, i)

    def norm_tmp(depth=2):
        junk = kb.sb([128, D], BF16)
        return [{"junk": junk, "sq": kb.sb([128, 1], F32), "rs": kb.sb([128, 1], F32),
                 "hn": kb.sb([128, D], BF16)} for _ in range(depth)]

    TWO_PI = 2.0 * math.pi

    def phase_filter(li, preload_down=None):
        phase_begin()
        HS = HT.view(HT.t[:, :, :].rearrange("p a b -> p (a b)")[:, 0:NT * 1024].rearrange(
            "p (n c) -> p n c", n=NT))
        w1 = kb.sb([33, 64], F32)
        w2 = kb.sb([64, 64], F32)
        w3 = kb.sb([64, 1024], F32)
        vec = kb.sb([64, 4], F32)
        pv = kb.sb([64, 2], F32)
        absd = kb.sb([128, DH], F32)
        tcol = kb.sb([128, NT], F32)
        H1 = kb.sb([64, L], F32)
        H2 = kb.sb([64, L], F32)
        kb.dma("sp", w1[:], P["hy_w1"][li], writes=[w1])
        kb.dma("sp", w2[:], P["hy_w2"][li], writes=[w2])
        kb.dma("sp", w3[:], P["hy_w3"][li], writes=[w3])
        kb.dma("sp", vec[:], P["hy_vec"][li], writes=[vec])
        kb.dma("sp", absd[:], C["absd"][:, :], writes=[absd])
        kb.dma("sp", tcol[:], C["tcol"][:, :], writes=[tcol])
        kb.op("dve", lambda e: e.tensor_tensor(out=pv[:, 0:1], in0=vec[:, 0:1], in1=vec[:, 1:2], op=ALU.mult),
              reads=[vec], writes=[pv])
        kb.op("dve", lambda e: e.tensor_tensor(out=pv[:, 1:2], in0=vec[:, 2:3], in1=vec[:, 3:4], op=ALU.mult),
              reads=[vec], writes=[pv])
        zt = [kb.sb([33, 512], F32) for _ in range(2)]
        arg = [kb.sb([64, 512], F32) for _ in range(2)]

        def sin_layer(wt, kdim, srcfn, dst, frcol, pvcol, j, bank):
            a = arg[j % 2]
            src, srcT = srcfn(j)
            kb.op("pe", lambda e: e.matmul(kb.ps(bank)[0:64, :], lhsT=wt[0:kdim, :], rhs=src,
                                           start=True, stop=True), reads=[wt, srcT], writes=[PB[bank]])
            kb.op("dve", lambda e: e.tensor_scalar(out=a[:], in0=kb.ps(bank)[0:64, :], scalar1=vec[:, frcol:frcol + 1],
                                                   scalar2=pv[:, pvcol:pvcol + 1], op0=ALU.mult, op1=ALU.add),
                  reads=[PB[bank], vec, pv], writes=[a])
            kb.op("dve", lambda e: e.tensor_scalar(out=ni[:], in0=a[:], scalar1=1.0 / TWO_PI, scalar2=None, op0=ALU.mult),
                  reads=[a], writes=[ni])
            kb.op("dve", lambda e: e.scalar_tensor_tensor(out=a[:], in0=ni[:], scalar=-TWO_PI, in1=a[:], op0=ALU.mult, op1=ALU.add),
                  reads=[ni, a], writes=[a])
            kb.op("dve", lambda e: e.tensor_scalar(out=m1[:], in0=a[:], scalar1=math.pi, scalar2=-TWO_PI, op0=ALU.is_gt, op1=ALU.mult),
                  reads=[a], writes=[m1])
            kb.op("dve", lambda e: e.tensor_scalar(out=m2[:], in0=a[:], scalar1=-math.pi, scalar2=TWO_PI, op0=ALU.is_lt, op1=ALU.mult),
                  reads=[a], writes=[m2])
            kb.op("dve", lambda e: e.tensor_tensor(out=a[:], in0=a[:], in1=m1[:], op=ALU.add), reads=[a, m1], writes=[a])
            kb.op("dve", lambda e: e.tensor_tensor(out=a[:], in0=a[:], in1=m2[:], op=ALU.add), reads=[a, m2], writes=[a])
            kb.op("dve", lambda e: e.tensor_scalar(out=a[:], in0=a[:], scalar1=math.pi, scalar2=-math.pi, op0=ALU.min, op1=ALU.max),
                  reads=[a], writes=[a])
            kb.op("act", lambda e: e.activation(out=dst[:, j * 512:(j + 1) * 512], in_=a[:], func=AF.Sin), reads=[a], writes=[(dst, j)])

        negpi = kb.sb([128, 1], F32)
        ni = kb.sb([64, 512], F32)
        ni = ni.view(ni.t.bitcast(mybir.dt.int32))
        m1 = kb.sb([64, 512], F32)
        m2 = kb.sb([64, 512], F32)
        kb.op("dve", lambda e: e.memset(negpi[:], -math.pi), writes=[negpi])
        for j in range(NQ):
            z = zt[j % 2]
            kb.dma("sp", z[:], C["zT"][:, j * 512:(j + 1) * 512], writes=[z])
            sin_layer(w1, 33, lambda j, z=z: (z[:], z), H1, 1, 0, j, next_pb())
        for j in range(NQ):
            sin_layer(w2, 64, lambda j: (H1[:, j * 512:(j + 1) * 512], H1), H2, 3, 1, j, next_pb())
        hsds = [kb.sb([128, 2, DH], BF16) for _ in range(2)]
        H2h = kb.sb([64, L], BF16)
        H2l = kb.sb([64, L], BF16)
        w3h = kb.sb([64, 1024], BF16)
        w3l = kb.sb([64, 1024], BF16)
        kb.op("pool", lambda e: e.tensor_copy(out=w3h[:], in_=w3[:]), reads=[w3], writes=[w3h])
        kb.op("pool", lambda e: e.tensor_tensor(out=w3l[:], in0=w3[:], in1=w3h[:], op=ALU.subtract), reads=[w3, w3h], writes=[w3l])
        for j in range(NQ):
            cs = slice(j * 512, (j + 1) * 512)
            kb.op("pool", lambda e: e.tensor_copy(out=H2h[:, cs], in_=H2[:, cs]), reads=[(H2, j)], writes=[(H2h, j)])
            kb.op("pool", lambda e: e.tensor_tensor(out=H2l[:, cs], in0=H2[:, cs], in1=H2h[:, cs], op=ALU.subtract),
                  reads=[(H2, j), (H2h, j)], writes=[(H2l, j)])
        dec = [kb.sb([128, DH], F32) for _ in range(2)]
        t1 = [kb.sb([128, DH], F32) for _ in range(2)]
        t2 = [kb.sb([128, DH], F32) for _ in range(2)]
        for i in range(NT):
            dc, a1, a2 = dec[i % 2], t1[i % 2], t2[i % 2]
            b0, b1 = next_pb(), next_pb()
            for half, bank in ((0, b0), (1, b1)):
                hs_ = slice(half * 512, (half + 1) * 512)
                ts_ = slice(i * 128, (i + 1) * 128)
                kb.op("pe", lambda e: e.matmul(kb.ps(bank), lhsT=H2h[:, ts_], rhs=w3h[:, hs_], start=True, stop=False),
                      reads=[H2h, w3h], writes=[PB[bank]])
                kb.op("pe", lambda e: e.matmul(kb.ps(bank), lhsT=H2l[:, ts_], rhs=w3h[:, hs_], start=False, stop=False),
                      reads=[H2l, w3h], writes=[PB[bank]])
                kb.op("pe", lambda e: e.matmul(kb.ps(bank), lhsT=H2h[:, ts_], rhs=w3l[:, hs_], start=False, stop=True),
                      reads=[H2h, w3l], writes=[PB[bank]])
            kb.op("act", lambda e, dc=dc: e.activation(out=dc[:], in_=absd[:], func=AF.Exp, scale=tcol[:, i:i + 1]),
                  reads=[absd, tcol], writes=[dc])
            kb.op("dve", lambda e, dc=dc, a1=a1, b0=b0: e.tensor_tensor(out=a1[:], in0=kb.ps(b0), in1=dc[:], op=ALU.mult),
                  reads=[PB[b0], dc], writes=[a1])
            kb.op("dve", lambda e, dc=dc, a2=a2, b1=b1: e.tensor_tensor(out=a2[:], in0=kb.ps(b1), in1=dc[:], op=ALU.mult),
                  reads=[PB[b1], dc], writes=[a2])
            if i == 0:
                kb.op("dve", lambda e, a2=a2: e.memset(a2[0:1, :], 0.0), reads=[a2], writes=[a2])
            hsd = hsds[i % 2]
            kb.op("pool", lambda e, a1=a1, a2=a2: e.tensor_tensor(out=hsd[:, 0, :], in0=a1[:], in1=a2[:], op=ALU.add),
                  reads=[a1, a2], writes=[(hsd, 0)])
            kb.op("pool", lambda e, a1=a1, a2=a2: e.tensor_tensor(out=hsd[:, 1, :], in0=a1[:], in1=a2[:], op=ALU.subtract),
                  reads=[a1, a2], writes=[(hsd, 1)])
            kb.dma("pool", HSD.t.rearrange("s n c -> n s c")[i * 128:(i + 1) * 128, :, :], hsd[:], reads=[hsd])
        phase_begin(label="filter_s1")
        tw1 = load_tw1()
        s1b = s1_bufs()
        fft_s1(HSD[0], A1K.t[0], tw1, s1b)
        fft_s1(HSD[1], A1K.t[1], tw1, s1b)
        phase_begin(0, True, label="filter_s2")
        preload = []
        if preload_down is not None:
            wd_pre = kb.sb([128, 22, D], BF16)
            preload = [(lambda c0=c0: load_w_into(wd_pre, P["ffn_down"][preload_down], 22, c0, 128, None, c0))
                       for c0 in range(0, D, 128)]
        dftm = kb.sb([64, 3, 64], BF16)
        kb.dma("sp", dftm[:], C["dftm"][:, :, :], writes=[dftm])
        FC = min(4, HH)
        at_ = [[kb.sb([64, FC, DH], BF16) for _ in range(4)] for _ in range(2)]
        ko = [[kb.sb([64, FC, DH], BF16) for _ in range(2)] for _ in range(2)]
        nch = HH // FC
        for c_ in range(nch):
            if preload and (c_ % max(1, nch // 8) == 0 or c_ == nch - 1):
                preload.pop(0)()
                if c_ == nch - 1:
                    while preload:
                        preload.pop(0)()
            f0 = c_ * FC
            A = at_[c_ % 2]
            for q_, (sg_, ri_) in enumerate(((0, 0), (0, 1), (1, 0), (1, 1))):
                kb.dma("sp", A[q_][:], A1K.t[sg_, ri_].rearrange("f n c -> n f c")[:, f0:f0 + FC, :], writes=[A[q_]])
            kos = ko[c_ % 2]
            for fl in range(FC):
                br, bi = next_pb(), next_pb()
                kb.op("pe", lambda e: e.matmul(kb.ps(br)[0:64, :], lhsT=dftm[:, 0, :], rhs=A[0][:, fl, :], start=True, stop=False),
                      reads=[dftm, A[0]], writes=[PB[br]])
                kb.op("pe", lambda e: e.matmul(kb.ps(br)[0:64, :], lhsT=dftm[:, 1, :], rhs=A[1][:, fl, :], start=False, stop=True),
                      reads=[dftm, A[1]], writes=[PB[br]])
                kb.op("pe", lambda e: e.matmul(kb.ps(bi)[0:64, :], lhsT=dftm[:, 0, :], rhs=A[3][:, fl, :], start=True, stop=False),
                      reads=[dftm, A[3]], writes=[PB[bi]])
                kb.op("pe", lambda e: e.matmul(kb.ps(bi)[0:64, :], lhsT=dftm[:, 2, :], rhs=A[2][:, fl, :], start=False, stop=True),
                      reads=[dftm, A[2]], writes=[PB[bi]])
                kb.op("act", lambda e: e.copy(out=kos[0][:, fl, :], in_=kb.ps(br)[0:64, :]), reads=[PB[br]], writes=[(kos[0], fl)])
                kb.op("dve", lambda e: e.tensor_copy(out=kos[1][:, fl, :], in_=kb.ps(bi)[0:64, :]), reads=[PB[bi]], writes=[(kos[1], fl)])
            for ri_ in range(2):
                kb.dma("pool", KS.t[ri_, f0:f0 + FC].rearrange("f k c -> k f c"), kos[ri_][:], reads=[kos[ri_]])

    def load_tw1():
        tw1 = kb.sb([HH, 2, 64, HH], BF16)
        kb.dma("sp", tw1[:], C["tw1"][:, :, :, :], writes=[tw1])
        return tw1

    def s1_bufs():
        return ([kb.sb([HH, 8, DH], BF16) for _ in range(2)],
                [[kb.sb([HH, 8, DH], BF16) for _ in range(2)] for _ in range(2)])

    def fft_s1(src, dst, tw1, s1b):
        xv = src.rearrange("(a b) c -> a b c", b=64)
        if not hasattr(fft_s1, "bufs"):
            pass
        xt, ot = s1b
        ne = 0
        for g in range(8):
            x_ = xt[g % 2]
            kb.dma("sp", x_[:], xv[:, g * 8:(g + 1) * 8, :], writes=[x_])
            for ri_ in range(2):
                o_ = ot[g % 2][ri_]
                for nl in range(8):
                    bank = next_pb()
                    kb.op("pe", lambda e: e.matmul(kb.ps(bank)[0:HH, :], lhsT=tw1[:, ri_, g * 8 + nl, :], rhs=x_[:, nl, :],
                                                   start=True, stop=True), reads=[tw1, x_], writes=[PB[bank]])
                    if ne % 2 == 0:
                        kb.op("act", lambda e: e.copy(out=o_[:, nl, :], in_=kb.ps(bank)[0:HH, :]), reads=[PB[bank]], writes=[(o_, nl)])
                    else:
                        kb.op("dve", lambda e: e.tensor_copy(out=o_[:, nl, :], in_=kb.ps(bank)[0:HH, :]), reads=[PB[bank]], writes=[(o_, nl)])
                    ne += 1
                kb.dma("pool", dst[ri_, :, g * 8:(g + 1) * 8, :], o_[:], reads=[o_])

    def phase_norm0(xsrc):
        phase_begin()
        tmps = norm_tmp()
        xts = [kb.sb([128, D], F32) for _ in range(2)]
        for i in range(NT):
            xt = xts[i % 2]
            kb.dma("sp", xt[:], xsrc[i * 128:(i + 1) * 128, :], writes=[xt])
            norm_transpose(xt, i, tmps[i % 2])

    def phase_proj(li):
        phase_begin(0)
        gt = kb.sb([128, 8], F32)
        kb.dma("sp", gt[:], P["g_mix_pre"][li], writes=[gt])
        W = P["w_in"][li]
        ev = [kb.sb([128, 512], BF16) for _ in range(4)]
        evc = [0]

        def next_ev():
            evc[0] += 1
            return ev[evc[0] % 4]

        cw = kb.sb([128, 12, 3], F32)
        cb = kb.sb([128, 12], F32)
        kb.dma("sp", cw[:], P["hy_cw"][li], writes=[cw])
        kb.dma("sp", cb[:], P["hy_cb"][li], writes=[cb])
        raws = [kb.sb([128, L + 2], BF16) for _ in range(2)]
        for r in raws:
            kb.op("pool", lambda e, r=r: e.memset(r[:, 0:1], 0.0), writes=[(r, "h0")])
            kb.op("pool", lambda e, r=r: e.memset(r[:, L + 1:L + 2], 0.0), writes=[(r, "h1")])
        acc = kb.sb([128, L], F32)
        x1c = kb.sb([128, L], BF16)
        oc = [kb.sb([128, L], BF16) for _ in range(2)]
        tz = [kb.sb([128, 8, 128], BF16) for _ in range(2)]
        wfs = [kb.sb([128, 8, 128], F32) for _ in range(2)]
        wbs = [kb.sb([128, 8, 128], BF16) for _ in range(2)]
        wsrc = W.rearrange("(k p) c -> p k c", p=128)
        order = [(b, part) for b in range(4) for part in range(3)]

        def prep(n_):
            b_, part_ = order[n_]
            blk_ = part_ * 4 + b_
            s_ = n_ % 2
            kb.dma("sp", wfs[s_][:], wsrc[:, :, blk_ * 128:(blk_ + 1) * 128], writes=[wfs[s_]])
            kb.op("pool", lambda e: e.tensor_tensor(out=wbs[s_][:], in0=wfs[s_][:],
                                                    in1=gt[:, :, None].to_broadcast([128, 8, 128]), op=ALU.mult),
                  reads=[wfs[s_], gt], writes=[wbs[s_]])
            return wbs[s_]

        zvt_v = ZVT.t.rearrange("(i p) c -> p i c", p=128)
        TB = min(8, NT)

        def transposes(dst, b):
            for i0 in range(0, NT, TB):
                tzt = tz[(i0 // TB) % 2]
                for ii in range(TB):
                    kb.op("pe", lambda e, ii=ii: e.transpose(out=kb.ps(7, [128, 8, 128], BF16)[:, ii, :],
                                                             in_=dst[:, (i0 + ii) * 128:(i0 + ii + 1) * 128], identity=ident[:]),
                          reads=[dst, ident], writes=[PB[7]])
                kb.op("act", lambda e: e.copy(out=tzt[:, 0:TB, :], in_=kb.ps(7, [128, 8, 128], BF16)[:, 0:TB, :]),
                      reads=[PB[7]], writes=[tzt])
                kb.dma("pool", zvt_v[:, i0:i0 + TB, b * 128:(b + 1) * 128], tzt[:, 0:TB, :], reads=[tzt])

        wnext = prep(0)
        pending = None
        for n_, (b, part) in enumerate(order):
            blk = part * 4 + b
            raw = raws[n_ % 2]
            wbv = wnext
            if n_ + 1 < len(order):
                wnext = prep(n_ + 1)
            for j in range(NQ):
                bank = next_pb()
                gemm_fm(wbv, 8, 0, 128, HT, j, bank)
                kb.op("act", lambda e: e.copy(out=raw[:, 1 + j * 512:1 + (j + 1) * 512], in_=kb.ps(bank)),
                      reads=[PB[bank]], writes=[(raw, j)])
            if pending is not None:
                transposes(*pending)
                pending = None
            dst = x1c if part == 1 else oc[0 if part == 0 else 1]
            kb.op("dve", lambda e: e.tensor_scalar(out=acc[:], in0=raw[:, 0:L], scalar1=cw[:, blk, 0:1], scalar2=cb[:, blk:blk + 1],
                                                   op0=ALU.mult, op1=ALU.add), reads=[raw, cw, cb], writes=[acc])
            kb.op("dve", lambda e: e.scalar_tensor_tensor(out=acc[:], in0=raw[:, 1:L + 1], scalar=cw[:, blk, 1:2], in1=acc[:],
                                                          op0=ALU.mult, op1=ALU.add), reads=[raw, cw, acc], writes=[acc])
            kb.op("dve", lambda e: e.scalar_tensor_tensor(out=dst[:], in0=raw[:, 2:L + 2], scalar=cw[:, blk, 2:3], in1=acc[:],
                                                          op0=ALU.mult, op1=ALU.add), reads=[raw, cw, acc], writes=[dst])
            if part == 0:
                kb.dma("pool", X0fm[b * 128:(b + 1) * 128, :], dst[:], reads=[dst])
            elif part == 2:
                kb.op("dve", lambda e: e.tensor_tensor(out=dst[:], in0=dst[:], in1=x1c[:], op=ALU.mult),
                      reads=[dst, x1c], writes=[dst])
                kb.dma("pool", ZVfm[b * 128:(b + 1) * 128, :], dst[:], reads=[dst])
                pending = (dst, b)
        transposes(*pending)

        phase_begin(2)
        gt = kb.sb([128, 8], F32)
        kb.dma("sp", gt[:], P["g_mix_pre"][li], writes=[gt])
        ev = [kb.sb([128, 512], BF16) for _ in range(4)]
        def fm_group(col0, ncols, dest, func, scale=1.0):
            for c0 in range(0, ncols, 512):
                nc_ = min(512, ncols - c0)
                wbv = load_w(W, 8, col0 + c0, nc_, gt)
                for m in range(nc_ // 128):
                    for j in range(NQ):
                        bank = next_pb()
                        gemm_fm(wbv, 8, m * 128, 128, HT, j, bank)
                        o = next_ev()
                        kb.op("act", lambda e, o=o, bank=bank: e.activation(out=o[:], in_=kb.ps(bank), func=func, scale=scale),
                              reads=[PB[bank]], writes=[o])
                        r0 = c0 + m * 128
                        kb.dma("pool", dest[r0:r0 + 128, j * 512:(j + 1) * 512], o[:], reads=[o])

        def tok_group(col0, dest):
            wbv = load_w(W, 8, col0, 512, gt)
            for i in range(NT):
                bank = next_pb()
                gemm_tok(wbv, 8, 0, 512, HT, i, bank)
                o = next_ev()
                if i % 2 == 0:
                    kb.op("dve", lambda e, o=o, bank=bank: e.tensor_copy(out=o[:], in_=kb.ps(bank)), reads=[PB[bank]], writes=[o])
                else:
                    kb.op("act", lambda e, o=o, bank=bank: e.copy(out=o[:], in_=kb.ps(bank)), reads=[PB[bank]], writes=[o])
                kb.dma("pool", dest[i * 128:(i + 1) * 128, :], o[:], reads=[o])

        fm_group(1536, 512, Qfm, AF.Copy, 128.0 ** -0.5)
        fm_group(2048, 512, Kfm, AF.Copy)
        tok_group(2048, Ktok)
        tok_group(2560, Vtok)
        fm_group(3072, 512, Gfm, AF.Silu)
        wbv = load_w(W, 8, 3584, 32, gt)
        for j in range(NQ):
            bank = next_pb()
            gemm_fm(wbv, 8, 0, 32, HT, j, bank)
            kb.op("act", lambda e, j=j, bank=bank: e.copy(out=LRH[0:32, j * 512:(j + 1) * 512], in_=kb.ps(bank)[0:32, :]),
                  reads=[PB[bank]], writes=[(LRH, j)])
            kb.op("dve", lambda e, j=j, bank=bank: e.tensor_tensor(out=LRL[0:32, j * 512:(j + 1) * 512], in0=kb.ps(bank)[0:32, :],
                                                                   in1=LRH[0:32, j * 512:(j + 1) * 512], op=ALU.subtract),
                  reads=[PB[bank], (LRH, j)], writes=[(LRL, j)])
        fm_group(4128, 3 * D, GATES, AF.Sigmoid)

        phase_begin(1)
        gt = kb.sb([128, 8], F32)
        kb.dma("sp", gt[:], P["g_mix_pre"][li], writes=[gt])
        ev = [kb.sb([128, 512], BF16) for _ in range(4)]
        PW = 16
        ua = kb.sb([128, L + 2 * PW], F32)
        ub = kb.sb([128, L + 2 * PW], F32)
        uc = kb.sb([128, L + 2 * PW], F32)
        icn = kb.sb([128, L], F32)
        pwt = kb.sb([128, 4, 128], F32)
        pwb = kb.sb([128, 4, 128], BF16)
        psc = kb.sb([128, 4], F32)
        dbf = kb.sb([128, L], BF16)
        kb.dma("sp", pwt[:], P["pool_w"][li], writes=[pwt])
        kb.dma("sp", psc[:], P["pool_scale"][li], writes=[psc])
        kb.op("pool", lambda e: e.tensor_copy(out=pwb[:], in_=pwt[:]), reads=[pwt], writes=[pwb])
        for t_ in (ua, ub, uc):
            kb.op("pool", lambda e, t_=t_: e.memset(t_[:], 0.0), writes=[t_])
        for gi, wv in enumerate((2, 4, 8, 16)):
            wbv = load_w(W, 8, 3616 + gi * 128, 128, gt)
            kb.dma("sp", icn[:], C["invcnt"][gi], writes=[icn])
            for j in range(NQ):
                bank = next_pb()
                gemm_fm(wbv, 8, 0, 128, HT, j, bank)
                kb.op("act", lambda e, j=j, bank=bank: e.copy(out=ua[:, PW + j * 512:PW + (j + 1) * 512], in_=kb.ps(bank)),
                      reads=[PB[bank]], writes=[ua])
            src, dsts = ua, [ub, uc]
            lo, hi = -14, L + 14
            kb.op("dve", lambda e, lo=lo, hi=hi: e.tensor_tensor(
                out=ub[:, PW + lo:PW + hi], in0=ua[:, PW + lo - 1:PW + hi - 1], in1=ua[:, PW + lo:PW + hi], op=ALU.add),
                reads=[ua], writes=[ub])
            cur, oth = ub, uc
            sh = 1
            rng = [(-12, L + 12), (-8, L + 8), (0, L)]
            for si in range(int(math.log2(wv)) - 1):
                lo, hi = rng[si]
                kb.op("dve", lambda e, lo=lo, hi=hi, cur=cur, oth=oth, sh=sh: e.tensor_tensor(
                    out=oth[:, PW + lo:PW + hi], in0=cur[:, PW + lo - sh:PW + hi - sh],
                    in1=cur[:, PW + lo + sh:PW + hi + sh], op=ALU.add), reads=[cur], writes=[oth])
                cur, oth = oth, cur
                sh *= 2
            kb.op("dve", lambda e, cur=cur, oth=oth: e.tensor_tensor(out=oth[:, PW:PW + L], in0=cur[:, PW:PW + L], in1=icn[:], op=ALU.mult),
                  reads=[cur, icn], writes=[oth])
            kb.op("dve", lambda e, oth=oth: e.tensor_tensor(out=dbf[:], in0=oth[:, PW:PW + L], in1=ua[:, PW:PW + L], op=ALU.subtract),
                  reads=[oth, ua], writes=[dbf])
            for t_ in (ub, uc):
                kb.op("pool", lambda e, t_=t_: e.memset(t_[:, 0:PW], 0.0), reads=[t_], writes=[t_])
                kb.op("pool", lambda e, t_=t_: e.memset(t_[:, PW + L:PW + L + PW], 0.0), reads=[t_], writes=[t_])
            for j in range(NQ):
                bank = next_pb()
                kb.op("pe", lambda e, j=j, bank=bank, gi=gi: e.matmul(kb.ps(bank), lhsT=pwb[:, gi, :], rhs=dbf[:, j * 512:(j + 1) * 512],
                                                               start=True, stop=True), reads=[pwb, dbf], writes=[PB[bank]])
                o = next_ev()
                kb.op("dve", lambda e, o=o, bank=bank, gi=gi: e.tensor_scalar(out=o[:], in0=kb.ps(bank), scalar1=psc[:, gi:gi + 1],
                                                                       scalar2=None, op0=ALU.mult), reads=[PB[bank], psc], writes=[o])
                kb.dma("pool", YP[gi * 128:(gi + 1) * 128, j * 512:(j + 1) * 512], o[:], reads=[o])

    def phase_hyena(li):
        phase_begin(label="hy_s1")
        tw1 = load_tw1()
        fft_s1(ZVT.t, A1Z.t, tw1, s1_bufs())
        phase_begin(label="hy_s2")
        dftm = kb.sb([64, 3, 64], BF16)
        gtw = kb.sb([64, 3, HH, 64], BF16)
        kb.dma("sp", dftm[:], C["dftm"][:, :, :], writes=[dftm])
        kb.dma("sp", gtw[:], C["gtw"][:, :, :, :], writes=[gtw])
        FC = min(4, HH)
        at_ = [[kb.sb([64, FC, DH], BF16) for _ in range(2)] for _ in range(2)]
        kt_ = [[kb.sb([64, FC, DH], BF16) for _ in range(2)] for _ in range(2)]
        bo = [[kb.sb([64, FC, DH], BF16) for _ in range(2)] for _ in range(2)]
        m = [[kb.sb([64, DH], F32) for _ in range(4)] for _ in range(2)]
        yy = [[kb.sb([64, DH], BF16) for _ in range(2)] for _ in range(2)]
        def s2_prod(f1_):
            c_, fl = divmod(f1_, FC)
            f0 = c_ * FC
            A, K_ = at_[c_ % 2], kt_[c_ % 2]
            if fl == 0:
                for ri_ in range(2):
                    kb.dma("sp", A[ri_][:], A1Z.t[ri_].rearrange("f n c -> n f c")[:, f0:f0 + FC, :], writes=[A[ri_]])
                    kb.dma("sp", K_[ri_][:], KS.t[ri_, f0:f0 + FC].rearrange("f k c -> k f c"), writes=[K_[ri_]])
            mm_, y_ = m[f1_ % 2], yy[f1_ % 2]
            zr, zi = f1_ % 2, 2 + f1_ % 2
            kb.op("pe", lambda e: e.matmul(kb.ps(zr)[0:64, :], lhsT=dftm[:, 0, :], rhs=A[0][:, fl, :], start=True, stop=False),
                  reads=[dftm, A[0]], writes=[PB[zr]])
            kb.op("pe", lambda e: e.matmul(kb.ps(zr)[0:64, :], lhsT=dftm[:, 1, :], rhs=A[1][:, fl, :], start=False, stop=True),
                  reads=[dftm, A[1]], writes=[PB[zr]])
            kb.op("pe", lambda e: e.matmul(kb.ps(zi)[0:64, :], lhsT=dftm[:, 0, :], rhs=A[1][:, fl, :], start=True, stop=False),
                  reads=[dftm, A[1]], writes=[PB[zi]])
            kb.op("pe", lambda e: e.matmul(kb.ps(zi)[0:64, :], lhsT=dftm[:, 2, :], rhs=A[0][:, fl, :], start=False, stop=True),
                  reads=[dftm, A[0]], writes=[PB[zi]])
            kb.op("dve", lambda e: e.tensor_tensor(out=mm_[0][:], in0=kb.ps(zr)[0:64, :], in1=K_[0][:, fl, :], op=ALU.mult),
                  reads=[PB[zr], K_[0]], writes=[mm_[0]])
            kb.op("dve", lambda e: e.tensor_tensor(out=mm_[1][:], in0=kb.ps(zi)[0:64, :], in1=K_[1][:, fl, :], op=ALU.mult),
                  reads=[PB[zi], K_[1]], writes=[mm_[1]])
            kb.op("dve", lambda e: e.tensor_tensor(out=mm_[2][:], in0=kb.ps(zr)[0:64, :], in1=K_[1][:, fl, :], op=ALU.mult),
                  reads=[PB[zr], K_[1]], writes=[mm_[2]])
            kb.op("dve", lambda e: e.tensor_tensor(out=mm_[3][:], in0=kb.ps(zi)[0:64, :], in1=K_[0][:, fl, :], op=ALU.mult),
                  reads=[PB[zi], K_[0]], writes=[mm_[3]])
            kb.op("pool", lambda e: e.tensor_tensor(out=y_[0][:], in0=mm_[0][:], in1=mm_[1][:], op=ALU.subtract),
                  reads=[mm_[0], mm_[1]], writes=[y_[0]])
            kb.op("pool", lambda e: e.tensor_tensor(out=y_[1][:], in0=mm_[2][:], in1=mm_[3][:], op=ALU.add),
                  reads=[mm_[2], mm_[3]], writes=[y_[1]])

        def inv_a(f1_):
            c_, fl = divmod(f1_, FC)
            f0 = c_ * FC
            y_ = yy[f1_ % 2]
            bos = bo[c_ % 2]
            br, bi = 4 + f1_ % 2, 6 + f1_ % 2
            kb.op("pe", lambda e: e.matmul(kb.ps(br)[0:64, :], lhsT=gtw[:, 0, f1_, :], rhs=y_[0][:], start=True, stop=False),
                  reads=[gtw, y_[0]], writes=[PB[br]])
            kb.op("pe", lambda e: e.matmul(kb.ps(br)[0:64, :], lhsT=gtw[:, 2, f1_, :], rhs=y_[1][:], start=False, stop=True),
                  reads=[gtw, y_[1]], writes=[PB[br]])
            kb.op("pe", lambda e: e.matmul(kb.ps(bi)[0:64, :], lhsT=gtw[:, 1, f1_, :], rhs=y_[0][:], start=True, stop=False),
                  reads=[gtw, y_[0]], writes=[PB[bi]])
            kb.op("pe", lambda e: e.matmul(kb.ps(bi)[0:64, :], lhsT=gtw[:, 0, f1_, :], rhs=y_[1][:], start=False, stop=True),
                  reads=[gtw, y_[1]], writes=[PB[bi]])
            kb.op("act", lambda e: e.copy(out=bos[0][:, fl, :], in_=kb.ps(br)[0:64, :]), reads=[PB[br]], writes=[(bos[0], fl)])
            kb.op("act", lambda e: e.copy(out=bos[1][:, fl, :], in_=kb.ps(bi)[0:64, :]), reads=[PB[bi]], writes=[(bos[1], fl)])
            if fl == FC - 1:
                for ri_ in range(2):
                    kb.dma("pool", B1.t[ri_, :, f0:f0 + FC, :], bos[ri_][:], reads=[bos[ri_]])

        for k_ in range(HH + 1):
            if k_ < HH:
                s2_prod(k_)
            if k_ >= 1:
                inv_a(k_ - 1)
        phase_begin(label="hy_ib")
        hb = kb.sb([128, 4], F32)
        kb.dma("sp", hb[:], P["hy_bias"][li], writes=[hb])
        m4 = kb.sb([HH, 2, HH], BF16)
        kb.dma("sp", m4[:], C["m4"][:, :, :], writes=[m4])
        ysb = kb.sb([128, 4, L], BF16)
        bt_ = [[kb.sb([HH, 8, DH], BF16) for _ in range(2)] for _ in range(2)]
        for g in range(8):
            Bt = bt_[g % 2]
            for ri_ in range(2):
                kb.dma("sp", Bt[ri_][:], B1.t[ri_].rearrange("t f c -> f t c")[:, g * 8:(g + 1) * 8, :], writes=[Bt[ri_]])
            for cb_ in range(4):
                bank = next_pb()
                for tl in range(8):
                    kb.op("pe", lambda e: e.matmul(kb.ps(bank, [128, 8, HH])[:, tl, :], lhsT=Bt[0][:, tl, cb_ * 128:(cb_ + 1) * 128],
                                                   rhs=m4[:, 0, :], start=True, stop=False), reads=[Bt[0], m4], writes=[PB[bank]])
                    kb.op("pe", lambda e: e.matmul(kb.ps(bank, [128, 8, HH])[:, tl, :], lhsT=Bt[1][:, tl, cb_ * 128:(cb_ + 1) * 128],
                                                   rhs=m4[:, 1, :], start=False, stop=True), reads=[Bt[1], m4], writes=[PB[bank]])
                dst = ysb[:, cb_, :].rearrange("p (a b) -> p a b", b=64)[:, :, g * 8:(g + 1) * 8]
                src_ = kb.ps(bank, [128, 8, HH]).rearrange("p a b -> p b a")
                if (g * 4 + cb_) % 2 == 0:
                    kb.op("act", lambda e: e.copy(out=dst, in_=src_), reads=[PB[bank]], writes=[(ysb, (cb_, g))])
                else:
                    kb.op("dve", lambda e: e.tensor_copy(out=dst, in_=src_), reads=[PB[bank]], writes=[(ysb, (cb_, g))])
        x0t = [kb.sb([128, 512], BF16) for _ in range(2)]
        zvf = [kb.sb([128, 512], BF16) for _ in range(2)]
        tmp = [kb.sb([128, 512], F32) for _ in range(2)]
        yo = [kb.sb([128, 512], BF16) for _ in range(2)]
        for tt in range(NQ):
            for cb_ in range(4):
                k = (tt * 4 + cb_) % 2
                kb.dma("sp", x0t[k][:], X0fm[cb_ * 128:(cb_ + 1) * 128, tt * 512:(tt + 1) * 512], writes=[x0t[k]])
                kb.dma("sp", zvf[k][:], ZVfm[cb_ * 128:(cb_ + 1) * 128, tt * 512:(tt + 1) * 512], writes=[zvf[k]])
                kb.op("dve", lambda e: e.scalar_tensor_tensor(
                    out=tmp[k][:], in0=zvf[k][:], scalar=hb[:, cb_:cb_ + 1], in1=ysb[:, cb_, tt * 512:(tt + 1) * 512],
                    op0=ALU.mult, op1=ALU.add), reads=[zvf[k], hb, ysb], writes=[tmp[k]])
                kb.op("dve", lambda e: e.tensor_tensor(out=yo[k][:], in0=tmp[k][:], in1=x0t[k][:], op=ALU.mult),
                      reads=[tmp[k], x0t[k]], writes=[yo[k]])
                kb.dma("pool", YH[cb_ * 128:(cb_ + 1) * 128, tt * 512:(tt + 1) * 512], yo[k][:], reads=[yo[k]])
        st["pb"] = 0

    def phase_gla(li):
        phase_begin(0, True)
        wbr_pre = [kb.sb([128, 4, D], BF16) for _ in range(3)]
        preload = [(lambda b=b, c0=c0: load_w_into(wbr_pre[b], P["w_br"][li, b], 4, c0, 512, None, c0))
                   for b in range(3) for c0 in (0, 512)]
        w2x = kb.sb([33, 2, 512], F32)
        gn = kb.sb([128, 1], F32)
        kb.dma("sp", w2x[:], P["gla_w2x"][li], writes=[w2x])
        kb.dma("sp", gn[:], P["gla_norm"][li], writes=[gn])
        w2h = kb.sb([33, 2, 512], BF16)
        w2l = kb.sb([33, 2, 512], BF16)
        kb.op("dve", lambda e: e.tensor_copy(out=w2h[:], in_=w2x[:]), reads=[w2x], writes=[w2h])
        kb.op("dve", lambda e: e.tensor_tensor(out=w2l[:], in0=w2x[:], in1=w2h[:], op=ALU.subtract), reads=[w2x, w2h], writes=[w2l])
        OB = HT.view(HT.t[:, 0:4, :])
        S32 = [kb.sb([128, 4, 128], F32) for _ in range(2)]
        Sbf = [kb.sb([128, 4, 128], BF16) for _ in range(2)]
        for d_ in range(2):
            kb.op("pool", lambda e, d_=d_: e.memset(S32[d_][:], 0.0), writes=[S32[d_]])
            kb.op("pool", lambda e, d_=d_: e.memset(Sbf[d_][:], 0.0), writes=[Sbf[d_]])
        lah = [kb.sb([128, 512], BF16) for _ in range(2)]
        lal = [kb.sb([128, 512], BF16) for _ in range(2)]
        PD = lambda shape, dt: [kb.sb(shape, dt) for _ in range(2)]
        PS = lambda shape, dt: [[kb.sb(shape, dt) for _ in range(2)] for _ in range(2)]
        e1, la, edec = PD([128, 512], F32), PD([128, 512], F32), PD([128, 512], BF16)
        EK = PD([128, 4, 128], BF16)
        kt, qf, kf, ke = PD([128, 512], BF16), PD([128, 4, 128], BF16), PD([128, 4, 128], BF16), PD([128, 4, 128], BF16)
        EQ = PS([128, 4, 128], BF16)
        DEC = PS([128, 4, 1], F32)
        vt, kdec = PS([128, 512], BF16), PS([128, 512], BF16)
        qe, msk = PS([128, 4, 128], BF16), PS([128, 4, 128], BF16)
        gf, sqb, yob = PD([128, 4, 128], BF16), PD([128, 4, 128], BF16), PD([128, 4, 128], BF16)
        o32, rsd = PD([128, 4, 128], F32), PD([128, 4, 128], F32)
        qv = Qfm.t.rearrange("(h d) t -> d h t", h=4)
        kv = Kfm.t.rearrange("(h d) t -> d h t", h=4)
        gv_ = Gfm.t.rearrange("(h d) t -> d h t", h=4)
        yv = YG.t.rearrange("(h d) t -> d h t", h=4)
        V3 = [128, 4, 128]

        def tile_of(step, d_):
            return step if d_ == 0 else NT - 1 - step

        def pre(step, d_, stage):
            i = tile_of(step, d_)
            sl = slice(i * 128, (i + 1) * 128)
            p = step % 2
            bA, bB = 4 * d_, 4 * d_ + 1
            tri_fm = 0 if d_ == 0 else 2
            tri_dec = 3 if d_ == 0 else 1
            if stage == 1:
                pre1(d_, p, sl, bA)
            elif stage == 2:
                pre2(d_, p, bA, bB, tri_fm, tri_dec)
            else:
                pre3(d_, p, bA)

        def pre1(d_, p, sl, bA):
            kb.dma("sp", kt[d_][:], Ktok[sl, :], writes=[kt[d_]])
            kb.dma("sp", vt[d_][p][:], Vtok[sl, :], writes=[vt[d_][p]])
            kb.dma("sp", qf[d_][:], qv[:, :, sl], writes=[qf[d_]])
            kb.dma("sp", kf[d_][:], kv[:, :, sl], writes=[kf[d_]])
            kb.op("pe", lambda e: e.matmul(kb.ps(bA), lhsT=LRH[:, sl], rhs=w2h[:, d_, :], start=True, stop=False),
                  reads=[LRH, w2h], writes=[PB[bA]])
            kb.op("pe", lambda e: e.matmul(kb.ps(bA), lhsT=LRL[:, sl], rhs=w2h[:, d_, :], start=False, stop=False),
                  reads=[LRL, w2h], writes=[PB[bA]])
            kb.op("pe", lambda e: e.matmul(kb.ps(bA), lhsT=LRH[:, sl], rhs=w2l[:, d_, :], start=False, stop=True),
                  reads=[LRH, w2l], writes=[PB[bA]])
            kb.op("act", lambda e: e.activation(out=e1[d_][:], in_=kb.ps(bA), func=AF.Exp, scale=-1.0),
                  reads=[PB[bA]], writes=[e1[d_]])
            kb.op("act", lambda e: e.activation(out=e1[d_][:], in_=e1[d_][:], func=AF.Ln, bias=1.0),
                  reads=[e1[d_]], writes=[e1[d_]])
            kb.op("dve", lambda e: e.tensor_scalar(out=la[d_][:], in0=e1[d_][:], scalar1=-1.0 / 16.0, scalar2=-1.0,
                                                   op0=ALU.mult, op1=ALU.max), reads=[e1[d_]], writes=[la[d_]])
            kb.op("act", lambda e: e.copy(out=lah[d_][:], in_=la[d_][:]), reads=[la[d_]], writes=[lah[d_]])
            kb.op("dve", lambda e: e.tensor_tensor(out=lal[d_][:], in0=la[d_][:], in1=lah[d_][:], op=ALU.subtract),
                  reads=[la[d_], lah[d_]], writes=[lal[d_]])

        def pre2(d_, p, bA, bB, tri_fm, tri_dec):
            kb.op("pe", lambda e: e.matmul(kb.ps(bA), lhsT=tri32[:, tri_dec, :], rhs=lah[d_][:], start=True, stop=False),
                  reads=[tri32, lah[d_]], writes=[PB[bA]])
            kb.op("pe", lambda e: e.matmul(kb.ps(bA), lhsT=tri32[:, tri_dec, :], rhs=lal[d_][:], start=False, stop=True),
                  reads=[tri32, lal[d_]], writes=[PB[bA]])
            for h in range(4):
                kb.op("pe", lambda e, h=h: e.matmul(kb.ps(bB, V3)[:, h, :], lhsT=lah[d_][:, h * 128:(h + 1) * 128],
                                                    rhs=tri32[:, tri_fm, :], start=True, stop=False),
                      reads=[tri32, lah[d_]], writes=[PB[bB]])
                kb.op("pe", lambda e, h=h: e.matmul(kb.ps(bB, V3)[:, h, :], lhsT=lal[d_][:, h * 128:(h + 1) * 128],
                                                    rhs=tri32[:, tri_fm, :], start=False, stop=True),
                      reads=[tri32, lal[d_]], writes=[PB[bB]])
            kb.op("act", lambda e: e.activation(out=edec[d_][:], in_=kb.ps(bA), func=AF.Exp), reads=[PB[bA]], writes=[edec[d_]])
            kb.op("act", lambda e: e.activation(out=EQ[d_][p][:], in_=kb.ps(bB, V3), func=AF.Exp), reads=[PB[bB]], writes=[EQ[d_][p]])
            kb.op("act", lambda e: e.activation(out=EK[d_][:], in_=kb.ps(bB, V3), func=AF.Exp, scale=-1.0),
                  reads=[PB[bB]], writes=[EK[d_]])
            dcol_ = 127 if d_ == 0 else 0
            kb.op("act", lambda e: e.activation(out=DEC[d_][p][:], in_=kb.ps(bB, V3)[:, :, dcol_:dcol_ + 1], func=AF.Exp),
                  reads=[PB[bB]], writes=[DEC[d_][p]])
            kb.op("dve", lambda e: e.tensor_tensor(out=kdec[d_][p][:], in0=kt[d_][:], in1=edec[d_][:], op=ALU.mult),
                  reads=[kt[d_], edec[d_]], writes=[kdec[d_][p]])
            kb.op("dve", lambda e: e.tensor_tensor(out=qe[d_][p][:], in0=qf[d_][:], in1=EQ[d_][p][:], op=ALU.mult),
                  reads=[qf[d_], EQ[d_][p]], writes=[qe[d_][p]])
            kb.op("dve", lambda e: e.tensor_tensor(out=ke[d_][:], in0=kf[d_][:], in1=EK[d_][:], op=ALU.mult),
                  reads=[kf[d_], EK[d_]], writes=[ke[d_]])

        def pre3(d_, p, bA):
            for h in range(4):
                kb.op("pe", lambda e, h=h: e.matmul(kb.ps(bA, V3)[:, h, :], lhsT=ke[d_][:, h, :], rhs=qe[d_][p][:, h, :],
                                                    start=True, stop=True), reads=[ke[d_], qe[d_][p]], writes=[PB[bA]])
            kb.op("dve", lambda e: e.tensor_tensor(out=msk[d_][p][:], in0=kb.ps(bA, V3),
                                                   in1=tri32[:, 3 * d_:3 * d_ + 1, :].to_broadcast(V3), op=ALU.mult),
                  reads=[PB[bA], tri32], writes=[msk[d_][p]])

        def seq(step, d_):
            i = tile_of(step, d_)
            sl = slice(i * 128, (i + 1) * 128)
            p = step % 2
            bC, bD = 4 * d_ + 2, 4 * d_ + 3
            final = step >= NT // 2
            for h in range(4):
                hs = slice(h * 128, (h + 1) * 128)
                kb.op("pe", lambda e, h=h, hs=hs: e.matmul(kb.ps(bC, V3)[:, h, :], lhsT=vt[d_][p][:, hs], rhs=msk[d_][p][:, h, :],
                                                           start=True, stop=False), reads=[vt[d_][p], msk[d_][p]], writes=[PB[bC]])
                kb.op("pe", lambda e, h=h: e.matmul(kb.ps(bC, V3)[:, h, :], lhsT=Sbf[d_][:, h, :], rhs=qe[d_][p][:, h, :],
                                                    start=False, stop=True), reads=[Sbf[d_], qe[d_][p]], writes=[PB[bC]])
            for h in range(4):
                hs = slice(h * 128, (h + 1) * 128)
                kb.op("pe", lambda e, h=h, hs=hs: e.matmul(kb.ps(bD, V3)[:, h, :], lhsT=kdec[d_][p][:, hs], rhs=vt[d_][p][:, hs],
                                                           start=True, stop=True), reads=[kdec[d_][p], vt[d_][p]], writes=[PB[bD]])
            kb.op("dve", lambda e: e.tensor_tensor(out=S32[d_][:], in0=S32[d_][:],
                                                   in1=DEC[d_][p][:, :, 0:1].to_broadcast(V3), op=ALU.mult),
                  reads=[S32[d_], DEC[d_][p]], writes=[S32[d_]])
            kb.op("dve", lambda e: e.tensor_tensor(out=S32[d_][:], in0=S32[d_][:], in1=kb.ps(bD, V3), op=ALU.add),
                  reads=[S32[d_], PB[bD]], writes=[S32[d_]])
            kb.op("act", lambda e: e.copy(out=Sbf[d_][:], in_=S32[d_][:]), reads=[S32[d_]], writes=[Sbf[d_]])
            if not final:
                kb.op("act", lambda e: e.copy(out=OB[:, :, sl], in_=kb.ps(bC, V3)), reads=[PB[bC]], writes=[(OB, i)])
            else:
                kb.dma("sp", gf[d_][:], gv_[:, :, sl], writes=[gf[d_]])
                kb.op("dve", lambda e: e.tensor_tensor(out=o32[d_][:], in0=kb.ps(bC, V3), in1=OB[:, :, sl], op=ALU.add),
                      reads=[PB[bC], (OB, i)], writes=[o32[d_]])
                kb.op("act", lambda e: e.activation(out=sqb[d_][:], in_=o32[d_][:], func=AF.Square), reads=[o32[d_]], writes=[sqb[d_]])
                kb.op("pe", lambda e: e.matmul(kb.ps(bD), lhsT=ones[:], rhs=sqb[d_][:].rearrange("p a b -> p (a b)"), start=True, stop=True),
                      reads=[ones, sqb[d_]], writes=[PB[bD]])
                kb.op("act", lambda e: e.activation(out=rsd[d_][:].rearrange("p a b -> p (a b)"), in_=kb.ps(bD), func=AF.Ln,
                                                    scale=1.0 / 128.0, bias=EPS), reads=[PB[bD]], writes=[rsd[d_]])
                kb.op("act", lambda e: e.activation(out=rsd[d_][:], in_=rsd[d_][:], func=AF.Exp, scale=-0.5), reads=[rsd[d_]], writes=[rsd[d_]])
                kb.op("dve", lambda e: e.tensor_tensor(out=o32[d_][:], in0=o32[d_][:], in1=rsd[d_][:], op=ALU.mult),
                      reads=[o32[d_], rsd[d_]], writes=[o32[d_]])
                kb.op("dve", lambda e: e.scalar_tensor_tensor(out=yob[d_][:], in0=o32[d_][:], scalar=gn[:, 0:1], in1=gf[d_][:],
                                                              op0=ALU.mult, op1=ALU.mult), reads=[o32[d_], gn, gf[d_]], writes=[yob[d_]])
                kb.dma("pool", yv[:, :, sl], yob[d_][:], reads=[yob[d_]])

        for step in range(NT + 1):
            if preload and step >= 1 and (step % max(1, NT // 8) == 0 or step == NT):
                preload.pop(0)()
                if step == NT:
                    while preload:
                        preload.pop(0)()
            if step == 0:
                pre(0, 0, 1)
                pre(0, 1, 1)
            if step >= 1:
                seq(step - 1, 0)
                seq(step - 1, 1)
            if step < NT:
                pre(step, 0, 2)
                pre(step, 1, 2)
                pre(step, 0, 3)
                pre(step, 1, 3)
            if step + 1 < NT:
                pre(step + 1, 0, 1)
                pre(step + 1, 1, 1)
        st["pb"] = 0

    def load_w_into(dst, wap, kblocks, c0, ncols, gt=None, dcol0=0):
        src = wap.rearrange("(k p) c -> p k c", p=128)
        WF = st["WF"]
        kper = max(1, 4096 // ncols)
        k0 = 0
        while k0 < kblocks:
            kn = min(kper, kblocks - k0)
            wfv = WF.t[:, 0:kn * ncols].rearrange("p (k c) -> p k c", k=kn)
            kb.dma("sp", wfv, src[:, k0:k0 + kn, c0:c0 + ncols], writes=[WF])
            if gt is None:
                kb.op("pool", lambda e, wfv=wfv, k0=k0, kn=kn: e.tensor_copy(out=dst.t[:, k0:k0 + kn, dcol0:dcol0 + ncols], in_=wfv),
                      reads=[WF], writes=[(dst, (k0, dcol0))])
            else:
                kb.op("pool", lambda e, wfv=wfv, k0=k0, kn=kn: e.tensor_tensor(
                    out=dst.t[:, k0:k0 + kn, dcol0:dcol0 + ncols], in0=wfv,
                    in1=gt.t[:, k0:k0 + kn, None].to_broadcast([128, kn, ncols]), op=ALU.mult),
                    reads=[WF, gt], writes=[(dst, (k0, dcol0))])
            k0 += kn

    def post_tiles(depth=2):
        return {"yt": [kb.sb([128, D], F32) for _ in range(depth)], "xo": [kb.sb([128, D], F32) for _ in range(depth)],
                "sq": [kb.sb([128, 1], F32) for _ in range(depth)], "rs": [kb.sb([128, 1], F32) for _ in range(depth)],
                "junk": kb.sb([128, D], BF16), "nt": norm_tmp(depth), "depth": depth}

    def post_stage(stage, i, banks, xsrc, gp, xdst, pt, do_norm):
        k = i % pt["depth"]
        yt, xo, sq, rs, junk = pt["yt"][k], pt["xo"][k], pt["sq"][k], pt["rs"][k], pt["junk"]
        nt = pt["nt"][k]
        if stage == "A":
            ba, bb = banks
            kb.dma("sp", xo[:], xsrc[i * 128:(i + 1) * 128, :], writes=[xo])
            kb.op("act", lambda e: e.copy(out=yt[:, 0:512], in_=kb.ps(ba)), reads=[PB[ba]], writes=[(yt, 0)])
            kb.op("dve", lambda e: e.tensor_copy(out=yt[:, 512:1024], in_=kb.ps(bb)), reads=[PB[bb]], writes=[(yt, 1)])
            kb.op("act", lambda e: e.activation(out=junk[:], in_=yt[:], func=AF.Square, scale=1.0 / 32.0, accum_out=sq[:]),
                  reads=[yt], writes=[junk, sq])
            kb.op("act", lambda e: e.activation(out=rs[:], in_=sq[:], func=AF.Ln, bias=EPS), reads=[sq], writes=[rs])
            kb.op("act", lambda e: e.activation(out=rs[:], in_=rs[:], func=AF.Exp, scale=-0.5), reads=[rs], writes=[rs])
        elif stage == "B":
            kb.op("dve", lambda e: e.scalar_tensor_tensor(out=yt[:], in0=yt[:], scalar=rs[:, 0:1], in1=gp[:], op0=ALU.mult, op1=ALU.mult),
                  reads=[yt, rs, gp], writes=[yt])
            kb.op("dve", lambda e: e.tensor_tensor(out=xo[:], in0=xo[:], in1=yt[:], op=ALU.add), reads=[xo, yt], writes=[xo])
            kb.dma("pool", xdst[i * 128:(i + 1) * 128, :], xo[:], reads=[xo])
        elif stage == "C":
            if do_norm:
                njunk, nsq, nrs, hn = nt["junk"], nt["sq"], nt["rs"], nt["hn"]
                kb.op("act", lambda e: e.activation(out=njunk[:], in_=xo[:], func=AF.Square, scale=1.0 / 32.0, accum_out=nsq[:]),
                      reads=[xo], writes=[njunk, nsq])
                kb.op("act", lambda e: e.activation(out=nrs[:], in_=nsq[:], func=AF.Ln, bias=EPS), reads=[nsq], writes=[nrs])
                kb.op("act", lambda e: e.activation(out=nrs[:], in_=nrs[:], func=AF.Exp, scale=-0.5), reads=[nrs], writes=[nrs])
                kb.op("dve", lambda e: e.tensor_scalar(out=hn[:], in0=xo[:], scalar1=nrs[:], scalar2=None, op0=ALU.mult),
                      reads=[xo, nrs], writes=[hn])
        elif stage == "D":
            if do_norm:
                nt_b(nt["hn"], i)

    def post_loop(mm, xsrc, gp, xdst, pt, do_norm, offs):
        banks = {}
        for j in range(min(2, NT)):
            banks[j] = mm(j)
        maxo = max(offs.values())
        for it in range(NT + maxo):
            if it + 2 < NT:
                banks[it + 2] = mm(it + 2)
            for stg in ("A", "B", "C", "D"):
                i = it - offs[stg]
                if 0 <= i < NT:
                    post_stage(stg, i, banks.get(i), xsrc, gp, xdst, pt, do_norm)

    def phase_merge(li, xsrc, xdst):
        phase_begin(0, True)
        MG = HT
        wbr = [kb.sb([128, 4, D], BF16) for _ in range(3)]
        wo_pre = kb.sb([128, 8, D], BF16)
        preload = [(lambda c0=c0: load_w_into(wo_pre, P["w_out"][li], 8, c0, 512, None, c0)) for c0 in (0, 512)]
        ysrc = [Y_.t.rearrange("(k p) t -> p k t", p=128) for Y_ in (YH, YG, YP)]
        gsrc = GATES.t.rearrange("(b r) t -> r b t", b=3)
        yt_ = [[kb.sb([128, 4, 512], BF16) for _ in range(3)] for _ in range(2)]
        gt_ = [kb.sb([128, 3, 512], BF16) for _ in range(2)]
        mm = [[kb.sb([128, 512], BF16) for _ in range(3)] for _ in range(2)]
        ng = 0
        for tt in range(NQ):
            if preload and (tt >= 1 or NQ == 1):
                preload.pop(0)()
                if tt == NQ - 1:
                    while preload:
                        preload.pop(0)()
            ys = yt_[tt % 2]
            for b in range(3):
                kb.dma("sp", ys[b][:], ysrc[b][:, :, tt * 512:(tt + 1) * 512], writes=[ys[b]])
            for db in range(8):
                g = gt_[ng % 2]
                m_ = mm[ng % 2]
                ng += 1
                kb.dma("sp", g[:], gsrc[db * 128:(db + 1) * 128, :, tt * 512:(tt + 1) * 512], writes=[g])
                banks = [next_pb(), next_pb(), next_pb()]
                for b in range(3):
                    for k in range(4):
                        kb.op("pe", lambda e, b=b, k=k, db=db, ys=ys, banks=banks: e.matmul(
                            kb.ps(banks[b]), lhsT=wbr[b][:, k, db * 128:(db + 1) * 128], rhs=ys[b][:, k, :],
                            start=(k == 0), stop=(k == 3)), reads=[wbr[b], ys[b]], writes=[PB[banks[b]]])
                for b in range(3):
                    kb.op("dve", lambda e, b=b, g=g, m_=m_, banks=banks: e.tensor_tensor(
                        out=m_[b][:], in0=kb.ps(banks[b]), in1=g[:, b, :], op=ALU.mult), reads=[PB[banks[b]], g], writes=[m_[b]])
                kb.op("dve", lambda e, m_=m_: e.tensor_tensor(out=m_[0][:], in0=m_[0][:], in1=m_[1][:], op=ALU.add),
                      reads=[m_[0], m_[1]], writes=[m_[0]])
                kb.op("dve", lambda e, m_=m_, db=db, tt=tt: e.tensor_tensor(
                    out=MG[:, db, tt * 512:(tt + 1) * 512], in0=m_[0][:], in1=m_[2][:], op=ALU.add),
                    reads=[m_[0], m_[2]], writes=[(MG, 4 * tt), (MG, 4 * tt + 1), (MG, 4 * tt + 2), (MG, 4 * tt + 3)])
        phase_begin(0, True)
        _pad = [kb.sb([128, 4, D], BF16) for _ in range(3)]
        wo = kb.sb([128, 8, D], BF16)
        gp = kb.sb([128, D], F32)
        kb.dma("sp", gp[:], P["g_mix_post"][li], writes=[gp])
        pt = post_tiles(4)

        def mm(i):
            ba, bb = next_pb(), next_pb()
            gemm_tok(wo, 8, 0, 512, MG, i, ba, srckey=i)
            gemm_tok(wo, 8, 512, 512, MG, i, bb, srckey=i)
            return ba, bb

        post_loop(mm, xsrc, gp, xdst, pt, True, {"A": 0, "B": 1, "C": 2, "D": 3})

    def phase_ffn_up(li):
        phase_begin(0)
        gt = kb.sb([128, 8], F32)
        kb.dma("sp", gt[:], P["g_ffn_pre"][li], writes=[gt])
        cw = kb.sb([128, 22, 3], F32)
        cb = kb.sb([128, 22], F32)
        kb.dma("sp", cw[:], P["ffn_cw"][li], writes=[cw])
        kb.dma("sp", cb[:], P["ffn_cb"][li], writes=[cb])
        raws = [kb.sb([128, L + 2], BF16) for _ in range(2)]
        for r in raws:
            kb.op("pool", lambda e, r=r: e.memset(r[:, 0:1], 0.0), writes=[(r, "h0")])
            kb.op("pool", lambda e, r=r: e.memset(r[:, L + 1:L + 2], 0.0), writes=[(r, "h1")])
        bts = [kb.sb([128, L], BF16) for _ in range(2)]
        gas = [kb.sb([128, L], BF16) for _ in range(2)]
        acc = kb.sb([128, L], F32)
        wfs = [kb.sb([128, 8, 256], F32) for _ in range(2)]
        wbs = [kb.sb([128, 8, 256], BF16) for _ in range(2)]
        wsrc = P["ffn_up"][li].rearrange("(k p) c -> p k c", p=128)

        def prep(m):
            s_ = m % 2
            kb.dma("sp", wfs[s_][:, :, 0:128], wsrc[:, :, m * 128:(m + 1) * 128], writes=[(wfs[s_], 0)])
            kb.dma("sp", wfs[s_][:, :, 128:256], wsrc[:, :, DFF + m * 128:DFF + (m + 1) * 128], writes=[(wfs[s_], 1)])
            kb.op("pool", lambda e: e.tensor_tensor(out=wbs[s_][:], in0=wfs[s_][:],
                                                    in1=gt[:, :, None].to_broadcast([128, 8, 256]), op=ALU.mult),
                  reads=[wfs[s_], gt], writes=[wbs[s_]])
            return wbs[s_]

        wnext = prep(0)
        for m in range(22):
            raw, bt, ga = raws[m % 2], bts[m % 2], gas[m % 2]
            wcur = wnext
            if m + 1 < 22:
                wnext = prep(m + 1)
            for j in range(NQ):
                b1_, b2_ = next_pb(), next_pb()
                gemm_fm(wcur, 8, 0, 128, HT, j, b1_)
                gemm_fm(wcur, 8, 128, 128, HT, j, b2_)
                kb.op("act", lambda e: e.copy(out=raw[:, 1 + j * 512:1 + (j + 1) * 512], in_=kb.ps(b1_)),
                      reads=[PB[b1_]], writes=[(raw, j)])
                kb.op("act", lambda e: e.copy(out=bt[:, j * 512:(j + 1) * 512], in_=kb.ps(b2_)),
                      reads=[PB[b2_]], writes=[(bt, j)])
            kb.op("dve", lambda e: e.tensor_scalar(out=acc[:], in0=raw[:, 0:L], scalar1=cw[:, m, 0:1], scalar2=cb[:, m:m + 1],
                                                   op0=ALU.mult, op1=ALU.add), reads=[raw, cw, cb], writes=[acc])
            kb.op("dve", lambda e: e.scalar_tensor_tensor(out=acc[:], in0=raw[:, 1:L + 1], scalar=cw[:, m, 1:2], in1=acc[:],
                                                          op0=ALU.mult, op1=ALU.add), reads=[raw, cw, acc], writes=[acc])
            kb.op("dve", lambda e: e.scalar_tensor_tensor(out=acc[:], in0=raw[:, 2:L + 2], scalar=cw[:, m, 2:3], in1=acc[:],
                                                          op0=ALU.mult, op1=ALU.add), reads=[raw, cw, acc], writes=[acc])
            kb.op("act", lambda e: e.activation(out=ga[:], in_=acc[:], func=AF.Gelu), reads=[acc], writes=[ga])
            kb.op("dve", lambda e: e.tensor_tensor(out=ga[:], in0=ga[:], in1=bt[:], op=ALU.mult), reads=[ga, bt], writes=[ga])
            kb.dma("pool", ACTfm[m * 128:(m + 1) * 128, :], ga[:], reads=[ga])

    def phase_ffn_down(li, xsrc, xdst, do_norm, preloaded=False):
        phase_begin(0, True)
        wd = kb.sb([128, 22, D], BF16)
        gp = kb.sb([128, D], F32)
        kb.dma("sp", gp[:], P["g_ffn_post"][li], writes=[gp])
        if not preloaded:
            for c0 in range(0, D, 128):
                load_w_into(wd, P["ffn_down"][li], 22, c0, 128, None, c0)
        asrc = ACTfm.t.rearrange("(k p) t -> p k t", p=128)
        at = [kb.sb([128, 22, 256], BF16) for _ in range(2)]
        pt = post_tiles()
        def load_a(c_):
            a = at[c_ % 2]
            kb.dma("sp", a[:, 0:11, :], asrc[:, 0:11, c_ * 256:(c_ + 1) * 256], writes=[(a, 0)])
            kb.dma("sp", a[:, 11:22, :], asrc[:, 11:22, c_ * 256:(c_ + 1) * 256], writes=[(a, 1)])

        load_a(0)

        def mm(i):
            if i % 2 == 0 and i // 2 + 1 < NT // 2:
                load_a(i // 2 + 1)
            a, ii = at[(i // 2) % 2], i % 2
            ba, bb = next_pb(), next_pb()
            for c0, bank in ((0, ba), (512, bb)):
                for k in range(22):
                    kb.op("pe", lambda e, k=k, c0=c0, bank=bank: e.matmul(
                        kb.ps(bank), lhsT=a[:, k, ii * 128:(ii + 1) * 128], rhs=wd[:, k, c0:c0 + 512],
                        start=(k == 0), stop=(k == 21)), reads=[a, wd], writes=[PB[bank]])
            return ba, bb

        post_loop(mm, xsrc, gp, xdst, pt, do_norm, {"A": 0, "B": 0, "C": 1, "D": 2})

    phase_filter(0)
    phase_norm0(x_in)
    for li in range(depth):
        xs = x_in if li == 0 else XB
        phase_proj(li)
        phase_hyena(li)
        phase_gla(li)
        phase_merge(li, xs, XA)
        phase_ffn_up(li)
        lastl = (li == depth - 1)
        if not lastl:
            phase_filter(li + 1, preload_down=li)
        phase_ffn_down(li, XA, out if lastl else XB, not lastl, preloaded=not lastl)
    nc = kb.finish()
    return nc, kb


_CACHE = {}


def _in_maps(inputs, L, depth, nb):
    consts = make_consts(L)
    params = relayout_params(inputs, depth)
    x = np.asarray(inputs["x"], np.float32)
    maps = []
    for b in range(nb):
        m = {"x": np.ascontiguousarray(x[b])}
        m.update(consts)
        m.update(params)
        maps.append(m)
    return maps


def kernel(**inputs):
    x = np.asarray(inputs["x"])
    B, L, _ = x.shape
    depth = int(np.asarray(inputs["w_in"]).shape[0])
    nc, _ = build(L, depth)
    maps = _in_maps(inputs, L, depth, B)
    res = run_bass_kernel_spmd(nc, maps, core_ids=list(range(B)))
    return np.stack([np.asarray(r["out"], np.float32) for r in res.results], 0)
```

```python
import contextlib
import math
import numpy as np
import ml_dtypes
import concourse.bass as bass
import concourse.mybir as mybir
from concourse.bass_utils import run_bass_kernel_spmd

F32 = mybir.dt.float32
BF16 = mybir.dt.bfloat16
AF = mybir.ActivationFunctionType
ALU = mybir.AluOpType
NDMA_SEM = 8

D = 1024
DH = 512
DIN = 7200
DFF = 2816
EPS = 1e-6


class T:
    def __init__(self, name, ap, parent=None):
        self.name = name
        self.t = ap
        if parent is None:
            self.w = {}
            self.r = {}
            self.root = self
        else:
            self.root = parent.root

    def __getitem__(self, idx):
        return self.t[idx]

    def view(self, ap):
        return T(self.name, ap, parent=self)


class Op:
    __slots__ = ("eng", "fn", "deps", "marked", "val", "sem", "isdma")

    def __init__(self, eng, fn):
        self.eng = eng
        self.fn = fn
        self.deps = []
        self.marked = False
        self.val = 0
        self.sem = None
        self.isdma = False


class _Rec:
    def __getattr__(self, name):
        def f(*a, **k):
            self.call = (name, a, k)
        return f


class KB:
    def __init__(self, sb_bytes):
        self.nc = bass.Bass("TRN2", target_bir_lowering=False)
        self.es = contextlib.ExitStack()
        self.ops = []
        nc = self.nc
        self.handles = {"pe": nc.tensor, "act": nc.scalar, "dve": nc.vector,
                        "pool": nc.gpsimd, "sp": nc.sync}
        self.sems = {e: self.es.enter_context(nc.semaphore("s_" + e)) for e in self.handles}
        self.dq = {}
        for q in ("sp", "act", "pool"):
            self.dq[q] = {"sems": [self.es.enter_context(nc.semaphore("d_%s%d" % (q, i)))
                                   for i in range(NDMA_SEM)],
                          "n": 0, "last": [None] * NDMA_SEM, "cnt": [0] * NDMA_SEM}
        self.last = {}
        self.arena = self.es.enter_context(nc.sbuf_tensor("arena", [128, sb_bytes // 2], BF16))
        self.sb_bytes = sb_bytes
        self.top = 0
        self.nbuf = 0
        self.pbanks = [self.es.enter_context(nc.psum_tensor("pb%d" % i, [128, 512], F32))
                       for i in range(8)]

    def sb(self, shape, dt, name=None):
        esz = 4 if dt == F32 else 2
        n = int(np.prod(shape[1:])) * esz
        n = (n + 63) // 64 * 64
        off = self.top
        self.top += n
        assert self.top <= self.sb_bytes, "SBUF arena overflow %d" % self.top
        ap = self.arena[:, off // 2:(off + n) // 2]
        if dt == F32:
            ap = ap.bitcast(F32)
        ap = ap[:, 0:int(np.prod(shape[1:]))]
        if len(shape) == 3:
            ap = ap.rearrange("p (a b) -> p a b", a=shape[1])
        elif len(shape) == 4:
            ap = ap.rearrange("p (a b c) -> p a b c", a=shape[1], b=shape[2])
        if shape[0] < 128:
            ap = ap[0:shape[0]]
        self.nbuf += 1
        return T(name or "sb%d" % self.nbuf, ap)

    def ps(self, bank, shape=None, dt=F32):
        ap = self.pbanks[bank][:]
        if dt == BF16:
            ap = ap.bitcast(BF16)
        if shape is not None and len(shape) == 3:
            ap = ap[:, 0:shape[1] * shape[2]].rearrange("p (a b) -> p a b", a=shape[1])
        elif shape is not None:
            ap = ap[:, 0:shape[1]]
        if shape is not None and shape[0] < 128:
            ap = ap[0:shape[0]]
        return ap

    def psT(self, bank):
        if not hasattr(self, "_pst"):
            self._pst = [T("pbank%d" % i, self.pbanks[i][:]) for i in range(8)]
        return self._pst[bank]

    def dram(self, name, shape, dt, kind="Internal"):
        return T(name, self.nc.dram_tensor(name, list(shape), dt, kind=kind).ap())

    @staticmethod
    def _norm(lst):
        out = []
        for x in lst:
            if isinstance(x, tuple):
                out.append((x[0].root, x[1]))
            else:
                out.append((x.root, None))
        return out

    def _hazards(self, op, reads, writes):
        deps = op.deps
        for t, key in reads:
            if key is None:
                deps.extend(t.w.values())
            else:
                for k in (key, None):
                    p = t.w.get(k)
                    if p is not None:
                        deps.append(p)
        for t, key in writes:
            if key is None:
                deps.extend(t.w.values())
                for l in t.r.values():
                    deps.extend(x for x in l if x.isdma or op.isdma or x.eng != op.eng or op.eng != 'pe')
            else:
                for k in (key, None):
                    p = t.w.get(k)
                    if p is not None:
                        deps.append(p)
                    deps.extend(x for x in t.r.get(k, ()) if x.isdma or op.isdma or x.eng != op.eng or op.eng != 'pe')
        for t, key in reads:
            l = t.r.setdefault(key, [])
            if not op.isdma:
                l[:] = [o for o in l if o.eng != op.eng or o.isdma]
            l.append(op)
        for t, key in writes:
            if key is None:
                t.w = {None: op}
                t.r = {}
            else:
                t.w[key] = op
                t.r[key] = []

    def op(self, eng, fn, reads=(), writes=()):
        rec = _Rec()
        fn(rec)
        name, a, k = rec.call
        o = Op(eng, lambda h: getattr(h, name)(*a, **k))
        self._hazards(o, self._norm(reads), self._norm(writes))
        if eng == "pe":
            o.deps = [d for d in o.deps if not (d.eng == "pe" and not d.isdma)]
        self.ops.append(o)
        self.last[eng] = o
        return o

    def dma(self, q, out, in_, reads=(), writes=()):
        o = Op(q, lambda e: e.dma_start(out=out, in_=in_))
        o.isdma = True
        dq = self.dq[q]
        i = dq["n"] % NDMA_SEM
        dq["n"] += 1
        if dq["last"][i] is not None:
            o.deps.append(dq["last"][i])
        dq["last"][i] = o
        dq["cnt"][i] += 16
        o.sem = dq["sems"][i]
        o.val = dq["cnt"][i]
        self._hazards(o, self._norm(reads), self._norm(writes))
        self.ops.append(o)
        return o

    def mark(self, label):
        o = Op("sp", None)
        o.sem = label
        o.marked = "label"
        self.ops.append(o)

    def barrier(self):
        deps = list(self.last.values())
        for q in self.dq.values():
            deps.extend(x for x in q["last"] if x is not None)
        for e in self.handles:
            o = Op(e, None)
            o.deps = list(deps)
            self.ops.append(o)

    def finish(self):
        self.barrier()
        for o in self.ops:
            for d in o.deps:
                if not d.isdma:
                    d.marked = True
        cnt = {e: 0 for e in self.handles}
        for o in self.ops:
            if not o.isdma and o.marked is True:
                cnt[o.eng] += 1
                o.val = cnt[o.eng]
                o.sem = self.sems[o.eng]
        seen = {e: {} for e in self.handles}
        nwait = 0
        self.marks = []
        for o in self.ops:
            if o.marked == "label":
                self.marks.append((o.sem, self.nc.get_next_instruction_name()))
                continue
            h = self.handles[o.eng]
            sn = seen[o.eng]
            for d in o.deps:
                k = id(d.sem)
                if sn.get(k, 0) >= d.val:
                    continue
                h.wait_ge(d.sem, d.val)
                nwait += 1
                sn[k] = d.val
            if o.fn is None:
                continue
            ins = o.fn(h)
            if o.isdma:
                ins.then_inc(o.sem, 16)
            elif o.marked is True:
                ins.then_inc(o.sem, 1)
        self.stats = {"ops": len(self.ops), "waits": nwait, "marked": dict(cnt)}
        return self.nc


def make_consts(L):
    bf = ml_dtypes.bfloat16
    c = {}
    c["ident"] = np.eye(128, dtype=np.float32).astype(bf)
    c["ones"] = np.ones((128, 128), np.float32).astype(bf)
    a = np.arange(128)
    uti = (a[:, None] <= a[None, :]).astype(np.float32)
    uts = (a[:, None] < a[None, :]).astype(np.float32)
    lti = (a[:, None] >= a[None, :]).astype(np.float32)
    lts = (a[:, None] > a[None, :]).astype(np.float32)
    c["tri32"] = np.stack([uti, uts, lti, lts], 1).astype(bf)
    t = np.linspace(0.0, 1.0, L, dtype=np.float32)
    bands = 16
    w = (2.0 * np.float32(math.pi) * np.arange(L, dtype=np.float32) / np.float32(L)).astype(np.float32)
    f = np.linspace(1e-4, bands - 1, bands, dtype=np.float32)
    ang = (f[None, :] * w[:, None]).astype(np.float32)
    z = np.concatenate([t[:, None], np.cos(ang), -np.sin(ang)], -1).astype(np.float32)
    c["zT"] = np.ascontiguousarray(z.T)
    c["tcol"] = np.ascontiguousarray(-t.reshape(L // 128, 128).T)
    max_decay = math.log(1e-2) / 0.3
    min_decay = math.log(1e-2) / 1.5
    deltas = np.abs(np.linspace(min_decay, max_decay, DH, dtype=np.float32))
    c["absd"] = np.ascontiguousarray(np.broadcast_to(deltas[None, :], (128, DH))).astype(np.float32)
    N = 2 * L
    N1 = N // 64
    H = N1 // 2
    n1 = np.arange(H, dtype=np.float64)[:, None, None]
    n2 = np.arange(64, dtype=np.float64)[None, :, None]
    f1 = np.arange(H, dtype=np.float64)[None, None, :]
    al = 2 * np.pi * ((f1 + 0.5) * n1 / N1 + (f1 + 0.5) * n2 / N)
    c["tw1"] = np.ascontiguousarray(np.stack([np.cos(al), -np.sin(al)], 1)).astype(bf)
    a64 = np.arange(64, dtype=np.float64)
    be = 2 * np.pi * np.outer(a64, a64) / 64
    c["dftm"] = np.ascontiguousarray(np.stack([np.cos(be), np.sin(be), -np.sin(be)], 1)).astype(bf)
    f2 = a64[:, None, None]
    f1b = np.arange(H, dtype=np.float64)[None, :, None]
    t2 = a64[None, None, :]
    ga = 2 * np.pi * (f2 * t2 / 64 + (f1b + 0.5) * t2 / N)
    c["gtw"] = np.ascontiguousarray(np.stack([np.cos(ga), np.sin(ga), -np.sin(ga)], 1)).astype(bf)
    f1c = np.arange(H, dtype=np.float64)[:, None]
    t1 = np.arange(H, dtype=np.float64)[None, :]
    ph = 2 * np.pi * (f1c + 0.5) * t1 / N1
    c["m4"] = np.ascontiguousarray(np.stack([(2.0 / N) * np.cos(ph), -(2.0 / N) * np.sin(ph)], 1)).astype(bf)
    pos = np.arange(L)
    inv = []
    for wv in (2, 4, 8, 16):
        half = wv // 2
        cntv = (np.minimum(pos + half, L) - np.maximum(pos - half, 0)).astype(np.float32)
        inv.append(1.0 / cntv)
    c["invcnt"] = np.ascontiguousarray(
        np.broadcast_to(np.stack(inv, 0)[:, None, :], (4, 128, L))).astype(np.float32)
    return c


def relayout_params(p, depth):
    f = np.float32
    o = {}

    def pk(v, nb):
        return np.ascontiguousarray(np.asarray(v, f).reshape(nb, 128).T)

    o["g_mix_pre"] = np.stack([pk(p["norm_mix_pre"][i], 8) for i in range(depth)])
    o["g_ffn_pre"] = np.stack([pk(p["norm_ffn_pre"][i], 8) for i in range(depth)])
    o["g_mix_post"] = np.ascontiguousarray(np.broadcast_to(
        np.asarray(p["norm_mix_post"], f)[:depth, None, :], (depth, 128, D)))
    o["g_ffn_post"] = np.ascontiguousarray(np.broadcast_to(
        np.asarray(p["norm_ffn_post"], f)[:depth, None, :], (depth, 128, D)))
    o["w_in"] = np.asarray(p["w_in"], f)[:depth]
    cw = np.asarray(p["hy_conv_w"], f)[:depth]
    o["hy_cw"] = np.ascontiguousarray(cw.reshape(depth, 3, 12, 128).transpose(0, 3, 2, 1))
    o["hy_cb"] = np.stack([pk(p["hy_conv_b"][i], 12) for i in range(depth)])
    o["hy_w1"] = np.asarray(p["hy_filt_w1"], f)[:depth]
    o["hy_w2"] = np.asarray(p["hy_filt_w2"], f)[:depth]
    o["hy_w3"] = np.asarray(p["hy_filt_w3"], f)[:depth]
    vec = np.stack([np.asarray(p["hy_filt_b1"], f)[:depth], np.asarray(p["hy_filt_freq1"], f)[:depth],
                    np.asarray(p["hy_filt_b2"], f)[:depth], np.asarray(p["hy_filt_freq2"], f)[:depth]], -1)
    o["hy_vec"] = np.ascontiguousarray(vec)
    o["hy_bias"] = np.stack([pk(p["hy_bias"][i], 4) for i in range(depth)])
    w2 = np.asarray(p["gla_gate_w2"], f)[:depth]
    gb = np.asarray(p["gla_gate_b"], f)[:depth]
    w2x = np.zeros((depth, 2, 33, 512), f)
    w2x[:, 0, 0:16] = w2[:, 0]
    w2x[:, 1, 16:32] = w2[:, 1]
    w2x[:, :, 32] = gb
    o["gla_w2x"] = np.ascontiguousarray(w2x.transpose(0, 2, 1, 3))
    o["gla_norm"] = np.asarray(p["gla_norm"], f)[:depth].reshape(depth, 128, 1)
    o["pool_w"] = np.ascontiguousarray(np.asarray(p["pool_w"], f)[:depth].transpose(0, 2, 1, 3))
    o["pool_scale"] = np.stack([pk(p["pool_scale"][i], 4) for i in range(depth)])
    o["w_br"] = np.ascontiguousarray(np.stack(
        [np.asarray(p["w_br_hyena"], f)[:depth], np.asarray(p["w_br_gla"], f)[:depth],
         np.asarray(p["w_br_pool"], f)[:depth]], 1))
    o["w_out"] = np.asarray(p["w_out"], f)[:depth]
    o["ffn_up"] = np.asarray(p["ffn_w_up"], f)[:depth]
    fw = np.asarray(p["ffn_conv_w"], f)[:depth]
    o["ffn_cw"] = np.ascontiguousarray(fw.reshape(depth, 3, 22, 128).transpose(0, 3, 2, 1))
    o["ffn_cb"] = np.stack([pk(p["ffn_conv_b"][i], 22) for i in range(depth)])
    o["ffn_down"] = np.asarray(p["ffn_w_down"], f)[:depth]
    return o


def build(L, depth, dbg=()):
    NT = L // 128
    NQ = L // 512
    NFB = 2 * NT
    HH = (2 * L // 64) // 2
    kb = KB(sb_bytes=200 * 1024)

    def din(name, shape, dt=F32):
        return kb.dram(name, shape, dt, kind="ExternalInput")

    def scr(name, shape, dt=BF16):
        return kb.dram(name, shape, dt, kind=("ExternalOutput" if name in dbg else "Internal"))

    x_in = din("x", [L, D])
    C = {k: din(k, list(v.shape), BF16 if v.dtype != np.float32 else F32)
         for k, v in make_consts(L if L <= 512 else 128 * 4).items()} if False else None
    cshapes = {"ident": ([128, 128], BF16), "ones": ([128, 128], BF16), "tri32": ([128, 4, 128], BF16), "zT": ([33, L], F32), "tcol": ([128, NT], F32),
               "absd": ([128, DH], F32), "tw1": ([HH, 2, 64, HH], BF16), "dftm": ([64, 3, 64], BF16),
               "gtw": ([64, 3, HH, 64], BF16), "m4": ([HH, 2, HH], BF16),
               "invcnt": ([4, 128, L], F32)}
    C = {k: din(k, s, dt) for k, (s, dt) in cshapes.items()}
    n = depth
    pshapes = {"g_mix_pre": [n, 128, 8], "g_ffn_pre": [n, 128, 8], "g_mix_post": [n, 128, D],
               "g_ffn_post": [n, 128, D], "w_in": [n, D, DIN], "hy_cw": [n, 128, 12, 3],
               "hy_cb": [n, 128, 12], "hy_w1": [n, 33, 64], "hy_w2": [n, 64, 64], "hy_w3": [n, 64, 1024],
               "hy_vec": [n, 64, 4], "hy_bias": [n, 128, 4], "gla_w2x": [n, 33, 2, 512],
               "gla_norm": [n, 128, 1], "pool_w": [n, 128, 4, 128], "pool_scale": [n, 128, 4],
               "w_br": [n, 3, DH, D], "w_out": [n, D, D], "ffn_up": [n, D, 2 * DFF],
               "ffn_cw": [n, 128, 22, 3], "ffn_cb": [n, 128, 22], "ffn_down": [n, DFF, D]}
    P = {k: din(k, s) for k, s in pshapes.items()}
    out = kb.dram("out", [L, D], F32, kind="ExternalOutput")

    XA = scr("XA", [L, D], F32)
    XB = scr("XB", [L, D], F32)
    X0fm = scr("X0fm", [DH, L])
    ZVfm = scr("ZVfm", [DH, L])
    ZVT = scr("ZVT", [L, DH])
    HSD = scr("HSD", [2, L, DH])
    A1Z = scr("A1Z", [2, HH, 64, DH])
    A1K = scr("A1K", [2, 2, HH, 64, DH])
    KS = scr("KS", [2, HH, 64, DH])
    B1 = scr("B1", [2, 64, HH, DH])
    Qfm = scr("Qfm", [DH, L])
    Kfm = scr("Kfm", [DH, L])
    Gfm = scr("Gfm", [DH, L])
    Ktok = scr("Ktok", [L, DH])
    Vtok = scr("Vtok", [L, DH])
    YH = scr("YH", [DH, L])
    YG = scr("YG", [DH, L])
    YP = scr("YP", [DH, L])
    GATES = scr("GATES", [3 * D, L])
    ACTfm = scr("ACTfm", [DFF, L])

    HT = kb.sb([128, 8, L], BF16, "HT")
    ident = kb.sb([128, 128], BF16, "ident")
    ones = kb.sb([128, 128], BF16, "ones")
    tri32 = kb.sb([128, 4, 128], BF16, "tri32")
    LRH = kb.sb([33, L], BF16, "LRH")
    LRL = kb.sb([33, L], BF16, "LRL")
    kb.dma("sp", ident[:], C["ident"][:, :], writes=[ident])
    kb.dma("sp", ones[:], C["ones"][:, :], writes=[ones])
    kb.dma("sp", tri32[:], C["tri32"][:, :, :], writes=[tri32])
    kb.op("dve", lambda e: e.memset(LRH[32:33, :], 1.0), writes=[LRH])
    kb.op("dve", lambda e: e.memset(LRL[32:33, :], 0.0), writes=[LRL])
    base_top = kb.top
    PB = [kb.psT(i) for i in range(8)]
    st = {"wb": 0, "pb": 0}

    def dump(name, t_, shape, dt=F32):
        if name in dbg:
            d_ = kb.dram(name, shape, dt, kind="ExternalOutput")
            kb.dma("sp", d_.t, t_.t, reads=[t_])

    def phase_begin(nwb=0, wf=False, label=None):
        kb.barrier()
        import inspect
        kb.mark(label or inspect.stack()[1].function + ":%d" % inspect.stack()[1].lineno)
        kb.top = base_top
        if wf or nwb:
            st["WF"] = kb.sb([128, 4096], F32, "WF")
        st["WB"] = [kb.sb([128, 6144], BF16, "WB%d" % i) for i in range(nwb)]

    def load_w(wap, kblocks, c0, ncols, gt=None):
        i = st["wb"] % len(st["WB"])
        st["wb"] += 1
        wb = st["WB"][i]
        WF = st["WF"]
        wbv = wb.view(wb.t[:, 0:kblocks * ncols].rearrange("p (k c) -> p k c", k=kblocks))
        kper = max(1, 4096 // ncols)
        src = wap.rearrange("(k p) c -> p k c", p=128)
        k0 = 0
        while k0 < kblocks:
            kn = min(kper, kblocks - k0)
            wfv = WF.t[:, 0:kn * ncols].rearrange("p (k c) -> p k c", k=kn)
            kb.dma("sp", wfv, src[:, k0:k0 + kn, c0:c0 + ncols], writes=[WF])
            if gt is None:
                kb.op("pool", lambda e, wfv=wfv, k0=k0, kn=kn: e.tensor_copy(out=wbv.t[:, k0:k0 + kn, :], in_=wfv),
                      reads=[WF], writes=[(wb, k0)])
            else:
                kb.op("pool", lambda e, wfv=wfv, k0=k0, kn=kn: e.tensor_tensor(
                    out=wbv.t[:, k0:k0 + kn, :], in0=wfv,
                    in1=gt.t[:, k0:k0 + kn, None].to_broadcast([128, kn, ncols]), op=ALU.mult),
                    reads=[WF, gt], writes=[(wb, k0)])
            k0 += kn
        return wbv

    def next_pb(nb=6):
        b = st["pb"] % nb
        st["pb"] += 1
        return b

    def gemm_fm(wbv, kblocks, mcol, mw, src, j, bank):
        for k in range(kblocks):
            kb.op("pe", lambda e, k=k: e.matmul(kb.ps(bank)[0:mw, :], lhsT=wbv.t[:, k, mcol:mcol + mw],
                                                rhs=src.t[:, k, j * 512:(j + 1) * 512],
                                                start=(k == 0), stop=(k == kblocks - 1)),
                  reads=[wbv, src], writes=[PB[bank]])

    def gemm_tok(wbv, kblocks, c0, ncols, src, i, bank, srckey=None):
        for k in range(kblocks):
            kb.op("pe", lambda e, k=k: e.matmul(kb.ps(bank)[:, 0:ncols], lhsT=src.t[:, k, i * 128:(i + 1) * 128],
                                                rhs=wbv.t[:, k, c0:c0 + ncols],
                                                start=(k == 0), stop=(k == kblocks - 1)),
                  reads=[wbv, (src, srckey) if srckey is not None else src], writes=[PB[bank]])

    def nt_b(hn, i):
        for k in range(8):
            kb.op("pe", lambda e, k=k: e.transpose(out=kb.ps(7, [128, 8, 128], BF16)[:, k, :],
                                                   in_=hn[:, k * 128:(k + 1) * 128], identity=ident[:]),
                  reads=[hn, ident], writes=[PB[7]])
        kb.op("act", lambda e: e.copy(out=HT[:, :, i * 128:(i + 1) * 128], in_=kb.ps(7, [128, 8, 128], BF16)),
              reads=[PB[7]], writes=[(HT, i)])

    def norm_transpose(xt, i, tmp, defer=None):
        junk, sq, rs, hn = tmp["junk"], tmp["sq"], tmp["rs"], tmp["hn"]
        kb.op("act", lambda e: e.activation(out=junk[:], in_=xt[:], func=AF.Square, scale=1.0 / 32.0,
                                            accum_out=sq[:]), reads=[xt], writes=[junk, sq])
        kb.op("act", lambda e: e.activation(out=rs[:], in_=sq[:], func=AF.Ln, bias=EPS), reads=[sq], writes=[rs])
        kb.op("act", lambda e: e.activation(out=rs[:], in_=rs[:], func=AF.Exp, scale=-0.5), reads=[rs], writes=[rs])
        kb.op("dve", lambda e: e.tensor_scalar(out=hn[:], in0=xt[:], scalar1=rs[:], scalar2=None, op0=ALU.mult),
              reads=[xt, rs], writes=[hn])
        if defer is not None:
            defer.append((hn, i))
        else:
            nt_b(hn, i)

    def norm_tmp(depth=2):
        junk = kb.sb([128, D], BF16)
        return [{"junk": junk, "sq": kb.sb([128, 1], F32), "rs": kb.sb([128, 1], F32),
                 "hn": kb.sb([128, D], BF16)} for _ in range(depth)]

    TWO_PI = 2.0 * math.pi

    def phase_filter(li, preload_down=None):
        phase_begin()
        HS = HT.view(HT.t[:, :, :].rearrange("p a b -> p (a b)")[:, 0:NT * 1024].rearrange(
            "p (n c) -> p n c", n=NT))
        w1 = kb.sb([33, 64], F32)
        w2 = kb.sb([64, 64], F32)
        w3 = kb.sb([64, 1024], F32)
        vec = kb.sb([64, 4], F32)
        pv = kb.sb([64, 2], F32)
        absd = kb.sb([128, DH], F32)
        tcol = kb.sb([128, NT], F32)
        H1 = kb.sb([64, L], F32)
        H2 = kb.sb([64, L], F32)
        kb.dma("sp", w1[:], P["hy_w1"][li], writes=[w1])
        kb.dma("sp", w2[:], P["hy_w2"][li], writes=[w2])
        kb.dma("sp", w3[:], P["hy_w3"][li], writes=[w3])
        kb.dma("sp", vec[:], P["hy_vec"][li], writes=[vec])
        kb.dma("sp", absd[:], C["absd"][:, :], writes=[absd])
        kb.dma("sp", tcol[:], C["tcol"][:, :], writes=[tcol])
        kb.op("dve", lambda e: e.tensor_tensor(out=pv[:, 0:1], in0=vec[:, 0:1], in1=vec[:, 1:2], op=ALU.mult),
              reads=[vec], writes=[pv])
        kb.op("dve", lambda e: e.tensor_tensor(out=pv[:, 1:2], in0=vec[:, 2:3], in1=vec[:, 3:4], op=ALU.mult),
              reads=[vec], writes=[pv])
        zt = [kb.sb([33, 512], F32) for _ in range(2)]
        arg = [kb.sb([64, 512], F32) for _ in range(2)]

        def sin_layer(wt, kdim, srcfn, dst, frcol, pvcol, j, bank):
            a = arg[j % 2]
            src, srcT = srcfn(j)
            kb.op("pe", lambda e: e.matmul(kb.ps(bank)[0:64, :], lhsT=wt[0:kdim, :], rhs=src,
                                           start=True, stop=True), reads=[wt, srcT], writes=[PB[bank]])
            kb.op("dve", lambda e: e.tensor_scalar(out=a[:], in0=kb.ps(bank)[0:64, :], scalar1=vec[:, frcol:frcol + 1],
                                                   scalar2=pv[:, pvcol:pvcol + 1], op0=ALU.mult, op1=ALU.add),
                  reads=[PB[bank], vec, pv], writes=[a])
            kb.op("dve", lambda e: e.tensor_scalar(out=ni[:], in0=a[:], scalar1=1.0 / TWO_PI, scalar2=None, op0=ALU.mult),
                  reads=[a], writes=[ni])
            kb.op("dve", lambda e: e.scalar_tensor_tensor(out=a[:], in0=ni[:], scalar=-TWO_PI, in1=a[:], op0=ALU.mult, op1=ALU.add),
                  reads=[ni, a], writes=[a])
            kb.op("dve", lambda e: e.tensor_scalar(out=m1[:], in0=a[:], scalar1=math.pi, scalar2=-TWO_PI, op0=ALU.is_gt, op1=ALU.mult),
                  reads=[a], writes=[m1])
            kb.op("dve", lambda e: e.tensor_scalar(out=m2[:], in0=a[:], scalar1=-math.pi, scalar2=TWO_PI, op0=ALU.is_lt, op1=ALU.mult),
                  reads=[a], writes=[m2])
            kb.op("dve", lambda e: e.tensor_tensor(out=a[:], in0=a[:], in1=m1[:], op=ALU.add), reads=[a, m1], writes=[a])
            kb.op("dve", lambda e: e.tensor_tensor(out=a[:], in0=a[:], in1=m2[:], op=ALU.add), reads=[a, m2], writes=[a])
            kb.op("dve", lambda e: e.tensor_scalar(out=a[:], in0=a[:], scalar1=math.pi, scalar2=-math.pi, op0=ALU.min, op1=ALU.max),
                  reads=[a], writes=[a])
            kb.op("act", lambda e: e.activation(out=dst[:, j * 512:(j + 1) * 512], in_=a[:], func=AF.Sin), reads=[a], writes=[(dst, j)])

        negpi = kb.sb([128, 1], F32)
        ni = kb.sb([64, 512], F32)
        ni = ni.view(ni.t.bitcast(mybir.dt.int32))
        m1 = kb.sb([64, 512], F32)
        m2 = kb.sb([64, 512], F32)
        kb.op("dve", lambda e: e.memset(negpi[:], -math.pi), writes=[negpi])
        for j in range(NQ):
            z = zt[j % 2]
            kb.dma("sp", z[:], C["zT"][:, j * 512:(j + 1) * 512], writes=[z])
            sin_layer(w1, 33, lambda j, z=z: (z[:], z), H1, 1, 0, j, next_pb())
        for j in range(NQ):
            sin_layer(w2, 64, lambda j: (H1[:, j * 512:(j + 1) * 512], H1), H2, 3, 1, j, next_pb())
        hsds = [kb.sb([128, 2, DH], BF16) for _ in range(2)]
        dec = [kb.sb([128, DH], F32) for _ in range(2)]
        t1 = [kb.sb([128, DH], F32) for _ in range(2)]
        t2 = [kb.sb([128, DH], F32) for _ in range(2)]
        for i in range(NT):
            dc, a1, a2 = dec[i % 2], t1[i % 2], t2[i % 2]
            b0, b1 = next_pb(), next_pb()
            for half, bank in ((0, b0), (1, b1)):
                kb.op("pe", lambda e, half=half, bank=bank: e.matmul(
                    kb.ps(bank), lhsT=H2[:, i * 128:(i + 1) * 128], rhs=w3[:, half * 512:(half + 1) * 512],
                    start=True, stop=True), reads=[H2, w3], writes=[PB[bank]])
            kb.op("act", lambda e, dc=dc: e.activation(out=dc[:], in_=absd[:], func=AF.Exp, scale=tcol[:, i:i + 1]),
                  reads=[absd, tcol], writes=[dc])
            kb.op("dve", lambda e, dc=dc, a1=a1, b0=b0: e.tensor_tensor(out=a1[:], in0=kb.ps(b0), in1=dc[:], op=ALU.mult),
                  reads=[PB[b0], dc], writes=[a1])
            kb.op("dve", lambda e, dc=dc, a2=a2, b1=b1: e.tensor_tensor(out=a2[:], in0=kb.ps(b1), in1=dc[:], op=ALU.mult),
                  reads=[PB[b1], dc], writes=[a2])
            if i == 0:
                kb.op("dve", lambda e, a2=a2: e.memset(a2[0:1, :], 0.0), reads=[a2], writes=[a2])
            hsd = hsds[i % 2]
            kb.op("pool", lambda e, a1=a1, a2=a2: e.tensor_tensor(out=hsd[:, 0, :], in0=a1[:], in1=a2[:], op=ALU.add),
                  reads=[a1, a2], writes=[(hsd, 0)])
            kb.op("pool", lambda e, a1=a1, a2=a2: e.tensor_tensor(out=hsd[:, 1, :], in0=a1[:], in1=a2[:], op=ALU.subtract),
                  reads=[a1, a2], writes=[(hsd, 1)])
            kb.dma("pool", HSD.t.rearrange("s n c -> n s c")[i * 128:(i + 1) * 128, :, :], hsd[:], reads=[hsd])
        phase_begin(label="filter_s1")
        tw1 = load_tw1()
        s1b = s1_bufs()
        fft_s1(HSD[0], A1K.t[0], tw1, s1b)
        fft_s1(HSD[1], A1K.t[1], tw1, s1b)
        phase_begin(0, True, label="filter_s2")
        preload = []
        if preload_down is not None:
            wd_pre = kb.sb([128, 22, D], BF16)
            preload = [(lambda c0=c0: load_w_into(wd_pre, P["ffn_down"][preload_down], 22, c0, 128, None, c0))
                       for c0 in range(0, D, 128)]
        dftm = kb.sb([64, 3, 64], BF16)
        kb.dma("sp", dftm[:], C["dftm"][:, :, :], writes=[dftm])
        FC = min(4, HH)
        at_ = [[kb.sb([64, FC, DH], BF16) for _ in range(4)] for _ in range(2)]
        ko = [[kb.sb([64, FC, DH], BF16) for _ in range(2)] for _ in range(2)]
        nch = HH // FC
        for c_ in range(nch):
            if preload and (c_ % max(1, nch // 8) == 0 or c_ == nch - 1):
                preload.pop(0)()
                if c_ == nch - 1:
                    while preload:
                        preload.pop(0)()
            f0 = c_ * FC
            A = at_[c_ % 2]
            for q_, (sg_, ri_) in enumerate(((0, 0), (0, 1), (1, 0), (1, 1))):
                kb.dma("sp", A[q_][:], A1K.t[sg_, ri_].rearrange("f n c -> n f c")[:, f0:f0 + FC, :], writes=[A[q_]])
            kos = ko[c_ % 2]
            for fl in range(FC):
                br, bi = next_pb(), next_pb()
                kb.op("pe", lambda e: e.matmul(kb.ps(br)[0:64, :], lhsT=dftm[:, 0, :], rhs=A[0][:, fl, :], start=True, stop=False),
                      reads=[dftm, A[0]], writes=[PB[br]])
                kb.op("pe", lambda e: e.matmul(kb.ps(br)[0:64, :], lhsT=dftm[:, 1, :], rhs=A[1][:, fl, :], start=False, stop=True),
                      reads=[dftm, A[1]], writes=[PB[br]])
                kb.op("pe", lambda e: e.matmul(kb.ps(bi)[0:64, :], lhsT=dftm[:, 0, :], rhs=A[3][:, fl, :], start=True, stop=False),
                      reads=[dftm, A[3]], writes=[PB[bi]])
                kb.op("pe", lambda e: e.matmul(kb.ps(bi)[0:64, :], lhsT=dftm[:, 2, :], rhs=A[2][:, fl, :], start=False, stop=True),
                      reads=[dftm, A[2]], writes=[PB[bi]])
                kb.op("act", lambda e: e.copy(out=kos[0][:, fl, :], in_=kb.ps(br)[0:64, :]), reads=[PB[br]], writes=[(kos[0], fl)])
                kb.op("dve", lambda e: e.tensor_copy(out=kos[1][:, fl, :], in_=kb.ps(bi)[0:64, :]), reads=[PB[bi]], writes=[(kos[1], fl)])
            for ri_ in range(2):
                kb.dma("pool", KS.t[ri_, f0:f0 + FC].rearrange("f k c -> k f c"), kos[ri_][:], reads=[kos[ri_]])

    def load_tw1():
        tw1 = kb.sb([HH, 2, 64, HH], BF16)
        kb.dma("sp", tw1[:], C["tw1"][:, :, :, :], writes=[tw1])
        return tw1

    def s1_bufs():
        return ([kb.sb([HH, 8, DH], BF16) for _ in range(2)],
                [[kb.sb([HH, 8, DH], BF16) for _ in range(2)] for _ in range(2)])

    def fft_s1(src, dst, tw1, s1b):
        xv = src.rearrange("(a b) c -> a b c", b=64)
        if not hasattr(fft_s1, "bufs"):
            pass
        xt, ot = s1b
        ne = 0
        for g in range(8):
            x_ = xt[g % 2]
            kb.dma("sp", x_[:], xv[:, g * 8:(g + 1) * 8, :], writes=[x_])
            for ri_ in range(2):
                o_ = ot[g % 2][ri_]
                for nl in range(8):
                    bank = next_pb()
                    kb.op("pe", lambda e: e.matmul(kb.ps(bank)[0:HH, :], lhsT=tw1[:, ri_, g * 8 + nl, :], rhs=x_[:, nl, :],
                                                   start=True, stop=True), reads=[tw1, x_], writes=[PB[bank]])
                    if ne % 2 == 0:
                        kb.op("act", lambda e: e.copy(out=o_[:, nl, :], in_=kb.ps(bank)[0:HH, :]), reads=[PB[bank]], writes=[(o_, nl)])
                    else:
                        kb.op("dve", lambda e: e.tensor_copy(out=o_[:, nl, :], in_=kb.ps(bank)[0:HH, :]), reads=[PB[bank]], writes=[(o_, nl)])
                    ne += 1
                kb.dma("pool", dst[ri_, :, g * 8:(g + 1) * 8, :], o_[:], reads=[o_])

    def phase_norm0(xsrc):
        phase_begin()
        tmps = norm_tmp()
        xts = [kb.sb([128, D], F32) for _ in range(2)]
        for i in range(NT):
            xt = xts[i % 2]
            kb.dma("sp", xt[:], xsrc[i * 128:(i + 1) * 128, :], writes=[xt])
            norm_transpose(xt, i, tmps[i % 2])

    def phase_proj(li):
        phase_begin(0)
        gt = kb.sb([128, 8], F32)
        kb.dma("sp", gt[:], P["g_mix_pre"][li], writes=[gt])
        W = P["w_in"][li]
        ev = [kb.sb([128, 512], BF16) for _ in range(4)]
        evc = [0]

        def next_ev():
            evc[0] += 1
            return ev[evc[0] % 4]

        cw = kb.sb([128, 12, 3], F32)
        cb = kb.sb([128, 12], F32)
        kb.dma("sp", cw[:], P["hy_cw"][li], writes=[cw])
        kb.dma("sp", cb[:], P["hy_cb"][li], writes=[cb])
        raws = [kb.sb([128, L + 2], BF16) for _ in range(2)]
        for r in raws:
            kb.op("pool", lambda e, r=r: e.memset(r[:, 0:1], 0.0), writes=[(r, "h0")])
            kb.op("pool", lambda e, r=r: e.memset(r[:, L + 1:L + 2], 0.0), writes=[(r, "h1")])
        acc = kb.sb([128, L], F32)
        x1c = kb.sb([128, L], BF16)
        oc = [kb.sb([128, L], BF16) for _ in range(2)]
        tz = [kb.sb([128, 8, 128], BF16) for _ in range(2)]
        wfs = [kb.sb([128, 8, 128], F32) for _ in range(2)]
        wbs = [kb.sb([128, 8, 128], BF16) for _ in range(2)]
        wsrc = W.rearrange("(k p) c -> p k c", p=128)
        order = [(b, part) for b in range(4) for part in range(3)]

        def prep(n_):
            b_, part_ = order[n_]
            blk_ = part_ * 4 + b_
            s_ = n_ % 2
            kb.dma("sp", wfs[s_][:], wsrc[:, :, blk_ * 128:(blk_ + 1) * 128], writes=[wfs[s_]])
            kb.op("pool", lambda e: e.tensor_tensor(out=wbs[s_][:], in0=wfs[s_][:],
                                                    in1=gt[:, :, None].to_broadcast([128, 8, 128]), op=ALU.mult),
                  reads=[wfs[s_], gt], writes=[wbs[s_]])
            return wbs[s_]

        zvt_v = ZVT.t.rearrange("(i p) c -> p i c", p=128)
        TB = min(8, NT)

        def transposes(dst, b):
            for i0 in range(0, NT, TB):
                tzt = tz[(i0 // TB) % 2]
                for ii in range(TB):
                    kb.op("pe", lambda e, ii=ii: e.transpose(out=kb.ps(7, [128, 8, 128], BF16)[:, ii, :],
                                                             in_=dst[:, (i0 + ii) * 128:(i0 + ii + 1) * 128], identity=ident[:]),
                          reads=[dst, ident], writes=[PB[7]])
                kb.op("act", lambda e: e.copy(out=tzt[:, 0:TB, :], in_=kb.ps(7, [128, 8, 128], BF16)[:, 0:TB, :]),
                      reads=[PB[7]], writes=[tzt])
                kb.dma("pool", zvt_v[:, i0:i0 + TB, b * 128:(b + 1) * 128], tzt[:, 0:TB, :], reads=[tzt])

        wnext = prep(0)
        pending = None
        for n_, (b, part) in enumerate(order):
            blk = part * 4 + b
            raw = raws[n_ % 2]
            wbv = wnext
            if n_ + 1 < len(order):
                wnext = prep(n_ + 1)
            for j in range(NQ):
                bank = next_pb()
                gemm_fm(wbv, 8, 0, 128, HT, j, bank)
                kb.op("act", lambda e: e.copy(out=raw[:, 1 + j * 512:1 + (j + 1) * 512], in_=kb.ps(bank)),
                      reads=[PB[bank]], writes=[(raw, j)])
            if pending is not None:
                transposes(*pending)
                pending = None
            dst = x1c if part == 1 else oc[0 if part == 0 else 1]
            kb.op("dve", lambda e: e.tensor_scalar(out=acc[:], in0=raw[:, 0:L], scalar1=cw[:, blk, 0:1], scalar2=cb[:, blk:blk + 1],
                                                   op0=ALU.mult, op1=ALU.add), reads=[raw, cw, cb], writes=[acc])
            kb.op("dve", lambda e: e.scalar_tensor_tensor(out=acc[:], in0=raw[:, 1:L + 1], scalar=cw[:, blk, 1:2], in1=acc[:],
                                                          op0=ALU.mult, op1=ALU.add), reads=[raw, cw, acc], writes=[acc])
            kb.op("dve", lambda e: e.scalar_tensor_tensor(out=dst[:], in0=raw[:, 2:L + 2], scalar=cw[:, blk, 2:3], in1=acc[:],
                                                          op0=ALU.mult, op1=ALU.add), reads=[raw, cw, acc], writes=[dst])
            if part == 0:
                kb.dma("pool", X0fm[b * 128:(b + 1) * 128, :], dst[:], reads=[dst])
            elif part == 2:
                kb.op("dve", lambda e: e.tensor_tensor(out=dst[:], in0=dst[:], in1=x1c[:], op=ALU.mult),
                      reads=[dst, x1c], writes=[dst])
                kb.dma("pool", ZVfm[b * 128:(b + 1) * 128, :], dst[:], reads=[dst])
                pending = (dst, b)
        transposes(*pending)

        phase_begin(2)
        gt = kb.sb([128, 8], F32)
        kb.dma("sp", gt[:], P["g_mix_pre"][li], writes=[gt])
        ev = [kb.sb([128, 512], BF16) for _ in range(4)]
        def fm_group(col0, ncols, dest, func, scale=1.0):
            for c0 in range(0, ncols, 512):
                nc_ = min(512, ncols - c0)
                wbv = load_w(W, 8, col0 + c0, nc_, gt)
                for m in range(nc_ // 128):
                    for j in range(NQ):
                        bank = next_pb()
                        gemm_fm(wbv, 8, m * 128, 128, HT, j, bank)
                        o = next_ev()
                        kb.op("act", lambda e, o=o, bank=bank: e.activation(out=o[:], in_=kb.ps(bank), func=func, scale=scale),
                              reads=[PB[bank]], writes=[o])
                        r0 = c0 + m * 128
                        kb.dma("pool", dest[r0:r0 + 128, j * 512:(j + 1) * 512], o[:], reads=[o])

        def tok_group(col0, dest):
            wbv = load_w(W, 8, col0, 512, gt)
            for i in range(NT):
                bank = next_pb()
                gemm_tok(wbv, 8, 0, 512, HT, i, bank)
                o = next_ev()
                if i % 2 == 0:
                    kb.op("dve", lambda e, o=o, bank=bank: e.tensor_copy(out=o[:], in_=kb.ps(bank)), reads=[PB[bank]], writes=[o])
                else:
                    kb.op("act", lambda e, o=o, bank=bank: e.copy(out=o[:], in_=kb.ps(bank)), reads=[PB[bank]], writes=[o])
                kb.dma("pool", dest[i * 128:(i + 1) * 128, :], o[:], reads=[o])

        fm_group(1536, 512, Qfm, AF.Copy, 128.0 ** -0.5)
        fm_group(2048, 512, Kfm, AF.Copy)
        tok_group(2048, Ktok)
        tok_group(2560, Vtok)
        fm_group(3072, 512, Gfm, AF.Silu)
        wbv = load_w(W, 8, 3584, 32, gt)
        for j in range(NQ):
            bank = next_pb()
            gemm_fm(wbv, 8, 0, 32, HT, j, bank)
            kb.op("act", lambda e, j=j, bank=bank: e.copy(out=LRH[0:32, j * 512:(j + 1) * 512], in_=kb.ps(bank)[0:32, :]),
                  reads=[PB[bank]], writes=[(LRH, j)])
            kb.op("dve", lambda e, j=j, bank=bank: e.tensor_tensor(out=LRL[0:32, j * 512:(j + 1) * 512], in0=kb.ps(bank)[0:32, :],
                                                                   in1=LRH[0:32, j * 512:(j + 1) * 512], op=ALU.subtract),
                  reads=[PB[bank], (LRH, j)], writes=[(LRL, j)])
        fm_group(4128, 3 * D, GATES, AF.Sigmoid)

        phase_begin(1)
        gt = kb.sb([128, 8], F32)
        kb.dma("sp", gt[:], P["g_mix_pre"][li], writes=[gt])
        ev = [kb.sb([128, 512], BF16) for _ in range(4)]
        PW = 16
        ua = kb.sb([128, L + 2 * PW], F32)
        ub = kb.sb([128, L + 2 * PW], F32)
        uc = kb.sb([128, L + 2 * PW], F32)
        icn = kb.sb([128, L], F32)
        pwt = kb.sb([128, 4, 128], F32)
        pwb = kb.sb([128, 4, 128], BF16)
        psc = kb.sb([128, 4], F32)
        dbf = kb.sb([128, L], BF16)
        kb.dma("sp", pwt[:], P["pool_w"][li], writes=[pwt])
        kb.dma("sp", psc[:], P["pool_scale"][li], writes=[psc])
        kb.op("pool", lambda e: e.tensor_copy(out=pwb[:], in_=pwt[:]), reads=[pwt], writes=[pwb])
        for t_ in (ua, ub, uc):
            kb.op("pool", lambda e, t_=t_: e.memset(t_[:], 0.0), writes=[t_])
        for gi, wv in enumerate((2, 4, 8, 16)):
            wbv = load_w(W, 8, 3616 + gi * 128, 128, gt)
            kb.dma("sp", icn[:], C["invcnt"][gi], writes=[icn])
            for j in range(NQ):
                bank = next_pb()
                gemm_fm(wbv, 8, 0, 128, HT, j, bank)
                kb.op("act", lambda e, j=j, bank=bank: e.copy(out=ua[:, PW + j * 512:PW + (j + 1) * 512], in_=kb.ps(bank)),
                      reads=[PB[bank]], writes=[ua])
            src, dsts = ua, [ub, uc]
            lo, hi = -14, L + 14
            kb.op("dve", lambda e, lo=lo, hi=hi: e.tensor_tensor(
                out=ub[:, PW + lo:PW + hi], in0=ua[:, PW + lo - 1:PW + hi - 1], in1=ua[:, PW + lo:PW + hi], op=ALU.add),
                reads=[ua], writes=[ub])
            cur, oth = ub, uc
            sh = 1
            rng = [(-12, L + 12), (-8, L + 8), (0, L)]
            for si in range(int(math.log2(wv)) - 1):
                lo, hi = rng[si]
                kb.op("dve", lambda e, lo=lo, hi=hi, cur=cur, oth=oth, sh=sh: e.tensor_tensor(
                    out=oth[:, PW + lo:PW + hi], in0=cur[:, PW + lo - sh:PW + hi - sh],
                    in1=cur[:, PW + lo + sh:PW + hi + sh], op=ALU.add), reads=[cur], writes=[oth])
                cur, oth = oth, cur
                sh *= 2
            kb.op("dve", lambda e, cur=cur, oth=oth: e.tensor_tensor(out=oth[:, PW:PW + L], in0=cur[:, PW:PW + L], in1=icn[:], op=ALU.mult),
                  reads=[cur, icn], writes=[oth])
            kb.op("dve", lambda e, oth=oth: e.tensor_tensor(out=dbf[:], in0=oth[:, PW:PW + L], in1=ua[:, PW:PW + L], op=ALU.subtract),
                  reads=[oth, ua], writes=[dbf])
            for t_ in (ub, uc):
                kb.op("pool", lambda e, t_=t_: e.memset(t_[:, 0:PW], 0.0), reads=[t_], writes=[t_])
                kb.op("pool", lambda e, t_=t_: e.memset(t_[:, PW + L:PW + L + PW], 0.0), reads=[t_], writes=[t_])
            for j in range(NQ):
                bank = next_pb()
                kb.op("pe", lambda e, j=j, bank=bank, gi=gi: e.matmul(kb.ps(bank), lhsT=pwb[:, gi, :], rhs=dbf[:, j * 512:(j + 1) * 512],
                                                               start=True, stop=True), reads=[pwb, dbf], writes=[PB[bank]])
                o = next_ev()
                kb.op("dve", lambda e, o=o, bank=bank, gi=gi: e.tensor_scalar(out=o[:], in0=kb.ps(bank), scalar1=psc[:, gi:gi + 1],
                                                                       scalar2=None, op0=ALU.mult), reads=[PB[bank], psc], writes=[o])
                kb.dma("pool", YP[gi * 128:(gi + 1) * 128, j * 512:(j + 1) * 512], o[:], reads=[o])

    def phase_hyena(li):
        phase_begin(label="hy_s1")
        tw1 = load_tw1()
        fft_s1(ZVT.t, A1Z.t, tw1, s1_bufs())
        phase_begin(label="hy_s2")
        dftm = kb.sb([64, 3, 64], BF16)
        gtw = kb.sb([64, 3, HH, 64], BF16)
        kb.dma("sp", dftm[:], C["dftm"][:, :, :], writes=[dftm])
        kb.dma("sp", gtw[:], C["gtw"][:, :, :, :], writes=[gtw])
        FC = min(4, HH)
        at_ = [[kb.sb([64, FC, DH], BF16) for _ in range(2)] for _ in range(2)]
        kt_ = [[kb.sb([64, FC, DH], BF16) for _ in range(2)] for _ in range(2)]
        bo = [[kb.sb([64, FC, DH], BF16) for _ in range(2)] for _ in range(2)]
        m = [[kb.sb([64, DH], F32) for _ in range(4)] for _ in range(2)]
        yy = [[kb.sb([64, DH], BF16) for _ in range(2)] for _ in range(2)]
        def s2_prod(f1_):
            c_, fl = divmod(f1_, FC)
            f0 = c_ * FC
            A, K_ = at_[c_ % 2], kt_[c_ % 2]
            if fl == 0:
                for ri_ in range(2):
                    kb.dma("sp", A[ri_][:], A1Z.t[ri_].rearrange("f n c -> n f c")[:, f0:f0 + FC, :], writes=[A[ri_]])
                    kb.dma("sp", K_[ri_][:], KS.t[ri_, f0:f0 + FC].rearrange("f k c -> k f c"), writes=[K_[ri_]])
            mm_, y_ = m[f1_ % 2], yy[f1_ % 2]
            zr, zi = f1_ % 2, 2 + f1_ % 2
            kb.op("pe", lambda e: e.matmul(kb.ps(zr)[0:64, :], lhsT=dftm[:, 0, :], rhs=A[0][:, fl, :], start=True, stop=False),
                  reads=[dftm, A[0]], writes=[PB[zr]])
            kb.op("pe", lambda e: e.matmul(kb.ps(zr)[0:64, :], lhsT=dftm[:, 1, :], rhs=A[1][:, fl, :], start=False, stop=True),
                  reads=[dftm, A[1]], writes=[PB[zr]])
            kb.op("pe", lambda e: e.matmul(kb.ps(zi)[0:64, :], lhsT=dftm[:, 0, :], rhs=A[1][:, fl, :], start=True, stop=False),
                  reads=[dftm, A[1]], writes=[PB[zi]])
            kb.op("pe", lambda e: e.matmul(kb.ps(zi)[0:64, :], lhsT=dftm[:, 2, :], rhs=A[0][:, fl, :], start=False, stop=True),
                  reads=[dftm, A[0]], writes=[PB[zi]])
            kb.op("dve", lambda e: e.tensor_tensor(out=mm_[0][:], in0=kb.ps(zr)[0:64, :], in1=K_[0][:, fl, :], op=ALU.mult),
                  reads=[PB[zr], K_[0]], writes=[mm_[0]])
            kb.op("dve", lambda e: e.tensor_tensor(out=mm_[1][:], in0=kb.ps(zi)[0:64, :], in1=K_[1][:, fl, :], op=ALU.mult),
                  reads=[PB[zi], K_[1]], writes=[mm_[1]])
            kb.op("dve", lambda e: e.tensor_tensor(out=mm_[2][:], in0=kb.ps(zr)[0:64, :], in1=K_[1][:, fl, :], op=ALU.mult),
                  reads=[PB[zr], K_[1]], writes=[mm_[2]])
            kb.op("dve", lambda e: e.tensor_tensor(out=mm_[3][:], in0=kb.ps(zi)[0:64, :], in1=K_[0][:, fl, :], op=ALU.mult),
                  reads=[PB[zi], K_[0]], writes=[mm_[3]])
            kb.op("pool", lambda e: e.tensor_tensor(out=y_[0][:], in0=mm_[0][:], in1=mm_[1][:], op=ALU.subtract),
                  reads=[mm_[0], mm_[1]], writes=[y_[0]])
            kb.op("pool", lambda e: e.tensor_tensor(out=y_[1][:], in0=mm_[2][:], in1=mm_[3][:], op=ALU.add),
                  reads=[mm_[2], mm_[3]], writes=[y_[1]])

        def inv_a(f1_):
            c_, fl = divmod(f1_, FC)
            f0 = c_ * FC
            y_ = yy[f1_ % 2]
            bos = bo[c_ % 2]
            br, bi = 4 + f1_ % 2, 6 + f1_ % 2
            kb.op("pe", lambda e: e.matmul(kb.ps(br)[0:64, :], lhsT=gtw[:, 0, f1_, :], rhs=y_[0][:], start=True, stop=False),
                  reads=[gtw, y_[0]], writes=[PB[br]])
            kb.op("pe", lambda e: e.matmul(kb.ps(br)[0:64, :], lhsT=gtw[:, 2, f1_, :], rhs=y_[1][:], start=False, stop=True),
                  reads=[gtw, y_[1]], writes=[PB[br]])
            kb.op("pe", lambda e: e.matmul(kb.ps(bi)[0:64, :], lhsT=gtw[:, 1, f1_, :], rhs=y_[0][:], start=True, stop=False),
                  reads=[gtw, y_[0]], writes=[PB[bi]])
            kb.op("pe", lambda e: e.matmul(kb.ps(bi)[0:64, :], lhsT=gtw[:, 0, f1_, :], rhs=y_[1][:], start=False, stop=True),
                  reads=[gtw, y_[1]], writes=[PB[bi]])
            kb.op("act", lambda e: e.copy(out=bos[0][:, fl, :], in_=kb.ps(br)[0:64, :]), reads=[PB[br]], writes=[(bos[0], fl)])
            kb.op("act", lambda e: e.copy(out=bos[1][:, fl, :], in_=kb.ps(bi)[0:64, :]), reads=[PB[bi]], writes=[(bos[1], fl)])
            if fl == FC - 1:
                for ri_ in range(2):
                    kb.dma("pool", B1.t[ri_, :, f0:f0 + FC, :], bos[ri_][:], reads=[bos[ri_]])

        for k_ in range(HH + 1):
            if k_ < HH:
                s2_prod(k_)
            if k_ >= 1:
                inv_a(k_ - 1)
        phase_begin(label="hy_ib")
        hb = kb.sb([128, 4], F32)
        kb.dma("sp", hb[:], P["hy_bias"][li], writes=[hb])
        m4 = kb.sb([HH, 2, HH], BF16)
        kb.dma("sp", m4[:], C["m4"][:, :, :], writes=[m4])
        ysb = kb.sb([128, 4, L], BF16)
        bt_ = [[kb.sb([HH, 8, DH], BF16) for _ in range(2)] for _ in range(2)]
        for g in range(8):
            Bt = bt_[g % 2]
            for ri_ in range(2):
                kb.dma("sp", Bt[ri_][:], B1.t[ri_].rearrange("t f c -> f t c")[:, g * 8:(g + 1) * 8, :], writes=[Bt[ri_]])
            for cb_ in range(4):
                bank = next_pb()
                for tl in range(8):
                    kb.op("pe", lambda e: e.matmul(kb.ps(bank, [128, 8, HH])[:, tl, :], lhsT=Bt[0][:, tl, cb_ * 128:(cb_ + 1) * 128],
                                                   rhs=m4[:, 0, :], start=True, stop=False), reads=[Bt[0], m4], writes=[PB[bank]])
                    kb.op("pe", lambda e: e.matmul(kb.ps(bank, [128, 8, HH])[:, tl, :], lhsT=Bt[1][:, tl, cb_ * 128:(cb_ + 1) * 128],
                                                   rhs=m4[:, 1, :], start=False, stop=True), reads=[Bt[1], m4], writes=[PB[bank]])
                dst = ysb[:, cb_, :].rearrange("p (a b) -> p a b", b=64)[:, :, g * 8:(g + 1) * 8]
                src_ = kb.ps(bank, [128, 8, HH]).rearrange("p a b -> p b a")
                if (g * 4 + cb_) % 2 == 0:
                    kb.op("act", lambda e: e.copy(out=dst, in_=src_), reads=[PB[bank]], writes=[(ysb, (cb_, g))])
                else:
                    kb.op("dve", lambda e: e.tensor_copy(out=dst, in_=src_), reads=[PB[bank]], writes=[(ysb, (cb_, g))])
        x0t = [kb.sb([128, 512], BF16) for _ in range(2)]
        zvf = [kb.sb([128, 512], BF16) for _ in range(2)]
        tmp = [kb.sb([128, 512], F32) for _ in range(2)]
        yo = [kb.sb([128, 512], BF16) for _ in range(2)]
        for tt in range(NQ):
            for cb_ in range(4):
                k = (tt * 4 + cb_) % 2
                kb.dma("sp", x0t[k][:], X0fm[cb_ * 128:(cb_ + 1) * 128, tt * 512:(tt + 1) * 512], writes=[x0t[k]])
                kb.dma("sp", zvf[k][:], ZVfm[cb_ * 128:(cb_ + 1) * 128, tt * 512:(tt + 1) * 512], writes=[zvf[k]])
                kb.op("dve", lambda e: e.scalar_tensor_tensor(
                    out=tmp[k][:], in0=zvf[k][:], scalar=hb[:, cb_:cb_ + 1], in1=ysb[:, cb_, tt * 512:(tt + 1) * 512],
                    op0=ALU.mult, op1=ALU.add), reads=[zvf[k], hb, ysb], writes=[tmp[k]])
                kb.op("dve", lambda e: e.tensor_tensor(out=yo[k][:], in0=tmp[k][:], in1=x0t[k][:], op=ALU.mult),
                      reads=[tmp[k], x0t[k]], writes=[yo[k]])
                kb.dma("pool", YH[cb_ * 128:(cb_ + 1) * 128, tt * 512:(tt + 1) * 512], yo[k][:], reads=[yo[k]])
        st["pb"] = 0

    def phase_gla(li):
        phase_begin(0, True)
        wbr_pre = [kb.sb([128, 4, D], BF16) for _ in range(3)]
        preload = [(lambda b=b, c0=c0: load_w_into(wbr_pre[b], P["w_br"][li, b], 4, c0, 512, None, c0))
                   for b in range(3) for c0 in (0, 512)]
        w2x = kb.sb([33, 2, 512], F32)
        gn = kb.sb([128, 1], F32)
        kb.dma("sp", w2x[:], P["gla_w2x"][li], writes=[w2x])
        kb.dma("sp", gn[:], P["gla_norm"][li], writes=[gn])
        w2h = kb.sb([33, 2, 512], BF16)
        w2l = kb.sb([33, 2, 512], BF16)
        kb.op("dve", lambda e: e.tensor_copy(out=w2h[:], in_=w2x[:]), reads=[w2x], writes=[w2h])
        kb.op("dve", lambda e: e.tensor_tensor(out=w2l[:], in0=w2x[:], in1=w2h[:], op=ALU.subtract), reads=[w2x, w2h], writes=[w2l])
        OB = HT.view(HT.t[:, 0:4, :])
        S32 = [kb.sb([128, 4, 128], F32) for _ in range(2)]
        Sbf = [kb.sb([128, 4, 128], BF16) for _ in range(2)]
        for d_ in range(2):
            kb.op("pool", lambda e, d_=d_: e.memset(S32[d_][:], 0.0), writes=[S32[d_]])
            kb.op("pool", lambda e, d_=d_: e.memset(Sbf[d_][:], 0.0), writes=[Sbf[d_]])
        lah = [kb.sb([128, 512], BF16) for _ in range(2)]
        lal = [kb.sb([128, 512], BF16) for _ in range(2)]
        PD = lambda shape, dt: [kb.sb(shape, dt) for _ in range(2)]
        PS = lambda shape, dt: [[kb.sb(shape, dt) for _ in range(2)] for _ in range(2)]
        e1, la, edec = PD([128, 512], F32), PD([128, 512], F32), PD([128, 512], BF16)
        EK = PD([128, 4, 128], BF16)
        kt, qf, kf, ke = PD([128, 512], BF16), PD([128, 4, 128], BF16), PD([128, 4, 128], BF16), PD([128, 4, 128], BF16)
        EQ = PS([128, 4, 128], BF16)
        DEC = PS([128, 4, 1], F32)
        vt, kdec = PS([128, 512], BF16), PS([128, 512], BF16)
        qe, msk = PS([128, 4, 128], BF16), PS([128, 4, 128], BF16)
        gf, sqb, yob = PD([128, 4, 128], BF16), PD([128, 4, 128], BF16), PD([128, 4, 128], BF16)
        o32, rsd = PD([128, 4, 128], F32), PD([128, 4, 128], F32)
        qv = Qfm.t.rearrange("(h d) t -> d h t", h=4)
        kv = Kfm.t.rearrange("(h d) t -> d h t", h=4)
        gv_ = Gfm.t.rearrange("(h d) t -> d h t", h=4)
        yv = YG.t.rearrange("(h d) t -> d h t", h=4)
        V3 = [128, 4, 128]

        deferred = []

        def tile_of(step, d_):
            return step if d_ == 0 else NT - 1 - step

        def pre(step, d_, stage):
            i = tile_of(step, d_)
            sl = slice(i * 128, (i + 1) * 128)
            p = step % 2
            bA, bB = 4 * d_, 4 * d_ + 1
            tri_fm = 0 if d_ == 0 else 2
            tri_dec = 3 if d_ == 0 else 1
            if stage == 1:
                pre1(d_, p, sl, bA)
            elif stage == 2:
                pre2(d_, p, bA, bB, tri_fm, tri_dec)
            else:
                pre3(d_, p, bA)

        def pre1(d_, p, sl, bA):
            kb.dma("sp", kt[d_][:], Ktok[sl, :], writes=[kt[d_]])
            kb.dma("sp", vt[d_][p][:], Vtok[sl, :], writes=[vt[d_][p]])
            kb.dma("sp", qf[d_][:], qv[:, :, sl], writes=[qf[d_]])
            kb.dma("sp", kf[d_][:], kv[:, :, sl], writes=[kf[d_]])
            kb.op("pe", lambda e: e.matmul(kb.ps(bA), lhsT=LRH[:, sl], rhs=w2h[:, d_, :], start=True, stop=False),
                  reads=[LRH, w2h], writes=[PB[bA]])
            kb.op("pe", lambda e: e.matmul(kb.ps(bA), lhsT=LRL[:, sl], rhs=w2h[:, d_, :], start=False, stop=False),
                  reads=[LRL, w2h], writes=[PB[bA]])
            kb.op("pe", lambda e: e.matmul(kb.ps(bA), lhsT=LRH[:, sl], rhs=w2l[:, d_, :], start=False, stop=True),
                  reads=[LRH, w2l], writes=[PB[bA]])
            kb.op("act", lambda e: e.activation(out=e1[d_][:], in_=kb.ps(bA), func=AF.Exp, scale=-1.0),
                  reads=[PB[bA]], writes=[e1[d_]])
            kb.op("act", lambda e: e.activation(out=e1[d_][:], in_=e1[d_][:], func=AF.Ln, bias=1.0),
                  reads=[e1[d_]], writes=[e1[d_]])
            kb.op("dve", lambda e: e.tensor_scalar(out=la[d_][:], in0=e1[d_][:], scalar1=-1.0 / 16.0, scalar2=-1.0,
                                                   op0=ALU.mult, op1=ALU.max), reads=[e1[d_]], writes=[la[d_]])
            kb.op("act", lambda e: e.copy(out=lah[d_][:], in_=la[d_][:]), reads=[la[d_]], writes=[lah[d_]])
            kb.op("dve", lambda e: e.tensor_tensor(out=lal[d_][:], in0=la[d_][:], in1=lah[d_][:], op=ALU.subtract),
                  reads=[la[d_], lah[d_]], writes=[lal[d_]])

        def pre2(d_, p, bA, bB, tri_fm, tri_dec):
            kb.op("pe", lambda e: e.matmul(kb.ps(bA), lhsT=tri32[:, tri_dec, :], rhs=lah[d_][:], start=True, stop=False),
                  reads=[tri32, lah[d_]], writes=[PB[bA]])
            kb.op("pe", lambda e: e.matmul(kb.ps(bA), lhsT=tri32[:, tri_dec, :], rhs=lal[d_][:], start=False, stop=True),
                  reads=[tri32, lal[d_]], writes=[PB[bA]])
            for h in range(4):
                kb.op("pe", lambda e, h=h: e.matmul(kb.ps(bB, V3)[:, h, :], lhsT=lah[d_][:, h * 128:(h + 1) * 128],
                                                    rhs=tri32[:, tri_fm, :], start=True, stop=False),
                      reads=[tri32, lah[d_]], writes=[PB[bB]])
                kb.op("pe", lambda e, h=h: e.matmul(kb.ps(bB, V3)[:, h, :], lhsT=lal[d_][:, h * 128:(h + 1) * 128],
                                                    rhs=tri32[:, tri_fm, :], start=False, stop=True),
                      reads=[tri32, lal[d_]], writes=[PB[bB]])
            kb.op("act", lambda e: e.activation(out=edec[d_][:], in_=kb.ps(bA), func=AF.Exp), reads=[PB[bA]], writes=[edec[d_]])
            kb.op("act", lambda e: e.activation(out=EQ[d_][p][:], in_=kb.ps(bB, V3), func=AF.Exp), reads=[PB[bB]], writes=[EQ[d_][p]])
            kb.op("act", lambda e: e.activation(out=EK[d_][:], in_=kb.ps(bB, V3), func=AF.Exp, scale=-1.0),
                  reads=[PB[bB]], writes=[EK[d_]])
            dcol_ = 127 if d_ == 0 else 0
            kb.op("act", lambda e: e.activation(out=DEC[d_][p][:], in_=kb.ps(bB, V3)[:, :, dcol_:dcol_ + 1], func=AF.Exp),
                  reads=[PB[bB]], writes=[DEC[d_][p]])
            kb.op("dve", lambda e: e.tensor_tensor(out=kdec[d_][p][:], in0=kt[d_][:], in1=edec[d_][:], op=ALU.mult),
                  reads=[kt[d_], edec[d_]], writes=[kdec[d_][p]])
            kb.op("dve", lambda e: e.tensor_tensor(out=qe[d_][p][:], in0=qf[d_][:], in1=EQ[d_][p][:], op=ALU.mult),
                  reads=[qf[d_], EQ[d_][p]], writes=[qe[d_][p]])
            kb.op("dve", lambda e: e.tensor_tensor(out=ke[d_][:], in0=kf[d_][:], in1=EK[d_][:], op=ALU.mult),
                  reads=[kf[d_], EK[d_]], writes=[ke[d_]])

        def pre3(d_, p, bA):
            for h in range(4):
                kb.op("pe", lambda e, h=h: e.matmul(kb.ps(bA, V3)[:, h, :], lhsT=ke[d_][:, h, :], rhs=qe[d_][p][:, h, :],
                                                    start=True, stop=True), reads=[ke[d_], qe[d_][p]], writes=[PB[bA]])
            kb.op("dve", lambda e: e.tensor_tensor(out=msk[d_][p][:], in0=kb.ps(bA, V3),
                                                   in1=tri32[:, 3 * d_:3 * d_ + 1, :].to_broadcast(V3), op=ALU.mult),
                  reads=[PB[bA], tri32], writes=[msk[d_][p]])

        def seq(step, d_):
            i = tile_of(step, d_)
            sl = slice(i * 128, (i + 1) * 128)
            p = step % 2
            bC, bD = 4 * d_ + 2, 4 * d_ + 3
            final = step >= NT // 2
            for h in range(4):
                hs = slice(h * 128, (h + 1) * 128)
                kb.op("pe", lambda e, h=h, hs=hs: e.matmul(kb.ps(bC, V3)[:, h, :], lhsT=vt[d_][p][:, hs], rhs=msk[d_][p][:, h, :],
                                                           start=True, stop=False), reads=[vt[d_][p], msk[d_][p]], writes=[PB[bC]])
                kb.op("pe", lambda e, h=h: e.matmul(kb.ps(bC, V3)[:, h, :], lhsT=Sbf[d_][:, h, :], rhs=qe[d_][p][:, h, :],
                                                    start=False, stop=True), reads=[Sbf[d_], qe[d_][p]], writes=[PB[bC]])
            for h in range(4):
                hs = slice(h * 128, (h + 1) * 128)
                kb.op("pe", lambda e, h=h, hs=hs: e.matmul(kb.ps(bD, V3)[:, h, :], lhsT=kdec[d_][p][:, hs], rhs=vt[d_][p][:, hs],
                                                           start=True, stop=True), reads=[kdec[d_][p], vt[d_][p]], writes=[PB[bD]])
            kb.op("dve", lambda e: e.tensor_tensor(out=S32[d_][:], in0=S32[d_][:],
                                                   in1=DEC[d_][p][:, :, 0:1].to_broadcast(V3), op=ALU.mult),
                  reads=[S32[d_], DEC[d_][p]], writes=[S32[d_]])
            kb.op("dve", lambda e: e.tensor_tensor(out=S32[d_][:], in0=S32[d_][:], in1=kb.ps(bD, V3), op=ALU.add),
                  reads=[S32[d_], PB[bD]], writes=[S32[d_]])
            kb.op("act", lambda e: e.copy(out=Sbf[d_][:], in_=S32[d_][:]), reads=[S32[d_]], writes=[Sbf[d_]])
            if not final:
                kb.op("act", lambda e: e.copy(out=OB[:, :, sl], in_=kb.ps(bC, V3)), reads=[PB[bC]], writes=[(OB, i)])
            else:
                kb.dma("sp", gf[d_][:], gv_[:, :, sl], writes=[gf[d_]])
                kb.op("dve", lambda e: e.tensor_tensor(out=o32[d_][:], in0=kb.ps(bC, V3), in1=OB[:, :, sl], op=ALU.add),
                      reads=[PB[bC], (OB, i)], writes=[o32[d_]])
                kb.op("act", lambda e: e.activation(out=sqb[d_][:], in_=o32[d_][:], func=AF.Square), reads=[o32[d_]], writes=[sqb[d_]])
                deferred.append((d_, sl, bD))

        def seq_b(d_, sl, bD):
                kb.op("pe", lambda e: e.matmul(kb.ps(bD), lhsT=ones[:], rhs=sqb[d_][:].rearrange("p a b -> p (a b)"), start=True, stop=True),
                      reads=[ones, sqb[d_]], writes=[PB[bD]])
                kb.op("act", lambda e: e.activation(out=rsd[d_][:].rearrange("p a b -> p (a b)"), in_=kb.ps(bD), func=AF.Ln,
                                                    scale=1.0 / 128.0, bias=EPS), reads=[PB[bD]], writes=[rsd[d_]])
                kb.op("act", lambda e: e.activation(out=rsd[d_][:], in_=rsd[d_][:], func=AF.Exp, scale=-0.5), reads=[rsd[d_]], writes=[rsd[d_]])
                kb.op("dve", lambda e: e.tensor_tensor(out=o32[d_][:], in0=o32[d_][:], in1=rsd[d_][:], op=ALU.mult),
                      reads=[o32[d_], rsd[d_]], writes=[o32[d_]])
                kb.op("dve", lambda e: e.scalar_tensor_tensor(out=yob[d_][:], in0=o32[d_][:], scalar=gn[:, 0:1], in1=gf[d_][:],
                                                              op0=ALU.mult, op1=ALU.mult), reads=[o32[d_], gn, gf[d_]], writes=[yob[d_]])
                kb.dma("pool", yv[:, :, sl], yob[d_][:], reads=[yob[d_]])

        for step in range(NT + 1):
            if preload and step >= 1 and (step % max(1, NT // 8) == 0 or step == NT):
                preload.pop(0)()
                if step == NT:
                    while preload:
                        preload.pop(0)()
            if step == 0:
                pre(0, 0, 1)
                pre(0, 1, 1)
            if step < NT:
                pre(step, 0, 2)
                pre(step, 1, 2)
            if step >= 1:
                seq(step - 1, 0)
                seq(step - 1, 1)
            if step < NT:
                pre(step, 0, 3)
                pre(step, 1, 3)
            if step + 1 < NT:
                pre(step + 1, 0, 1)
                pre(step + 1, 1, 1)
            while deferred:
                seq_b(*deferred.pop(0))
        st["pb"] = 0

    def load_w_into(dst, wap, kblocks, c0, ncols, gt=None, dcol0=0):
        src = wap.rearrange("(k p) c -> p k c", p=128)
        WF = st["WF"]
        kper = max(1, 4096 // ncols)
        k0 = 0
        while k0 < kblocks:
            kn = min(kper, kblocks - k0)
            wfv = WF.t[:, 0:kn * ncols].rearrange("p (k c) -> p k c", k=kn)
            kb.dma("sp", wfv, src[:, k0:k0 + kn, c0:c0 + ncols], writes=[WF])
            if gt is None:
                kb.op("pool", lambda e, wfv=wfv, k0=k0, kn=kn: e.tensor_copy(out=dst.t[:, k0:k0 + kn, dcol0:dcol0 + ncols], in_=wfv),
                      reads=[WF], writes=[(dst, (k0, dcol0))])
            else:
                kb.op("pool", lambda e, wfv=wfv, k0=k0, kn=kn: e.tensor_tensor(
                    out=dst.t[:, k0:k0 + kn, dcol0:dcol0 + ncols], in0=wfv,
                    in1=gt.t[:, k0:k0 + kn, None].to_broadcast([128, kn, ncols]), op=ALU.mult),
                    reads=[WF, gt], writes=[(dst, (k0, dcol0))])
            k0 += kn

    def post_tiles(depth=2):
        return {"yt": [kb.sb([128, D], F32) for _ in range(depth)], "xo": [kb.sb([128, D], F32) for _ in range(depth)],
                "sq": [kb.sb([128, 1], F32) for _ in range(depth)], "rs": [kb.sb([128, 1], F32) for _ in range(depth)],
                "junk": kb.sb([128, D], BF16), "nt": norm_tmp(depth), "depth": depth}

    def post_stage(stage, i, banks, xsrc, gp, xdst, pt, do_norm):
        k = i % pt["depth"]
        yt, xo, sq, rs, junk = pt["yt"][k], pt["xo"][k], pt["sq"][k], pt["rs"][k], pt["junk"]
        nt = pt["nt"][k]
        if stage == "A":
            ba, bb = banks
            kb.dma("sp", xo[:], xsrc[i * 128:(i + 1) * 128, :], writes=[xo])
            kb.op("act", lambda e: e.copy(out=yt[:, 0:512], in_=kb.ps(ba)), reads=[PB[ba]], writes=[(yt, 0)])
            kb.op("dve", lambda e: e.tensor_copy(out=yt[:, 512:1024], in_=kb.ps(bb)), reads=[PB[bb]], writes=[(yt, 1)])
            kb.op("act", lambda e: e.activation(out=junk[:], in_=yt[:], func=AF.Square, scale=1.0 / 32.0, accum_out=sq[:]),
                  reads=[yt], writes=[junk, sq])
            kb.op("act", lambda e: e.activation(out=rs[:], in_=sq[:], func=AF.Ln, bias=EPS), reads=[sq], writes=[rs])
            kb.op("act", lambda e: e.activation(out=rs[:], in_=rs[:], func=AF.Exp, scale=-0.5), reads=[rs], writes=[rs])
        elif stage == "B":
            kb.op("dve", lambda e: e.scalar_tensor_tensor(out=yt[:], in0=yt[:], scalar=rs[:, 0:1], in1=gp[:], op0=ALU.mult, op1=ALU.mult),
                  reads=[yt, rs, gp], writes=[yt])
            kb.op("dve", lambda e: e.tensor_tensor(out=xo[:], in0=xo[:], in1=yt[:], op=ALU.add), reads=[xo, yt], writes=[xo])
            kb.dma("pool", xdst[i * 128:(i + 1) * 128, :], xo[:], reads=[xo])
        elif stage == "C":
            if do_norm:
                njunk, nsq, nrs, hn = nt["junk"], nt["sq"], nt["rs"], nt["hn"]
                kb.op("act", lambda e: e.activation(out=njunk[:], in_=xo[:], func=AF.Square, scale=1.0 / 32.0, accum_out=nsq[:]),
                      reads=[xo], writes=[njunk, nsq])
                kb.op("act", lambda e: e.activation(out=nrs[:], in_=nsq[:], func=AF.Ln, bias=EPS), reads=[nsq], writes=[nrs])
                kb.op("act", lambda e: e.activation(out=nrs[:], in_=nrs[:], func=AF.Exp, scale=-0.5), reads=[nrs], writes=[nrs])
                kb.op("dve", lambda e: e.tensor_scalar(out=hn[:], in0=xo[:], scalar1=nrs[:], scalar2=None, op0=ALU.mult),
                      reads=[xo, nrs], writes=[hn])
        elif stage == "D":
            if do_norm:
                nt_b(nt["hn"], i)

    def post_loop(mm, xsrc, gp, xdst, pt, do_norm, offs):
        banks = {}
        for j in range(min(2, NT)):
            banks[j] = mm(j)
        maxo = max(offs.values())
        for it in range(NT + maxo):
            if it + 2 < NT:
                banks[it + 2] = mm(it + 2)
            for stg in ("A", "B", "C", "D"):
                i = it - offs[stg]
                if 0 <= i < NT:
                    post_stage(stg, i, banks.get(i), xsrc, gp, xdst, pt, do_norm)

    def phase_merge(li, xsrc, xdst):
        phase_begin(0, True)
        MG = HT
        wbr = [kb.sb([128, 4, D], BF16) for _ in range(3)]
        wo_pre = kb.sb([128, 8, D], BF16)
        preload = [(lambda c0=c0: load_w_into(wo_pre, P["w_out"][li], 8, c0, 512, None, c0)) for c0 in (0, 512)]
        ysrc = [Y_.t.rearrange("(k p) t -> p k t", p=128) for Y_ in (YH, YG, YP)]
        gsrc = GATES.t.rearrange("(b r) t -> r b t", b=3)
        yt_ = [[kb.sb([128, 4, 512], BF16) for _ in range(3)] for _ in range(2)]
        gt_ = [kb.sb([128, 3, 512], BF16) for _ in range(2)]
        mm = [[kb.sb([128, 512], BF16) for _ in range(3)] for _ in range(2)]
        ng = 0
        for tt in range(NQ):
            if preload and (tt >= 1 or NQ == 1):
                preload.pop(0)()
                if tt == NQ - 1:
                    while preload:
                        preload.pop(0)()
            ys = yt_[tt % 2]
            for b in range(3):
                kb.dma("sp", ys[b][:], ysrc[b][:, :, tt * 512:(tt + 1) * 512], writes=[ys[b]])
            for db in range(8):
                g = gt_[ng % 2]
                m_ = mm[ng % 2]
                ng += 1
                kb.dma("sp", g[:], gsrc[db * 128:(db + 1) * 128, :, tt * 512:(tt + 1) * 512], writes=[g])
                banks = [next_pb(), next_pb(), next_pb()]
                for b in range(3):
                    for k in range(4):
                        kb.op("pe", lambda e, b=b, k=k, db=db, ys=ys, banks=banks: e.matmul(
                            kb.ps(banks[b]), lhsT=wbr[b][:, k, db * 128:(db + 1) * 128], rhs=ys[b][:, k, :],
                            start=(k == 0), stop=(k == 3)), reads=[wbr[b], ys[b]], writes=[PB[banks[b]]])
                for b in range(3):
                    kb.op("dve", lambda e, b=b, g=g, m_=m_, banks=banks: e.tensor_tensor(
                        out=m_[b][:], in0=kb.ps(banks[b]), in1=g[:, b, :], op=ALU.mult), reads=[PB[banks[b]], g], writes=[m_[b]])
                kb.op("dve", lambda e, m_=m_: e.tensor_tensor(out=m_[0][:], in0=m_[0][:], in1=m_[1][:], op=ALU.add),
                      reads=[m_[0], m_[1]], writes=[m_[0]])
                kb.op("dve", lambda e, m_=m_, db=db, tt=tt: e.tensor_tensor(
                    out=MG[:, db, tt * 512:(tt + 1) * 512], in0=m_[0][:], in1=m_[2][:], op=ALU.add),
                    reads=[m_[0], m_[2]], writes=[(MG, 4 * tt), (MG, 4 * tt + 1), (MG, 4 * tt + 2), (MG, 4 * tt + 3)])
        phase_begin(0, True)
        _pad = [kb.sb([128, 4, D], BF16) for _ in range(3)]
        wo = kb.sb([128, 8, D], BF16)
        gp = kb.sb([128, D], F32)
        kb.dma("sp", gp[:], P["g_mix_post"][li], writes=[gp])
        pt = post_tiles(4)

        def mm(i):
            ba, bb = next_pb(), next_pb()
            gemm_tok(wo, 8, 0, 512, MG, i, ba, srckey=i)
            gemm_tok(wo, 8, 512, 512, MG, i, bb, srckey=i)
            return ba, bb

        post_loop(mm, xsrc, gp, xdst, pt, True, {"A": 0, "B": 1, "C": 2, "D": 3})

    def phase_ffn_up(li):
        phase_begin(0)
        gt = kb.sb([128, 8], F32)
        kb.dma("sp", gt[:], P["g_ffn_pre"][li], writes=[gt])
        cw = kb.sb([128, 22, 3], F32)
        cb = kb.sb([128, 22], F32)
        kb.dma("sp", cw[:], P["ffn_cw"][li], writes=[cw])
        kb.dma("sp", cb[:], P["ffn_cb"][li], writes=[cb])
        raws = [kb.sb([128, L + 2], BF16) for _ in range(2)]
        for r in raws:
            kb.op("pool", lambda e, r=r: e.memset(r[:, 0:1], 0.0), writes=[(r, "h0")])
            kb.op("pool", lambda e, r=r: e.memset(r[:, L + 1:L + 2], 0.0), writes=[(r, "h1")])
        bts = [kb.sb([128, L], BF16) for _ in range(2)]
        gas = [kb.sb([128, L], BF16) for _ in range(2)]
        acc = kb.sb([128, L], F32)
        wfs = [kb.sb([128, 8, 256], F32) for _ in range(2)]
        wbs = [kb.sb([128, 8, 256], BF16) for _ in range(2)]
        wsrc = P["ffn_up"][li].rearrange("(k p) c -> p k c", p=128)

        def prep(m):
            s_ = m % 2
            kb.dma("sp", wfs[s_][:, :, 0:128], wsrc[:, :, m * 128:(m + 1) * 128], writes=[(wfs[s_], 0)])
            kb.dma("sp", wfs[s_][:, :, 128:256], wsrc[:, :, DFF + m * 128:DFF + (m + 1) * 128], writes=[(wfs[s_], 1)])
            kb.op("pool", lambda e: e.tensor_tensor(out=wbs[s_][:], in0=wfs[s_][:],
                                                    in1=gt[:, :, None].to_broadcast([128, 8, 256]), op=ALU.mult),
                  reads=[wfs[s_], gt], writes=[wbs[s_]])
            return wbs[s_]

        wnext = prep(0)
        for m in range(22):
            raw, bt, ga = raws[m % 2], bts[m % 2], gas[m % 2]
            wcur = wnext
            if m + 1 < 22:
                wnext = prep(m + 1)
            for j in range(NQ):
                b1_, b2_ = next_pb(), next_pb()
                gemm_fm(wcur, 8, 0, 128, HT, j, b1_)
                gemm_fm(wcur, 8, 128, 128, HT, j, b2_)
                kb.op("act", lambda e: e.copy(out=raw[:, 1 + j * 512:1 + (j + 1) * 512], in_=kb.ps(b1_)),
                      reads=[PB[b1_]], writes=[(raw, j)])
                kb.op("act", lambda e: e.copy(out=bt[:, j * 512:(j + 1) * 512], in_=kb.ps(b2_)),
                      reads=[PB[b2_]], writes=[(bt, j)])
            kb.op("dve", lambda e: e.tensor_scalar(out=acc[:], in0=raw[:, 0:L], scalar1=cw[:, m, 0:1], scalar2=cb[:, m:m + 1],
                                                   op0=ALU.mult, op1=ALU.add), reads=[raw, cw, cb], writes=[acc])
            kb.op("dve", lambda e: e.scalar_tensor_tensor(out=acc[:], in0=raw[:, 1:L + 1], scalar=cw[:, m, 1:2], in1=acc[:],
                                                          op0=ALU.mult, op1=ALU.add), reads=[raw, cw, acc], writes=[acc])
            kb.op("dve", lambda e: e.scalar_tensor_tensor(out=acc[:], in0=raw[:, 2:L + 2], scalar=cw[:, m, 2:3], in1=acc[:],
                                                          op0=ALU.mult, op1=ALU.add), reads=[raw, cw, acc], writes=[acc])
            kb.op("act", lambda e: e.activation(out=ga[:], in_=acc[:], func=AF.Gelu), reads=[acc], writes=[ga])
            kb.op("dve", lambda e: e.tensor_tensor(out=ga[:], in0=ga[:], in1=bt[:], op=ALU.mult), reads=[ga, bt], writes=[ga])
            kb.dma("pool", ACTfm[m * 128:(m + 1) * 128, :], ga[:], reads=[ga])

    def phase_ffn_down(li, xsrc, xdst, do_norm, preloaded=False):
        phase_begin(0, True)
        wd = kb.sb([128, 22, D], BF16)
        gp = kb.sb([128, D], F32)
        kb.dma("sp", gp[:], P["g_ffn_post"][li], writes=[gp])
        if not preloaded:
            for c0 in range(0, D, 128):
                load_w_into(wd, P["ffn_down"][li], 22, c0, 128, None, c0)
        asrc = ACTfm.t.rearrange("(k p) t -> p k t", p=128)
        at = [kb.sb([128, 22, 256], BF16) for _ in range(2)]
        pt = post_tiles()
        def load_a(c_):
            a = at[c_ % 2]
            kb.dma("sp", a[:, 0:11, :], asrc[:, 0:11, c_ * 256:(c_ + 1) * 256], writes=[(a, 0)])
            kb.dma("sp", a[:, 11:22, :], asrc[:, 11:22, c_ * 256:(c_ + 1) * 256], writes=[(a, 1)])

        load_a(0)

        def mm(i):
            if i % 2 == 0 and i // 2 + 1 < NT // 2:
                load_a(i // 2 + 1)
            a, ii = at[(i // 2) % 2], i % 2
            ba, bb = next_pb(), next_pb()
            for c0, bank in ((0, ba), (512, bb)):
                for k in range(22):
                    kb.op("pe", lambda e, k=k, c0=c0, bank=bank: e.matmul(
                        kb.ps(bank), lhsT=a[:, k, ii * 128:(ii + 1) * 128], rhs=wd[:, k, c0:c0 + 512],
                        start=(k == 0), stop=(k == 21)), reads=[a, wd], writes=[PB[bank]])
            return ba, bb

        post_loop(mm, xsrc, gp, xdst, pt, do_norm, {"A": 0, "B": 0, "C": 1, "D": 2})

    phase_filter(0)
    phase_norm0(x_in)
    for li in range(depth):
        xs = x_in if li == 0 else XB
        phase_proj(li)
        phase_hyena(li)
        phase_gla(li)
        phase_merge(li, xs, XA)
        phase_ffn_up(li)
        lastl = (li == depth - 1)
        if not lastl:
            phase_filter(li + 1, preload_down=li)
        phase_ffn_down(li, XA, out if lastl else XB, not lastl, preloaded=not lastl)
    nc = kb.finish()
    return nc, kb


_CACHE = {}


def _in_maps(inputs, L, depth, nb):
    consts = make_consts(L)
    params = relayout_params(inputs, depth)
    x = np.asarray(inputs["x"], np.float32)
    maps = []
    for b in range(nb):
        m = {"x": np.ascontiguousarray(x[b])}
        m.update(consts)
        m.update(params)
        maps.append(m)
    return maps


def kernel(**inputs):
    x = np.asarray(inputs["x"])
    B, L, _ = x.shape
    depth = int(np.asarray(inputs["w_in"]).shape[0])
    nc, _ = build(L, depth)
    maps = _in_maps(inputs, L, depth, B)
    res = run_bass_kernel_spmd(nc, maps, core_ids=list(range(B)))
    return np.stack([np.asarray(r["out"], np.float32) for r in res.results], 0)
```
